# Optimizing a Trainium2 kernel written in Bass

```python
import math
import jax, jax.numpy as jnp
from jax import lax
import numpy as np

D_MODEL = 1024
BATCH = 16
SEQ = 256
DEPTH = 4
DEC_BATCH = 4
DEC_SEQ = 1024
PAST_LEN = 256

GRID_W = 64
DA_HEADS = 4
DA_HD = 64
DA_VD = 2 * DA_HD
DA_W = DA_HEADS * DA_VD
QBLK = 128
ROPE_THETA = 10000.0
ML_HEADS = 4
ML_HD = 128
ML_W = ML_HEADS * ML_HD
ML_CHUNK = 128
SG_GROUPS = 4
SG_CHUNK = 128
SG_W = 512
SG_GD = SG_W // SG_GROUPS
N_BRANCH = 3
BR_W = 512
D_FF = 2816
CONV_W = 3
EPS = 1e-6
NEG = -1e30
IN_SPLITS = (DA_W, 2 * DA_W, 3 * DA_W,
             3 * DA_W + ML_W, 3 * DA_W + 2 * ML_W, 3 * DA_W + 3 * ML_W, 3 * DA_W + 4 * ML_W,
             3 * DA_W + 4 * ML_W + 4 * ML_HEADS,
             3 * DA_W + 4 * ML_W + 4 * ML_HEADS + 2 * SG_W)
N_IN = 3 * DA_W + 4 * ML_W + 4 * ML_HEADS + 2 * SG_W + N_BRANCH * D_MODEL

kernel_name = 'hybrid_diffusion_prefix_trunk_step'


def rmsnorm(x, g=None):
    xf = x.astype(jnp.float32)
    y = xf * lax.rsqrt(jnp.mean(xf * xf, axis=-1, keepdims=True) + EPS)
    if g is not None:
        y = y * g.astype(jnp.float32)
    return y.astype(x.dtype)


def layernorm(x, g):
    xf = x.astype(jnp.float32)
    xc = xf - jnp.mean(xf, axis=-1, keepdims=True)
    y = xc * lax.rsqrt(jnp.mean(xc * xc, axis=-1, keepdims=True) + EPS) * g.astype(jnp.float32)
    return y.astype(x.dtype)


def dwconv3(x, w, b):
    T = x.shape[1]
    xp = jnp.pad(x, ((0, 0), (1, 1), (0, 0)))
    return xp[:, :T] * w[0] + xp[:, 1:T + 1] * w[1] + xp[:, 2:] * w[2] + b


def to_heads(a, n_heads):
    B, T, W = a.shape
    return a.reshape(B, T, n_heads, W // n_heads).transpose(0, 2, 1, 3)


def from_heads(a):
    B, H, T, Dh = a.shape
    return a.transpose(0, 2, 1, 3).reshape(B, T, H * Dh)


def axial_rope_tables(rows):
    t = jnp.arange(rows * GRID_W)
    row = (t // GRID_W).astype(jnp.float32)
    col = (t % GRID_W).astype(jnp.float32)
    nf = DA_HD // 4
    inv = ROPE_THETA ** (-jnp.arange(nf, dtype=jnp.float32) / nf)
    ang = jnp.stack([row[:, None] * inv, col[:, None] * inv], axis=1)
    return jnp.cos(ang), jnp.sin(ang)


def apply_axial_rope(x, cos, sin):
    nf = DA_HD // 4
    xs = x.reshape(x.shape[:-1] + (2, 2, 2, nf))
    x1, x2 = xs[..., 0, :], xs[..., 1, :]
    c = cos[:, None].astype(x.dtype)
    s = sin[:, None].astype(x.dtype)
    out = jnp.stack([x1 * c - x2 * s, x1 * s + x2 * c], axis=-2)
    return out.reshape(x.shape)


def diff_attention(q, k, v, lam):
    B, H, Tq, _ = q.shape
    nb = Tq // QBLK
    qb = jnp.moveaxis(q.reshape(B, H, nb, QBLK, 2 * DA_HD), 2, 0)
    k1, k2 = k[..., :DA_HD], k[..., DA_HD:]
    scale = DA_HD ** -0.5

    def block(qblk):
        q1, q2 = qblk[..., :DA_HD], qblk[..., DA_HD:]
        p1 = jax.nn.softmax((jnp.einsum('bhqd,bhkd->bhqk', q1, k1) * scale).astype(jnp.float32), axis=-1)
        p2 = jax.nn.softmax((jnp.einsum('bhqd,bhkd->bhqk', q2, k2) * scale).astype(jnp.float32), axis=-1)
        a = (p1 - lam * p2).astype(v.dtype)
        return jnp.einsum('bhqk,bhkv->bhqv', a, v)

    o = lax.map(block, qb)
    return jnp.moveaxis(o, 0, 2).reshape(B, H, Tq, DA_VD)


def mlstm_chunkwise(q, k, v, i_pre, logf, init):
    B, H, T, Dh = q.shape
    nc = T // ML_CHUNK

    def chunks(a):
        return jnp.moveaxis(a.reshape((B, H, nc, ML_CHUNK) + a.shape[3:]), 2, 0)

    causal = jnp.tril(jnp.ones((ML_CHUNK, ML_CHUNK), dtype=bool))

    def step(carry, inp):
        C, n, m = carry
        qc, kc, vc, ic, fc = inp
        b = jnp.cumsum(fc, axis=-1)
        log_w = jnp.where(causal, b[..., :, None] - b[..., None, :] + ic[..., None, :], NEG)
        inter = b + m[..., None]
        m_t = jnp.maximum(inter, jnp.max(log_w, axis=-1))
        w = jnp.exp(log_w - m_t[..., None])
        s_inter = jnp.exp(inter - m_t)
        qk = jnp.einsum('bhtd,bhsd->bhts', qc, kc) * w
        num = jnp.einsum('bhts,bhsv->bhtv', qk, vc) + s_inter[..., None] * jnp.einsum('bhtd,bhdv->bhtv', qc, C)
        den = jnp.sum(qk, axis=-1) + s_inter * jnp.einsum('bhtd,bhd->bht', qc, n)
        h = num / jnp.maximum(jnp.abs(den), jnp.exp(-m_t))[..., None]
        m_new = m_t[..., -1]
        g = jnp.exp(b[..., -1:] - b + ic - m_new[..., None])
        decay = jnp.exp(b[..., -1] + m - m_new)
        C_new = decay[..., None, None] * C + jnp.einsum('bhs,bhsd,bhsv->bhdv', g, kc, vc)
        n_new = decay[..., None] * n + jnp.einsum('bhs,bhsd->bhd', g, kc)
        return (C_new, n_new, m_new), h

    state, hs = lax.scan(step, init, (chunks(q), chunks(k), chunks(v), chunks(i_pre), chunks(logf)))
    return jnp.moveaxis(hs, 0, 2).reshape(B, H, T, Dh), state


def token_mixer(h, l, P, rope, ctx):
    B, T, _ = h.shape
    z = h @ P['w_in'][l]
    da_q, da_k, da_v, ml_q, ml_k, ml_v, ml_o, ml_g, sg_uv, gate_pre = jnp.split(z, IN_SPLITS, axis=-1)

    q = to_heads(da_q, DA_HEADS)
    k = to_heads(da_k, DA_HEADS)
    v = to_heads(da_v, DA_HEADS)
    if rope is not None:
        q = apply_axial_rope(q, rope[0], rope[1])
        k = apply_axial_rope(k, rope[0], rope[1])
    if ctx is None:
        k_all, v_all = k, v
    else:
        k_all = jnp.concatenate([ctx[0].astype(k.dtype), k], axis=2)
        v_all = jnp.concatenate([ctx[1].astype(v.dtype), v], axis=2)
    lq1, lk1, lq2, lk2 = P['da_lambda'][l].astype(jnp.float32)
    lam_init = 0.8 - 0.6 * math.exp(-0.3 * l)
    lam = jnp.exp(jnp.sum(lq1 * lk1)) - jnp.exp(jnp.sum(lq2 * lk2)) + lam_init
    o = diff_attention(q, k_all, v_all, lam)
    y_da = from_heads(rmsnorm(o, P['da_norm_g'][l]) * (1.0 - lam_init))

    qk = jax.nn.silu(dwconv3(jnp.concatenate([ml_q, ml_k], axis=-1), P['ml_conv_w'][l], P['ml_conv_b'][l]))
    mq = to_heads(qk[..., :ML_W], ML_HEADS).astype(jnp.float32)
    mk = to_heads(qk[..., ML_W:], ML_HEADS).astype(jnp.float32) * (ML_HD ** -0.5)
    mv = to_heads(ml_v, ML_HEADS).astype(jnp.float32)
    gp = (ml_g.reshape(B, T, 4, ML_HEADS).astype(jnp.float32)
          + P['ml_gate_b'][l].astype(jnp.float32)).transpose(2, 0, 3, 1)
    i_fw, i_bw = gp[0], gp[1]
    lf_fw, lf_bw = jax.nn.log_sigmoid(gp[2]), jax.nn.log_sigmoid(gp[3])
    if ctx is None:
        zero_state = (jnp.zeros((B, ML_HEADS, ML_HD, ML_HD), jnp.float32),
                      jnp.zeros((B, ML_HEADS, ML_HD), jnp.float32),
                      jnp.zeros((B, ML_HEADS), jnp.float32))
        init_fw, init_bw = zero_state, zero_state
    else:
        C0, n0, m0 = ctx[2].astype(jnp.float32), ctx[3].astype(jnp.float32), ctx[4].astype(jnp.float32)
        init_fw = (C0[:, 0], n0[:, 0], m0[:, 0])
        init_bw = (C0[:, 1], n0[:, 1], m0[:, 1])
    h_fw, st_fw = mlstm_chunkwise(mq, mk, mv, i_fw, lf_fw, init_fw)
    h_bw_r, st_bw = mlstm_chunkwise(jnp.flip(mq, 2), jnp.flip(mk, 2), jnp.flip(mv, 2),
                                    jnp.flip(i_bw, 2), jnp.flip(lf_bw, 2), init_bw)
    h_ml = rmsnorm(h_fw + jnp.flip(h_bw_r, 2), P['ml_norm_g'][l]).astype(h.dtype)
    y_ml = jax.nn.sigmoid(ml_o) * from_heads(h_ml)

    zz = jax.nn.gelu(sg_uv)
    u = zz[..., :SG_W]
    sv = layernorm(zz[..., SG_W:], P['sg_norm_g'][l])
    nc = T // SG_CHUNK
    sv = sv.reshape(B, nc, SG_CHUNK, SG_GROUPS, SG_GD)
    sv = jnp.einsum('gpq,bnqgc->bnpgc', P['sg_w'][l], sv) + P['sg_b'][l].T[:, :, None]
    y_sg = u * sv.reshape(B, T, SG_W)

    br = jnp.stack([y_da, y_ml, y_sg], axis=2)
    proj = jnp.einsum('btnc,ncd->btnd', br, P['w_branch'][l])
    gates = jax.nn.sigmoid(gate_pre).reshape(B, T, N_BRANCH, D_MODEL)
    out = jnp.sum(gates * proj, axis=2) @ P['w_out'][l]
    if ctx is None:
        ctx_out = (k, v,
                   jnp.stack([st_fw[0], st_bw[0]], axis=1),
                   jnp.stack([st_fw[1], st_bw[1]], axis=1),
                   jnp.stack([st_fw[2], st_bw[2]], axis=1))
    else:
        ctx_out = None
    return out, ctx_out


def trunk_layer(x, cond, l, P, rope, ctx):
    mod = (jax.nn.silu(cond) @ P['w_mod'][l] + P['b_mod'][l])[:, None, :]
    sh1, sc1, g1, sh2, sc2, g2 = jnp.split(mod, 6, axis=-1)
    h = rmsnorm(x) * (1.0 + sc1) + sh1
    mix, ctx_out = token_mixer(h, l, P, rope, ctx)
    x = x + g1 * mix
    h = rmsnorm(x) * (1.0 + sc2) + sh2
    u = dwconv3(h @ P['w_up'][l], P['ffn_conv_w'][l], P['ffn_conv_b'][l])
    x = x + g2 * ((jax.nn.silu(u[..., :D_FF]) * u[..., D_FF:]) @ P['w_down'][l])
    return x, ctx_out


def setup_inputs(seed: int = 0) -> dict:
    key = jax.random.key(seed)
    ks = jax.random.split(key, 32)

    def nrm(k, shape, scale):
        return scale * jax.random.normal(k, shape, jnp.float32)

    L = DEPTH
    return {
        'x_prompt': nrm(ks[0], (BATCH, SEQ, D_MODEL), 1.0),
        'x_sample': nrm(ks[1], (DEC_BATCH, DEC_SEQ, D_MODEL), 1.0),
        'c': nrm(ks[2], (DEC_BATCH, D_MODEL), 1.0),
        'cache_k': nrm(ks[3], (DEC_BATCH, L, DA_HEADS, PAST_LEN, 2 * DA_HD), 1.0),
        'cache_v': nrm(ks[4], (DEC_BATCH, L, DA_HEADS, PAST_LEN, DA_VD), 1.0),
        'state_C': nrm(ks[5], (DEC_BATCH, L, 2, ML_HEADS, ML_HD, ML_HD), 0.1),
        'state_n': nrm(ks[6], (DEC_BATCH, L, 2, ML_HEADS, ML_HD), 0.1),
        'state_m': 1.0 + nrm(ks[7], (DEC_BATCH, L, 2, ML_HEADS), 0.5),
        'c_ctx': nrm(ks[8], (D_MODEL,), 1.0),
        'w_mod': nrm(ks[9], (L, D_MODEL, 6 * D_MODEL), D_MODEL ** -0.5),
        'b_mod': nrm(ks[10], (L, 6 * D_MODEL), 0.01),
        'w_in': nrm(ks[11], (L, D_MODEL, N_IN), D_MODEL ** -0.5),
        'da_lambda': nrm(ks[12], (L, 4, DA_HD), 0.1),
        'da_norm_g': 1.0 + nrm(ks[13], (L, DA_VD), 0.01),
        'ml_conv_w': nrm(ks[14], (L, CONV_W, 2 * ML_W), CONV_W ** -0.5),
        'ml_conv_b': nrm(ks[15], (L, 2 * ML_W), 0.01),
        'ml_gate_b': jnp.concatenate([nrm(ks[16], (L, 2, ML_HEADS), 0.1),
                                      3.0 + nrm(ks[17], (L, 2, ML_HEADS), 0.1)], axis=1),
        'ml_norm_g': 1.0 + nrm(ks[18], (L, ML_HD), 0.01),
        'sg_norm_g': 1.0 + nrm(ks[19], (L, SG_W), 0.01),
        'sg_w': nrm(ks[20], (L, SG_GROUPS, SG_CHUNK, SG_CHUNK), SG_CHUNK ** -0.5),
        'sg_b': 1.0 + nrm(ks[21], (L, SG_GROUPS, SG_CHUNK), 0.01),
        'w_branch': nrm(ks[22], (L, N_BRANCH, BR_W, D_MODEL), BR_W ** -0.5),
        'w_out': nrm(ks[23], (L, D_MODEL, D_MODEL), D_MODEL ** -0.5),
        'w_up': nrm(ks[24], (L, D_MODEL, 2 * D_FF), D_MODEL ** -0.5),
        'ffn_conv_w': nrm(ks[25], (L, CONV_W, 2 * D_FF), CONV_W ** -0.5),
        'ffn_conv_b': nrm(ks[26], (L, 2 * D_FF), 0.01),
        'w_down': nrm(ks[27], (L, D_FF, D_MODEL), D_FF ** -0.5),
        'final_g': 1.0 + nrm(ks[28], (D_MODEL,), 0.01),
    }


def reference(x_prompt, x_sample, c, cache_k, cache_v, state_C, state_n, state_m, c_ctx,
              w_mod, b_mod, w_in, da_lambda, da_norm_g, ml_conv_w, ml_conv_b, ml_gate_b,
              ml_norm_g, sg_norm_g, sg_w, sg_b, w_branch, w_out, w_up, ffn_conv_w, ffn_conv_b,
              w_down, final_g):
    P = {'w_mod': w_mod, 'b_mod': b_mod, 'w_in': w_in, 'da_lambda': da_lambda, 'da_norm_g': da_norm_g,
         'ml_conv_w': ml_conv_w, 'ml_conv_b': ml_conv_b, 'ml_gate_b': ml_gate_b, 'ml_norm_g': ml_norm_g,
         'sg_norm_g': sg_norm_g, 'sg_w': sg_w, 'sg_b': sg_b, 'w_branch': w_branch, 'w_out': w_out,
         'w_up': w_up, 'ffn_conv_w': ffn_conv_w, 'ffn_conv_b': ffn_conv_b, 'w_down': w_down}

    xp = x_prompt
    ks_, vs_, Cs_, ns_, ms_ = [], [], [], [], []
    for l in range(DEPTH):
        xp, (k_l, v_l, C_l, n_l, m_l) = trunk_layer(xp, c_ctx[None, :], l, P, None, None)
        ks_.append(k_l)
        vs_.append(v_l)
        Cs_.append(C_l)
        ns_.append(n_l)
        ms_.append(m_l)
    y_prompt = rmsnorm(xp, final_g)
    new_cache_k = jnp.stack(ks_, axis=1)
    new_cache_v = jnp.stack(vs_, axis=1)
    new_state_C = jnp.stack(Cs_, axis=1).astype(x_prompt.dtype)
    new_state_n = jnp.stack(ns_, axis=1).astype(x_prompt.dtype)
    new_state_m = jnp.stack(ms_, axis=1).astype(x_prompt.dtype)

    rows = x_sample.shape[1] // GRID_W
    rope = axial_rope_tables(rows)
    xs = x_sample
    for l in range(DEPTH):
        ctx = (cache_k[:, l], cache_v[:, l], state_C[:, l], state_n[:, l], state_m[:, l])
        xs, _ = trunk_layer(xs, c, l, P, rope, ctx)
    y_sample = rmsnorm(xs, final_g)

    return (y_prompt, y_sample, new_cache_k, new_cache_v, new_state_C, new_state_n, new_state_m)
```

```python
import math
from contextlib import ExitStack
import numpy as np
import concourse.bass as bass
import concourse.mybir as mybir
from concourse.bass_utils import run_bass_kernel_spmd

F32 = mybir.dt.float32
BF16 = mybir.dt.bfloat16
AF = mybir.ActivationFunctionType
ALU = mybir.AluOpType
AX = mybir.AxisListType

L = 4
NIN = 7696
DFF = 2816
EPS = 1e-6


class Tr:
    EP = 30000
    NSEM = 8

    def __init__(s, nc):
        s.nc = nc
        s.E = dict(pe=nc.tensor, act=nc.scalar, dve=nc.vector, pool=nc.gpsimd, sp=nc.sync)
        s.sems = {}
        s.cnt = {}
        s.known = {e: {} for e in s.E}
        s.res = {}
        s.grave = {}
        s.nwait = 0
        s.nops = 0
        s.dcount = {}
        s.dlast = {}

    def _sem(s, st, c):
        mult = 1 if st in s.E else 16
        ep = s.EP // mult
        e = (c - 1) // ep
        lst = s.sems.setdefault(st, [])
        while len(lst) <= e:
            nm = st if isinstance(st, str) else f"{st[0]}{st[1]}"
            lst.append(s.nc.alloc_semaphore(f"s_{nm}_{len(lst)}"))
        return lst[e], ((c - 1) % ep + 1) * mult

    def op(s, eng, fn, reads=(), writes=(), sig=True, dma=None):
        deps = {}

        def add(ev):
            if ev is None:
                return
            st = ev[0]
            if st == 'pe' and eng == 'pe' and dma is None:
                return
            if st not in deps or deps[st][1] < ev[1]:
                deps[st] = ev

        for k in reads:
            r = s.res.get(k)
            if r is None:
                continue
            add(r[0])
            if k[0] == 'ps':
                for ev in r[1].values():
                    add(ev)
        for k in writes:
            r = s.res.get(k)
            if r is None:
                continue
            add(r[0])
            for ev in r[1].values():
                add(ev)
        if dma is not None:
            i = s.dcount.get(dma, 0)
            s.dcount[dma] = i + 1
            dma = (dma, i % s.NSEM)
            add(s.dlast.get(dma))
        kn = s.known[eng]
        for st in sorted(deps, key=lambda a: -deps[a][1]):
            _, c, clk = deps[st]
            if kn.get(st, 0) >= c:
                continue
            if st == 'pe':
                assert s.cnt.get('pe', 0) >= c, "dependency on unsignalled PE op"
            sem, v = s._sem(st, c)
            s.E[eng].wait_ge(sem, v)
            s.nwait += 1
            kn[st] = c
            for a, b in clk.items():
                if kn.get(a, 0) < b:
                    kn[a] = b
        ins = fn()
        s.nops += 1
        st = dma or eng
        c = s.cnt.get(st, 0) + 1
        if sig:
            s.cnt[st] = c
            sem, v = s._sem(st, c)
            ins.then_inc(sem, 16 if dma else 1)
        clk = dict(kn)
        clk[st] = c
        ev = (st, c, clk)
        if dma is not None:
            s.dlast[dma] = ev
        for k in writes:
            s.res[k] = [ev, {}]
        for k in reads:
            r = s.res.setdefault(k, [None, {}])
            r[1][st] = ev
        return ins

    def free(s, keys):
        for k in keys:
            r = s.res.pop(k, None)
            if r is None:
                continue
            evs = list(r[1].values())
            if r[0] is not None:
                evs.append(r[0])
            for ev in evs:
                st = ev[0]
                if st not in s.grave or s.grave[st][1] < ev[1]:
                    s.grave[st] = ev

    def adopt(s, keys):
        for k in keys:
            s.res[k] = [None, dict(s.grave)]

    def finish(s, eng='sp'):
        for st, c in s.cnt.items():
            if st in s.E:
                continue
            ep = s.EP // 16
            for e in range((c - 1) // ep + 1 if c > 0 else 0):
                last = min(c, (e + 1) * ep)
                sem, v = s._sem(st, last)
                s.E[eng].wait_ge(sem, v)


def build(depth=L, phases=('ml', 'da', 'sg', 'ffn'), dbg=False, wsched_n=None):
    nc = bass.Bass("TRN2", target_bir_lowering=False)
    T = Tr(nc)
    LW = depth
    dumps = []
    wrec = []
    wissued = [0]
    LOOK = 2

    def din(n, sh):
        return nc.dram_tensor(n, list(sh), F32, kind="ExternalInput").ap()

    def dout(n, sh):
        return nc.dram_tensor(n, list(sh), F32, kind="ExternalOutput").ap()

    xin = din("xin", [1024, 1024])
    cond = din("cond", [8, 128])
    ck = din("ck", [L, 4, 256, 128])
    cv = din("cv", [L, 4, 256, 128])
    sC = din("sC", [L, 2, 4, 128, 128])
    sn = din("sn", [L, 2, 4, 128])
    sm = din("sm", [L, 8, 1])
    c_ident = din("c_ident", [128, 128])
    c_prot = din("c_prot", [128, 128])
    c_maskF = din("c_maskF", [128, 128])
    c_maskB = din("c_maskB", [128, 128])
    c_sel = din("c_sel", [8, 1024])
    c_dirm = din("c_dirm", [8, 2])
    c_ropeC = din("c_ropeC", [128, 1024])
    c_ropeS = din("c_ropeS", [128, 1024])
    c_maskb = din("c_maskb", [128, 20])
    c_link = din("c_link", [128, 2])
    w_mod = din("w_mod", [LW, 1024, 6144])
    b_mod = din("b_mod", [L, 48, 128])
    w_in = din("w_in", [LW, 1024, NIN])
    da_lambda = din("da_lambda", [1, L * 256])
    da_norm_g = din("da_norm_g", [L, 1, 128])
    ml_conv_w = din("ml_conv_w", [L, 24, 128])
    ml_conv_b = din("ml_conv_b", [L, 8, 128])
    ml_gate_b = din("ml_gate_b", [L, 16, 1])
    ml_norm_g = din("ml_norm_g", [L, 1, 128])
    sg_norm_g = din("sg_norm_g", [L, 1, 512])
    sg_w = din("sg_w", [L, 4, 128, 128])
    sg_b = din("sg_b", [L, 1, 512])
    w_branch = din("w_branch", [LW, 3, 512, 1024])
    w_out = din("w_out", [LW, 1024, 1024])
    w_up = din("w_up", [LW, 1024, 2 * DFF])
    ffn_conv_w = din("ffn_conv_w", [L, 132, 128])
    ffn_conv_b = din("ffn_conv_b", [L, 44, 128])
    w_down = din("w_down", [LW, DFF, 1024])
    final_g = din("final_g", [8, 128])

    y_o = dout("y_o", [1024, 1024])
    ok_o = dout("ok_o", [L, 4, 1024, 128])
    ov_o = dout("ov_o", [L, 4, 1024, 128])
    oC_o = dout("oC_o", [L, 2, 4, 4, 128, 128])
    on_o = dout("on_o", [L, 2, 4, 4, 128, 1])
    om_o = dout("om_o", [L, 8, 4])

    WTinit = dict(w_in=w_in, w_mod=w_mod, w_up=w_up, w_down=w_down, w_out=w_out, w_branch=w_branch)
    def veng(e):
        return nc.vector if e == 'dve' else nc.gpsimd

    def ACT(out, in_, func, r, w, bias=None, scale=None):
        kw = {}
        if bias is not None:
            kw['bias'] = bias
        if scale is not None:
            kw['scale'] = scale
        return T.op('act', lambda: nc.scalar.activation(out=out, in_=in_, func=func, **kw), r, w)

    def TT(e, out, a, b, op, r, w):
        return T.op(e, lambda: veng(e).tensor_tensor(out=out, in0=a, in1=b, op=op), r, w)

    def TS(e, out, a, s1, s2, op0, op1, r, w):
        if op1 is None:
            return T.op(e, lambda: veng(e).tensor_scalar(out=out, in0=a, scalar1=s1, scalar2=None, op0=op0), r, w)
        return T.op(e, lambda: veng(e).tensor_scalar(out=out, in0=a, scalar1=s1, scalar2=s2, op0=op0, op1=op1), r, w)

    def STT(out, a, s, b, op0, op1, r, w):
        return T.op('dve', lambda: nc.vector.scalar_tensor_tensor(out=out, in0=a, scalar=s, in1=b, op0=op0, op1=op1), r, w)

    def CP(e, out, in_, r, w):
        if e == 'act':
            return T.op('act', lambda: nc.scalar.copy(out=out, in_=in_), r, w)
        return T.op(e, lambda: veng(e).tensor_copy(out=out, in_=in_), r, w)

    def MM(out, lhsT, rhs, start, stop, r, w, sig=None):
        if sig is None:
            sig = stop
        return T.op('pe', lambda: nc.tensor.matmul(out, lhsT=lhsT, rhs=rhs, start=start, stop=stop), r, w, sig=sig)

    def TR(out, in_, ident, r, w):
        return T.op('pe', lambda: nc.tensor.transpose(out, in_, ident), r, w)

    def DMA(e, out, in_, r, w, st):
        eng = {'sp': nc.sync, 'pool': nc.gpsimd, 'act': nc.scalar}[e]
        if e == 'pool':
            st = 'dw'
        return T.op(e, lambda: eng.dma_start(out=out, in_=in_), r, w, dma=st)

    def dump(name, ap, keys):
        if not dbg:
            return
        o = dout("dbg_" + name, list(ap.shape))
        dumps.append("dbg_" + name)
        DMA('pool' if ap.dtype != F32 else 'sp', o, ap, keys, (), 'do')

    def MS(e, ap, val, w):
        return T.op(e, lambda: veng(e).memset(ap, val), (), w)

    def sb(n, sh, dt=F32):
        return nc.alloc_sbuf_tensor(n, list(sh), dt)

    uniq = [0]

    def sbt(n, sh, dt=F32):
        uniq[0] += 1
        return nc.sbuf_tensor(f"{n}_u{uniq[0]}", list(sh), dt)

    xT = sb("xT", [128, 8, 1024])
    hT = sb("hT", [128, 8, 1024], BF16)
    NW = 4
    wpool = [sb(f"wp{i}", [128, 4096], BF16) for i in range(NW)]
    wstate = [0]
    ident_f = sb("ident_f", [128, 128])
    ident_b = sb("ident_b", [128, 128], BF16)
    prot_b = sb("prot_b", [128, 128], BF16)
    ones_b = sb("ones_b", [128, 128], BF16)
    maskF = sb("maskF", [128, 128])
    maskB = sb("maskB", [128, 128])
    sel = sb("sel", [8, 1024])
    dirm = sb("dirm", [8, 2])
    ropeC = sb("ropeC", [128, 1024])
    ropeS = sb("ropeS", [128, 1024])
    maskb = sb("maskb", [128, 20])
    link = sb("link", [128, 2])
    eps_t = sb("eps_t", [128, 1])
    one_t = sb("one_t", [128, 1])
    mhalf = sb("mhalf", [128, 4])
    CPm = sb("CPm", [128, L, 3, 128])
    gcol = sb("gcol", [128, 16])
    cond_b = sb("cond_b", [128, 8], BF16)
    modT = sb("modT", [128, 2, 48])
    scp = sb("scp", [128, 2, 16])
    lam = sb("lam", [128, L])
    nlam = sb("nlam", [128, L])
    GB = sb("GB", [8, L, 2])
    rstd = sb("rstd", [128, 1024])
    FS = [sb(f"fs{i}", [128, 1024]) for i in range(4)]
    fstate = [0]
    ones8 = sb("ones8", [8, 128])
    w0n = sb("w0n", [128, 2, 52])

    PA = nc.alloc_psum_tensor("PA", [128, 1024], F32)
    PB = nc.alloc_psum_tensor("PB", [128, 1024], F32)
    PC = nc.alloc_psum_tensor("PC", [128, 1024], F32)
    PD = nc.alloc_psum_tensor("PD", [128, 512], F32)
    PT = nc.alloc_psum_tensor("PT", [128, 1024], BF16)
    P2 = [PA, PB, PC]
    p2state = [0]

    def bank(i):
        if i < 6:
            return P2[i // 2][:, (i % 2) * 512:(i % 2 + 1) * 512]
        return PD[:, :]

    def ps(i):
        return ('ps', i)

    def fs_next():
        i = fstate[0] % 4
        fstate[0] += 1
        return FS[i], ('fs', i)

    def p2_next():
        i = p2state[0] % 3
        p2state[0] += 1
        return P2[i], [ps(2 * i), ps(2 * i + 1)]

    WT = WTinit

    def wmk(desc):
        (name, idx), r0, nk, c0, ncol = desc
        t = WT[name]
        for i in idx:
            t = t[i]
        return t[r0:r0 + nk * 128, c0:c0 + ncol].rearrange("(k p) n -> p k n", p=128)

    def w_issue(j):
        desc = wsched_n[j]
        nk, ncol = desc[2], desc[4]
        i = j % NW
        dst = wpool[i][:, 0:nk * ncol].rearrange("p (k n) -> p k n", k=nk)
        DMA('pool', dst, wmk(desc), (), [('w', i)], 'dw')

    def wload(w2d, r0, nk, c0, ncol):
        desc = (w2d, r0, nk, c0, ncol)
        k = wstate[0]
        wstate[0] += 1
        i = k % NW
        dst = wpool[i][:, 0:nk * ncol].rearrange("p (k n) -> p k n", k=nk)
        if wsched_n is None:
            wrec.append(desc)
            DMA('pool', dst, wmk(desc), (), [('w', i)], 'dw')
        else:
            assert wsched_n[k] == desc
            while wissued[0] <= min(k + LOOK, len(wsched_n) - 1):
                w_issue(wissued[0])
                wissued[0] += 1
        return dst, ('w', i)

    def wview(w2d, r0, nk, c0, ncol):
        return w2d[r0:r0 + nk * 128, c0:c0 + ncol].rearrange("(k p) n -> p k n", p=128)

    DMA('sp', ident_f[:], c_ident, (), ['ident_f'], 'di')
    DMA('pool', ident_b[:], c_ident, (), ['ident_b'], 'dw')
    DMA('pool', prot_b[:], c_prot, (), ['prot_b'], 'dw')
    DMA('sp', maskF[:], c_maskF, (), ['maskF'], 'di')
    DMA('sp', maskB[:], c_maskB, (), ['maskB'], 'di')
    DMA('sp', sel[:], c_sel, (), ['sel'], 'di')
    DMA('sp', dirm[:], c_dirm, (), ['dirm'], 'di')
    DMA('sp', ropeC[:], c_ropeC, (), ['ropeC'], 'di')
    DMA('sp', ropeS[:], c_ropeS, (), ['ropeS'], 'di')
    DMA('sp', maskb[:], c_maskb, (), ['maskb'], 'di')
    DMA('sp', link[:], c_link, (), ['link'], 'di')
    MS('dve', ones_b[:], 1.0, ['ones_b'])
    MS('dve', eps_t[:], EPS, ['eps_t'])
    MS('dve', one_t[:], 1.0, ['one_t'])
    MS('dve', mhalf[:], -0.5, ['mhalf'])
    MS('dve', ones8[:], 1.0, ['ones8'])
    for l in range(L):
        DMA('sp', GB[:, l, 0:1], ml_gate_b[l, 0:8, :], (), ['GB'], 'di')
        DMA('sp', GB[:, l, 1:2], ml_gate_b[l, 8:16, :], (), ['GB'], 'di')

    def colparams(rows_list, dst, dkey):
        st, sk = fs_next()
        r0 = 0
        for ap, R in rows_list:
            DMA('sp', st[r0:r0 + R, 0:128], ap, (), [sk], 'di')
            r0 += R
        TR(bank(6)[:, 0:r0], st[0:r0, 0:128], ident_f[0:r0, 0:r0], [sk, 'ident_f'], [ps(6)])
        CP('dve', dst[:, 0:r0], bank(6)[:, 0:r0], [ps(6)], [dkey])

    for l in range(depth):
        colparams([(b_mod[l], 48), (ml_conv_w[l], 24), (ml_conv_b[l], 8), (ffn_conv_b[l], 44), (da_norm_g[l], 1)],
                  CPm[:, l, 0, :], 'CPm')
        colparams([(ffn_conv_w[l, 0:128, :], 128)], CPm[:, l, 1, :], 'CPm')
        colparams([(ffn_conv_w[l, 128:132, :], 4)], CPm[:, l, 2, :], 'CPm')
    colparams([(final_g, 8), (cond, 8)], gcol[:, :], 'gcol')

    def bmod_c(l):
        return CPm[:, l, 0, 0:48]

    def mlw_c(l, k, c):
        return CPm[:, l, 0, 48 + k * 8 + c:48 + k * 8 + c + 1]

    def mlb_c(l, c):
        return CPm[:, l, 0, 72 + c:73 + c]

    def ffb_c(l, c):
        return CPm[:, l, 0, 80 + c:81 + c]

    def dag_c(l):
        return CPm[:, l, 0, 124:125]

    def ffw_c(l, k, c):
        j = k * 44 + c
        if j < 128:
            return CPm[:, l, 1, j:j + 1]
        return CPm[:, l, 2, j - 128:j - 127]

    ACT(cond_b[:], gcol[:, 8:16], AF.Silu, ['gcol'], ['cond_b'])

    with ExitStack() as es:
        dl = es.enter_context(sbt("dl", [128, L * 256], F32))
        pr = es.enter_context(sbt("pr", [128, L * 128], F32))
        sm2 = es.enter_context(sbt("sm2", [128, L * 2], F32))
        DMA('sp', dl[:], da_lambda.partition_broadcast(128), (), ['dl'], 'di')
        dlv = dl[:].rearrange("p (l a b d) -> p l a b d", l=L, a=2, b=2)
        TT('dve', pr[:].rearrange("p (l a d) -> p l a d", l=L, a=2), dlv[:, :, :, 0, :], dlv[:, :, :, 1, :], ALU.mult,
           ['dl'], ['pr'])
        T.op('dve', lambda: nc.vector.tensor_reduce(out=sm2[:], in_=pr[:].rearrange("p (q d) -> p q d", d=64),
                                                     axis=AX.X, op=ALU.add), ['pr'], ['sm2'])
        ACT(sm2[:], sm2[:], AF.Exp, ['sm2'], ['sm2'])
        s2v = sm2[:].rearrange("p (l a) -> p l a", a=2)
        TT('dve', lam[:], s2v[:, :, 0], s2v[:, :, 1], ALU.subtract, ['sm2'], ['lam'])
        for l in range(L):
            li = 0.8 - 0.6 * math.exp(-0.3 * l)
            TS('dve', lam[:, l:l + 1], lam[:, l:l + 1], li, None, ALU.add, None, ['lam'], ['lam'])
        TS('dve', nlam[:], lam[:], -1.0, None, ALU.mult, None, ['lam'], ['nlam'])
        T.free(['dl', 'pr', 'sm2'])

    for tc in range(8):
        st, sk = fs_next()
        DMA('sp', st[:, :], xin[tc * 128:(tc + 1) * 128, :], (), [sk], 'di')
        for half in range(2):
            bk = half
            for j in range(4):
                dc = half * 4 + j
                TR(bank(bk)[:, j * 128:(j + 1) * 128], st[:, dc * 128:(dc + 1) * 128], ident_f[:], [sk, 'ident_f'], [ps(bk)])
            CP('dve' if half == 0 else 'act', xT[:, half * 4:half * 4 + 4, tc * 128:(tc + 1) * 128],
               bank(bk).rearrange("p (a b) -> p a b", a=4), [ps(bk)], [('x', half * 4 + j) for j in range(4)])

    dump('xT0', xT[:, :, :], [('x', dc) for dc in range(8)])
    def compute_mod_gen(l):
        par = l % 2
        for g in range(12):
            w, wk = wload(('w_mod', (l,)), 0, 8, g * 512, 512)
            for j in range(4):
                col = g * 4 + j
                for kc in range(8):
                    MM(bank(6)[:, col:col + 1], w[:, kc, j * 128:(j + 1) * 128], cond_b[:, kc:kc + 1], kc == 0, kc == 7,
                       [wk, 'cond_b'], [ps(6)])
            yield
        TT('dve', modT[:, par, :], bank(6)[:, 0:48], bmod_c(l), ALU.add, [ps(6), 'CPm'], [('mod', par)])
        TS('dve', scp[:, par, 0:8], modT[:, par, 8:16], 1.0, None, ALU.add, None, [('mod', par)], [('scp', par)])
        TS('dve', scp[:, par, 8:16], modT[:, par, 32:40], 1.0, None, ALU.add, None, [('mod', par)], [('scp', par)])

    def compute_mod(l):
        for _ in compute_mod_gen(l):
            pass


    def rms_stats(src_chunks, rkeys, sq_dst, sqkeys, nfeat):
        n = len(src_chunks)
        for i in range(n):
            ACT(sq_dst[i], src_chunks[i], AF.Square, [rkeys[i]], [sqkeys[i]])
        for th in range(2):
            for i in range(n):
                MM(PA[:, th * 512:(th + 1) * 512], ones_b[:], sq_dst[i][:, th * 512:(th + 1) * 512], i == 0, i == n - 1,
                   [sqkeys[i], 'ones_b'], [ps(th)])
        ACT(rstd[:], PA[:, :], AF.Ln, [ps(0), ps(1), 'eps_t'], ['rstd'], bias=eps_t[:], scale=1.0 / nfeat)
        ACT(rstd[:], rstd[:], AF.Exp, ['rstd'], ['rstd'], scale=-0.5)

    def norm_mod(l, which):
        par = l % 2
        rms_stats([xT[:, dc, :] for dc in range(8)], [('x', dc) for dc in range(8)],
                  [hT[:, dc, :] for dc in range(8)], [('h', dc) for dc in range(8)], 1024.0)
        so = 0 if which == 0 else 24
        for dc in range(8):
            tmp, tk = fs_next()
            STT(tmp[:], xT[:, dc, :], scp[:, par, which * 8 + dc:which * 8 + dc + 1], rstd[:], ALU.mult, ALU.mult,
                [('x', dc), ('scp', par), 'rstd'], [tk])
            ACT(hT[:, dc, :], tmp[:], AF.Identity, [tk, ('mod', par)], [('h', dc)], bias=modT[:, par, so + dc:so + dc + 1])

    def dwconv(l, zps, zkeys, acc, akey, w0, w1, w2, bia, w0nn, w2nn):
        ACT(acc[:, :], zps[:, :], AF.Identity, zkeys + ['CPm'], [akey], bias=bia, scale=w1)
        STT(acc[:, 1:1024], zps[:, 0:1023], w0, acc[:, 1:1024], ALU.mult, ALU.add, zkeys + ['CPm', akey], [akey])
        STT(acc[:, 0:1023], zps[:, 1:1024], w2, acc[:, 0:1023], ALU.mult, ALU.add, zkeys + ['CPm', akey], [akey])
        STT(acc[:, 256:1024:256], zps[:, 255:1023:256], w0nn, acc[:, 256:1024:256], ALU.mult, ALU.add,
            zkeys + [('w0n', l % 2), akey], [akey])
        STT(acc[:, 255:1023:256], zps[:, 256:1024:256], w2nn, acc[:, 255:1023:256], ALU.mult, ALU.add,
            zkeys + [('w0n', l % 2), akey], [akey])

    def prep_w0n(l):
        par = l % 2
        TS('pool', w0n[:, par, 0:44], CPm[:, l, 1, 0:44], link[:, 1:2], -1.0, ALU.mult, ALU.mult, ['CPm', 'link'], [('w0n', par)])
        TS('pool', w2n[:, par, 0:40], CPm[:, l, 1, 88:128], link[:, 1:2], -1.0, ALU.mult, ALU.mult, ['CPm', 'link'], [('w0n', par)])
        TS('pool', w2n[:, par, 40:44], CPm[:, l, 2, 0:4], link[:, 1:2], -1.0, ALU.mult, ALU.mult, ['CPm', 'link'], [('w0n', par)])
        TS('pool', w0n[:, par, 44:52], CPm[:, l, 0, 48:56], link[:, 1:2], -1.0, ALU.mult, ALU.mult, ['CPm', 'link'], [('w0n', par)])
        TS('pool', w2n[:, par, 44:52], CPm[:, l, 0, 64:72], link[:, 1:2], -1.0, ALU.mult, ALU.mult, ['CPm', 'link'], [('w0n', par)])

    w2n = sb("w2n", [128, 2, 52])

    def ffn(l):
        par = l % 2
        with ExitStack() as es:
            act = es.enter_context(sbt("ffn_act", [128, 22, 1024], BF16))
            sa = es.enter_context(sbt("ffn_sa", [128, 4, 1024], F32))
            T.adopt([('act', j) for j in range(22)] + [('sa', j) for j in range(4)])
            norm_mod(l, 1)
            if l == 0:
                dump('mod0', modT[:, 0, :], [('mod', 0)])
                dump('h2', hT[:, :, :], [('h', dc) for dc in range(8)])
                dump('rstd', rstd[:, :], ['rstd'])
            for g in range(6):
                nchunk = 4 if g < 5 else 2
                for ab in range(2):
                    c0 = ab * DFF + g * 512
                    w, wk = wload(('w_up', (l,)), 0, 8, c0, nchunk * 128)
                    for j in range(nchunk):
                        cc = ab * 22 + g * 4 + j
                        zp, zk = p2_next()
                        for th in range(2):
                            for kc in range(8):
                                MM(zp[:, th * 512:(th + 1) * 512], w[:, kc, j * 128:(j + 1) * 128],
                                   hT[:, kc, th * 512:(th + 1) * 512], kc == 0, kc == 7, [wk, ('h', kc)], [zk[th]])
                        acc, ak = fs_next()
                        dwconv(l, zp, zk, acc, ak, ffw_c(l, 0, cc), ffw_c(l, 1, cc), ffw_c(l, 2, cc), ffb_c(l, cc),
                               w0n[:, par, cc:cc + 1],
                               w2n[:, par, cc:cc + 1])
                        if ab == 0:
                            ACT(sa[:, j, :], acc[:, :], AF.Silu, [ak], [('sa', j)])
                        else:
                            TT('pool', act[:, g * 4 + j, :], sa[:, j, :], acc[:, :], ALU.mult, [('sa', j), ak],
                               [('act', g * 4 + j)])
            if l == 0:
                dump('act', act[:, :, :], [('act', j) for j in range(22)])
            for jp in range(4):
                wA, wkA = wload(('w_down', (l,)), 0, 11, jp * 256, 256)
                wB, wkB = wload(('w_down', (l,)), 1408, 11, jp * 256, 256)
                for half, (w, wk) in enumerate(((wA, wkA), (wB, wkB))):
                    for dj in range(2):
                        for th in range(2):
                            bk = dj * 2 + th
                            for kk in range(11):
                                kc = half * 11 + kk
                                MM(bank(bk), w[:, kk, dj * 128:(dj + 1) * 128], act[:, kc, th * 512:(th + 1) * 512],
                                   half == 0 and kk == 0, half == 1 and kk == 10, [wk, ('act', kc)], [ps(bk)])
                for dj in range(2):
                    dc = jp * 2 + dj
                    for th in range(2):
                        bk = dj * 2 + th
                        STT(xT[:, dc, th * 512:(th + 1) * 512], bank(bk), modT[:, par, 40 + dc:41 + dc],
                            xT[:, dc, th * 512:(th + 1) * 512], ALU.mult, ALU.add, [ps(bk), ('mod', par), ('x', dc)],
                            [('x', dc)])
            T.free([('act', j) for j in range(22)] + [('sa', j) for j in range(4)])


    def run_pipeline(jobs, make_gen, nslots, extra=None, extra_every=1):
        active = []
        free_slots = list(range(nslots))
        nxt = 0
        step = 0
        while nxt < len(jobs) or active:
            if nxt < len(jobs) and free_slots:
                sl_ = free_slots.pop(0)
                g = make_gen(jobs[nxt], sl_)
                nxt += 1
                next(g)
                active.append((g, sl_, True))
            still = []
            for (g, sl_, fresh) in active:
                if fresh:
                    still.append((g, sl_, False))
                    continue
                try:
                    next(g)
                    still.append((g, sl_, False))
                except StopIteration:
                    free_slots.append(sl_)
            active = still
            step += 1
            if extra is not None and extra[0] is not None and step % extra_every == 0:
                try:
                    next(extra[0])
                except StopIteration:
                    extra[0] = None
        if extra is not None and extra[0] is not None:
            for _ in extra[0]:
                pass

    def sg_phase(l, ysgT):
        wl = w_in[l]
        with ExitStack() as es:
            uT = es.enter_context(sbt("sg_uT", [128, 4, 1024], BF16))
            sgwT = es.enter_context(sbt("sg_wT", [128, 4, 128], BF16))
            sgwf = es.enter_context(sbt("sg_wf", [128, 4, 128], F32))
            gb = es.enter_context(sbt("sg_gb", [128, 2, 512], F32))
            zz = es.enter_context(sbt("sg_zz", [128, 2, 512], F32))
            svb = es.enter_context(sbt("sg_svb", [128, 2, 512], BF16))
            st6 = es.enter_context(sbt("sg_st", [128, 2, 8], F32))
            keys = [('sg_u', j) for j in range(4)] + ['sg_wT', 'sg_wf', 'sg_gb', ('sg_zz', 0), ('sg_zz', 1), ('sg_svb', 0),
                                                    ('sg_svb', 1), ('sg_st', 0), ('sg_st', 1)]
            T.adopt(keys)
            DMA('sp', gb[:, 0, :], sg_norm_g[l].partition_broadcast(128), (), ['sg_gb'], 'di')
            DMA('sp', gb[:, 1, :], sg_b[l].partition_broadcast(128), (), ['sg_gb'], 'di')
            DMA('sp', sgwf[:, :, :], sg_w[l].rearrange("g p q -> p g q"), (), ['sg_wf'], 'di')
            for g in range(4):
                TR(bank(6)[:, g * 128:(g + 1) * 128], sgwf[:, g, :], ident_f[:], ['sg_wf', 'ident_f'], [ps(6)])
            CP('dve', sgwT[:, :, :], bank(6).rearrange("p (g q) -> p g q", g=4), [ps(6)], ['sg_wT'])
            w, wk = wload(('w_in', (l,)), 0, 8, 3600, 512)
            for j in range(4):
                zp, zk = p2_next()
                for th in range(2):
                    for kc in range(8):
                        MM(zp[:, th * 512:(th + 1) * 512], w[:, kc, j * 128:(j + 1) * 128], hT[:, kc, th * 512:(th + 1) * 512],
                           kc == 0, kc == 7, [wk, ('h', kc)], [zk[th]])
                ACT(uT[:, j, :], zp[:, :], AF.Gelu_apprx_tanh, zk, [('sg_u', j)])
            w, wk = wload(('w_in', (l,)), 0, 8, 4112, 512)
            for tc in range(8):
                b = tc % 2
                bk = 4 + b
                for kc in range(8):
                    MM(bank(bk), hT[:, kc, tc * 128:(tc + 1) * 128], w[:, kc, :], kc == 0, kc == 7, [('h', kc), wk], [ps(bk)])
                ACT(zz[:, b, :], bank(bk), AF.Gelu_apprx_tanh, [ps(bk)], [('sg_zz', b)])
                T.op('dve', lambda: nc.vector.bn_stats(out=st6[:, b, 0:6], in_=zz[:, b, :]), [('sg_zz', b)], [('sg_st', b)])
                T.op('dve', lambda: nc.vector.bn_aggr(out=st6[:, b, 6:8], in_=st6[:, b, 0:6]), [('sg_st', b)], [('sg_st', b)])
                TS('pool', st6[:, b, 7:8], st6[:, b, 7:8], EPS, None, ALU.add, None, [('sg_st', b)], [('sg_st', b)])
                TT('pool', st6[:, b, 7:8], st6[:, b, 7:8], mhalf[:, 0:1], ALU.pow, [('sg_st', b), 'mhalf'], [('sg_st', b)])
                TS('dve', zz[:, b, :], zz[:, b, :], st6[:, b, 6:7], st6[:, b, 7:8], ALU.subtract, ALU.mult,
                   [('sg_zz', b), ('sg_st', b)], [('sg_zz', b)])
                TT('pool', svb[:, b, :], zz[:, b, :], gb[:, 0, :], ALU.mult, [('sg_zz', b), 'sg_gb'], [('sg_svb', b)])
                for g in range(4):
                    MM(bank(6)[:, g * 128:(g + 1) * 128], svb[:, b, g * 128:(g + 1) * 128], sgwT[:, g, :], True, True,
                       [('sg_svb', b), 'sg_wT'], [ps(6)])
                tmp, tk = fs_next()
                TT('dve', tmp[:, 0:512], bank(6), gb[:, 1, :], ALU.add, [ps(6), 'sg_gb'], [tk])
                TT('pool', ysgT[:, :, tc * 128:(tc + 1) * 128], tmp[:, 0:512].rearrange("p (g t) -> p g t", g=4),
                   uT[:, :, tc * 128:(tc + 1) * 128], ALU.mult, [tk] + [('sg_u', j) for j in range(4)],
                   [('ysg', j) for j in range(4)])
            T.free(keys)

    def merge(l, ys):
        par = l % 2
        ynames = ['yda', 'yml', 'ysg']
        with ExitStack() as es:
            mg = es.enter_context(sbt("mg", [128, 8, 1024], BF16))
            accf = es.enter_context(sbt("mg_acc", [128, 4, 1024], F32))
            keys = [('mg', dc) for dc in range(8)] + [('mg_acc', j) for j in range(4)]
            T.adopt(keys)
            for dcg in range(2):
                for n in range(3):
                    w, wk = wload(('w_in', (l,)), 0, 8, 4624 + n * 1024 + dcg * 512, 512)
                    wb, wbk = wload(('w_branch', (l, n)), 0, 4, 0, 1024)
                    for j in range(4):
                        dc = dcg * 4 + j
                        gp, gk = p2_next()
                        for th in range(2):
                            for kc in range(8):
                                MM(gp[:, th * 512:(th + 1) * 512], w[:, kc, j * 128:(j + 1) * 128],
                                   hT[:, kc, th * 512:(th + 1) * 512], kc == 0, kc == 7, [wk, ('h', kc)], [gk[th]])
                        pp, pk = p2_next()
                        for th in range(2):
                            for kc in range(4):
                                MM(pp[:, th * 512:(th + 1) * 512], wb[:, kc, dc * 128:(dc + 1) * 128],
                                   ys[n][:, kc, th * 512:(th + 1) * 512], kc == 0, kc == 3, [wbk, (ynames[n], kc)], [pk[th]])
                        sgt, sk = fs_next()
                        ACT(sgt[:, :], gp[:, :], AF.Sigmoid, gk, [sk])
                        if n == 0:
                            TT('dve', accf[:, j, :], pp[:, :], sgt[:, :], ALU.mult, pk + [sk], [('mg_acc', j)])
                        else:
                            t2, t2k = fs_next()
                            TT('dve', t2[:, :], pp[:, :], sgt[:, :], ALU.mult, pk + [sk], [t2k])
                            if n == 1:
                                TT('pool', accf[:, j, :], accf[:, j, :], t2[:, :], ALU.add, [('mg_acc', j), t2k], [('mg_acc', j)])
                            else:
                                TT('pool', mg[:, dc, :], accf[:, j, :], t2[:, :], ALU.add, [('mg_acc', j), t2k], [('mg', dc)])
            for og_ in range(2):
                w, wk = wload(('w_out', (l,)), 0, 8, og_ * 512, 512)
                for j in range(4):
                    dc = og_ * 4 + j
                    zp, zk = p2_next()
                    for th in range(2):
                        for kc in range(8):
                            MM(zp[:, th * 512:(th + 1) * 512], w[:, kc, j * 128:(j + 1) * 128], mg[:, kc, th * 512:(th + 1) * 512],
                               kc == 0, kc == 7, [wk, ('mg', kc)], [zk[th]])
                    for th in range(2):
                        STT(xT[:, dc, th * 512:(th + 1) * 512], zp[:, th * 512:(th + 1) * 512], modT[:, par, 16 + dc:17 + dc],
                            xT[:, dc, th * 512:(th + 1) * 512], ALU.mult, ALU.add, [zk[th], ('mod', par), ('x', dc)], [('x', dc)])
            T.free(keys)

    def da_phase(l, ydaT):
        wl = w_in[l]
        with ExitStack() as es:
            KT = es.enter_context(sbt("da_KT", [128, 4, 1280], BF16))
            qT = es.enter_context(sbt("da_qT", [128, 4, 1024], BF16))
            V = es.enter_context(sbt("da_V", [128, 10, 512], BF16))
            oall = es.enter_context(sbt("da_oall", [128, 4, 1024], F32))
            ET = es.enter_context(sbt("da_ET", [128, 3, 512], BF16))
            kbf = es.enter_context(sbt("da_kbf", [128, 2, 1024], BF16))
            ckf = es.enter_context(sbt("da_ckf", [128, 2, 128], F32))
            osb = es.enter_context(sbt("da_osb", [128, 2, 256], F32))
            rs = es.enter_context(sbt("da_rs", [128, 2, 256], F32))
            vst = es.enter_context(sbt("da_vst", [128, 2, 512], F32))
            dagl = es.enter_context(sbt("da_gl", [128, 1], F32))
            keys = ([('KT', h) for h in range(4)] + [('qT', h) for h in range(4)] + [('V', c) for c in range(10)] +
                    [('oall', h) for h in range(4)] + [('ET', i) for i in range(3)] + [('kbf', 0), ('kbf', 1), 'ckf', ('osb', 0), ('osb', 1),
                                                                                        ('rs', 0), ('rs', 1), ('vst', 0), ('vst', 1), 'dagl'])
            T.adopt(keys)
            TS('dve', dagl[:, :], dag_c(l), 1.0 - (0.8 - 0.6 * math.exp(-0.3 * l)), None, ALU.mult, None, ['CPm'], ['dagl'])
            for hd in range(4):
                DMA('pool', V[:, 0:2, hd * 128:(hd + 1) * 128], cv[l, hd].rearrange("(c p) d -> p c d", p=128), (),
                    [('V', 0), ('V', 1)], 'dw')
            for hd in range(4):
                DMA('sp', ckf[:, :, 0:128], ck[l, hd].rearrange("(c p) d -> p c d", p=128), (), ['ckf'], 'di')
                for c in range(2):
                    TR(bank(6)[:, c * 128:(c + 1) * 128], ckf[:, c, 0:128], ident_f[:], ['ckf', 'ident_f'], [ps(6)])
                CP('dve', KT[:, hd, 0:256], bank(6)[:, 0:256], [ps(6)], [('KT', hd)])
            wqk = {}

            def proj_job(job, slot):
                which, hd = job
                if hd == 0:
                    c0 = 512 if which == 0 else 0
                    wqk[which] = wload(('w_in', (l,)), 0, 8, c0, 512)
                w, wk = wqk[which]
                zp, zk = P2[slot], [ps(2 * slot), ps(2 * slot + 1)]
                for th in range(2):
                    for kc in range(8):
                        MM(zp[:, th * 512:(th + 1) * 512], w[:, kc, hd * 128:(hd + 1) * 128], hT[:, kc, th * 512:(th + 1) * 512],
                           kc == 0, kc == 7, [wk, ('h', kc)], [zk[th]])
                yield
                CP('act', kbf[:, slot, :], zp[:, :], zk, [('kbf', slot)])
                yield
                sp_, spk = PC, [ps(4), ps(5)]
                for th in range(2):
                    MM(sp_[:, th * 512:(th + 1) * 512], prot_b[:], kbf[:, slot, th * 512:(th + 1) * 512], True, True,
                       ['prot_b', ('kbf', slot)], [spk[th]])
                yield
                t1, t1k = fs_next()
                t2, t2k = fs_next()
                TT('dve', t1[:, :], zp[:, :], ropeC[:, :], ALU.mult, zk + ['ropeC'], [t1k])
                TT('dve', t2[:, :], sp_[:, :], ropeS[:, :], ALU.mult, spk + ['ropeS'], [t2k])
                yield
                if which == 0:
                    TT('pool', t1[:, :], t1[:, :], t2[:, :], ALU.add, [t1k, t2k], [t1k])
                    CP('pool', KT[:, hd, 256:1280], t1[:, :], [t1k], [('KT', hd)])
                    yield
                    for tcg in range(2):
                        for j in range(4):
                            tc = tcg * 4 + j
                            TR(bank(6)[:, j * 128:(j + 1) * 128], t1[:, tc * 128:(tc + 1) * 128], ident_f[:],
                               [t1k, 'ident_f'], [ps(6)])
                        CP('act', vst[:, tcg, :], bank(6), [ps(6)], [('vst', tcg)])
                        DMA('sp', ok_o[l, hd, tcg * 512:(tcg + 1) * 512, :].rearrange("(j p) d -> p j d", p=128),
                            vst[:, tcg, :].rearrange("p (j d) -> p j d", j=4), [('vst', tcg)], (), 'do')
                else:
                    TT('pool', qT[:, hd, :], t1[:, :], t2[:, :], ALU.add, [t1k, t2k], [('qT', hd)])

            run_pipeline([(which, hd) for which in range(2) for hd in range(4)], proj_job, 2)
            w, wk = wload(('w_in', (l,)), 0, 8, 1024, 512)
            for tc in range(8):
                bk = 4 + tc % 2
                for kc in range(8):
                    MM(bank(bk), hT[:, kc, tc * 128:(tc + 1) * 128], w[:, kc, :], kc == 0, kc == 7, [('h', kc), wk], [ps(bk)])
                CP('act', V[:, 2 + tc, :], bank(bk), [ps(bk)], [('V', 2 + tc)])
                CP('dve', vst[:, tc % 2, :], bank(bk), [ps(bk)], [('vst', tc % 2)])
                DMA('sp', ov_o[l, :, tc * 128:(tc + 1) * 128, :].rearrange("h p d -> p h d"),
                    vst[:, tc % 2, :].rearrange("p (h d) -> p h d", h=4), [('vst', tc % 2)], (), 'do')
            iters = [(hd, qt, kc) for hd in range(4) for qt in range(4) for kc in range(10)]
            rsf = rs[:, :, :].rearrange("p a b -> p (a b)")
            osf = osb[:, :, :].rearrange("p a b -> p (a b)")

            PT32 = PT[:, :].bitcast(F32)
            accb = [(bank(4), bank(5), ps(4), ps(5)), (bank(6), PT32, ps(6), ps(7))]
            SR = [(PA, ps(0), ps(1)), (PB, ps(2), ps(3))]

            def emit_qk(i):
                hd, qt, kc = iters[i]
                reg, k0, k1 = SR[i % 2]
                for half in range(2):
                    lo, hi = half * 64, half * 64 + 64
                    MM(reg[:, half * 512:half * 512 + 256], KT[lo:hi, hd, kc * 128:(kc + 1) * 128],
                       qT[lo:hi, hd, qt * 256:(qt + 1) * 256], True, True, [('KT', hd), ('qT', hd)], [k0 if half == 0 else k1])

            def emit_rest(i):
                hd, qt, kc = iters[i]
                reg, k0, k1 = SR[i % 2]
                sbk = i % 3
                ao, as_, ko, ks = accb[(hd * 4 + qt) % 2]
                ACT(ET[:, sbk, :].rearrange("p (a b) -> p a b", a=2), reg[:, :].rearrange("p (a b) -> p a b", a=2)[:, :, 0:256],
                    AF.Exp, [k0, k1, 'maskb'], [('ET', sbk)],
                    bias=maskb[:, (kc // 2) * 4 + qt:(kc // 2) * 4 + qt + 1], scale=0.125)
                MM(ao, V[:, kc, hd * 128:(hd + 1) * 128], ET[:, sbk, :], kc == 0, kc == 9, [('V', kc), ('ET', sbk)], [ko])
                MM(as_, ones_b[:], ET[:, sbk, :], kc == 0, kc == 9, ['ones_b', ('ET', sbk)], [ks])
                if kc == 9:
                    T.op('dve', lambda: nc.vector.reciprocal(out=rsf, in_=as_), [ks], [('rs', 0), ('rs', 1)])
                    TT('dve', osf, ao, rsf, ALU.mult, [ko, ('rs', 0), ('rs', 1)], [('osb', 0), ('osb', 1)])
                    STT(oall[:, hd, qt * 256:(qt + 1) * 256], osb[:, 1, :], nlam[:, l:l + 1], osb[:, 0, :], ALU.mult, ALU.add,
                        [('osb', 0), ('osb', 1), 'nlam'], [('oall', hd)])

            emit_qk(0)
            for i in range(len(iters)):
                if i + 1 < len(iters):
                    emit_qk(i + 1)
                emit_rest(i)
            for hd in range(4):
                ACT(ydaT[:, hd, :], oall[:, hd, :], AF.Square, [('oall', hd)], [('yda', hd)])
                for th in range(2):
                    MM(PA[:, th * 512:(th + 1) * 512], ones_b[:], ydaT[:, hd, th * 512:(th + 1) * 512], True, True,
                       [('yda', hd), 'ones_b'], [ps(th)])
                ACT(rstd[:], PA[:, :], AF.Ln, [ps(0), ps(1), 'eps_t'], ['rstd'], bias=eps_t[:], scale=1.0 / 128.0)
                ACT(rstd[:], rstd[:], AF.Exp, ['rstd'], ['rstd'], scale=-0.5)
                STT(ydaT[:, hd, :], oall[:, hd, :], dagl[:, 0:1], rstd[:], ALU.mult, ALU.mult, [('oall', hd), 'dagl', 'rstd'],
                    [('yda', hd)])
            T.free(keys)

    def ml_phase(l, ymlT):
        par = l % 2
        wl = w_in[l]
        LK = (2, 4, 6)
        with ExitStack() as es:
            COLS = es.enter_context(sbt("ml_cols", [128, 8, 4, 8], F32))
            DECB = es.enter_context(sbt("ml_decb", [128, 8, 8], F32))
            MP = es.enter_context(sbt("ml_mp", [8, 16], F32))
            MPL = es.enter_context(sbt("ml_mpl", [8, 16], F32))
            NM = es.enter_context(sbt("ml_nm", [8, 16], F32))
            DT = es.enter_context(sbt("ml_dt", [8, 16], F32))
            DEC = es.enter_context(sbt("ml_dec", [8, 16], F32))
            k1 = ['ml_cols', 'ml_decb', 'ml_mp', 'ml_mpl', 'ml_nm', 'ml_dt', 'ml_dec']
            T.adopt(k1)
            with ExitStack() as es2:
                Rr = [es2.enter_context(sbt(f"ml_r{i}", [8, 1024], F32)) for i in range(9)]
                rk = [('ml_r', i) for i in range(9)]
                T.adopt(rk)
                R0, R1, R2, R3, R4, R5, R6, R7a, R7b = Rr
                w, wk = wload(('w_in', (l,)), 0, 8, 3584, 16)
                for (pp, c0, kk) in ((PA, 0, [ps(0), ps(1)]), (PB, 8, [ps(2), ps(3)])):
                    for th in range(2):
                        for kc in range(8):
                            MM(pp[0:8, th * 512:(th + 1) * 512], w[:, kc, c0:c0 + 8], hT[:, kc, th * 512:(th + 1) * 512],
                               kc == 0, kc == 7, [wk, ('h', kc)], [kk[th]])
                for (pp, kk, dst, dk, bcol) in ((PA, [ps(0), ps(1)], R0, rk[0], 0), (PB, [ps(2), ps(3)], R1, rk[1], 1)):
                    ACT(R4[:, :], pp[0:8, :], AF.Identity, kk + ['GB'], [rk[4]], bias=GB[:, l, bcol:bcol + 1])
                    ACT(R5[:, :], pp[0:8, ::-1], AF.Identity, kk + ['GB'], [rk[5]], bias=GB[:, l, bcol:bcol + 1])
                    TS('dve', dst[:, :], R4[:, :], dirm[:, 0:1], None, ALU.mult, None, [rk[4], 'dirm'], [dk])
                    STT(dst[:, :], R5[:, :], dirm[:, 1:2], dst[:, :], ALU.mult, ALU.add, [rk[5], 'dirm', dk], [dk])
                ACT(R1[:, :], R1[:, :], AF.Exp, [rk[1]], [rk[1]], scale=-1.0)
                ACT(R1[:, :], R1[:, :], AF.Ln, [rk[1], 'one_t'], [rk[1]], bias=one_t[0:8, :], scale=1.0)
                for c in range(8):
                    sl = slice(c * 128, (c + 1) * 128)
                    T.op('dve', lambda: nc.vector.tensor_tensor_scan(out=R2[:, sl], data0=ones8[:, :], data1=R1[:, sl], initial=0.0,
                                                                      op0=ALU.mult, op1=ALU.add), [rk[1], 'ones8'], [rk[2]])
                TT('dve', R0[:, :], R0[:, :], R2[:, :], ALU.add, [rk[0], rk[2]], [rk[0]])
                for c in range(8):
                    sl = slice(c * 128, (c + 1) * 128)
                    T.op('dve', lambda: nc.vector.tensor_tensor_scan(out=R3[:, sl], data0=R0[:, sl], data1=R0[:, sl], initial=-1e30,
                                                                      op0=ALU.max, op1=ALU.max), [rk[0]], [rk[3]])
                DMA('sp', MP[:, 0:1], sm[l], (), ['ml_mp'], 'di')
                for cs in range(8):
                    sl = slice(cs * 128, (cs + 1) * 128)
                    la = cs * 128 + 127
                    mprev = MP[:, cs:cs + 1]
                    mpk = 'ml_mp'
                    if cs in LK:
                        TS('dve', MPL[:, cs:cs + 1], mprev, link[0:8, 0:1], None, ALU.mult, None, ['ml_mp', 'link'], ['ml_mpl'])
                        mprev = MPL[:, cs:cs + 1]
                        mpk = 'ml_mpl'
                    TS('dve', R3[:, sl], R3[:, sl], mprev, None, ALU.max, None, [rk[3], mpk], [rk[3]])
                    TT('dve', MP[:, cs + 1:cs + 2], R3[:, la:la + 1], R2[:, la:la + 1], ALU.subtract, [rk[3], rk[2]], ['ml_mp'])
                    ACT(R5[:, sl], R3[:, sl], AF.Exp, [rk[3], mpk], [rk[5]], bias=mprev, scale=-1.0)
                    if cs in LK:
                        TS('dve', R5[:, sl], R5[:, sl], link[0:8, 0:1], None, ALU.mult, None, [rk[5], 'link'], [rk[5]])
                    ACT(R4[:, sl], R3[:, sl], AF.Exp, [rk[3]], [rk[4]], bias=R3[:, la:la + 1], scale=-1.0)
                    TS('dve', NM[:, cs:cs + 1], R3[:, la:la + 1], -1.0, None, ALU.mult, None, [rk[3]], ['ml_nm'])
                    ACT(R0[:, sl], R0[:, sl], AF.Exp, [rk[0], 'ml_nm'], [rk[0]], bias=NM[:, cs:cs + 1], scale=1.0)
                    TT('dve', DT[:, cs:cs + 1], mprev, R2[:, la:la + 1], ALU.subtract, [mpk, rk[2]], ['ml_dt'])
                    TT('dve', DT[:, cs:cs + 1], DT[:, cs:cs + 1], MP[:, cs + 1:cs + 2], ALU.subtract, ['ml_dt', 'ml_mp'], ['ml_dt'])
                    ACT(DEC[:, cs:cs + 1], DT[:, cs:cs + 1], AF.Exp, ['ml_dt'], ['ml_dec'])
                    if cs in LK:
                        TS('dve', DEC[:, cs:cs + 1], DEC[:, cs:cs + 1], link[0:8, 0:1], None, ALU.mult, None, ['ml_dec', 'link'],
                           ['ml_dec'])
                TT('dve', R2[:, :], R2[:, :], R3[:, :], ALU.subtract, [rk[2], rk[3]], [rk[2]])
                ACT(R2[:, :], R2[:, :], AF.Exp, [rk[2]], [rk[2]])
                CP('dve', MPL[:, 12:16], MP[:, 2:10:2], ['ml_mp', 'ml_mpl'], ['ml_mpl'])
                DMA('sp', om_o[l], MPL[:, 12:16], ['ml_mpl'], (), 'do')
                for qi, (Q, qk) in enumerate(((R0, rk[0]), (R4, rk[4]), (R5, rk[5]), (R2, rk[2]))):
                    Ro, rok = (R7a, rk[7]) if qi % 2 == 0 else (R7b, rk[8])
                    TS('dve', R6[:, :], Q[:, :], dirm[:, 0:1], None, ALU.mult, None, [qk, 'dirm'], [rk[6]])
                    STT(Ro[:, :], Q[:, ::-1], dirm[:, 1:2], R6[:, :], ALU.mult, ALU.add, [rk[6], 'dirm', qk], [rok])
                    for c in range(8):
                        o0 = (c * 4 + qi) * 8
                        TR(bank(6)[:, o0:o0 + 8], Ro[:, c * 128:(c + 1) * 128], ident_f[0:8, 0:8], [rok, 'ident_f'], [ps(6)])
                CP('dve', COLS[:, :, :, :].rearrange("p a b c -> p (a b c)"), bank(6)[:, 0:256], [ps(6)], ['ml_cols'])
                for r in range(8):
                    MM(bank(5)[:, r * 8:(r + 1) * 8], sel[0:8, r * 128:(r + 1) * 128], DEC[0:8, 0:8], True, True, ['sel', 'ml_dec'],
                       [ps(5)])
                CP('dve', DECB[:, :, :].rearrange("p a b -> p (a b)"), bank(5)[:, 0:64], [ps(5)], ['ml_decb'])
                T.free(rk)
            hsum = es.enter_context(sbt("ml_hs", [128, 8, 512], F32))
            Cn32 = es.enter_context(sbt("ml_cn", [128, 8, 130], F32))
            T.adopt([('hs', c) for c in range(8)] + [('cn', r) for r in range(8)])
            es3 = ExitStack()
            mqT = es3.enter_context(sbt("ml_qT", [128, 4, 1024], BF16))
            mkT = es3.enter_context(sbt("ml_kT", [128, 4, 1024], BF16))
            mva = es3.enter_context(sbt("ml_va", [128, 8, 4, 130], BF16))
            Cnb = es3.enter_context(sbt("ml_cnb", [128, 8, 130], BF16))
            PTs = es3.enter_context(sbt("ml_pts", [128, 6, 128], BF16))
            kg = es3.enter_context(sbt("ml_kg", [128, 6, 128], BF16))
            vg = es3.enter_context(sbt("ml_vg", [128, 6, 130], BF16))
            tB = es3.enter_context(sbt("ml_tb", [128, 6, 130], F32))
            nd = es3.enter_context(sbt("ml_nd", [128, 6, 130], F32))
            dn = es3.enter_context(sbt("ml_dn", [128, 6, 2], F32))
            k2 = ([('mqT', h) for h in range(4)] + [('mkT', h) for h in range(4)] + [('mva', c) for c in range(8)] +
                  [('cnb', r) for r in range(8)] +
                  [(nm_, b) for nm_ in ('pts', 'kg', 'vg', 'tb', 'nd', 'dn') for b in range(6)])
            T.adopt(k2)
            for dr in range(2):
                for hd in range(4):
                    r = dr * 4 + hd
                    DMA('sp', Cn32[:, r, 0:128], sC[l, dr, hd], (), [('cn', r)], 'di')
                    DMA('sp', Cn32[:, r, 128:129], sn[l, dr, hd].rearrange("(p o) -> p o", o=1), (), [('cn', r)], 'di')
                    CP('pool', Cnb[:, r, 0:129], Cn32[:, r, 0:129], [('cn', r)], [('cnb', r)])
            for which, c0 in ((0, 1536), (1, 2048)):
                w, wk = wload(('w_in', (l,)), 0, 8, c0, 512)
                for hd in range(4):
                    zp, zk = p2_next()
                    for th in range(2):
                        for kc in range(8):
                            MM(zp[:, th * 512:(th + 1) * 512], w[:, kc, hd * 128:(hd + 1) * 128], hT[:, kc, th * 512:(th + 1) * 512],
                               kc == 0, kc == 7, [wk, ('h', kc)], [zk[th]])
                    ch = which * 4 + hd
                    acc, ak = fs_next()
                    dwconv(l, zp, zk, acc, ak, mlw_c(l, 0, ch), mlw_c(l, 1, ch), mlw_c(l, 2, ch), mlb_c(l, ch),
                           w0n[:, par, 44 + ch:45 + ch], w2n[:, par, 44 + ch:45 + ch])
                    if which == 0:
                        ACT(mqT[:, hd, :], acc[:, :], AF.Silu, [ak], [('mqT', hd)])
                    else:
                        sgt, sk = fs_next()
                        ACT(sgt[:, :], acc[:, :], AF.Sigmoid, [ak], [sk])
                        STT(mkT[:, hd, :], acc[:, :], 128.0 ** -0.5, sgt[:, :], ALU.mult, ALU.mult, [ak, sk], [('mkT', hd)])
            w, wk = wload(('w_in', (l,)), 0, 8, 2560, 512)
            MS('pool', mva[:, :, :, 128:130], 1.0, [('mva', c) for c in range(8)])
            for tc in range(8):
                bk = 4 + tc % 2
                for kc in range(8):
                    MM(bank(bk), hT[:, kc, tc * 128:(tc + 1) * 128], w[:, kc, :], kc == 0, kc == 7, [('h', kc), wk], [ps(bk)])
                CP('act', mva[:, tc, :, 0:128], bank(bk).rearrange("p (h d) -> p h d", h=4), [ps(bk)], [('mva', tc)])
            def core_iter(cs, hd, dr, slot):
                c = cs if dr == 0 else 7 - cs
                r = dr * 4 + hd
                tsl = slice(c * 128, (c + 1) * 128)
                mask = maskF if dr == 0 else maskB
                mkey = 'maskF' if dr == 0 else 'maskB'
                pb = bank(slot)
                pk_ = ps(slot)
                Sps, Aps, Bps, CNps = pb[:, 0:128], pb[:, 130:259], pb[:, 260:389], pb[:, 0:129]
                MM(Sps, mkT[:, hd, tsl], mqT[:, hd, tsl], True, True, [('mkT', hd), ('mqT', hd)], [pk_])
                TR(PT[:, slot * 128:(slot + 1) * 128], mkT[:, hd, tsl], ident_b[:], [('mkT', hd), 'ident_b'], [ps(7)])
                yield
                TT('dve', PTs[:, slot, :], Sps, mask[:, :], ALU.mult, [pk_, mkey], [('pts', slot)])
                ACT(kg[:, slot, :], PT[:, slot * 128:(slot + 1) * 128], AF.Identity, [ps(7), 'ml_cols'], [('kg', slot)],
                    scale=COLS[:, c, 0, r:r + 1])
                ACT(vg[:, slot, :], mva[:, c, hd, :], AF.Identity, [('mva', c), 'ml_cols'], [('vg', slot)],
                    scale=COLS[:, c, 0, r:r + 1])
                yield
                MM(Aps, PTs[:, slot, :], vg[:, slot, 0:129], True, True, [('pts', slot), ('vg', slot)], [pk_])
                MM(Bps, mqT[:, hd, tsl], Cnb[:, r, 0:129], True, True, [('mqT', hd), ('cnb', r)], [pk_])
                yield
                ACT(tB[:, slot, 0:129], Bps, AF.Identity, [pk_, 'ml_cols'], [('tb', slot)], scale=COLS[:, c, 2, r:r + 1])
                STT(nd[:, slot, 0:129], Aps, COLS[:, c, 1, r:r + 1], tB[:, slot, 0:129], ALU.mult, ALU.add,
                    [pk_, 'ml_cols', ('tb', slot)], [('nd', slot)])
                STT(dn[:, slot, 0:1], nd[:, slot, 128:129], -1.0, nd[:, slot, 128:129], ALU.mult, ALU.max, [('nd', slot)],
                    [('dn', slot)])
                TS('dve', dn[:, slot, 0:1], dn[:, slot, 0:1], COLS[:, c, 3, r:r + 1], None, ALU.max, None,
                   [('dn', slot), 'ml_cols'], [('dn', slot)])
                T.op('dve', lambda: nc.vector.reciprocal(out=dn[:, slot, 1:2], in_=dn[:, slot, 0:1]), [('dn', slot)], [('dn', slot)])
                hs = hsum[:, c, hd * 128:(hd + 1) * 128]
                if cs < 4:
                    TS('dve', hs, nd[:, slot, 0:128], dn[:, slot, 1:2], None, ALU.mult, None, [('nd', slot), ('dn', slot)],
                       [('hs', c)])
                else:
                    STT(hs, nd[:, slot, 0:128], dn[:, slot, 1:2], hs, ALU.mult, ALU.add, [('nd', slot), ('dn', slot), ('hs', c)],
                        [('hs', c)])
                yield
                MM(CNps, kg[:, slot, :], mva[:, c, hd, 0:129], True, True, [('kg', slot), ('mva', c)], [pk_])
                yield
                STT(Cn32[:, r, 0:129], Cn32[:, r, 0:129], DECB[:, r, cs:cs + 1], CNps, ALU.mult, ALU.add,
                    [('cn', r), 'ml_decb', pk_], [('cn', r)])
                CP('pool', Cnb[:, r, 0:129], Cn32[:, r, 0:129], [('cn', r)], [('cnb', r)])
                if cs % 2 == 1:
                    DMA('sp', oC_o[l, dr, cs // 2, hd], Cn32[:, r, 0:128], [('cn', r)], (), 'do')
                    DMA('sp', on_o[l, dr, cs // 2, hd], Cn32[:, r, 128:129], [('cn', r)], (), 'do')

            todo = [(cs, hd, dr) for cs in range(8) for hd in range(4) for dr in range(2)]
            modgen = [compute_mod_gen(l + 1) if l + 1 < depth else None]
            run_pipeline(todo, lambda job, sl_: core_iter(job[0], job[1], job[2], sl_), 6, modgen, 5)
            if l == 0:
                dump('hs', hsum[:, :, :], [('hs', c) for c in range(8)])
                dump('cols', COLS[:, :, :, :], ['ml_cols'])
            T.free(k2)
            es3.close()
            og = es.enter_context(sbt("ml_og", [128, 8, 512], BF16))
            gml4 = es.enter_context(sbt("ml_g4", [128, 512], F32))
            ssq = es.enter_context(sbt("ml_ssq", [128, 2, 4], F32))
            ytm = es.enter_context(sbt("ml_ytm", [128, 2, 512], BF16))
            k3 = [('og', c) for c in range(8)] + ['gml4', ('ssq', 0), ('ssq', 1), ('ytm', 0), ('ytm', 1)]
            T.adopt(k3)
            for h in range(4):
                DMA('sp', gml4[:, h * 128:(h + 1) * 128], ml_norm_g[l].partition_broadcast(128), (), ['gml4'], 'di')
            w, wk = wload(('w_in', (l,)), 0, 8, 3072, 512)
            for tc in range(8):
                bk = 4 + tc % 2
                for kc in range(8):
                    MM(bank(bk), hT[:, kc, tc * 128:(tc + 1) * 128], w[:, kc, :], kc == 0, kc == 7, [('h', kc), wk], [ps(bk)])
                tmp, tk = fs_next()
                ACT(tmp[:, 0:512], bank(bk), AF.Sigmoid, [ps(bk)], [tk])
                TT('pool', og[:, tc, :], tmp[:, 0:512], gml4[:, :], ALU.mult, [tk, 'gml4'], [('og', tc)])
            for c in range(8):
                b2 = c % 2
                sq, sqk = fs_next()
                ACT(sq[:, 0:512], hsum[:, c, :], AF.Square, [('hs', c)], [sqk])
                T.op('dve', lambda: nc.vector.tensor_reduce(out=ssq[:, b2, 0:4], in_=sq[:, 0:512].rearrange("p (h d) -> p h d", h=4),
                                                             axis=AX.X, op=ALU.add), [sqk], [('ssq', b2)])
                TS('pool', ssq[:, b2, 0:4], ssq[:, b2, 0:4], 1.0 / 128.0, EPS, ALU.mult, ALU.add, [('ssq', b2)], [('ssq', b2)])
                TT('pool', ssq[:, b2, 0:4], ssq[:, b2, 0:4], mhalf[:, 0:4], ALU.pow, [('ssq', b2), 'mhalf'], [('ssq', b2)])
                for hd in range(4):
                    hsl = slice(hd * 128, (hd + 1) * 128)
                    STT(ytm[:, b2, hsl], hsum[:, c, hsl], ssq[:, b2, hd:hd + 1], og[:, c, hsl], ALU.mult, ALU.mult,
                        [('hs', c), ('ssq', b2), ('og', c)], [('ytm', b2)])
                for hd in range(4):
                    hsl = slice(hd * 128, (hd + 1) * 128)
                    TR(PT[:, hsl], ytm[:, b2, hsl], ident_b[:], [('ytm', b2), 'ident_b'], [ps(7)])
                CP('act', ymlT[:, :, c * 128:(c + 1) * 128], PT[:, 0:512].rearrange("p (h t) -> p h t", h=4), [ps(7)],
                   [('yml', h) for h in range(4)])
            T.free(k1 + k3 + [('hs', c) for c in range(8)] + [('cn', r) for r in range(8)])

    def mixer(l):
        norm_mod(l, 0)
        with ExitStack() as es:
            ydaT = es.enter_context(sbt("ydaT", [128, 4, 1024], BF16))
            ymlT = es.enter_context(sbt("ymlT", [128, 4, 1024], BF16))
            ysgT = es.enter_context(sbt("ysgT", [128, 4, 1024], BF16))
            yk = [(n, h) for n in ('yda', 'yml', 'ysg') for h in range(4)]
            T.adopt(yk)
            if 'ml' in phases:
                ml_phase(l, ymlT)
            else:
                MS('pool', ymlT[:, :, :], 0.0, [('yml', h) for h in range(4)])
            if 'da' in phases:
                da_phase(l, ydaT)
            else:
                MS('pool', ydaT[:, :, :], 0.0, [('yda', h) for h in range(4)])
            if 'sg' in phases:
                sg_phase(l, ysgT)
            else:
                MS('pool', ysgT[:, :, :], 0.0, [('ysg', h) for h in range(4)])
            if l == 0:
                dump('yda', ydaT[:, :, :], [('yda', h) for h in range(4)])
                dump('yml', ymlT[:, :, :], [('yml', h) for h in range(4)])
                dump('ysg', ysgT[:, :, :], [('ysg', h) for h in range(4)])
            merge(l, [ydaT, ymlT, ysgT])
            T.free(yk)

    compute_mod(0)
    for l in range(depth):
        prep_w0n(l)
        if l + 1 < depth and 'ml' not in phases:
            compute_mod(l + 1)
        if any(p in phases for p in ('ml', 'da', 'sg')):
            mixer(l)
        if 'ffn' in phases:
            ffn(l)

    with ExitStack() as es:
        yT = es.enter_context(sbt("yT", [128, 8, 1024], F32))
        T.adopt([('yT', dc) for dc in range(8)])
        rms_stats([xT[:, dc, :] for dc in range(8)], [('x', dc) for dc in range(8)],
                  [hT[:, dc, :] for dc in range(8)], [('h', dc) for dc in range(8)], 1024.0)
        for dc in range(8):
            STT(yT[:, dc, :], xT[:, dc, :], gcol[:, dc:dc + 1], rstd[:], ALU.mult, ALU.mult, [('x', dc), 'gcol', 'rstd'],
                [('yT', dc)])
        for tc in range(8):
            st, sk = fs_next()
            for half in range(2):
                bk = 2 + half
                for j in range(4):
                    dc = half * 4 + j
                    TR(bank(bk)[:, j * 128:(j + 1) * 128], yT[:, dc, tc * 128:(tc + 1) * 128], ident_f[:],
                       [('yT', dc), 'ident_f'], [ps(bk)])
                CP('dve' if half == 0 else 'act', st[:, half * 512:(half + 1) * 512], bank(bk), [ps(bk)], [sk])
            DMA('sp', y_o[tc * 128:(tc + 1) * 128, :], st[:, :], [sk], (), 'do')
        T.free([('yT', dc) for dc in range(8)])
    T.finish()
    return nc, T, dumps, wrec


def _consts():
    ident = np.eye(128, dtype=np.float32)
    prot = np.zeros((128, 128), np.float32)
    for m in range(128):
        prot[m ^ 16, m] = 1.0
    s = np.arange(128)[:, None]
    t = np.arange(128)[None, :]
    maskF = (s <= t).astype(np.float32)
    maskB = (s >= t).astype(np.float32)
    sel = np.zeros((8, 8, 128), np.float32)
    for r in range(8):
        sel[r, r, :] = 1.0
    dirm = np.zeros((8, 2), np.float32)
    dirm[0:4, 0] = 1.0
    dirm[4:8, 1] = 1.0
    return dict(c_ident=ident, c_prot=prot, c_maskF=maskF, c_maskB=maskB, c_sel=sel.reshape(8, 1024), c_dirm=dirm)


def _rope_tables():
    t = np.arange(1024)
    row = (t // 64).astype(np.float32)
    col = (t % 64).astype(np.float32)
    nf = 16
    inv = (10000.0 ** (-np.arange(nf, dtype=np.float32) / nf)).astype(np.float32)
    ang = np.stack([row[:, None] * inv, col[:, None] * inv], axis=1)
    cos = np.cos(ang).astype(np.float32)
    sin = np.sin(ang).astype(np.float32)
    C = np.zeros((128, 1024), np.float32)
    S = np.zeros((128, 1024), np.float32)
    for d in range(128):
        axis = (d >> 5) & 1
        half = (d >> 4) & 1
        f = d & 15
        C[d] = cos[:, axis, f]
        S[d] = sin[:, axis, f] * (-1.0 if half == 0 else 1.0)
    return C, S


_CACHE = {}


def kernel(x_prompt, x_sample, c, cache_k, cache_v, state_C, state_n, state_m, c_ctx,
           w_mod, b_mod, w_in, da_lambda, da_norm_g, ml_conv_w, ml_conv_b, ml_gate_b,
           ml_norm_g, sg_norm_g, sg_w, sg_b, w_branch, w_out, w_up, ffn_conv_w, ffn_conv_b,
           w_down, final_g, _depth=L, _phases=('ml', 'da', 'sg', 'ffn'), _dbg=False):
    f = lambda a: np.ascontiguousarray(np.asarray(a, dtype=np.float32))
    key = (_depth, tuple(_phases), _dbg)
    if key not in _CACHE:
        rec = build(_depth, _phases, _dbg is True)[3]
        _CACHE[key] = build(_depth, _phases, _dbg is True, rec)
    nc, T, dumps, _ = _CACHE[key]
    dp = _depth
    consts = _consts()
    rC, rS = _rope_tables()
    shared = dict(consts)
    shared.update(
        w_mod=f(w_mod[:dp]), b_mod=f(b_mod).reshape(L, 48, 128), w_in=f(w_in[:dp]), da_lambda=f(da_lambda).reshape(1, L * 256),
        da_norm_g=f(da_norm_g).reshape(L, 1, 128), ml_conv_w=f(ml_conv_w).reshape(L, 24, 128),
        ml_conv_b=f(ml_conv_b).reshape(L, 8, 128), ml_gate_b=f(ml_gate_b).reshape(L, 16, 1),
        ml_norm_g=f(ml_norm_g).reshape(L, 1, 128), sg_norm_g=f(sg_norm_g).reshape(L, 1, 512), sg_w=f(sg_w),
        sg_b=f(sg_b).reshape(L, 1, 512), w_branch=f(w_branch[:dp]), w_out=f(w_out[:dp]), w_up=f(w_up[:dp]),
        ffn_conv_w=f(ffn_conv_w).reshape(L, 132, 128), ffn_conv_b=f(ffn_conv_b).reshape(L, 44, 128), w_down=f(w_down[:dp]),
        final_g=f(final_g).reshape(8, 128))
    x_prompt = f(x_prompt)
    x_sample = f(x_sample)
    in_maps = []
    for core in range(8):
        m = dict(shared)
        if core < 4:
            b = core
            m['xin'] = x_sample[b]
            m['cond'] = f(c)[b].reshape(8, 128)
            m['ck'] = f(cache_k)[b]
            m['cv'] = f(cache_v)[b]
            m['sC'] = f(state_C)[b]
            m['sn'] = f(state_n)[b]
            m['sm'] = f(state_m)[b].reshape(L, 8, 1)
            m['c_ropeC'] = rC
            m['c_ropeS'] = rS
            m['c_maskb'] = np.zeros((128, 20), np.float32)
            lk = np.zeros((128, 2), np.float32)
            lk[:, 0] = 1.0
            m['c_link'] = lk
        else:
            j = core - 4
            m['xin'] = x_prompt[4 * j:4 * j + 4].reshape(1024, 1024)
            m['cond'] = f(c_ctx).reshape(8, 128)
            m['ck'] = np.zeros((L, 4, 256, 128), np.float32)
            m['cv'] = np.zeros((L, 4, 256, 128), np.float32)
            m['sC'] = np.zeros((L, 2, 4, 128, 128), np.float32)
            m['sn'] = np.zeros((L, 2, 4, 128), np.float32)
            m['sm'] = np.zeros((L, 8, 1), np.float32)
            m['c_ropeC'] = np.ones((128, 1024), np.float32)
            m['c_ropeS'] = np.zeros((128, 1024), np.float32)
            mb = np.full((5, 4), -30000.0, np.float32)
            for qt in range(4):
                mb[1 + qt, qt] = 0.0
            m['c_maskb'] = np.tile(mb.reshape(1, 20), (128, 1))
            lk = np.zeros((128, 2), np.float32)
            lk[:, 1] = 1.0
            m['c_link'] = lk
        in_maps.append(m)
    if _dbg == 'maps':
        return nc, in_maps
    if _dbg == 'time':
        res = run_bass_kernel_spmd(nc, in_maps, core_ids=list(range(8)), trace=True)
        return res.exec_time_ns
    res = run_bass_kernel_spmd(nc, in_maps, core_ids=list(range(8)))
    R = res.results
    if _dbg:
        kernel.dbg = [{n: R[c][n] for n in dumps} for c in range(8)]
    y_sample = np.stack([R[b]['y_o'] for b in range(4)], axis=0)
    y_prompt = np.concatenate([R[4 + j]['y_o'].reshape(4, 256, 1024) for j in range(4)], axis=0)
    nk = np.concatenate([R[4 + j]['ok_o'].reshape(L, 4, 4, 256, 128).transpose(2, 0, 1, 3, 4) for j in range(4)], axis=0)
    nv = np.concatenate([R[4 + j]['ov_o'].reshape(L, 4, 4, 256, 128).transpose(2, 0, 1, 3, 4) for j in range(4)], axis=0)
    nC, nn, nm = [], [], []
    for j in range(4):
        oC = R[4 + j]['oC_o']
        on = R[4 + j]['on_o'][..., 0]
        om = R[4 + j]['om_o'].reshape(L, 2, 4, 4)
        for sq in range(4):
            nC.append(np.stack([oC[:, 0, sq], oC[:, 1, 3 - sq]], axis=1))
            nn.append(np.stack([on[:, 0, sq], on[:, 1, 3 - sq]], axis=1))
            nm.append(np.stack([om[:, 0, :, sq], om[:, 1, :, 3 - sq]], axis=1))
    return (y_prompt.astype(np.float32), y_sample.astype(np.float32), np.ascontiguousarray(nk), np.ascontiguousarray(nv),
            np.stack(nC, axis=0), np.stack(nn, axis=0), np.stack(nm, axis=0))
```

```python
import math
from contextlib import ExitStack
import numpy as np
import concourse.bass as bass
import concourse.mybir as mybir
from concourse.bass_utils import run_bass_kernel_spmd

F32 = mybir.dt.float32
BF16 = mybir.dt.bfloat16
AF = mybir.ActivationFunctionType
ALU = mybir.AluOpType
AX = mybir.AxisListType

L = 4
NIN = 7696
DFF = 2816
EPS = 1e-6


class Tr:
    EP = 30000
    NSEM = 8

    def __init__(s, nc):
        s.nc = nc
        s.E = dict(pe=nc.tensor, act=nc.scalar, dve=nc.vector, pool=nc.gpsimd, sp=nc.sync)
        s.sems = {}
        s.cnt = {}
        s.known = {e: {} for e in s.E}
        s.res = {}
        s.grave = {}
        s.nwait = 0
        s.nops = 0
        s.dcount = {}
        s.dlast = {}

    def _sem(s, st, c):
        mult = 1 if st in s.E else 16
        ep = s.EP // mult
        e = (c - 1) // ep
        lst = s.sems.setdefault(st, [])
        while len(lst) <= e:
            nm = st if isinstance(st, str) else f"{st[0]}{st[1]}"
            lst.append(s.nc.alloc_semaphore(f"s_{nm}_{len(lst)}"))
        return lst[e], ((c - 1) % ep + 1) * mult

    def op(s, eng, fn, reads=(), writes=(), sig=True, dma=None):
        deps = {}

        def add(ev):
            if ev is None:
                return
            st = ev[0]
            if st == 'pe' and eng == 'pe' and dma is None:
                return
            if st not in deps or deps[st][1] < ev[1]:
                deps[st] = ev

        for k in reads:
            r = s.res.get(k)
            if r is None:
                continue
            add(r[0])
            if k[0] == 'ps':
                for ev in r[1].values():
                    add(ev)
        for k in writes:
            r = s.res.get(k)
            if r is None:
                continue
            add(r[0])
            for ev in r[1].values():
                add(ev)
        if dma is not None:
            i = s.dcount.get(dma, 0)
            s.dcount[dma] = i + 1
            dma = (dma, i % s.NSEM)
            add(s.dlast.get(dma))
        kn = s.known[eng]
        for st in sorted(deps, key=lambda a: -deps[a][1]):
            _, c, clk = deps[st]
            if kn.get(st, 0) >= c:
                continue
            if st == 'pe':
                assert s.cnt.get('pe', 0) >= c, "dependency on unsignalled PE op"
            sem, v = s._sem(st, c)
            s.E[eng].wait_ge(sem, v)
            s.nwait += 1
            kn[st] = c
            for a, b in clk.items():
                if kn.get(a, 0) < b:
                    kn[a] = b
        ins = fn()
        s.nops += 1
        st = dma or eng
        c = s.cnt.get(st, 0) + 1
        if sig:
            s.cnt[st] = c
            sem, v = s._sem(st, c)
            ins.then_inc(sem, 16 if dma else 1)
        clk = dict(kn)
        clk[st] = c
        ev = (st, c, clk)
        if dma is not None:
            s.dlast[dma] = ev
        for k in writes:
            s.res[k] = [ev, {}]
        for k in reads:
            r = s.res.setdefault(k, [None, {}])
            r[1][st] = ev
        return ins

    def free(s, keys):
        for k in keys:
            r = s.res.pop(k, None)
            if r is None:
                continue
            evs = list(r[1].values())
            if r[0] is not None:
                evs.append(r[0])
            for ev in evs:
                st = ev[0]
                if st not in s.grave or s.grave[st][1] < ev[1]:
                    s.grave[st] = ev

    def adopt(s, keys):
        for k in keys:
            s.res[k] = [None, dict(s.grave)]

    def finish(s, eng='sp'):
        for st, c in s.cnt.items():
            if st in s.E:
                continue
            ep = s.EP // 16
            for e in range((c - 1) // ep + 1 if c > 0 else 0):
                last = min(c, (e + 1) * ep)
                sem, v = s._sem(st, last)
                s.E[eng].wait_ge(sem, v)


def build(depth=L, phases=('ml', 'da', 'sg', 'ffn'), dbg=False, wsched_n=None):
    nc = bass.Bass("TRN2", target_bir_lowering=False)
    T = Tr(nc)
    LW = depth
    dumps = []
    wrec = []
    wissued = [0]
    LOOK = 2

    def din(n, sh):
        return nc.dram_tensor(n, list(sh), F32, kind="ExternalInput").ap()

    def dout(n, sh):
        return nc.dram_tensor(n, list(sh), F32, kind="ExternalOutput").ap()

    xin = din("xin", [1024, 1024])
    cond = din("cond", [8, 128])
    ck = din("ck", [L, 4, 256, 128])
    cv = din("cv", [L, 4, 256, 128])
    sCn = din("sCn", [L, 2, 4, 128, 129])
    sm = din("sm", [L, 8, 1])
    c_ident = din("c_ident", [128, 128])
    c_prot = din("c_prot", [128, 128])
    c_maskF = din("c_maskF", [128, 128])
    c_maskB = din("c_maskB", [128, 128])
    c_sel = din("c_sel", [8, 1024])
    c_dirm = din("c_dirm", [8, 2])
    c_ropeC = din("c_ropeC", [128, 1024])
    c_ropeS = din("c_ropeS", [128, 1024])
    c_maskb = din("c_maskb", [128, 20])
    c_link = din("c_link", [128, 2])
    w_mod = din("w_mod", [LW, 1024, 6144])
    b_mod = din("b_mod", [L, 48, 128])
    w_in = din("w_in", [LW, 1024, NIN])
    da_lambda = din("da_lambda", [1, L * 256])
    da_norm_g = din("da_norm_g", [L, 1, 128])
    ml_conv_w = din("ml_conv_w", [L, 24, 128])
    ml_conv_b = din("ml_conv_b", [L, 8, 128])
    ml_gate_b = din("ml_gate_b", [L, 16, 1])
    ml_norm_g = din("ml_norm_g", [L, 1, 128])
    sg_norm_g = din("sg_norm_g", [L, 1, 512])
    sg_w = din("sg_w", [L, 4, 128, 128])
    sg_b = din("sg_b", [L, 1, 512])
    w_branch = din("w_branch", [LW, 3, 512, 1024])
    w_out = din("w_out", [LW, 1024, 1024])
    w_up = din("w_up", [LW, 1024, 2 * DFF])
    ffn_conv_w = din("ffn_conv_w", [L, 132, 128])
    ffn_conv_b = din("ffn_conv_b", [L, 44, 128])
    w_down = din("w_down", [LW, DFF, 1024])
    final_g = din("final_g", [8, 128])

    y_o = dout("y_o", [1024, 1024])
    ok_o = dout("ok_o", [L, 4, 1024, 128])
    ov_o = dout("ov_o", [L, 4, 1024, 128])
    oCn_o = dout("oCn_o", [L, 2, 4, 4, 128, 129])
    om_o = dout("om_o", [L, 8, 4])

    WTinit = dict(w_in=w_in, w_mod=w_mod, w_up=w_up, w_down=w_down, w_out=w_out, w_branch=w_branch)
    def veng(e):
        return nc.vector if e == 'dve' else nc.gpsimd

    def ACT(out, in_, func, r, w, bias=None, scale=None):
        kw = {}
        if bias is not None:
            kw['bias'] = bias
        if scale is not None:
            kw['scale'] = scale
        return T.op('act', lambda: nc.scalar.activation(out=out, in_=in_, func=func, **kw), r, w)

    def TT(e, out, a, b, op, r, w):
        return T.op(e, lambda: veng(e).tensor_tensor(out=out, in0=a, in1=b, op=op), r, w)

    def TS(e, out, a, s1, s2, op0, op1, r, w):
        if op1 is None:
            return T.op(e, lambda: veng(e).tensor_scalar(out=out, in0=a, scalar1=s1, scalar2=None, op0=op0), r, w)
        return T.op(e, lambda: veng(e).tensor_scalar(out=out, in0=a, scalar1=s1, scalar2=s2, op0=op0, op1=op1), r, w)

    def STT(out, a, s, b, op0, op1, r, w):
        return T.op('dve', lambda: nc.vector.scalar_tensor_tensor(out=out, in0=a, scalar=s, in1=b, op0=op0, op1=op1), r, w)

    def CP(e, out, in_, r, w):
        if e == 'act':
            return T.op('act', lambda: nc.scalar.copy(out=out, in_=in_), r, w)
        return T.op(e, lambda: veng(e).tensor_copy(out=out, in_=in_), r, w)

    def MM(out, lhsT, rhs, start, stop, r, w, sig=None):
        if sig is None:
            sig = stop
        return T.op('pe', lambda: nc.tensor.matmul(out, lhsT=lhsT, rhs=rhs, start=start, stop=stop), r, w, sig=sig)

    def TR(out, in_, ident, r, w):
        return T.op('pe', lambda: nc.tensor.transpose(out, in_, ident), r, w)

    def DMA(e, out, in_, r, w, st):
        eng = {'sp': nc.sync, 'pool': nc.gpsimd, 'act': nc.scalar}[e]
        if e == 'pool':
            st = 'dw'
        return T.op(e, lambda: eng.dma_start(out=out, in_=in_), r, w, dma=st)

    def dump(name, ap, keys):
        if not dbg:
            return
        o = dout("dbg_" + name, list(ap.shape))
        dumps.append("dbg_" + name)
        DMA('pool' if ap.dtype != F32 else 'sp', o, ap, keys, (), 'do')

    def MS(e, ap, val, w):
        return T.op(e, lambda: veng(e).memset(ap, val), (), w)

    def sb(n, sh, dt=F32):
        return nc.alloc_sbuf_tensor(n, list(sh), dt)

    uniq = [0]

    def sbt(n, sh, dt=F32):
        uniq[0] += 1
        return nc.sbuf_tensor(f"{n}_u{uniq[0]}", list(sh), dt)

    xT = sb("xT", [128, 8, 1024])
    hT = sb("hT", [128, 8, 1024], BF16)
    NW = 4
    wpool = [sb(f"wp{i}", [128, 4096], BF16) for i in range(NW)]
    wstate = [0]
    ident_f = sb("ident_f", [128, 128])
    ident_b = sb("ident_b", [128, 128], BF16)
    prot_b = sb("prot_b", [128, 128], BF16)
    ones_b = sb("ones_b", [128, 128], BF16)
    maskF = sb("maskF", [128, 128])
    maskB = sb("maskB", [128, 128])
    sel = sb("sel", [8, 1024])
    dirm = sb("dirm", [8, 2])
    ropeC = sb("ropeC", [128, 1024])
    ropeS = sb("ropeS", [128, 1024])
    maskb = sb("maskb", [128, 20])
    link = sb("link", [128, 2])
    eps_t = sb("eps_t", [128, 1])
    one_t = sb("one_t", [128, 1])
    mhalf = sb("mhalf", [128, 4])
    CPm = sb("CPm", [128, L, 3, 128])
    gcol = sb("gcol", [128, 16])
    cond_b = sb("cond_b", [128, 8], BF16)
    modT = sb("modT", [128, 2, 48])
    scp = sb("scp", [128, 2, 16])
    lam = sb("lam", [128, L])
    nlam = sb("nlam", [128, L])
    GB = sb("GB", [8, L, 2])
    rstd = sb("rstd", [128, 1024])
    FS = [sb(f"fs{i}", [128, 1024]) for i in range(4)]
    fstate = [0]
    ones8 = sb("ones8", [8, 128])
    w0n = sb("w0n", [128, 2, 52])

    PA = nc.alloc_psum_tensor("PA", [128, 1024], F32)
    PB = nc.alloc_psum_tensor("PB", [128, 1024], F32)
    PC = nc.alloc_psum_tensor("PC", [128, 1024], F32)
    PD = nc.alloc_psum_tensor("PD", [128, 512], F32)
    PT = nc.alloc_psum_tensor("PT", [128, 1024], BF16)
    P2 = [PA, PB, PC]
    p2state = [0]

    def bank(i):
        if i < 6:
            return P2[i // 2][:, (i % 2) * 512:(i % 2 + 1) * 512]
        return PD[:, :]

    def ps(i):
        return ('ps', i)

    def fs_next():
        i = fstate[0] % 4
        fstate[0] += 1
        return FS[i], ('fs', i)

    def p2_next():
        i = p2state[0] % 3
        p2state[0] += 1
        return P2[i], [ps(2 * i), ps(2 * i + 1)]

    WT = WTinit

    def wmk(desc):
        (name, idx), r0, nk, c0, ncol = desc
        t = WT[name]
        for i in idx:
            t = t[i]
        return t[r0:r0 + nk * 128, c0:c0 + ncol].rearrange("(k p) n -> p k n", p=128)

    def w_issue(j):
        desc = wsched_n[j]
        nk, ncol = desc[2], desc[4]
        i = j % NW
        dst = wpool[i][:, 0:nk * ncol].rearrange("p (k n) -> p k n", k=nk)
        DMA('pool', dst, wmk(desc), (), [('w', i)], 'dw')

    def wload(w2d, r0, nk, c0, ncol):
        desc = (w2d, r0, nk, c0, ncol)
        k = wstate[0]
        wstate[0] += 1
        i = k % NW
        dst = wpool[i][:, 0:nk * ncol].rearrange("p (k n) -> p k n", k=nk)
        if wsched_n is None:
            wrec.append(desc)
            DMA('pool', dst, wmk(desc), (), [('w', i)], 'dw')
        else:
            assert wsched_n[k] == desc
            while wissued[0] <= min(k + LOOK, len(wsched_n) - 1):
                w_issue(wissued[0])
                wissued[0] += 1
        return dst, ('w', i)

    def wview(w2d, r0, nk, c0, ncol):
        return w2d[r0:r0 + nk * 128, c0:c0 + ncol].rearrange("(k p) n -> p k n", p=128)

    DMA('sp', ident_f[:], c_ident, (), ['ident_f'], 'di')
    DMA('pool', ident_b[:], c_ident, (), ['ident_b'], 'dw')
    DMA('pool', prot_b[:], c_prot, (), ['prot_b'], 'dw')
    DMA('sp', maskF[:], c_maskF, (), ['maskF'], 'di')
    DMA('sp', maskB[:], c_maskB, (), ['maskB'], 'di')
    DMA('sp', sel[:], c_sel, (), ['sel'], 'di')
    DMA('sp', dirm[:], c_dirm, (), ['dirm'], 'di')
    DMA('sp', ropeC[:], c_ropeC, (), ['ropeC'], 'di')
    DMA('sp', ropeS[:], c_ropeS, (), ['ropeS'], 'di')
    DMA('sp', maskb[:], c_maskb, (), ['maskb'], 'di')
    DMA('sp', link[:], c_link, (), ['link'], 'di')
    MS('dve', ones_b[:], 1.0, ['ones_b'])
    MS('dve', eps_t[:], EPS, ['eps_t'])
    MS('dve', one_t[:], 1.0, ['one_t'])
    MS('dve', mhalf[:], -0.5, ['mhalf'])
    MS('dve', ones8[:], 1.0, ['ones8'])
    for l in range(L):
        DMA('sp', GB[:, l, 0:1], ml_gate_b[l, 0:8, :], (), ['GB'], 'di')
        DMA('sp', GB[:, l, 1:2], ml_gate_b[l, 8:16, :], (), ['GB'], 'di')

    def colparams(rows_list, dst, dkey):
        st, sk = fs_next()
        r0 = 0
        for ap, R in rows_list:
            DMA('sp', st[r0:r0 + R, 0:128], ap, (), [sk], 'di')
            r0 += R
        TR(bank(6)[:, 0:r0], st[0:r0, 0:128], ident_f[0:r0, 0:r0], [sk, 'ident_f'], [ps(6)])
        CP('dve', dst[:, 0:r0], bank(6)[:, 0:r0], [ps(6)], [dkey])

    for l in range(depth):
        colparams([(b_mod[l], 48), (ml_conv_w[l], 24), (ml_conv_b[l], 8), (ffn_conv_b[l], 44), (da_norm_g[l], 1)],
                  CPm[:, l, 0, :], 'CPm')
        colparams([(ffn_conv_w[l, 0:128, :], 128)], CPm[:, l, 1, :], 'CPm')
        colparams([(ffn_conv_w[l, 128:132, :], 4)], CPm[:, l, 2, :], 'CPm')
    colparams([(final_g, 8), (cond, 8)], gcol[:, :], 'gcol')

    def bmod_c(l):
        return CPm[:, l, 0, 0:48]

    def mlw_c(l, k, c):
        return CPm[:, l, 0, 48 + k * 8 + c:48 + k * 8 + c + 1]

    def mlb_c(l, c):
        return CPm[:, l, 0, 72 + c:73 + c]

    def ffb_c(l, c):
        return CPm[:, l, 0, 80 + c:81 + c]

    def dag_c(l):
        return CPm[:, l, 0, 124:125]

    def ffw_c(l, k, c):
        j = k * 44 + c
        if j < 128:
            return CPm[:, l, 1, j:j + 1]
        return CPm[:, l, 2, j - 128:j - 127]

    ACT(cond_b[:], gcol[:, 8:16], AF.Silu, ['gcol'], ['cond_b'])

    with ExitStack() as es:
        dl = es.enter_context(sbt("dl", [128, L * 256], F32))
        pr = es.enter_context(sbt("pr", [128, L * 128], F32))
        sm2 = es.enter_context(sbt("sm2", [128, L * 2], F32))
        DMA('sp', dl[:], da_lambda.partition_broadcast(128), (), ['dl'], 'di')
        dlv = dl[:].rearrange("p (l a b d) -> p l a b d", l=L, a=2, b=2)
        TT('dve', pr[:].rearrange("p (l a d) -> p l a d", l=L, a=2), dlv[:, :, :, 0, :], dlv[:, :, :, 1, :], ALU.mult,
           ['dl'], ['pr'])
        T.op('dve', lambda: nc.vector.tensor_reduce(out=sm2[:], in_=pr[:].rearrange("p (q d) -> p q d", d=64),
                                                     axis=AX.X, op=ALU.add), ['pr'], ['sm2'])
        ACT(sm2[:], sm2[:], AF.Exp, ['sm2'], ['sm2'])
        s2v = sm2[:].rearrange("p (l a) -> p l a", a=2)
        TT('dve', lam[:], s2v[:, :, 0], s2v[:, :, 1], ALU.subtract, ['sm2'], ['lam'])
        for l in range(L):
            li = 0.8 - 0.6 * math.exp(-0.3 * l)
            TS('dve', lam[:, l:l + 1], lam[:, l:l + 1], li, None, ALU.add, None, ['lam'], ['lam'])
        TS('dve', nlam[:], lam[:], -1.0, None, ALU.mult, None, ['lam'], ['nlam'])
        T.free(['dl', 'pr', 'sm2'])

    for tc in range(8):
        st, sk = fs_next()
        DMA('sp', st[:, :], xin[tc * 128:(tc + 1) * 128, :], (), [sk], 'di')
        for half in range(2):
            bk = half
            for j in range(4):
                dc = half * 4 + j
                TR(bank(bk)[:, j * 128:(j + 1) * 128], st[:, dc * 128:(dc + 1) * 128], ident_f[:], [sk, 'ident_f'], [ps(bk)])
            CP('dve' if half == 0 else 'act', xT[:, half * 4:half * 4 + 4, tc * 128:(tc + 1) * 128],
               bank(bk).rearrange("p (a b) -> p a b", a=4), [ps(bk)], [('x', half * 4 + j) for j in range(4)])

    dump('xT0', xT[:, :, :], [('x', dc) for dc in range(8)])
    def compute_mod_gen(l):
        par = l % 2
        for g in range(12):
            w, wk = wload(('w_mod', (l,)), 0, 8, g * 512, 512)
            for j in range(4):
                col = g * 4 + j
                for kc in range(8):
                    MM(bank(6)[:, col:col + 1], w[:, kc, j * 128:(j + 1) * 128], cond_b[:, kc:kc + 1], kc == 0, kc == 7,
                       [wk, 'cond_b'], [ps(6)])
            yield
        TT('dve', modT[:, par, :], bank(6)[:, 0:48], bmod_c(l), ALU.add, [ps(6), 'CPm'], [('mod', par)])
        TS('dve', scp[:, par, 0:8], modT[:, par, 8:16], 1.0, None, ALU.add, None, [('mod', par)], [('scp', par)])
        TS('dve', scp[:, par, 8:16], modT[:, par, 32:40], 1.0, None, ALU.add, None, [('mod', par)], [('scp', par)])

    def compute_mod(l):
        for _ in compute_mod_gen(l):
            pass


    def rms_stats(src_chunks, rkeys, sq_dst, sqkeys, nfeat):
        n = len(src_chunks)
        for i in range(n):
            ACT(sq_dst[i], src_chunks[i], AF.Square, [rkeys[i]], [sqkeys[i]])
        for th in range(2):
            for i in range(n):
                MM(PA[:, th * 512:(th + 1) * 512], ones_b[:], sq_dst[i][:, th * 512:(th + 1) * 512], i == 0, i == n - 1,
                   [sqkeys[i], 'ones_b'], [ps(th)])
        ACT(rstd[:], PA[:, :], AF.Ln, [ps(0), ps(1), 'eps_t'], ['rstd'], bias=eps_t[:], scale=1.0 / nfeat)
        ACT(rstd[:], rstd[:], AF.Exp, ['rstd'], ['rstd'], scale=-0.5)

    def norm_mod(l, which):
        par = l % 2
        rms_stats([xT[:, dc, :] for dc in range(8)], [('x', dc) for dc in range(8)],
                  [hT[:, dc, :] for dc in range(8)], [('h', dc) for dc in range(8)], 1024.0)
        so = 0 if which == 0 else 24
        for dc in range(8):
            tmp, tk = fs_next()
            STT(tmp[:], xT[:, dc, :], scp[:, par, which * 8 + dc:which * 8 + dc + 1], rstd[:], ALU.mult, ALU.mult,
                [('x', dc), ('scp', par), 'rstd'], [tk])
            ACT(hT[:, dc, :], tmp[:], AF.Identity, [tk, ('mod', par)], [('h', dc)], bias=modT[:, par, so + dc:so + dc + 1])

    def dwconv(l, zps, zkeys, acc, akey, w0, w1, w2, bia, w0nn, w2nn):
        ACT(acc[:, :], zps[:, :], AF.Identity, zkeys + ['CPm'], [akey], bias=bia, scale=w1)
        STT(acc[:, 1:1024], zps[:, 0:1023], w0, acc[:, 1:1024], ALU.mult, ALU.add, zkeys + ['CPm', akey], [akey])
        STT(acc[:, 0:1023], zps[:, 1:1024], w2, acc[:, 0:1023], ALU.mult, ALU.add, zkeys + ['CPm', akey], [akey])
        STT(acc[:, 256:1024:256], zps[:, 255:1023:256], w0nn, acc[:, 256:1024:256], ALU.mult, ALU.add,
            zkeys + [('w0n', l % 2), akey], [akey])
        STT(acc[:, 255:1023:256], zps[:, 256:1024:256], w2nn, acc[:, 255:1023:256], ALU.mult, ALU.add,
            zkeys + [('w0n', l % 2), akey], [akey])

    def prep_w0n(l):
        par = l % 2
        TS('pool', w0n[:, par, 0:44], CPm[:, l, 1, 0:44], link[:, 1:2], -1.0, ALU.mult, ALU.mult, ['CPm', 'link'], [('w0n', par)])
        TS('pool', w2n[:, par, 0:40], CPm[:, l, 1, 88:128], link[:, 1:2], -1.0, ALU.mult, ALU.mult, ['CPm', 'link'], [('w0n', par)])
        TS('pool', w2n[:, par, 40:44], CPm[:, l, 2, 0:4], link[:, 1:2], -1.0, ALU.mult, ALU.mult, ['CPm', 'link'], [('w0n', par)])
        TS('pool', w0n[:, par, 44:52], CPm[:, l, 0, 48:56], link[:, 1:2], -1.0, ALU.mult, ALU.mult, ['CPm', 'link'], [('w0n', par)])
        TS('pool', w2n[:, par, 44:52], CPm[:, l, 0, 64:72], link[:, 1:2], -1.0, ALU.mult, ALU.mult, ['CPm', 'link'], [('w0n', par)])

    w2n = sb("w2n", [128, 2, 52])

    def ffn(l):
        par = l % 2
        with ExitStack() as es:
            act = es.enter_context(sbt("ffn_act", [128, 22, 1024], BF16))
            sa = es.enter_context(sbt("ffn_sa", [128, 4, 1024], F32))
            T.adopt([('act', j) for j in range(22)] + [('sa', j) for j in range(4)])
            norm_mod(l, 1)
            if l == 0:
                dump('mod0', modT[:, 0, :], [('mod', 0)])
                dump('h2', hT[:, :, :], [('h', dc) for dc in range(8)])
                dump('rstd', rstd[:, :], ['rstd'])
            for g in range(6):
                nchunk = 4 if g < 5 else 2
                for ab in range(2):
                    c0 = ab * DFF + g * 512
                    w, wk = wload(('w_up', (l,)), 0, 8, c0, nchunk * 128)
                    for j in range(nchunk):
                        cc = ab * 22 + g * 4 + j
                        zp, zk = p2_next()
                        for th in range(2):
                            for kc in range(8):
                                MM(zp[:, th * 512:(th + 1) * 512], w[:, kc, j * 128:(j + 1) * 128],
                                   hT[:, kc, th * 512:(th + 1) * 512], kc == 0, kc == 7, [wk, ('h', kc)], [zk[th]])
                        acc, ak = fs_next()
                        dwconv(l, zp, zk, acc, ak, ffw_c(l, 0, cc), ffw_c(l, 1, cc), ffw_c(l, 2, cc), ffb_c(l, cc),
                               w0n[:, par, cc:cc + 1],
                               w2n[:, par, cc:cc + 1])
                        if ab == 0:
                            ACT(sa[:, j, :], acc[:, :], AF.Silu, [ak], [('sa', j)])
                        else:
                            TT('pool', act[:, g * 4 + j, :], sa[:, j, :], acc[:, :], ALU.mult, [('sa', j), ak],
                               [('act', g * 4 + j)])
            if l == 0:
                dump('act', act[:, :, :], [('act', j) for j in range(22)])
            for jp in range(4):
                wA, wkA = wload(('w_down', (l,)), 0, 11, jp * 256, 256)
                wB, wkB = wload(('w_down', (l,)), 1408, 11, jp * 256, 256)
                for half, (w, wk) in enumerate(((wA, wkA), (wB, wkB))):
                    for dj in range(2):
                        for th in range(2):
                            bk = dj * 2 + th
                            for kk in range(11):
                                kc = half * 11 + kk
                                MM(bank(bk), w[:, kk, dj * 128:(dj + 1) * 128], act[:, kc, th * 512:(th + 1) * 512],
                                   half == 0 and kk == 0, half == 1 and kk == 10, [wk, ('act', kc)], [ps(bk)])
                for dj in range(2):
                    dc = jp * 2 + dj
                    for th in range(2):
                        bk = dj * 2 + th
                        STT(xT[:, dc, th * 512:(th + 1) * 512], bank(bk), modT[:, par, 40 + dc:41 + dc],
                            xT[:, dc, th * 512:(th + 1) * 512], ALU.mult, ALU.add, [ps(bk), ('mod', par), ('x', dc)],
                            [('x', dc)])
            T.free([('act', j) for j in range(22)] + [('sa', j) for j in range(4)])


    def run_pipeline(jobs, make_gen, nslots, extra=None, extra_every=1):
        active = []
        free_slots = list(range(nslots))
        nxt = 0
        step = 0
        while nxt < len(jobs) or active:
            if nxt < len(jobs) and free_slots:
                sl_ = free_slots.pop(0)
                g = make_gen(jobs[nxt], sl_)
                nxt += 1
                next(g)
                active.append((g, sl_, True))
            still = []
            for (g, sl_, fresh) in active:
                if fresh:
                    still.append((g, sl_, False))
                    continue
                try:
                    next(g)
                    still.append((g, sl_, False))
                except StopIteration:
                    free_slots.append(sl_)
            active = still
            step += 1
            if extra is not None and extra[0] is not None and step % extra_every == 0:
                try:
                    next(extra[0])
                except StopIteration:
                    extra[0] = None
        if extra is not None and extra[0] is not None:
            for _ in extra[0]:
                pass

    def sg_phase(l, ysgT):
        wl = w_in[l]
        with ExitStack() as es:
            uT = es.enter_context(sbt("sg_uT", [128, 4, 1024], BF16))
            sgwT = es.enter_context(sbt("sg_wT", [128, 4, 128], BF16))
            sgwf = es.enter_context(sbt("sg_wf", [128, 4, 128], F32))
            gb = es.enter_context(sbt("sg_gb", [128, 2, 512], F32))
            zz = es.enter_context(sbt("sg_zz", [128, 2, 512], F32))
            svb = es.enter_context(sbt("sg_svb", [128, 2, 512], BF16))
            st6 = es.enter_context(sbt("sg_st", [128, 2, 8], F32))
            keys = [('sg_u', j) for j in range(4)] + ['sg_wT', 'sg_wf', 'sg_gb', ('sg_zz', 0), ('sg_zz', 1), ('sg_svb', 0),
                                                    ('sg_svb', 1), ('sg_st', 0), ('sg_st', 1)]
            T.adopt(keys)
            DMA('sp', gb[:, 0, :], sg_norm_g[l].partition_broadcast(128), (), ['sg_gb'], 'di')
            DMA('sp', gb[:, 1, :], sg_b[l].partition_broadcast(128), (), ['sg_gb'], 'di')
            DMA('sp', sgwf[:, :, :], sg_w[l].rearrange("g p q -> p g q"), (), ['sg_wf'], 'di')
            for g in range(4):
                TR(bank(6)[:, g * 128:(g + 1) * 128], sgwf[:, g, :], ident_f[:], ['sg_wf', 'ident_f'], [ps(6)])
            CP('dve', sgwT[:, :, :], bank(6).rearrange("p (g q) -> p g q", g=4), [ps(6)], ['sg_wT'])
            w, wk = wload(('w_in', (l,)), 0, 8, 3600, 512)
            for j in range(4):
                zp, zk = p2_next()
                for th in range(2):
                    for kc in range(8):
                        MM(zp[:, th * 512:(th + 1) * 512], w[:, kc, j * 128:(j + 1) * 128], hT[:, kc, th * 512:(th + 1) * 512],
                           kc == 0, kc == 7, [wk, ('h', kc)], [zk[th]])
                ACT(uT[:, j, :], zp[:, :], AF.Gelu_apprx_tanh, zk, [('sg_u', j)])
            w, wk = wload(('w_in', (l,)), 0, 8, 4112, 512)
            for tc in range(8):
                b = tc % 2
                bk = 4 + b
                for kc in range(8):
                    MM(bank(bk), hT[:, kc, tc * 128:(tc + 1) * 128], w[:, kc, :], kc == 0, kc == 7, [('h', kc), wk], [ps(bk)])
                ACT(zz[:, b, :], bank(bk), AF.Gelu_apprx_tanh, [ps(bk)], [('sg_zz', b)])
                T.op('dve', lambda: nc.vector.bn_stats(out=st6[:, b, 0:6], in_=zz[:, b, :]), [('sg_zz', b)], [('sg_st', b)])
                T.op('dve', lambda: nc.vector.bn_aggr(out=st6[:, b, 6:8], in_=st6[:, b, 0:6]), [('sg_st', b)], [('sg_st', b)])
                TS('pool', st6[:, b, 7:8], st6[:, b, 7:8], EPS, None, ALU.add, None, [('sg_st', b)], [('sg_st', b)])
                TT('pool', st6[:, b, 7:8], st6[:, b, 7:8], mhalf[:, 0:1], ALU.pow, [('sg_st', b), 'mhalf'], [('sg_st', b)])
                TS('dve', zz[:, b, :], zz[:, b, :], st6[:, b, 6:7], st6[:, b, 7:8], ALU.subtract, ALU.mult,
                   [('sg_zz', b), ('sg_st', b)], [('sg_zz', b)])
                TT('pool', svb[:, b, :], zz[:, b, :], gb[:, 0, :], ALU.mult, [('sg_zz', b), 'sg_gb'], [('sg_svb', b)])
                for g in range(4):
                    MM(bank(6)[:, g * 128:(g + 1) * 128], svb[:, b, g * 128:(g + 1) * 128], sgwT[:, g, :], True, True,
                       [('sg_svb', b), 'sg_wT'], [ps(6)])
                tmp, tk = fs_next()
                TT('dve', tmp[:, 0:512], bank(6), gb[:, 1, :], ALU.add, [ps(6), 'sg_gb'], [tk])
                TT('pool', ysgT[:, :, tc * 128:(tc + 1) * 128], tmp[:, 0:512].rearrange("p (g t) -> p g t", g=4),
                   uT[:, :, tc * 128:(tc + 1) * 128], ALU.mult, [tk] + [('sg_u', j) for j in range(4)],
                   [('ysg', j) for j in range(4)])
            T.free(keys)

    def merge(l, ys):
        par = l % 2
        ynames = ['yda', 'yml', 'ysg']
        with ExitStack() as es:
            mg = es.enter_context(sbt("mg", [128, 8, 1024], BF16))
            accf = es.enter_context(sbt("mg_acc", [128, 4, 1024], F32))
            keys = [('mg', dc) for dc in range(8)] + [('mg_acc', j) for j in range(4)]
            T.adopt(keys)
            for dcg in range(2):
                for n in range(3):
                    w, wk = wload(('w_in', (l,)), 0, 8, 4624 + n * 1024 + dcg * 512, 512)
                    wb, wbk = wload(('w_branch', (l, n)), 0, 4, 0, 1024)
                    for j in range(4):
                        dc = dcg * 4 + j
                        gp, gk = p2_next()
                        for th in range(2):
                            for kc in range(8):
                                MM(gp[:, th * 512:(th + 1) * 512], w[:, kc, j * 128:(j + 1) * 128],
                                   hT[:, kc, th * 512:(th + 1) * 512], kc == 0, kc == 7, [wk, ('h', kc)], [gk[th]])
                        pp, pk = p2_next()
                        for th in range(2):
                            for kc in range(4):
                                MM(pp[:, th * 512:(th + 1) * 512], wb[:, kc, dc * 128:(dc + 1) * 128],
                                   ys[n][:, kc, th * 512:(th + 1) * 512], kc == 0, kc == 3, [wbk, (ynames[n], kc)], [pk[th]])
                        sgt, sk = fs_next()
                        ACT(sgt[:, :], gp[:, :], AF.Sigmoid, gk, [sk])
                        if n == 0:
                            TT('dve', accf[:, j, :], pp[:, :], sgt[:, :], ALU.mult, pk + [sk], [('mg_acc', j)])
                        else:
                            t2, t2k = fs_next()
                            TT('dve', t2[:, :], pp[:, :], sgt[:, :], ALU.mult, pk + [sk], [t2k])
                            if n == 1:
                                TT('pool', accf[:, j, :], accf[:, j, :], t2[:, :], ALU.add, [('mg_acc', j), t2k], [('mg_acc', j)])
                            else:
                                TT('pool', mg[:, dc, :], accf[:, j, :], t2[:, :], ALU.add, [('mg_acc', j), t2k], [('mg', dc)])
            for og_ in range(2):
                w, wk = wload(('w_out', (l,)), 0, 8, og_ * 512, 512)
                for j in range(4):
                    dc = og_ * 4 + j
                    zp, zk = p2_next()
                    for th in range(2):
                        for kc in range(8):
                            MM(zp[:, th * 512:(th + 1) * 512], w[:, kc, j * 128:(j + 1) * 128], mg[:, kc, th * 512:(th + 1) * 512],
                               kc == 0, kc == 7, [wk, ('mg', kc)], [zk[th]])
                    for th in range(2):
                        STT(xT[:, dc, th * 512:(th + 1) * 512], zp[:, th * 512:(th + 1) * 512], modT[:, par, 16 + dc:17 + dc],
                            xT[:, dc, th * 512:(th + 1) * 512], ALU.mult, ALU.add, [zk[th], ('mod', par), ('x', dc)], [('x', dc)])
            T.free(keys)

    def da_phase(l, ydaT):
        wl = w_in[l]
        with ExitStack() as es:
            KT = es.enter_context(sbt("da_KT", [128, 4, 1280], BF16))
            qT = es.enter_context(sbt("da_qT", [128, 4, 1024], BF16))
            V = es.enter_context(sbt("da_V", [128, 10, 512], BF16))
            oall = es.enter_context(sbt("da_oall", [128, 4, 1024], F32))
            ET = es.enter_context(sbt("da_ET", [128, 3, 512], BF16))
            kbf = es.enter_context(sbt("da_kbf", [128, 2, 1024], BF16))
            ckf = es.enter_context(sbt("da_ckf", [128, 2, 128], F32))
            osb = es.enter_context(sbt("da_osb", [128, 2, 256], F32))
            rs = es.enter_context(sbt("da_rs", [128, 2, 256], F32))
            vst = es.enter_context(sbt("da_vst", [128, 2, 512], F32))
            dagl = es.enter_context(sbt("da_gl", [128, 1], F32))
            keys = ([('KT', h) for h in range(4)] + [('qT', h) for h in range(4)] + [('V', c) for c in range(10)] +
                    [('oall', h) for h in range(4)] + [('ET', i) for i in range(3)] + [('kbf', 0), ('kbf', 1), 'ckf', ('osb', 0), ('osb', 1),
                                                                                        ('rs', 0), ('rs', 1), ('vst', 0), ('vst', 1), 'dagl'])
            T.adopt(keys)
            TS('dve', dagl[:, :], dag_c(l), 1.0 - (0.8 - 0.6 * math.exp(-0.3 * l)), None, ALU.mult, None, ['CPm'], ['dagl'])
            for hd in range(4):
                DMA('pool', V[:, 0:2, hd * 128:(hd + 1) * 128], cv[l, hd].rearrange("(c p) d -> p c d", p=128), (),
                    [('V', 0), ('V', 1)], 'dw')
            for hd in range(4):
                DMA('sp', ckf[:, :, 0:128], ck[l, hd].rearrange("(c p) d -> p c d", p=128), (), ['ckf'], 'di')
                for c in range(2):
                    TR(bank(6)[:, c * 128:(c + 1) * 128], ckf[:, c, 0:128], ident_f[:], ['ckf', 'ident_f'], [ps(6)])
                CP('dve', KT[:, hd, 0:256], bank(6)[:, 0:256], [ps(6)], [('KT', hd)])
            wqk = {}

            def proj_job(job, slot):
                which, hd = job
                if hd == 0:
                    c0 = 512 if which == 0 else 0
                    wqk[which] = wload(('w_in', (l,)), 0, 8, c0, 512)
                w, wk = wqk[which]
                zp, zk = P2[slot], [ps(2 * slot), ps(2 * slot + 1)]
                for th in range(2):
                    for kc in range(8):
                        MM(zp[:, th * 512:(th + 1) * 512], w[:, kc, hd * 128:(hd + 1) * 128], hT[:, kc, th * 512:(th + 1) * 512],
                           kc == 0, kc == 7, [wk, ('h', kc)], [zk[th]])
                yield
                CP('act', kbf[:, slot, :], zp[:, :], zk, [('kbf', slot)])
                yield
                sp_, spk = PC, [ps(4), ps(5)]
                for th in range(2):
                    MM(sp_[:, th * 512:(th + 1) * 512], prot_b[:], kbf[:, slot, th * 512:(th + 1) * 512], True, True,
                       ['prot_b', ('kbf', slot)], [spk[th]])
                yield
                t1, t1k = fs_next()
                t2, t2k = fs_next()
                TT('dve', t1[:, :], zp[:, :], ropeC[:, :], ALU.mult, zk + ['ropeC'], [t1k])
                TT('dve', t2[:, :], sp_[:, :], ropeS[:, :], ALU.mult, spk + ['ropeS'], [t2k])
                yield
                if which == 0:
                    TT('pool', t1[:, :], t1[:, :], t2[:, :], ALU.add, [t1k, t2k], [t1k])
                    CP('pool', KT[:, hd, 256:1280], t1[:, :], [t1k], [('KT', hd)])
                    yield
                    for tcg in range(2):
                        for j in range(4):
                            tc = tcg * 4 + j
                            TR(bank(6)[:, j * 128:(j + 1) * 128], t1[:, tc * 128:(tc + 1) * 128], ident_f[:],
                               [t1k, 'ident_f'], [ps(6)])
                        CP('act', vst[:, tcg, :], bank(6), [ps(6)], [('vst', tcg)])
                        DMA('sp', ok_o[l, hd, tcg * 512:(tcg + 1) * 512, :].rearrange("(j p) d -> p j d", p=128),
                            vst[:, tcg, :].rearrange("p (j d) -> p j d", j=4), [('vst', tcg)], (), 'do')
                else:
                    TT('pool', qT[:, hd, :], t1[:, :], t2[:, :], ALU.add, [t1k, t2k], [('qT', hd)])

            run_pipeline([(which, hd) for which in range(2) for hd in range(4)], proj_job, 2)
            w, wk = wload(('w_in', (l,)), 0, 8, 1024, 512)
            for tc in range(8):
                bk = 4 + tc % 2
                for kc in range(8):
                    MM(bank(bk), hT[:, kc, tc * 128:(tc + 1) * 128], w[:, kc, :], kc == 0, kc == 7, [('h', kc), wk], [ps(bk)])
                CP('act', V[:, 2 + tc, :], bank(bk), [ps(bk)], [('V', 2 + tc)])
                CP('dve', vst[:, tc % 2, :], bank(bk), [ps(bk)], [('vst', tc % 2)])
                DMA('sp', ov_o[l, :, tc * 128:(tc + 1) * 128, :].rearrange("h p d -> p h d"),
                    vst[:, tc % 2, :].rearrange("p (h d) -> p h d", h=4), [('vst', tc % 2)], (), 'do')
            iters = [(hd, qt, kc) for hd in range(4) for qt in range(4) for kc in range(10)]
            rsf = rs[:, :, :].rearrange("p a b -> p (a b)")
            osf = osb[:, :, :].rearrange("p a b -> p (a b)")

            PT32 = PT[:, :].bitcast(F32)
            accb = [(bank(4), bank(5), ps(4), ps(5)), (bank(6), PT32, ps(6), ps(7))]
            SR = [(PA, ps(0), ps(1)), (PB, ps(2), ps(3))]

            def emit_qk(i):
                hd, qt, kc = iters[i]
                reg, k0, k1 = SR[i % 2]
                for half in range(2):
                    lo, hi = half * 64, half * 64 + 64
                    MM(reg[:, half * 512:half * 512 + 256], KT[lo:hi, hd, kc * 128:(kc + 1) * 128],
                       qT[lo:hi, hd, qt * 256:(qt + 1) * 256], True, True, [('KT', hd), ('qT', hd)], [k0 if half == 0 else k1])

            def emit_rest(i):
                hd, qt, kc = iters[i]
                reg, k0, k1 = SR[i % 2]
                sbk = i % 3
                ao, as_, ko, ks = accb[(hd * 4 + qt) % 2]
                ACT(ET[:, sbk, :].rearrange("p (a b) -> p a b", a=2), reg[:, :].rearrange("p (a b) -> p a b", a=2)[:, :, 0:256],
                    AF.Exp, [k0, k1, 'maskb'], [('ET', sbk)],
                    bias=maskb[:, (kc // 2) * 4 + qt:(kc // 2) * 4 + qt + 1], scale=0.125)
                MM(ao, V[:, kc, hd * 128:(hd + 1) * 128], ET[:, sbk, :], kc == 0, kc == 9, [('V', kc), ('ET', sbk)], [ko])
                MM(as_, ones_b[:], ET[:, sbk, :], kc == 0, kc == 9, ['ones_b', ('ET', sbk)], [ks])
                if kc == 9:
                    T.op('dve', lambda: nc.vector.reciprocal(out=rsf, in_=as_), [ks], [('rs', 0), ('rs', 1)])
                    TT('dve', osf, ao, rsf, ALU.mult, [ko, ('rs', 0), ('rs', 1)], [('osb', 0), ('osb', 1)])
                    STT(oall[:, hd, qt * 256:(qt + 1) * 256], osb[:, 1, :], nlam[:, l:l + 1], osb[:, 0, :], ALU.mult, ALU.add,
                        [('osb', 0), ('osb', 1), 'nlam'], [('oall', hd)])

            emit_qk(0)
            for i in range(len(iters)):
                if i + 1 < len(iters):
                    emit_qk(i + 1)
                emit_rest(i)
            for hd in range(4):
                ACT(ydaT[:, hd, :], oall[:, hd, :], AF.Square, [('oall', hd)], [('yda', hd)])
                for th in range(2):
                    MM(PA[:, th * 512:(th + 1) * 512], ones_b[:], ydaT[:, hd, th * 512:(th + 1) * 512], True, True,
                       [('yda', hd), 'ones_b'], [ps(th)])
                ACT(rstd[:], PA[:, :], AF.Ln, [ps(0), ps(1), 'eps_t'], ['rstd'], bias=eps_t[:], scale=1.0 / 128.0)
                ACT(rstd[:], rstd[:], AF.Exp, ['rstd'], ['rstd'], scale=-0.5)
                STT(ydaT[:, hd, :], oall[:, hd, :], dagl[:, 0:1], rstd[:], ALU.mult, ALU.mult, [('oall', hd), 'dagl', 'rstd'],
                    [('yda', hd)])
            T.free(keys)

    def ml_phase(l, ymlT):
        par = l % 2
        wl = w_in[l]
        LK = (2, 4, 6)
        with ExitStack() as es:
            COLS = es.enter_context(sbt("ml_cols", [128, 8, 4, 8], F32))
            DECB = es.enter_context(sbt("ml_decb", [128, 8, 8], F32))
            MP = es.enter_context(sbt("ml_mp", [8, 16], F32))
            MPL = es.enter_context(sbt("ml_mpl", [8, 16], F32))
            NM = es.enter_context(sbt("ml_nm", [8, 16], F32))
            DT = es.enter_context(sbt("ml_dt", [8, 16], F32))
            DEC = es.enter_context(sbt("ml_dec", [8, 16], F32))
            k1 = ['ml_cols', 'ml_decb', 'ml_mp', 'ml_mpl', 'ml_nm', 'ml_dt', 'ml_dec']
            T.adopt(k1)
            with ExitStack() as es2:
                Rr = [es2.enter_context(sbt(f"ml_r{i}", [8, 1024], F32)) for i in range(9)]
                rk = [('ml_r', i) for i in range(9)]
                T.adopt(rk)
                R0, R1, R2, R3, R4, R5, R6, R7a, R7b = Rr
                w, wk = wload(('w_in', (l,)), 0, 8, 3584, 16)
                for (pp, c0, kk) in ((PA, 0, [ps(0), ps(1)]), (PB, 8, [ps(2), ps(3)])):
                    for th in range(2):
                        for kc in range(8):
                            MM(pp[0:8, th * 512:(th + 1) * 512], w[:, kc, c0:c0 + 8], hT[:, kc, th * 512:(th + 1) * 512],
                               kc == 0, kc == 7, [wk, ('h', kc)], [kk[th]])
                for (pp, kk, dst, dk, bcol) in ((PA, [ps(0), ps(1)], R0, rk[0], 0), (PB, [ps(2), ps(3)], R1, rk[1], 1)):
                    ACT(R4[:, :], pp[0:8, :], AF.Identity, kk + ['GB'], [rk[4]], bias=GB[:, l, bcol:bcol + 1])
                    ACT(R5[:, :], pp[0:8, ::-1], AF.Identity, kk + ['GB'], [rk[5]], bias=GB[:, l, bcol:bcol + 1])
                    TS('dve', dst[:, :], R4[:, :], dirm[:, 0:1], None, ALU.mult, None, [rk[4], 'dirm'], [dk])
                    STT(dst[:, :], R5[:, :], dirm[:, 1:2], dst[:, :], ALU.mult, ALU.add, [rk[5], 'dirm', dk], [dk])
                ACT(R1[:, :], R1[:, :], AF.Exp, [rk[1]], [rk[1]], scale=-1.0)
                ACT(R1[:, :], R1[:, :], AF.Ln, [rk[1], 'one_t'], [rk[1]], bias=one_t[0:8, :], scale=1.0)
                for c in range(8):
                    sl = slice(c * 128, (c + 1) * 128)
                    T.op('dve', lambda: nc.vector.tensor_tensor_scan(out=R2[:, sl], data0=ones8[:, :], data1=R1[:, sl], initial=0.0,
                                                                      op0=ALU.mult, op1=ALU.add), [rk[1], 'ones8'], [rk[2]])
                TT('dve', R0[:, :], R0[:, :], R2[:, :], ALU.add, [rk[0], rk[2]], [rk[0]])
                for c in range(8):
                    sl = slice(c * 128, (c + 1) * 128)
                    T.op('dve', lambda: nc.vector.tensor_tensor_scan(out=R3[:, sl], data0=R0[:, sl], data1=R0[:, sl], initial=-1e30,
                                                                      op0=ALU.max, op1=ALU.max), [rk[0]], [rk[3]])
                DMA('sp', MP[:, 0:1], sm[l], (), ['ml_mp'], 'di')
                for cs in range(8):
                    sl = slice(cs * 128, (cs + 1) * 128)
                    la = cs * 128 + 127
                    mprev = MP[:, cs:cs + 1]
                    mpk = 'ml_mp'
                    if cs in LK:
                        TS('dve', MPL[:, cs:cs + 1], mprev, link[0:8, 0:1], None, ALU.mult, None, ['ml_mp', 'link'], ['ml_mpl'])
                        mprev = MPL[:, cs:cs + 1]
                        mpk = 'ml_mpl'
                    TS('dve', R3[:, sl], R3[:, sl], mprev, None, ALU.max, None, [rk[3], mpk], [rk[3]])
                    TT('dve', MP[:, cs + 1:cs + 2], R3[:, la:la + 1], R2[:, la:la + 1], ALU.subtract, [rk[3], rk[2]], ['ml_mp'])
                    ACT(R5[:, sl], R3[:, sl], AF.Exp, [rk[3], mpk], [rk[5]], bias=mprev, scale=-1.0)
                    if cs in LK:
                        TS('dve', R5[:, sl], R5[:, sl], link[0:8, 0:1], None, ALU.mult, None, [rk[5], 'link'], [rk[5]])
                    ACT(R4[:, sl], R3[:, sl], AF.Exp, [rk[3]], [rk[4]], bias=R3[:, la:la + 1], scale=-1.0)
                    TS('dve', NM[:, cs:cs + 1], R3[:, la:la + 1], -1.0, None, ALU.mult, None, [rk[3]], ['ml_nm'])
                    ACT(R0[:, sl], R0[:, sl], AF.Exp, [rk[0], 'ml_nm'], [rk[0]], bias=NM[:, cs:cs + 1], scale=1.0)
                    TT('dve', DT[:, cs:cs + 1], mprev, R2[:, la:la + 1], ALU.subtract, [mpk, rk[2]], ['ml_dt'])
                    TT('dve', DT[:, cs:cs + 1], DT[:, cs:cs + 1], MP[:, cs + 1:cs + 2], ALU.subtract, ['ml_dt', 'ml_mp'], ['ml_dt'])
                    ACT(DEC[:, cs:cs + 1], DT[:, cs:cs + 1], AF.Exp, ['ml_dt'], ['ml_dec'])
                    if cs in LK:
                        TS('dve', DEC[:, cs:cs + 1], DEC[:, cs:cs + 1], link[0:8, 0:1], None, ALU.mult, None, ['ml_dec', 'link'],
                           ['ml_dec'])
                TT('dve', R2[:, :], R2[:, :], R3[:, :], ALU.subtract, [rk[2], rk[3]], [rk[2]])
                ACT(R2[:, :], R2[:, :], AF.Exp, [rk[2]], [rk[2]])
                CP('dve', MPL[:, 12:16], MP[:, 2:10:2], ['ml_mp', 'ml_mpl'], ['ml_mpl'])
                DMA('sp', om_o[l], MPL[:, 12:16], ['ml_mpl'], (), 'do')
                for qi, (Q, qk) in enumerate(((R0, rk[0]), (R4, rk[4]), (R5, rk[5]), (R2, rk[2]))):
                    Ro, rok = (R7a, rk[7]) if qi % 2 == 0 else (R7b, rk[8])
                    TS('dve', R6[:, :], Q[:, :], dirm[:, 0:1], None, ALU.mult, None, [qk, 'dirm'], [rk[6]])
                    STT(Ro[:, :], Q[:, ::-1], dirm[:, 1:2], R6[:, :], ALU.mult, ALU.add, [rk[6], 'dirm', qk], [rok])
                    for c in range(8):
                        o0 = (c * 4 + qi) * 8
                        TR(bank(6)[:, o0:o0 + 8], Ro[:, c * 128:(c + 1) * 128], ident_f[0:8, 0:8], [rok, 'ident_f'], [ps(6)])
                CP('dve', COLS[:, :, :, :].rearrange("p a b c -> p (a b c)"), bank(6)[:, 0:256], [ps(6)], ['ml_cols'])
                for r in range(8):
                    MM(bank(5)[:, r * 8:(r + 1) * 8], sel[0:8, r * 128:(r + 1) * 128], DEC[0:8, 0:8], True, True, ['sel', 'ml_dec'],
                       [ps(5)])
                CP('dve', DECB[:, :, :].rearrange("p a b -> p (a b)"), bank(5)[:, 0:64], [ps(5)], ['ml_decb'])
                T.free(rk)
            hsum = es.enter_context(sbt("ml_hs", [128, 8, 512], F32))
            Cn32 = es.enter_context(sbt("ml_cn", [128, 8, 130], F32))
            T.adopt([('hs', c) for c in range(8)] + [('cn', r) for r in range(8)])
            es3 = ExitStack()
            mqT = es3.enter_context(sbt("ml_qT", [128, 4, 1024], BF16))
            mkT = es3.enter_context(sbt("ml_kT", [128, 4, 1024], BF16))
            mva = es3.enter_context(sbt("ml_va", [128, 8, 4, 130], BF16))
            Cnb = es3.enter_context(sbt("ml_cnb", [128, 8, 130], BF16))
            PTs = es3.enter_context(sbt("ml_pts", [128, 6, 128], BF16))
            kg = es3.enter_context(sbt("ml_kg", [128, 6, 128], BF16))
            vg = es3.enter_context(sbt("ml_vg", [128, 6, 130], BF16))
            tB = es3.enter_context(sbt("ml_tb", [128, 6, 130], F32))
            nd = es3.enter_context(sbt("ml_nd", [128, 6, 130], F32))
            dn = es3.enter_context(sbt("ml_dn", [128, 6, 2], F32))
            k2 = ([('mqT', h) for h in range(4)] + [('mkT', h) for h in range(4)] + [('mva', c) for c in range(8)] +
                  [('cnb', r) for r in range(8)] +
                  [(nm_, b) for nm_ in ('pts', 'kg', 'vg', 'tb', 'nd', 'dn') for b in range(6)])
            T.adopt(k2)
            for dr in range(2):
                for hd in range(4):
                    r = dr * 4 + hd
                    DMA('sp', Cn32[:, r, 0:129], sCn[l, dr, hd], (), [('cn', r)], 'di')
                    CP('pool', Cnb[:, r, 0:129], Cn32[:, r, 0:129], [('cn', r)], [('cnb', r)])
            for which, c0 in ((0, 1536), (1, 2048)):
                w, wk = wload(('w_in', (l,)), 0, 8, c0, 512)
                for hd in range(4):
                    zp, zk = p2_next()
                    for th in range(2):
                        for kc in range(8):
                            MM(zp[:, th * 512:(th + 1) * 512], w[:, kc, hd * 128:(hd + 1) * 128], hT[:, kc, th * 512:(th + 1) * 512],
                               kc == 0, kc == 7, [wk, ('h', kc)], [zk[th]])
                    ch = which * 4 + hd
                    acc, ak = fs_next()
                    dwconv(l, zp, zk, acc, ak, mlw_c(l, 0, ch), mlw_c(l, 1, ch), mlw_c(l, 2, ch), mlb_c(l, ch),
                           w0n[:, par, 44 + ch:45 + ch], w2n[:, par, 44 + ch:45 + ch])
                    if which == 0:
                        ACT(mqT[:, hd, :], acc[:, :], AF.Silu, [ak], [('mqT', hd)])
                    else:
                        sgt, sk = fs_next()
                        ACT(sgt[:, :], acc[:, :], AF.Sigmoid, [ak], [sk])
                        STT(mkT[:, hd, :], acc[:, :], 128.0 ** -0.5, sgt[:, :], ALU.mult, ALU.mult, [ak, sk], [('mkT', hd)])
            w, wk = wload(('w_in', (l,)), 0, 8, 2560, 512)
            MS('pool', mva[:, :, :, 128:130], 1.0, [('mva', c) for c in range(8)])
            for tc in range(8):
                bk = 4 + tc % 2
                for kc in range(8):
                    MM(bank(bk), hT[:, kc, tc * 128:(tc + 1) * 128], w[:, kc, :], kc == 0, kc == 7, [('h', kc), wk], [ps(bk)])
                CP('act', mva[:, tc, :, 0:128], bank(bk).rearrange("p (h d) -> p h d", h=4), [ps(bk)], [('mva', tc)])
            def core_iter(cs, hd, dr, slot):
                c = cs if dr == 0 else 7 - cs
                r = dr * 4 + hd
                tsl = slice(c * 128, (c + 1) * 128)
                mask = maskF if dr == 0 else maskB
                mkey = 'maskF' if dr == 0 else 'maskB'
                pb = bank(slot)
                pk_ = ps(slot)
                Sps, Aps, Bps, CNps = pb[:, 0:128], pb[:, 130:259], pb[:, 260:389], pb[:, 0:129]
                MM(Sps, mkT[:, hd, tsl], mqT[:, hd, tsl], True, True, [('mkT', hd), ('mqT', hd)], [pk_])
                TR(PT[:, slot * 128:(slot + 1) * 128], mkT[:, hd, tsl], ident_b[:], [('mkT', hd), 'ident_b'], [ps(7)])
                yield
                TT('dve', PTs[:, slot, :], Sps, mask[:, :], ALU.mult, [pk_, mkey], [('pts', slot)])
                ACT(kg[:, slot, :], PT[:, slot * 128:(slot + 1) * 128], AF.Identity, [ps(7), 'ml_cols'], [('kg', slot)],
                    scale=COLS[:, c, 0, r:r + 1])
                ACT(vg[:, slot, :], mva[:, c, hd, :], AF.Identity, [('mva', c), 'ml_cols'], [('vg', slot)],
                    scale=COLS[:, c, 0, r:r + 1])
                yield
                MM(Aps, PTs[:, slot, :], vg[:, slot, 0:129], True, True, [('pts', slot), ('vg', slot)], [pk_])
                MM(Bps, mqT[:, hd, tsl], Cnb[:, r, 0:129], True, True, [('mqT', hd), ('cnb', r)], [pk_])
                yield
                ACT(tB[:, slot, 0:129], Bps, AF.Identity, [pk_, 'ml_cols'], [('tb', slot)], scale=COLS[:, c, 2, r:r + 1])
                STT(nd[:, slot, 0:129], Aps, COLS[:, c, 1, r:r + 1], tB[:, slot, 0:129], ALU.mult, ALU.add,
                    [pk_, 'ml_cols', ('tb', slot)], [('nd', slot)])
                STT(dn[:, slot, 0:1], nd[:, slot, 128:129], -1.0, nd[:, slot, 128:129], ALU.mult, ALU.max, [('nd', slot)],
                    [('dn', slot)])
                TS('dve', dn[:, slot, 0:1], dn[:, slot, 0:1], COLS[:, c, 3, r:r + 1], None, ALU.max, None,
                   [('dn', slot), 'ml_cols'], [('dn', slot)])
                T.op('dve', lambda: nc.vector.reciprocal(out=dn[:, slot, 1:2], in_=dn[:, slot, 0:1]), [('dn', slot)], [('dn', slot)])
                hs = hsum[:, c, hd * 128:(hd + 1) * 128]
                if cs < 4:
                    TS('dve', hs, nd[:, slot, 0:128], dn[:, slot, 1:2], None, ALU.mult, None, [('nd', slot), ('dn', slot)],
                       [('hs', c)])
                else:
                    STT(hs, nd[:, slot, 0:128], dn[:, slot, 1:2], hs, ALU.mult, ALU.add, [('nd', slot), ('dn', slot), ('hs', c)],
                        [('hs', c)])
                yield
                MM(CNps, kg[:, slot, :], mva[:, c, hd, 0:129], True, True, [('kg', slot), ('mva', c)], [pk_])
                yield
                STT(Cn32[:, r, 0:129], Cn32[:, r, 0:129], DECB[:, r, cs:cs + 1], CNps, ALU.mult, ALU.add,
                    [('cn', r), 'ml_decb', pk_], [('cn', r)])
                CP('pool', Cnb[:, r, 0:129], Cn32[:, r, 0:129], [('cn', r)], [('cnb', r)])
                if cs % 2 == 1:
                    DMA('sp', oCn_o[l, dr, cs // 2, hd], Cn32[:, r, 0:129], [('cn', r)], (), 'do')

            todo = [(cs, hd, dr) for cs in range(8) for hd in range(4) for dr in range(2)]
            modgen = [compute_mod_gen(l + 1) if l + 1 < depth else None]
            run_pipeline(todo, lambda job, sl_: core_iter(job[0], job[1], job[2], sl_), 6, modgen, 5)
            if l == 0:
                dump('hs', hsum[:, :, :], [('hs', c) for c in range(8)])
                dump('cols', COLS[:, :, :, :], ['ml_cols'])
            T.free(k2)
            es3.close()
            og = es.enter_context(sbt("ml_og", [128, 8, 512], BF16))
            gml4 = es.enter_context(sbt("ml_g4", [128, 512], F32))
            ssq = es.enter_context(sbt("ml_ssq", [128, 2, 4], F32))
            ytm = es.enter_context(sbt("ml_ytm", [128, 2, 512], BF16))
            k3 = [('og', c) for c in range(8)] + ['gml4', ('ssq', 0), ('ssq', 1), ('ytm', 0), ('ytm', 1)]
            T.adopt(k3)
            for h in range(4):
                DMA('sp', gml4[:, h * 128:(h + 1) * 128], ml_norm_g[l].partition_broadcast(128), (), ['gml4'], 'di')
            w, wk = wload(('w_in', (l,)), 0, 8, 3072, 512)
            for tc in range(8):
                bk = 4 + tc % 2
                for kc in range(8):
                    MM(bank(bk), hT[:, kc, tc * 128:(tc + 1) * 128], w[:, kc, :], kc == 0, kc == 7, [('h', kc), wk], [ps(bk)])
                tmp, tk = fs_next()
                ACT(tmp[:, 0:512], bank(bk), AF.Sigmoid, [ps(bk)], [tk])
                TT('pool', og[:, tc, :], tmp[:, 0:512], gml4[:, :], ALU.mult, [tk, 'gml4'], [('og', tc)])
            for c in range(8):
                b2 = c % 2
                sq, sqk = fs_next()
                ACT(sq[:, 0:512], hsum[:, c, :], AF.Square, [('hs', c)], [sqk])
                T.op('dve', lambda: nc.vector.tensor_reduce(out=ssq[:, b2, 0:4], in_=sq[:, 0:512].rearrange("p (h d) -> p h d", h=4),
                                                             axis=AX.X, op=ALU.add), [sqk], [('ssq', b2)])
                TS('pool', ssq[:, b2, 0:4], ssq[:, b2, 0:4], 1.0 / 128.0, EPS, ALU.mult, ALU.add, [('ssq', b2)], [('ssq', b2)])
                TT('pool', ssq[:, b2, 0:4], ssq[:, b2, 0:4], mhalf[:, 0:4], ALU.pow, [('ssq', b2), 'mhalf'], [('ssq', b2)])
                for hd in range(4):
                    hsl = slice(hd * 128, (hd + 1) * 128)
                    STT(ytm[:, b2, hsl], hsum[:, c, hsl], ssq[:, b2, hd:hd + 1], og[:, c, hsl], ALU.mult, ALU.mult,
                        [('hs', c), ('ssq', b2), ('og', c)], [('ytm', b2)])
                for hd in range(4):
                    hsl = slice(hd * 128, (hd + 1) * 128)
                    TR(PT[:, hsl], ytm[:, b2, hsl], ident_b[:], [('ytm', b2), 'ident_b'], [ps(7)])
                CP('act', ymlT[:, :, c * 128:(c + 1) * 128], PT[:, 0:512].rearrange("p (h t) -> p h t", h=4), [ps(7)],
                   [('yml', h) for h in range(4)])
            T.free(k1 + k3 + [('hs', c) for c in range(8)] + [('cn', r) for r in range(8)])

    def mixer(l):
        norm_mod(l, 0)
        with ExitStack() as es:
            ydaT = es.enter_context(sbt("ydaT", [128, 4, 1024], BF16))
            ymlT = es.enter_context(sbt("ymlT", [128, 4, 1024], BF16))
            ysgT = es.enter_context(sbt("ysgT", [128, 4, 1024], BF16))
            yk = [(n, h) for n in ('yda', 'yml', 'ysg') for h in range(4)]
            T.adopt(yk)
            if 'ml' in phases:
                ml_phase(l, ymlT)
            else:
                MS('pool', ymlT[:, :, :], 0.0, [('yml', h) for h in range(4)])
            if 'da' in phases:
                da_phase(l, ydaT)
            else:
                MS('pool', ydaT[:, :, :], 0.0, [('yda', h) for h in range(4)])
            if 'sg' in phases:
                sg_phase(l, ysgT)
            else:
                MS('pool', ysgT[:, :, :], 0.0, [('ysg', h) for h in range(4)])
            if l == 0:
                dump('yda', ydaT[:, :, :], [('yda', h) for h in range(4)])
                dump('yml', ymlT[:, :, :], [('yml', h) for h in range(4)])
                dump('ysg', ysgT[:, :, :], [('ysg', h) for h in range(4)])
            merge(l, [ydaT, ymlT, ysgT])
            T.free(yk)

    compute_mod(0)
    for l in range(depth):
        prep_w0n(l)
        if l + 1 < depth and 'ml' not in phases:
            compute_mod(l + 1)
        if any(p in phases for p in ('ml', 'da', 'sg')):
            mixer(l)
        if 'ffn' in phases:
            ffn(l)

    with ExitStack() as es:
        yT = es.enter_context(sbt("yT", [128, 8, 1024], F32))
        T.adopt([('yT', dc) for dc in range(8)])
        rms_stats([xT[:, dc, :] for dc in range(8)], [('x', dc) for dc in range(8)],
                  [hT[:, dc, :] for dc in range(8)], [('h', dc) for dc in range(8)], 1024.0)
        for dc in range(8):
            STT(yT[:, dc, :], xT[:, dc, :], gcol[:, dc:dc + 1], rstd[:], ALU.mult, ALU.mult, [('x', dc), 'gcol', 'rstd'],
                [('yT', dc)])
        for tc in range(8):
            st, sk = fs_next()
            for half in range(2):
                bk = 2 + half
                for j in range(4):
                    dc = half * 4 + j
                    TR(bank(bk)[:, j * 128:(j + 1) * 128], yT[:, dc, tc * 128:(tc + 1) * 128], ident_f[:],
                       [('yT', dc), 'ident_f'], [ps(bk)])
                CP('dve' if half == 0 else 'act', st[:, half * 512:(half + 1) * 512], bank(bk), [ps(bk)], [sk])
            DMA('sp', y_o[tc * 128:(tc + 1) * 128, :], st[:, :], [sk], (), 'do')
        T.free([('yT', dc) for dc in range(8)])
    T.finish()
    return nc, T, dumps, wrec


def _consts():
    ident = np.eye(128, dtype=np.float32)
    prot = np.zeros((128, 128), np.float32)
    for m in range(128):
        prot[m ^ 16, m] = 1.0
    s = np.arange(128)[:, None]
    t = np.arange(128)[None, :]
    maskF = (s <= t).astype(np.float32)
    maskB = (s >= t).astype(np.float32)
    sel = np.zeros((8, 8, 128), np.float32)
    for r in range(8):
        sel[r, r, :] = 1.0
    dirm = np.zeros((8, 2), np.float32)
    dirm[0:4, 0] = 1.0
    dirm[4:8, 1] = 1.0
    return dict(c_ident=ident, c_prot=prot, c_maskF=maskF, c_maskB=maskB, c_sel=sel.reshape(8, 1024), c_dirm=dirm)


def _rope_tables():
    t = np.arange(1024)
    row = (t // 64).astype(np.float32)
    col = (t % 64).astype(np.float32)
    nf = 16
    inv = (10000.0 ** (-np.arange(nf, dtype=np.float32) / nf)).astype(np.float32)
    ang = np.stack([row[:, None] * inv, col[:, None] * inv], axis=1)
    cos = np.cos(ang).astype(np.float32)
    sin = np.sin(ang).astype(np.float32)
    C = np.zeros((128, 1024), np.float32)
    S = np.zeros((128, 1024), np.float32)
    for d in range(128):
        axis = (d >> 5) & 1
        half = (d >> 4) & 1
        f = d & 15
        C[d] = cos[:, axis, f]
        S[d] = sin[:, axis, f] * (-1.0 if half == 0 else 1.0)
    return C, S


_CACHE = {}


def kernel(x_prompt, x_sample, c, cache_k, cache_v, state_C, state_n, state_m, c_ctx,
           w_mod, b_mod, w_in, da_lambda, da_norm_g, ml_conv_w, ml_conv_b, ml_gate_b,
           ml_norm_g, sg_norm_g, sg_w, sg_b, w_branch, w_out, w_up, ffn_conv_w, ffn_conv_b,
           w_down, final_g, _depth=L, _phases=('ml', 'da', 'sg', 'ffn'), _dbg=False):
    f = lambda a: np.ascontiguousarray(np.asarray(a, dtype=np.float32))
    key = (_depth, tuple(_phases), _dbg)
    if key not in _CACHE:
        rec = build(_depth, _phases, _dbg is True)[3]
        _CACHE[key] = build(_depth, _phases, _dbg is True, rec)
    nc, T, dumps, _ = _CACHE[key]
    dp = _depth
    consts = _consts()
    rC, rS = _rope_tables()
    shared = dict(consts)
    shared.update(
        w_mod=f(w_mod[:dp]), b_mod=f(b_mod).reshape(L, 48, 128), w_in=f(w_in[:dp]), da_lambda=f(da_lambda).reshape(1, L * 256),
        da_norm_g=f(da_norm_g).reshape(L, 1, 128), ml_conv_w=f(ml_conv_w).reshape(L, 24, 128),
        ml_conv_b=f(ml_conv_b).reshape(L, 8, 128), ml_gate_b=f(ml_gate_b).reshape(L, 16, 1),
        ml_norm_g=f(ml_norm_g).reshape(L, 1, 128), sg_norm_g=f(sg_norm_g).reshape(L, 1, 512), sg_w=f(sg_w),
        sg_b=f(sg_b).reshape(L, 1, 512), w_branch=f(w_branch[:dp]), w_out=f(w_out[:dp]), w_up=f(w_up[:dp]),
        ffn_conv_w=f(ffn_conv_w).reshape(L, 132, 128), ffn_conv_b=f(ffn_conv_b).reshape(L, 44, 128), w_down=f(w_down[:dp]),
        final_g=f(final_g).reshape(8, 128))
    x_prompt = f(x_prompt)
    x_sample = f(x_sample)
    in_maps = []
    for core in range(8):
        m = dict(shared)
        if core < 4:
            b = core
            m['xin'] = x_sample[b]
            m['cond'] = f(c)[b].reshape(8, 128)
            m['ck'] = f(cache_k)[b]
            m['cv'] = f(cache_v)[b]
            m['sCn'] = np.ascontiguousarray(np.concatenate([f(state_C)[b], f(state_n)[b][..., None]], axis=-1))
            m['sm'] = f(state_m)[b].reshape(L, 8, 1)
            m['c_ropeC'] = rC
            m['c_ropeS'] = rS
            m['c_maskb'] = np.zeros((128, 20), np.float32)
            lk = np.zeros((128, 2), np.float32)
            lk[:, 0] = 1.0
            m['c_link'] = lk
        else:
            j = core - 4
            m['xin'] = x_prompt[4 * j:4 * j + 4].reshape(1024, 1024)
            m['cond'] = f(c_ctx).reshape(8, 128)
            m['ck'] = np.zeros((L, 4, 256, 128), np.float32)
            m['cv'] = np.zeros((L, 4, 256, 128), np.float32)
            m['sCn'] = np.zeros((L, 2, 4, 128, 129), np.float32)
            m['sm'] = np.zeros((L, 8, 1), np.float32)
            m['c_ropeC'] = np.ones((128, 1024), np.float32)
            m['c_ropeS'] = np.zeros((128, 1024), np.float32)
            mb = np.full((5, 4), -30000.0, np.float32)
            for qt in range(4):
                mb[1 + qt, qt] = 0.0
            m['c_maskb'] = np.tile(mb.reshape(1, 20), (128, 1))
            lk = np.zeros((128, 2), np.float32)
            lk[:, 1] = 1.0
            m['c_link'] = lk
        in_maps.append(m)
    if _dbg == 'maps':
        return nc, in_maps
    if _dbg == 'time':
        res = run_bass_kernel_spmd(nc, in_maps, core_ids=list(range(8)), trace=True)
        return res.exec_time_ns
    res = run_bass_kernel_spmd(nc, in_maps, core_ids=list(range(8)))
    R = res.results
    if _dbg:
        kernel.dbg = [{n: R[c][n] for n in dumps} for c in range(8)]
    y_sample = np.stack([R[b]['y_o'] for b in range(4)], axis=0)
    y_prompt = np.concatenate([R[4 + j]['y_o'].reshape(4, 256, 1024) for j in range(4)], axis=0)
    nk = np.concatenate([R[4 + j]['ok_o'].reshape(L, 4, 4, 256, 128).transpose(2, 0, 1, 3, 4) for j in range(4)], axis=0)
    nv = np.concatenate([R[4 + j]['ov_o'].reshape(L, 4, 4, 256, 128).transpose(2, 0, 1, 3, 4) for j in range(4)], axis=0)
    nC, nn, nm = [], [], []
    for j in range(4):
        oCn = R[4 + j]['oCn_o']
        oC = oCn[..., 0:128]
        on = oCn[..., 128]
        om = R[4 + j]['om_o'].reshape(L, 2, 4, 4)
        for sq in range(4):
            nC.append(np.stack([oC[:, 0, sq], oC[:, 1, 3 - sq]], axis=1))
            nn.append(np.stack([on[:, 0, sq], on[:, 1, 3 - sq]], axis=1))
            nm.append(np.stack([om[:, 0, :, sq], om[:, 1, :, 3 - sq]], axis=1))
    return (y_prompt.astype(np.float32), y_sample.astype(np.float32), np.ascontiguousarray(nk), np.ascontiguousarray(nv),
            np.stack(nC, axis=0), np.stack(nn, axis=0), np.stack(nm, axis=0))
```

```python
import math
from contextlib import ExitStack
import numpy as np
import concourse.bass as bass
import concourse.mybir as mybir
from concourse.bass_utils import run_bass_kernel_spmd

F32 = mybir.dt.float32
BF16 = mybir.dt.bfloat16
AF = mybir.ActivationFunctionType
ALU = mybir.AluOpType
AX = mybir.AxisListType

L = 4
NIN = 7696
DFF = 2816
EPS = 1e-6


class Tr:
    EP = 30000
    NSEM = 8

    def __init__(s, nc):
        s.nc = nc
        s.E = dict(pe=nc.tensor, act=nc.scalar, dve=nc.vector, pool=nc.gpsimd, sp=nc.sync)
        s.sems = {}
        s.cnt = {}
        s.known = {e: {} for e in s.E}
        s.res = {}
        s.grave = {}
        s.nwait = 0
        s.nops = 0
        s.dcount = {}
        s.dlast = {}

    def _sem(s, st, c):
        mult = 1 if st in s.E else 16
        ep = s.EP // mult
        e = (c - 1) // ep
        lst = s.sems.setdefault(st, [])
        while len(lst) <= e:
            nm = st if isinstance(st, str) else f"{st[0]}{st[1]}"
            lst.append(s.nc.alloc_semaphore(f"s_{nm}_{len(lst)}"))
        return lst[e], ((c - 1) % ep + 1) * mult

    def op(s, eng, fn, reads=(), writes=(), sig=True, dma=None):
        deps = {}

        def add(ev):
            if ev is None:
                return
            st = ev[0]
            if st == 'pe' and eng == 'pe' and dma is None:
                return
            if st not in deps or deps[st][1] < ev[1]:
                deps[st] = ev

        for k in reads:
            r = s.res.get(k)
            if r is None:
                continue
            add(r[0])
            if k[0] == 'ps':
                for ev in r[1].values():
                    add(ev)
        for k in writes:
            r = s.res.get(k)
            if r is None:
                continue
            add(r[0])
            for ev in r[1].values():
                add(ev)
        if dma is not None:
            i = s.dcount.get(dma, 0)
            s.dcount[dma] = i + 1
            dma = (dma, i % s.NSEM)
            add(s.dlast.get(dma))
        kn = s.known[eng]
        for st in sorted(deps, key=lambda a: -deps[a][1]):
            _, c, clk = deps[st]
            if kn.get(st, 0) >= c:
                continue
            if st == 'pe':
                assert s.cnt.get('pe', 0) >= c, "dependency on unsignalled PE op"
            sem, v = s._sem(st, c)
            s.E[eng].wait_ge(sem, v)
            s.nwait += 1
            kn[st] = c
            for a, b in clk.items():
                if kn.get(a, 0) < b:
                    kn[a] = b
        ins = fn()
        s.nops += 1
        st = dma or eng
        c = s.cnt.get(st, 0) + 1
        if sig:
            s.cnt[st] = c
            sem, v = s._sem(st, c)
            ins.then_inc(sem, 16 if dma else 1)
        clk = dict(kn)
        clk[st] = c
        ev = (st, c, clk)
        if dma is not None:
            s.dlast[dma] = ev
        for k in writes:
            s.res[k] = [ev, {}]
        for k in reads:
            r = s.res.setdefault(k, [None, {}])
            r[1][st] = ev
        return ins

    def free(s, keys):
        for k in keys:
            r = s.res.pop(k, None)
            if r is None:
                continue
            evs = list(r[1].values())
            if r[0] is not None:
                evs.append(r[0])
            for ev in evs:
                st = ev[0]
                if st not in s.grave or s.grave[st][1] < ev[1]:
                    s.grave[st] = ev

    def adopt(s, keys):
        for k in keys:
            s.res[k] = [None, dict(s.grave)]

    def finish(s, eng='sp'):
        for st, c in s.cnt.items():
            if st in s.E:
                continue
            ep = s.EP // 16
            for e in range((c - 1) // ep + 1 if c > 0 else 0):
                last = min(c, (e + 1) * ep)
                sem, v = s._sem(st, last)
                s.E[eng].wait_ge(sem, v)


def build(depth=L, phases=('ml', 'da', 'sg', 'ffn'), dbg=False, wsched_n=None):
    nc = bass.Bass("TRN2", target_bir_lowering=False)
    T = Tr(nc)
    LW = depth
    dumps = []
    wrec = []
    wissued = [0]
    LOOK = 2

    def din(n, sh):
        return nc.dram_tensor(n, list(sh), F32, kind="ExternalInput").ap()

    def dout(n, sh):
        return nc.dram_tensor(n, list(sh), F32, kind="ExternalOutput").ap()

    xin = din("xin", [1024, 1024])
    cond = din("cond", [8, 128])
    ck = din("ck", [L, 4, 256, 128])
    cv = din("cv", [L, 4, 256, 128])
    sCn = din("sCn", [L, 2, 4, 128, 129])
    sm = din("sm", [L, 8, 1])
    c_ident = din("c_ident", [128, 128])
    c_prot = din("c_prot", [128, 128])
    c_maskF = din("c_maskF", [128, 128])
    c_maskB = din("c_maskB", [128, 128])
    c_sel = din("c_sel", [8, 1024])
    c_dirm = din("c_dirm", [8, 2])
    c_ropeC = din("c_ropeC", [128, 1024])
    c_ropeS = din("c_ropeS", [128, 1024])
    c_maskb = din("c_maskb", [128, 20])
    c_link = din("c_link", [128, 2])
    w_mod = din("w_mod", [LW, 1024, 6144])
    b_mod = din("b_mod", [L, 48, 128])
    w_in = din("w_in", [LW, 1024, NIN])
    da_lambda = din("da_lambda", [1, L * 256])
    da_norm_g = din("da_norm_g", [L, 1, 128])
    ml_conv_w = din("ml_conv_w", [L, 24, 128])
    ml_conv_b = din("ml_conv_b", [L, 8, 128])
    ml_gate_b = din("ml_gate_b", [L, 16, 1])
    ml_norm_g = din("ml_norm_g", [L, 1, 128])
    sg_norm_g = din("sg_norm_g", [L, 1, 512])
    sg_w = din("sg_w", [L, 4, 128, 128])
    sg_b = din("sg_b", [L, 1, 512])
    w_branch = din("w_branch", [LW, 3, 512, 1024])
    w_out = din("w_out", [LW, 1024, 1024])
    w_up = din("w_up", [LW, 1024, 2 * DFF])
    ffn_conv_w = din("ffn_conv_w", [L, 132, 128])
    ffn_conv_b = din("ffn_conv_b", [L, 44, 128])
    w_down = din("w_down", [LW, DFF, 1024])
    final_g = din("final_g", [8, 128])

    y_o = dout("y_o", [1024, 1024])
    ok_o = dout("ok_o", [L, 4, 1024, 128])
    ov_o = dout("ov_o", [L, 4, 1024, 128])
    oCn_o = dout("oCn_o", [L, 2, 4, 4, 128, 129])
    om_o = dout("om_o", [L, 8, 4])

    WTinit = dict(w_in=w_in, w_mod=w_mod, w_up=w_up, w_down=w_down, w_out=w_out, w_branch=w_branch)
    def veng(e):
        return nc.vector if e == 'dve' else nc.gpsimd

    def ACT(out, in_, func, r, w, bias=None, scale=None):
        kw = {}
        if bias is not None:
            kw['bias'] = bias
        if scale is not None:
            kw['scale'] = scale
        return T.op('act', lambda: nc.scalar.activation(out=out, in_=in_, func=func, **kw), r, w)

    def TT(e, out, a, b, op, r, w):
        return T.op(e, lambda: veng(e).tensor_tensor(out=out, in0=a, in1=b, op=op), r, w)

    def TS(e, out, a, s1, s2, op0, op1, r, w):
        if op1 is None:
            return T.op(e, lambda: veng(e).tensor_scalar(out=out, in0=a, scalar1=s1, scalar2=None, op0=op0), r, w)
        return T.op(e, lambda: veng(e).tensor_scalar(out=out, in0=a, scalar1=s1, scalar2=s2, op0=op0, op1=op1), r, w)

    def STT(out, a, s, b, op0, op1, r, w):
        return T.op('dve', lambda: nc.vector.scalar_tensor_tensor(out=out, in0=a, scalar=s, in1=b, op0=op0, op1=op1), r, w)

    def CP(e, out, in_, r, w):
        if e == 'act':
            return T.op('act', lambda: nc.scalar.copy(out=out, in_=in_), r, w)
        return T.op(e, lambda: veng(e).tensor_copy(out=out, in_=in_), r, w)

    def MM(out, lhsT, rhs, start, stop, r, w, sig=None):
        if sig is None:
            sig = stop
        return T.op('pe', lambda: nc.tensor.matmul(out, lhsT=lhsT, rhs=rhs, start=start, stop=stop), r, w, sig=sig)

    def TR(out, in_, ident, r, w):
        return T.op('pe', lambda: nc.tensor.transpose(out, in_, ident), r, w)

    def DMA(e, out, in_, r, w, st):
        eng = {'sp': nc.sync, 'pool': nc.gpsimd, 'act': nc.scalar}[e]
        if e == 'pool':
            st = 'dw'
        return T.op(e, lambda: eng.dma_start(out=out, in_=in_), r, w, dma=st)

    def dump(name, ap, keys):
        if not dbg:
            return
        o = dout("dbg_" + name, list(ap.shape))
        dumps.append("dbg_" + name)
        DMA('pool' if ap.dtype != F32 else 'sp', o, ap, keys, (), 'do')

    def MS(e, ap, val, w):
        return T.op(e, lambda: veng(e).memset(ap, val), (), w)

    def sb(n, sh, dt=F32):
        return nc.alloc_sbuf_tensor(n, list(sh), dt)

    uniq = [0]

    def sbt(n, sh, dt=F32):
        uniq[0] += 1
        return nc.sbuf_tensor(f"{n}_u{uniq[0]}", list(sh), dt)

    xT = sb("xT", [128, 8, 1024])
    hT = sb("hT", [128, 8, 1024], BF16)
    NW = 4
    wpool = [sb(f"wp{i}", [128, 4096], BF16) for i in range(NW)]
    wstate = [0]
    ident_f = sb("ident_f", [128, 128])
    ident_b = sb("ident_b", [128, 128], BF16)
    prot_b = sb("prot_b", [128, 128], BF16)
    ones_b = sb("ones_b", [128, 128], BF16)
    maskF = sb("maskF", [128, 128])
    maskB = sb("maskB", [128, 128])
    sel = sb("sel", [8, 1024])
    dirm = sb("dirm", [8, 2])
    ropeC = sb("ropeC", [128, 1024])
    ropeS = sb("ropeS", [128, 1024])
    maskb = sb("maskb", [128, 20])
    link = sb("link", [128, 2])
    eps_t = sb("eps_t", [128, 1])
    one_t = sb("one_t", [128, 1])
    mhalf = sb("mhalf", [128, 4])
    CPm = sb("CPm", [128, L, 3, 128])
    gcol = sb("gcol", [128, 16])
    cond_b = sb("cond_b", [128, 8], BF16)
    modT = sb("modT", [128, 2, 48])
    scp = sb("scp", [128, 2, 16])
    lam = sb("lam", [128, L])
    nlam = sb("nlam", [128, L])
    GB = sb("GB", [8, L, 2])
    rstd = sb("rstd", [128, 1024])
    FS = [sb(f"fs{i}", [128, 1024]) for i in range(4)]
    fstate = [0]
    ones8 = sb("ones8", [8, 128])
    w0n = sb("w0n", [128, 2, 52])

    PA = nc.alloc_psum_tensor("PA", [128, 1024], F32)
    PB = nc.alloc_psum_tensor("PB", [128, 1024], F32)
    PC = nc.alloc_psum_tensor("PC", [128, 1024], F32)
    PD = nc.alloc_psum_tensor("PD", [128, 512], F32)
    PT = nc.alloc_psum_tensor("PT", [128, 1024], BF16)
    P2 = [PA, PB, PC]
    p2state = [0]

    def bank(i):
        if i < 6:
            return P2[i // 2][:, (i % 2) * 512:(i % 2 + 1) * 512]
        return PD[:, :]

    def ps(i):
        return ('ps', i)

    def fs_next():
        i = fstate[0] % 4
        fstate[0] += 1
        return FS[i], ('fs', i)

    def p2_next():
        i = p2state[0] % 3
        p2state[0] += 1
        return P2[i], [ps(2 * i), ps(2 * i + 1)]

    WT = WTinit

    def wmk(desc):
        (name, idx), r0, nk, c0, ncol = desc
        t = WT[name]
        for i in idx:
            t = t[i]
        return t[r0:r0 + nk * 128, c0:c0 + ncol].rearrange("(k p) n -> p k n", p=128)

    def w_issue(j):
        desc = wsched_n[j]
        nk, ncol = desc[2], desc[4]
        i = j % NW
        dst = wpool[i][:, 0:nk * ncol].rearrange("p (k n) -> p k n", k=nk)
        DMA('pool', dst, wmk(desc), (), [('w', i)], 'dw')

    def wload(w2d, r0, nk, c0, ncol):
        desc = (w2d, r0, nk, c0, ncol)
        k = wstate[0]
        wstate[0] += 1
        i = k % NW
        dst = wpool[i][:, 0:nk * ncol].rearrange("p (k n) -> p k n", k=nk)
        if wsched_n is None:
            wrec.append(desc)
            DMA('pool', dst, wmk(desc), (), [('w', i)], 'dw')
        else:
            assert wsched_n[k] == desc
            while wissued[0] <= min(k + LOOK, len(wsched_n) - 1):
                w_issue(wissued[0])
                wissued[0] += 1
        return dst, ('w', i)

    def wview(w2d, r0, nk, c0, ncol):
        return w2d[r0:r0 + nk * 128, c0:c0 + ncol].rearrange("(k p) n -> p k n", p=128)

    DMA('sp', ident_f[:], c_ident, (), ['ident_f'], 'di')
    DMA('pool', ident_b[:], c_ident, (), ['ident_b'], 'dw')
    DMA('pool', prot_b[:], c_prot, (), ['prot_b'], 'dw')
    DMA('sp', maskF[:], c_maskF, (), ['maskF'], 'di')
    DMA('sp', maskB[:], c_maskB, (), ['maskB'], 'di')
    DMA('sp', sel[:], c_sel, (), ['sel'], 'di')
    DMA('sp', dirm[:], c_dirm, (), ['dirm'], 'di')
    DMA('sp', ropeC[:], c_ropeC, (), ['ropeC'], 'di')
    DMA('sp', ropeS[:], c_ropeS, (), ['ropeS'], 'di')
    DMA('sp', maskb[:], c_maskb, (), ['maskb'], 'di')
    DMA('sp', link[:], c_link, (), ['link'], 'di')
    MS('dve', ones_b[:], 1.0, ['ones_b'])
    MS('dve', eps_t[:], EPS, ['eps_t'])
    MS('dve', one_t[:], 1.0, ['one_t'])
    MS('dve', mhalf[:], -0.5, ['mhalf'])
    MS('dve', ones8[:], 1.0, ['ones8'])
    for l in range(L):
        DMA('sp', GB[:, l, 0:1], ml_gate_b[l, 0:8, :], (), ['GB'], 'di')
        DMA('sp', GB[:, l, 1:2], ml_gate_b[l, 8:16, :], (), ['GB'], 'di')

    def colparams(rows_list, dst, dkey):
        st, sk = fs_next()
        r0 = 0
        for ap, R in rows_list:
            DMA('sp', st[r0:r0 + R, 0:128], ap, (), [sk], 'di')
            r0 += R
        TR(bank(6)[:, 0:r0], st[0:r0, 0:128], ident_f[0:r0, 0:r0], [sk, 'ident_f'], [ps(6)])
        CP('dve', dst[:, 0:r0], bank(6)[:, 0:r0], [ps(6)], [dkey])

    for l in range(depth):
        colparams([(b_mod[l], 48), (ml_conv_w[l], 24), (ml_conv_b[l], 8), (ffn_conv_b[l], 44), (da_norm_g[l], 1)],
                  CPm[:, l, 0, :], 'CPm')
        colparams([(ffn_conv_w[l, 0:128, :], 128)], CPm[:, l, 1, :], 'CPm')
        colparams([(ffn_conv_w[l, 128:132, :], 4)], CPm[:, l, 2, :], 'CPm')
    colparams([(final_g, 8), (cond, 8)], gcol[:, :], 'gcol')

    def bmod_c(l):
        return CPm[:, l, 0, 0:48]

    def mlw_c(l, k, c):
        return CPm[:, l, 0, 48 + k * 8 + c:48 + k * 8 + c + 1]

    def mlb_c(l, c):
        return CPm[:, l, 0, 72 + c:73 + c]

    def ffb_c(l, c):
        return CPm[:, l, 0, 80 + c:81 + c]

    def dag_c(l):
        return CPm[:, l, 0, 124:125]

    def ffw_c(l, k, c):
        j = k * 44 + c
        if j < 128:
            return CPm[:, l, 1, j:j + 1]
        return CPm[:, l, 2, j - 128:j - 127]

    ACT(cond_b[:], gcol[:, 8:16], AF.Silu, ['gcol'], ['cond_b'])

    with ExitStack() as es:
        dl = es.enter_context(sbt("dl", [128, L * 256], F32))
        pr = es.enter_context(sbt("pr", [128, L * 128], F32))
        sm2 = es.enter_context(sbt("sm2", [128, L * 2], F32))
        DMA('sp', dl[:], da_lambda.partition_broadcast(128), (), ['dl'], 'di')
        dlv = dl[:].rearrange("p (l a b d) -> p l a b d", l=L, a=2, b=2)
        TT('dve', pr[:].rearrange("p (l a d) -> p l a d", l=L, a=2), dlv[:, :, :, 0, :], dlv[:, :, :, 1, :], ALU.mult,
           ['dl'], ['pr'])
        T.op('dve', lambda: nc.vector.tensor_reduce(out=sm2[:], in_=pr[:].rearrange("p (q d) -> p q d", d=64),
                                                     axis=AX.X, op=ALU.add), ['pr'], ['sm2'])
        ACT(sm2[:], sm2[:], AF.Exp, ['sm2'], ['sm2'])
        s2v = sm2[:].rearrange("p (l a) -> p l a", a=2)
        TT('dve', lam[:], s2v[:, :, 0], s2v[:, :, 1], ALU.subtract, ['sm2'], ['lam'])
        for l in range(L):
            li = 0.8 - 0.6 * math.exp(-0.3 * l)
            TS('dve', lam[:, l:l + 1], lam[:, l:l + 1], li, None, ALU.add, None, ['lam'], ['lam'])
        TS('dve', nlam[:], lam[:], -1.0, None, ALU.mult, None, ['lam'], ['nlam'])
        T.free(['dl', 'pr', 'sm2'])

    for tc in range(8):
        st, sk = fs_next()
        DMA('sp', st[:, :], xin[tc * 128:(tc + 1) * 128, :], (), [sk], 'di')
        for half in range(2):
            bk = half
            for j in range(4):
                dc = half * 4 + j
                TR(bank(bk)[:, j * 128:(j + 1) * 128], st[:, dc * 128:(dc + 1) * 128], ident_f[:], [sk, 'ident_f'], [ps(bk)])
            CP('dve' if half == 0 else 'act', xT[:, half * 4:half * 4 + 4, tc * 128:(tc + 1) * 128],
               bank(bk).rearrange("p (a b) -> p a b", a=4), [ps(bk)], [('x', half * 4 + j) for j in range(4)])

    dump('xT0', xT[:, :, :], [('x', dc) for dc in range(8)])
    def compute_mod_gen(l):
        par = l % 2
        for g in range(12):
            w, wk = wload(('w_mod', (l,)), 0, 8, g * 512, 512)
            for j in range(4):
                col = g * 4 + j
                for kc in range(8):
                    MM(bank(6)[:, col:col + 1], w[:, kc, j * 128:(j + 1) * 128], cond_b[:, kc:kc + 1], kc == 0, kc == 7,
                       [wk, 'cond_b'], [ps(6)])
            yield
        TT('dve', modT[:, par, :], bank(6)[:, 0:48], bmod_c(l), ALU.add, [ps(6), 'CPm'], [('mod', par)])
        TS('dve', scp[:, par, 0:8], modT[:, par, 8:16], 1.0, None, ALU.add, None, [('mod', par)], [('scp', par)])
        TS('dve', scp[:, par, 8:16], modT[:, par, 32:40], 1.0, None, ALU.add, None, [('mod', par)], [('scp', par)])

    def compute_mod(l):
        for _ in compute_mod_gen(l):
            pass


    def rms_stats(src_chunks, rkeys, sq_dst, sqkeys, nfeat):
        n = len(src_chunks)
        for i in range(n):
            ACT(sq_dst[i], src_chunks[i], AF.Square, [rkeys[i]], [sqkeys[i]])
        for th in range(2):
            for i in range(n):
                MM(PA[:, th * 512:(th + 1) * 512], ones_b[:], sq_dst[i][:, th * 512:(th + 1) * 512], i == 0, i == n - 1,
                   [sqkeys[i], 'ones_b'], [ps(th)])
        ACT(rstd[:], PA[:, :], AF.Ln, [ps(0), ps(1), 'eps_t'], ['rstd'], bias=eps_t[:], scale=1.0 / nfeat)
        ACT(rstd[:], rstd[:], AF.Exp, ['rstd'], ['rstd'], scale=-0.5)

    def norm_mod(l, which):
        par = l % 2
        rms_stats([xT[:, dc, :] for dc in range(8)], [('x', dc) for dc in range(8)],
                  [hT[:, dc, :] for dc in range(8)], [('h', dc) for dc in range(8)], 1024.0)
        so = 0 if which == 0 else 24
        for dc in range(8):
            tmp, tk = fs_next()
            STT(tmp[:], xT[:, dc, :], scp[:, par, which * 8 + dc:which * 8 + dc + 1], rstd[:], ALU.mult, ALU.mult,
                [('x', dc), ('scp', par), 'rstd'], [tk])
            ACT(hT[:, dc, :], tmp[:], AF.Identity, [tk, ('mod', par)], [('h', dc)], bias=modT[:, par, so + dc:so + dc + 1])

    def dwconv(l, zps, zkeys, acc, akey, w0, w1, w2, bia, w0nn, w2nn):
        ACT(acc[:, :], zps[:, :], AF.Identity, zkeys + ['CPm'], [akey], bias=bia, scale=w1)
        STT(acc[:, 1:1024], zps[:, 0:1023], w0, acc[:, 1:1024], ALU.mult, ALU.add, zkeys + ['CPm', akey], [akey])
        STT(acc[:, 0:1023], zps[:, 1:1024], w2, acc[:, 0:1023], ALU.mult, ALU.add, zkeys + ['CPm', akey], [akey])
        STT(acc[:, 256:1024:256], zps[:, 255:1023:256], w0nn, acc[:, 256:1024:256], ALU.mult, ALU.add,
            zkeys + [('w0n', l % 2), akey], [akey])
        STT(acc[:, 255:1023:256], zps[:, 256:1024:256], w2nn, acc[:, 255:1023:256], ALU.mult, ALU.add,
            zkeys + [('w0n', l % 2), akey], [akey])

    def prep_w0n(l):
        par = l % 2
        TS('pool', w0n[:, par, 0:44], CPm[:, l, 1, 0:44], link[:, 1:2], -1.0, ALU.mult, ALU.mult, ['CPm', 'link'], [('w0n', par)])
        TS('pool', w2n[:, par, 0:40], CPm[:, l, 1, 88:128], link[:, 1:2], -1.0, ALU.mult, ALU.mult, ['CPm', 'link'], [('w0n', par)])
        TS('pool', w2n[:, par, 40:44], CPm[:, l, 2, 0:4], link[:, 1:2], -1.0, ALU.mult, ALU.mult, ['CPm', 'link'], [('w0n', par)])
        TS('pool', w0n[:, par, 44:52], CPm[:, l, 0, 48:56], link[:, 1:2], -1.0, ALU.mult, ALU.mult, ['CPm', 'link'], [('w0n', par)])
        TS('pool', w2n[:, par, 44:52], CPm[:, l, 0, 64:72], link[:, 1:2], -1.0, ALU.mult, ALU.mult, ['CPm', 'link'], [('w0n', par)])

    w2n = sb("w2n", [128, 2, 52])

    def ffn(l):
        par = l % 2
        with ExitStack() as es:
            act = es.enter_context(sbt("ffn_act", [128, 22, 1024], BF16))
            sa = es.enter_context(sbt("ffn_sa", [128, 4, 1024], F32))
            T.adopt([('act', j) for j in range(22)] + [('sa', j) for j in range(4)])
            norm_mod(l, 1)
            if l == 0:
                dump('mod0', modT[:, 0, :], [('mod', 0)])
                dump('h2', hT[:, :, :], [('h', dc) for dc in range(8)])
                dump('rstd', rstd[:, :], ['rstd'])
            for g in range(6):
                nchunk = 4 if g < 5 else 2
                for ab in range(2):
                    c0 = ab * DFF + g * 512
                    w, wk = wload(('w_up', (l,)), 0, 8, c0, nchunk * 128)
                    for j in range(nchunk):
                        cc = ab * 22 + g * 4 + j
                        zp, zk = p2_next()
                        for th in range(2):
                            for kc in range(8):
                                MM(zp[:, th * 512:(th + 1) * 512], w[:, kc, j * 128:(j + 1) * 128],
                                   hT[:, kc, th * 512:(th + 1) * 512], kc == 0, kc == 7, [wk, ('h', kc)], [zk[th]])
                        acc, ak = fs_next()
                        dwconv(l, zp, zk, acc, ak, ffw_c(l, 0, cc), ffw_c(l, 1, cc), ffw_c(l, 2, cc), ffb_c(l, cc),
                               w0n[:, par, cc:cc + 1],
                               w2n[:, par, cc:cc + 1])
                        if ab == 0:
                            ACT(sa[:, j, :], acc[:, :], AF.Silu, [ak], [('sa', j)])
                        else:
                            TT('pool', act[:, g * 4 + j, :], sa[:, j, :], acc[:, :], ALU.mult, [('sa', j), ak],
                               [('act', g * 4 + j)])
            if l == 0:
                dump('act', act[:, :, :], [('act', j) for j in range(22)])
            for jp in range(4):
                wA, wkA = wload(('w_down', (l,)), 0, 11, jp * 256, 256)
                wB, wkB = wload(('w_down', (l,)), 1408, 11, jp * 256, 256)
                for half, (w, wk) in enumerate(((wA, wkA), (wB, wkB))):
                    for dj in range(2):
                        for th in range(2):
                            bk = dj * 2 + th
                            for kk in range(11):
                                kc = half * 11 + kk
                                MM(bank(bk), w[:, kk, dj * 128:(dj + 1) * 128], act[:, kc, th * 512:(th + 1) * 512],
                                   half == 0 and kk == 0, half == 1 and kk == 10, [wk, ('act', kc)], [ps(bk)])
                for dj in range(2):
                    dc = jp * 2 + dj
                    for th in range(2):
                        bk = dj * 2 + th
                        STT(xT[:, dc, th * 512:(th + 1) * 512], bank(bk), modT[:, par, 40 + dc:41 + dc],
                            xT[:, dc, th * 512:(th + 1) * 512], ALU.mult, ALU.add, [ps(bk), ('mod', par), ('x', dc)],
                            [('x', dc)])
            T.free([('act', j) for j in range(22)] + [('sa', j) for j in range(4)])


    def run_pipeline(jobs, make_gen, nslots, extra=None, extra_every=1):
        active = []
        free_slots = list(range(nslots))
        nxt = 0
        step = 0
        while nxt < len(jobs) or active:
            if nxt < len(jobs) and free_slots:
                sl_ = free_slots.pop(0)
                g = make_gen(jobs[nxt], sl_)
                nxt += 1
                next(g)
                active.append((g, sl_, True))
            still = []
            for (g, sl_, fresh) in active:
                if fresh:
                    still.append((g, sl_, False))
                    continue
                try:
                    next(g)
                    still.append((g, sl_, False))
                except StopIteration:
                    free_slots.append(sl_)
            active = still
            step += 1
            if extra is not None and extra[0] is not None and step % extra_every == 0:
                try:
                    next(extra[0])
                except StopIteration:
                    extra[0] = None
        if extra is not None and extra[0] is not None:
            for _ in extra[0]:
                pass

    def sg_phase(l, ysgT):
        wl = w_in[l]
        with ExitStack() as es:
            uT = es.enter_context(sbt("sg_uT", [128, 4, 1024], BF16))
            sgwT = es.enter_context(sbt("sg_wT", [128, 4, 128], BF16))
            sgwf = es.enter_context(sbt("sg_wf", [128, 4, 128], F32))
            gb = es.enter_context(sbt("sg_gb", [128, 2, 512], F32))
            zz = es.enter_context(sbt("sg_zz", [128, 4, 512], F32))
            svb = es.enter_context(sbt("sg_svb", [128, 4, 512], BF16))
            st6 = es.enter_context(sbt("sg_st", [128, 4, 8], F32))
            keys = [('sg_u', j) for j in range(4)] + ['sg_wT', 'sg_wf', 'sg_gb'] + [(n_, b_) for n_ in ('sg_zz', 'sg_svb', 'sg_st') for b_ in range(4)]
            T.adopt(keys)
            DMA('sp', gb[:, 0, :], sg_norm_g[l].partition_broadcast(128), (), ['sg_gb'], 'di')
            DMA('sp', gb[:, 1, :], sg_b[l].partition_broadcast(128), (), ['sg_gb'], 'di')
            DMA('sp', sgwf[:, :, :], sg_w[l].rearrange("g p q -> p g q"), (), ['sg_wf'], 'di')
            for g in range(4):
                TR(bank(6)[:, g * 128:(g + 1) * 128], sgwf[:, g, :], ident_f[:], ['sg_wf', 'ident_f'], [ps(6)])
            CP('dve', sgwT[:, :, :], bank(6).rearrange("p (g q) -> p g q", g=4), [ps(6)], ['sg_wT'])
            w, wk = wload(('w_in', (l,)), 0, 8, 3600, 512)
            for j in range(4):
                zp, zk = p2_next()
                for th in range(2):
                    for kc in range(8):
                        MM(zp[:, th * 512:(th + 1) * 512], w[:, kc, j * 128:(j + 1) * 128], hT[:, kc, th * 512:(th + 1) * 512],
                           kc == 0, kc == 7, [wk, ('h', kc)], [zk[th]])
                ACT(uT[:, j, :], zp[:, :], AF.Gelu_apprx_tanh, zk, [('sg_u', j)])
            w, wk = wload(('w_in', (l,)), 0, 8, 4112, 512)

            def sg_job(tc, b):
                bk = b
                gbk = 4 + b % 2
                for kc in range(8):
                    MM(bank(bk), hT[:, kc, tc * 128:(tc + 1) * 128], w[:, kc, :], kc == 0, kc == 7, [('h', kc), wk], [ps(bk)])
                yield
                ACT(zz[:, b, :], bank(bk), AF.Gelu_apprx_tanh, [ps(bk)], [('sg_zz', b)])
                yield
                T.op('dve', lambda: nc.vector.bn_stats(out=st6[:, b, 0:6], in_=zz[:, b, :]), [('sg_zz', b)], [('sg_st', b)])
                T.op('dve', lambda: nc.vector.bn_aggr(out=st6[:, b, 6:8], in_=st6[:, b, 0:6]), [('sg_st', b)], [('sg_st', b)])
                yield
                TS('pool', st6[:, b, 7:8], st6[:, b, 7:8], EPS, None, ALU.add, None, [('sg_st', b)], [('sg_st', b)])
                TT('pool', st6[:, b, 7:8], st6[:, b, 7:8], mhalf[:, 0:1], ALU.pow, [('sg_st', b), 'mhalf'], [('sg_st', b)])
                yield
                TS('dve', zz[:, b, :], zz[:, b, :], st6[:, b, 6:7], st6[:, b, 7:8], ALU.subtract, ALU.mult,
                   [('sg_zz', b), ('sg_st', b)], [('sg_zz', b)])
                yield
                TT('pool', svb[:, b, :], zz[:, b, :], gb[:, 0, :], ALU.mult, [('sg_zz', b), 'sg_gb'], [('sg_svb', b)])
                yield
                for g in range(4):
                    MM(bank(gbk)[:, g * 128:(g + 1) * 128], svb[:, b, g * 128:(g + 1) * 128], sgwT[:, g, :], True, True,
                       [('sg_svb', b), 'sg_wT'], [ps(gbk)])
                yield
                tmp, tk = fs_next()
                TT('dve', tmp[:, 0:512], bank(gbk), gb[:, 1, :], ALU.add, [ps(gbk), 'sg_gb'], [tk])
                yield
                TT('pool', ysgT[:, :, tc * 128:(tc + 1) * 128], tmp[:, 0:512].rearrange("p (g t) -> p g t", g=4),
                   uT[:, :, tc * 128:(tc + 1) * 128], ALU.mult, [tk] + [('sg_u', j) for j in range(4)],
                   [('ysg', j) for j in range(4)])

            run_pipeline(list(range(8)), sg_job, 4)
            T.free(keys)

    def merge(l, ys):
        par = l % 2
        ynames = ['yda', 'yml', 'ysg']
        with ExitStack() as es:
            mg = es.enter_context(sbt("mg", [128, 8, 1024], BF16))
            accf = es.enter_context(sbt("mg_acc", [128, 4, 1024], F32))
            keys = [('mg', dc) for dc in range(8)] + [('mg_acc', j) for j in range(4)]
            T.adopt(keys)
            for dcg in range(2):
                for n in range(3):
                    w, wk = wload(('w_in', (l,)), 0, 8, 4624 + n * 1024 + dcg * 512, 512)
                    wb, wbk = wload(('w_branch', (l, n)), 0, 4, 0, 1024)
                    for j in range(4):
                        dc = dcg * 4 + j
                        gp, gk = p2_next()
                        for th in range(2):
                            for kc in range(8):
                                MM(gp[:, th * 512:(th + 1) * 512], w[:, kc, j * 128:(j + 1) * 128],
                                   hT[:, kc, th * 512:(th + 1) * 512], kc == 0, kc == 7, [wk, ('h', kc)], [gk[th]])
                        pp, pk = p2_next()
                        for th in range(2):
                            for kc in range(4):
                                MM(pp[:, th * 512:(th + 1) * 512], wb[:, kc, dc * 128:(dc + 1) * 128],
                                   ys[n][:, kc, th * 512:(th + 1) * 512], kc == 0, kc == 3, [wbk, (ynames[n], kc)], [pk[th]])
                        sgt, sk = fs_next()
                        ACT(sgt[:, :], gp[:, :], AF.Sigmoid, gk, [sk])
                        if n == 0:
                            TT('dve', accf[:, j, :], pp[:, :], sgt[:, :], ALU.mult, pk + [sk], [('mg_acc', j)])
                        else:
                            t2, t2k = fs_next()
                            TT('dve', t2[:, :], pp[:, :], sgt[:, :], ALU.mult, pk + [sk], [t2k])
                            if n == 1:
                                TT('pool', accf[:, j, :], accf[:, j, :], t2[:, :], ALU.add, [('mg_acc', j), t2k], [('mg_acc', j)])
                            else:
                                TT('pool', mg[:, dc, :], accf[:, j, :], t2[:, :], ALU.add, [('mg_acc', j), t2k], [('mg', dc)])
            for og_ in range(2):
                w, wk = wload(('w_out', (l,)), 0, 8, og_ * 512, 512)
                for j in range(4):
                    dc = og_ * 4 + j
                    zp, zk = p2_next()
                    for th in range(2):
                        for kc in range(8):
                            MM(zp[:, th * 512:(th + 1) * 512], w[:, kc, j * 128:(j + 1) * 128], mg[:, kc, th * 512:(th + 1) * 512],
                               kc == 0, kc == 7, [wk, ('mg', kc)], [zk[th]])
                    for th in range(2):
                        STT(xT[:, dc, th * 512:(th + 1) * 512], zp[:, th * 512:(th + 1) * 512], modT[:, par, 16 + dc:17 + dc],
                            xT[:, dc, th * 512:(th + 1) * 512], ALU.mult, ALU.add, [zk[th], ('mod', par), ('x', dc)], [('x', dc)])
            T.free(keys)

    def da_phase(l, ydaT):
        wl = w_in[l]
        with ExitStack() as es:
            KT = es.enter_context(sbt("da_KT", [128, 4, 1280], BF16))
            qT = es.enter_context(sbt("da_qT", [128, 4, 1024], BF16))
            V = es.enter_context(sbt("da_V", [128, 10, 512], BF16))
            oall = es.enter_context(sbt("da_oall", [128, 4, 1024], F32))
            ET = es.enter_context(sbt("da_ET", [128, 3, 512], BF16))
            kbf = es.enter_context(sbt("da_kbf", [128, 2, 1024], BF16))
            ckf = es.enter_context(sbt("da_ckf", [128, 2, 128], F32))
            osb = es.enter_context(sbt("da_osb", [128, 2, 256], F32))
            rs = es.enter_context(sbt("da_rs", [128, 2, 256], F32))
            vst = es.enter_context(sbt("da_vst", [128, 2, 512], F32))
            dagl = es.enter_context(sbt("da_gl", [128, 1], F32))
            keys = ([('KT', h) for h in range(4)] + [('qT', h) for h in range(4)] + [('V', c) for c in range(10)] +
                    [('oall', h) for h in range(4)] + [('ET', i) for i in range(3)] + [('kbf', 0), ('kbf', 1), 'ckf', ('osb', 0), ('osb', 1),
                                                                                        ('rs', 0), ('rs', 1), ('vst', 0), ('vst', 1), 'dagl'])
            T.adopt(keys)
            TS('dve', dagl[:, :], dag_c(l), 1.0 - (0.8 - 0.6 * math.exp(-0.3 * l)), None, ALU.mult, None, ['CPm'], ['dagl'])
            for hd in range(4):
                DMA('pool', V[:, 0:2, hd * 128:(hd + 1) * 128], cv[l, hd].rearrange("(c p) d -> p c d", p=128), (),
                    [('V', 0), ('V', 1)], 'dw')
            vck = vst[:, :, :].rearrange("p a (h d) -> p (a h) d", h=2)
            for hd in range(4):
                DMA('sp', vck[:, hd, :].rearrange("p (c d) -> p c d", c=2), ck[l, hd].rearrange("(c p) d -> p c d", p=128), (),
                    [('vst', hd // 2)], 'di')
            for hp in range(2):
                for hh in range(2):
                    hd = hp * 2 + hh
                    for c in range(2):
                        TR(bank(6)[:, (hh * 2 + c) * 128:(hh * 2 + c + 1) * 128], vck[:, hd, c * 128:(c + 1) * 128], ident_f[:],
                           [('vst', hp), 'ident_f'], [ps(6)])
                CP('dve', KT[:, hp * 2:hp * 2 + 2, 0:256], bank(6).rearrange("p (h k) -> p h k", h=2), [ps(6)],
                   [('KT', hp * 2), ('KT', hp * 2 + 1)])
            wqk = {}

            def proj_job(job, slot):
                which, hd = job
                if hd == 0:
                    c0 = 512 if which == 0 else 0
                    wqk[which] = wload(('w_in', (l,)), 0, 8, c0, 512)
                w, wk = wqk[which]
                zp, zk = P2[slot], [ps(2 * slot), ps(2 * slot + 1)]
                for th in range(2):
                    for kc in range(8):
                        MM(zp[:, th * 512:(th + 1) * 512], w[:, kc, hd * 128:(hd + 1) * 128], hT[:, kc, th * 512:(th + 1) * 512],
                           kc == 0, kc == 7, [wk, ('h', kc)], [zk[th]])
                yield
                CP('act', kbf[:, slot, :], zp[:, :], zk, [('kbf', slot)])
                yield
                sp_, spk = PC, [ps(4), ps(5)]
                for th in range(2):
                    MM(sp_[:, th * 512:(th + 1) * 512], prot_b[:], kbf[:, slot, th * 512:(th + 1) * 512], True, True,
                       ['prot_b', ('kbf', slot)], [spk[th]])
                yield
                t1, t1k = fs_next()
                t2, t2k = fs_next()
                TT('dve', t1[:, :], zp[:, :], ropeC[:, :], ALU.mult, zk + ['ropeC'], [t1k])
                TT('dve', t2[:, :], sp_[:, :], ropeS[:, :], ALU.mult, spk + ['ropeS'], [t2k])
                yield
                if which == 0:
                    TT('pool', t1[:, :], t1[:, :], t2[:, :], ALU.add, [t1k, t2k], [t1k])
                    CP('pool', KT[:, hd, 256:1280], t1[:, :], [t1k], [('KT', hd)])
                    yield
                    for tcg in range(2):
                        for j in range(4):
                            tc = tcg * 4 + j
                            TR(bank(6)[:, j * 128:(j + 1) * 128], t1[:, tc * 128:(tc + 1) * 128], ident_f[:],
                               [t1k, 'ident_f'], [ps(6)])
                        CP('act', vst[:, tcg, :], bank(6), [ps(6)], [('vst', tcg)])
                        DMA('sp', ok_o[l, hd, tcg * 512:(tcg + 1) * 512, :].rearrange("(j p) d -> p j d", p=128),
                            vst[:, tcg, :].rearrange("p (j d) -> p j d", j=4), [('vst', tcg)], (), 'do')
                else:
                    TT('pool', qT[:, hd, :], t1[:, :], t2[:, :], ALU.add, [t1k, t2k], [('qT', hd)])

            run_pipeline([(which, hd) for which in range(2) for hd in range(4)], proj_job, 2)
            w, wk = wload(('w_in', (l,)), 0, 8, 1024, 512)
            for tc in range(8):
                bk = 4 + tc % 2
                for kc in range(8):
                    MM(bank(bk), hT[:, kc, tc * 128:(tc + 1) * 128], w[:, kc, :], kc == 0, kc == 7, [('h', kc), wk], [ps(bk)])
                CP('act', V[:, 2 + tc, :], bank(bk), [ps(bk)], [('V', 2 + tc)])
                CP('dve', vst[:, tc % 2, :], bank(bk), [ps(bk)], [('vst', tc % 2)])
                DMA('sp', ov_o[l, :, tc * 128:(tc + 1) * 128, :].rearrange("h p d -> p h d"),
                    vst[:, tc % 2, :].rearrange("p (h d) -> p h d", h=4), [('vst', tc % 2)], (), 'do')
            iters = [(hd, qt, kc) for hd in range(4) for qt in range(4) for kc in range(10)]
            rsf = rs[:, :, :].rearrange("p a b -> p (a b)")
            osf = osb[:, :, :].rearrange("p a b -> p (a b)")

            PT32 = PT[:, :].bitcast(F32)
            accb = [(bank(4), bank(5), ps(4), ps(5)), (bank(6), PT32, ps(6), ps(7))]
            SR = [(PA, ps(0), ps(1)), (PB, ps(2), ps(3))]

            def emit_qk(i):
                hd, qt, kc = iters[i]
                reg, k0, k1 = SR[i % 2]
                for half in range(2):
                    lo, hi = half * 64, half * 64 + 64
                    MM(reg[:, half * 512:half * 512 + 256], KT[lo:hi, hd, kc * 128:(kc + 1) * 128],
                       qT[lo:hi, hd, qt * 256:(qt + 1) * 256], True, True, [('KT', hd), ('qT', hd)], [k0 if half == 0 else k1])

            def emit_rest(i):
                hd, qt, kc = iters[i]
                reg, k0, k1 = SR[i % 2]
                sbk = i % 3
                ao, as_, ko, ks = accb[(hd * 4 + qt) % 2]
                ACT(ET[:, sbk, :].rearrange("p (a b) -> p a b", a=2), reg[:, :].rearrange("p (a b) -> p a b", a=2)[:, :, 0:256],
                    AF.Exp, [k0, k1, 'maskb'], [('ET', sbk)],
                    bias=maskb[:, (kc // 2) * 4 + qt:(kc // 2) * 4 + qt + 1], scale=0.125)
                MM(ao, V[:, kc, hd * 128:(hd + 1) * 128], ET[:, sbk, :], kc == 0, kc == 9, [('V', kc), ('ET', sbk)], [ko])
                MM(as_, ones_b[:], ET[:, sbk, :], kc == 0, kc == 9, ['ones_b', ('ET', sbk)], [ks])
                if kc == 9:
                    T.op('dve', lambda: nc.vector.reciprocal(out=rsf, in_=as_), [ks], [('rs', 0), ('rs', 1)])
                    TT('dve', osf, ao, rsf, ALU.mult, [ko, ('rs', 0), ('rs', 1)], [('osb', 0), ('osb', 1)])
                    STT(oall[:, hd, qt * 256:(qt + 1) * 256], osb[:, 1, :], nlam[:, l:l + 1], osb[:, 0, :], ALU.mult, ALU.add,
                        [('osb', 0), ('osb', 1), 'nlam'], [('oall', hd)])

            emit_qk(0)
            for i in range(len(iters)):
                if i + 1 < len(iters):
                    emit_qk(i + 1)
                emit_rest(i)
            for hd in range(4):
                ACT(ydaT[:, hd, :], oall[:, hd, :], AF.Square, [('oall', hd)], [('yda', hd)])
                for th in range(2):
                    MM(PA[:, th * 512:(th + 1) * 512], ones_b[:], ydaT[:, hd, th * 512:(th + 1) * 512], True, True,
                       [('yda', hd), 'ones_b'], [ps(th)])
                ACT(rstd[:], PA[:, :], AF.Ln, [ps(0), ps(1), 'eps_t'], ['rstd'], bias=eps_t[:], scale=1.0 / 128.0)
                ACT(rstd[:], rstd[:], AF.Exp, ['rstd'], ['rstd'], scale=-0.5)
                STT(ydaT[:, hd, :], oall[:, hd, :], dagl[:, 0:1], rstd[:], ALU.mult, ALU.mult, [('oall', hd), 'dagl', 'rstd'],
                    [('yda', hd)])
            T.free(keys)

    def ml_phase(l, ymlT):
        par = l % 2
        wl = w_in[l]
        LK = (2, 4, 6)
        with ExitStack() as es:
            COLS = es.enter_context(sbt("ml_cols", [128, 8, 4, 8], F32))
            DECB = es.enter_context(sbt("ml_decb", [128, 8, 8], F32))
            MP = es.enter_context(sbt("ml_mp", [8, 16], F32))
            MPL = es.enter_context(sbt("ml_mpl", [8, 16], F32))
            NM = es.enter_context(sbt("ml_nm", [8, 16], F32))
            DT = es.enter_context(sbt("ml_dt", [8, 16], F32))
            DEC = es.enter_context(sbt("ml_dec", [8, 16], F32))
            k1 = ['ml_cols', 'ml_decb', 'ml_mp', 'ml_mpl', 'ml_nm', 'ml_dt', 'ml_dec']
            T.adopt(k1)
            with ExitStack() as es2:
                Rr = [es2.enter_context(sbt(f"ml_r{i}", [8, 1024], F32)) for i in range(9)]
                rk = [('ml_r', i) for i in range(9)]
                T.adopt(rk)
                R0, R1, R2, R3, R4, R5, R6, R7a, R7b = Rr
                w, wk = wload(('w_in', (l,)), 0, 8, 3584, 16)
                for (pp, c0, kk) in ((PA, 0, [ps(0), ps(1)]), (PB, 8, [ps(2), ps(3)])):
                    for th in range(2):
                        for kc in range(8):
                            MM(pp[0:8, th * 512:(th + 1) * 512], w[:, kc, c0:c0 + 8], hT[:, kc, th * 512:(th + 1) * 512],
                               kc == 0, kc == 7, [wk, ('h', kc)], [kk[th]])
                for (pp, kk, dst, dk, bcol) in ((PA, [ps(0), ps(1)], R0, rk[0], 0), (PB, [ps(2), ps(3)], R1, rk[1], 1)):
                    ACT(R4[:, :], pp[0:8, :], AF.Identity, kk + ['GB'], [rk[4]], bias=GB[:, l, bcol:bcol + 1])
                    ACT(R5[:, :], pp[0:8, ::-1], AF.Identity, kk + ['GB'], [rk[5]], bias=GB[:, l, bcol:bcol + 1])
                    TS('dve', dst[:, :], R4[:, :], dirm[:, 0:1], None, ALU.mult, None, [rk[4], 'dirm'], [dk])
                    STT(dst[:, :], R5[:, :], dirm[:, 1:2], dst[:, :], ALU.mult, ALU.add, [rk[5], 'dirm', dk], [dk])
                ACT(R1[:, :], R1[:, :], AF.Exp, [rk[1]], [rk[1]], scale=-1.0)
                ACT(R1[:, :], R1[:, :], AF.Ln, [rk[1], 'one_t'], [rk[1]], bias=one_t[0:8, :], scale=1.0)
                for c in range(8):
                    sl = slice(c * 128, (c + 1) * 128)
                    T.op('dve', lambda: nc.vector.tensor_tensor_scan(out=R2[:, sl], data0=ones8[:, :], data1=R1[:, sl], initial=0.0,
                                                                      op0=ALU.mult, op1=ALU.add), [rk[1], 'ones8'], [rk[2]])
                TT('dve', R0[:, :], R0[:, :], R2[:, :], ALU.add, [rk[0], rk[2]], [rk[0]])
                for c in range(8):
                    sl = slice(c * 128, (c + 1) * 128)
                    T.op('dve', lambda: nc.vector.tensor_tensor_scan(out=R3[:, sl], data0=R0[:, sl], data1=R0[:, sl], initial=-1e30,
                                                                      op0=ALU.max, op1=ALU.max), [rk[0]], [rk[3]])
                DMA('sp', MP[:, 0:1], sm[l], (), ['ml_mp'], 'di')
                for cs in range(8):
                    sl = slice(cs * 128, (cs + 1) * 128)
                    la = cs * 128 + 127
                    mprev = MP[:, cs:cs + 1]
                    mpk = 'ml_mp'
                    if cs in LK:
                        TS('dve', MPL[:, cs:cs + 1], mprev, link[0:8, 0:1], None, ALU.mult, None, ['ml_mp', 'link'], ['ml_mpl'])
                        mprev = MPL[:, cs:cs + 1]
                        mpk = 'ml_mpl'
                    TS('dve', R3[:, sl], R3[:, sl], mprev, None, ALU.max, None, [rk[3], mpk], [rk[3]])
                    TT('dve', MP[:, cs + 1:cs + 2], R3[:, la:la + 1], R2[:, la:la + 1], ALU.subtract, [rk[3], rk[2]], ['ml_mp'])
                    ACT(R5[:, sl], R3[:, sl], AF.Exp, [rk[3], mpk], [rk[5]], bias=mprev, scale=-1.0)
                    if cs in LK:
                        TS('dve', R5[:, sl], R5[:, sl], link[0:8, 0:1], None, ALU.mult, None, [rk[5], 'link'], [rk[5]])
                    ACT(R4[:, sl], R3[:, sl], AF.Exp, [rk[3]], [rk[4]], bias=R3[:, la:la + 1], scale=-1.0)
                    TS('dve', NM[:, cs:cs + 1], R3[:, la:la + 1], -1.0, None, ALU.mult, None, [rk[3]], ['ml_nm'])
                    ACT(R0[:, sl], R0[:, sl], AF.Exp, [rk[0], 'ml_nm'], [rk[0]], bias=NM[:, cs:cs + 1], scale=1.0)
                    TT('dve', DT[:, cs:cs + 1], mprev, R2[:, la:la + 1], ALU.subtract, [mpk, rk[2]], ['ml_dt'])
                    TT('dve', DT[:, cs:cs + 1], DT[:, cs:cs + 1], MP[:, cs + 1:cs + 2], ALU.subtract, ['ml_dt', 'ml_mp'], ['ml_dt'])
                    ACT(DEC[:, cs:cs + 1], DT[:, cs:cs + 1], AF.Exp, ['ml_dt'], ['ml_dec'])
                    if cs in LK:
                        TS('dve', DEC[:, cs:cs + 1], DEC[:, cs:cs + 1], link[0:8, 0:1], None, ALU.mult, None, ['ml_dec', 'link'],
                           ['ml_dec'])
                TT('dve', R2[:, :], R2[:, :], R3[:, :], ALU.subtract, [rk[2], rk[3]], [rk[2]])
                ACT(R2[:, :], R2[:, :], AF.Exp, [rk[2]], [rk[2]])
                CP('dve', MPL[:, 12:16], MP[:, 2:10:2], ['ml_mp', 'ml_mpl'], ['ml_mpl'])
                DMA('sp', om_o[l], MPL[:, 12:16], ['ml_mpl'], (), 'do')
                for qi, (Q, qk) in enumerate(((R0, rk[0]), (R4, rk[4]), (R5, rk[5]), (R2, rk[2]))):
                    Ro, rok = (R7a, rk[7]) if qi % 2 == 0 else (R7b, rk[8])
                    TS('dve', R6[:, :], Q[:, :], dirm[:, 0:1], None, ALU.mult, None, [qk, 'dirm'], [rk[6]])
                    STT(Ro[:, :], Q[:, ::-1], dirm[:, 1:2], R6[:, :], ALU.mult, ALU.add, [rk[6], 'dirm', qk], [rok])
                    for c in range(8):
                        o0 = (c * 4 + qi) * 8
                        TR(bank(6)[:, o0:o0 + 8], Ro[:, c * 128:(c + 1) * 128], ident_f[0:8, 0:8], [rok, 'ident_f'], [ps(6)])
                CP('dve', COLS[:, :, :, :].rearrange("p a b c -> p (a b c)"), bank(6)[:, 0:256], [ps(6)], ['ml_cols'])
                for r in range(8):
                    MM(bank(5)[:, r * 8:(r + 1) * 8], sel[0:8, r * 128:(r + 1) * 128], DEC[0:8, 0:8], True, True, ['sel', 'ml_dec'],
                       [ps(5)])
                CP('dve', DECB[:, :, :].rearrange("p a b -> p (a b)"), bank(5)[:, 0:64], [ps(5)], ['ml_decb'])
                T.free(rk)
            hsum = es.enter_context(sbt("ml_hs", [128, 8, 512], F32))
            Cn32 = es.enter_context(sbt("ml_cn", [128, 8, 130], F32))
            T.adopt([('hs', c) for c in range(8)] + [('cn', r) for r in range(8)])
            es3 = ExitStack()
            mqT = es3.enter_context(sbt("ml_qT", [128, 4, 1024], BF16))
            mkT = es3.enter_context(sbt("ml_kT", [128, 4, 1024], BF16))
            mva = es3.enter_context(sbt("ml_va", [128, 8, 4, 130], BF16))
            Cnb = es3.enter_context(sbt("ml_cnb", [128, 8, 130], BF16))
            PTs = es3.enter_context(sbt("ml_pts", [128, 6, 128], BF16))
            kg = es3.enter_context(sbt("ml_kg", [128, 6, 128], BF16))
            vg = es3.enter_context(sbt("ml_vg", [128, 6, 130], BF16))
            tB = es3.enter_context(sbt("ml_tb", [128, 6, 130], F32))
            nd = es3.enter_context(sbt("ml_nd", [128, 6, 130], F32))
            dn = es3.enter_context(sbt("ml_dn", [128, 6, 2], F32))
            k2 = ([('mqT', h) for h in range(4)] + [('mkT', h) for h in range(4)] + [('mva', c) for c in range(8)] +
                  [('cnb', r) for r in range(8)] +
                  [(nm_, b) for nm_ in ('pts', 'kg', 'vg', 'tb', 'nd', 'dn') for b in range(6)])
            T.adopt(k2)
            for dr in range(2):
                for hd in range(4):
                    r = dr * 4 + hd
                    DMA('sp', Cn32[:, r, 0:129], sCn[l, dr, hd], (), [('cn', r)], 'di')
                    CP('pool', Cnb[:, r, 0:129], Cn32[:, r, 0:129], [('cn', r)], [('cnb', r)])
            for which, c0 in ((0, 1536), (1, 2048)):
                w, wk = wload(('w_in', (l,)), 0, 8, c0, 512)
                for hd in range(4):
                    zp, zk = p2_next()
                    for th in range(2):
                        for kc in range(8):
                            MM(zp[:, th * 512:(th + 1) * 512], w[:, kc, hd * 128:(hd + 1) * 128], hT[:, kc, th * 512:(th + 1) * 512],
                               kc == 0, kc == 7, [wk, ('h', kc)], [zk[th]])
                    ch = which * 4 + hd
                    acc, ak = fs_next()
                    dwconv(l, zp, zk, acc, ak, mlw_c(l, 0, ch), mlw_c(l, 1, ch), mlw_c(l, 2, ch), mlb_c(l, ch),
                           w0n[:, par, 44 + ch:45 + ch], w2n[:, par, 44 + ch:45 + ch])
                    if which == 0:
                        ACT(mqT[:, hd, :], acc[:, :], AF.Silu, [ak], [('mqT', hd)])
                    else:
                        sgt, sk = fs_next()
                        ACT(sgt[:, :], acc[:, :], AF.Sigmoid, [ak], [sk])
                        STT(mkT[:, hd, :], acc[:, :], 128.0 ** -0.5, sgt[:, :], ALU.mult, ALU.mult, [ak, sk], [('mkT', hd)])
            w, wk = wload(('w_in', (l,)), 0, 8, 2560, 512)
            MS('pool', mva[:, :, :, 128:130], 1.0, [('mva', c) for c in range(8)])
            for tc in range(8):
                bk = 4 + tc % 2
                for kc in range(8):
                    MM(bank(bk), hT[:, kc, tc * 128:(tc + 1) * 128], w[:, kc, :], kc == 0, kc == 7, [('h', kc), wk], [ps(bk)])
                CP('act', mva[:, tc, :, 0:128], bank(bk).rearrange("p (h d) -> p h d", h=4), [ps(bk)], [('mva', tc)])
            def core_iter(cs, hd, dr, slot):
                c = cs if dr == 0 else 7 - cs
                r = dr * 4 + hd
                tsl = slice(c * 128, (c + 1) * 128)
                mask = maskF if dr == 0 else maskB
                mkey = 'maskF' if dr == 0 else 'maskB'
                pb = bank(slot)
                pk_ = ps(slot)
                Sps, Aps, Bps, CNps = pb[:, 0:128], pb[:, 130:259], pb[:, 260:389], pb[:, 0:129]
                MM(Sps, mkT[:, hd, tsl], mqT[:, hd, tsl], True, True, [('mkT', hd), ('mqT', hd)], [pk_])
                TR(PT[:, slot * 128:(slot + 1) * 128], mkT[:, hd, tsl], ident_b[:], [('mkT', hd), 'ident_b'], [ps(7)])
                yield
                TT('dve', PTs[:, slot, :], Sps, mask[:, :], ALU.mult, [pk_, mkey], [('pts', slot)])
                ACT(kg[:, slot, :], PT[:, slot * 128:(slot + 1) * 128], AF.Identity, [ps(7), 'ml_cols'], [('kg', slot)],
                    scale=COLS[:, c, 0, r:r + 1])
                ACT(vg[:, slot, :], mva[:, c, hd, :], AF.Identity, [('mva', c), 'ml_cols'], [('vg', slot)],
                    scale=COLS[:, c, 0, r:r + 1])
                yield
                MM(Aps, PTs[:, slot, :], vg[:, slot, 0:129], True, True, [('pts', slot), ('vg', slot)], [pk_])
                MM(Bps, mqT[:, hd, tsl], Cnb[:, r, 0:129], True, True, [('mqT', hd), ('cnb', r)], [pk_])
                yield
                ACT(tB[:, slot, 0:129], Bps, AF.Identity, [pk_, 'ml_cols'], [('tb', slot)], scale=COLS[:, c, 2, r:r + 1])
                STT(nd[:, slot, 0:129], Aps, COLS[:, c, 1, r:r + 1], tB[:, slot, 0:129], ALU.mult, ALU.add,
                    [pk_, 'ml_cols', ('tb', slot)], [('nd', slot)])
                STT(dn[:, slot, 0:1], nd[:, slot, 128:129], -1.0, nd[:, slot, 128:129], ALU.mult, ALU.max, [('nd', slot)],
                    [('dn', slot)])
                TS('dve', dn[:, slot, 0:1], dn[:, slot, 0:1], COLS[:, c, 3, r:r + 1], None, ALU.max, None,
                   [('dn', slot), 'ml_cols'], [('dn', slot)])
                T.op('dve', lambda: nc.vector.reciprocal(out=dn[:, slot, 1:2], in_=dn[:, slot, 0:1]), [('dn', slot)], [('dn', slot)])
                hs = hsum[:, c, hd * 128:(hd + 1) * 128]
                if cs < 4:
                    TS('dve', hs, nd[:, slot, 0:128], dn[:, slot, 1:2], None, ALU.mult, None, [('nd', slot), ('dn', slot)],
                       [('hs', c)])
                else:
                    STT(hs, nd[:, slot, 0:128], dn[:, slot, 1:2], hs, ALU.mult, ALU.add, [('nd', slot), ('dn', slot), ('hs', c)],
                        [('hs', c)])
                yield
                MM(CNps, kg[:, slot, :], mva[:, c, hd, 0:129], True, True, [('kg', slot), ('mva', c)], [pk_])
                yield
                STT(Cn32[:, r, 0:129], Cn32[:, r, 0:129], DECB[:, r, cs:cs + 1], CNps, ALU.mult, ALU.add,
                    [('cn', r), 'ml_decb', pk_], [('cn', r)])
                CP('pool', Cnb[:, r, 0:129], Cn32[:, r, 0:129], [('cn', r)], [('cnb', r)])
                if cs % 2 == 1:
                    DMA('sp', oCn_o[l, dr, cs // 2, hd], Cn32[:, r, 0:129], [('cn', r)], (), 'do')

            todo = [(cs, hd, dr) for cs in range(8) for hd in range(4) for dr in range(2)]
            modgen = [compute_mod_gen(l + 1) if l + 1 < depth else None]
            run_pipeline(todo, lambda job, sl_: core_iter(job[0], job[1], job[2], sl_), 6, modgen, 5)
            if l == 0:
                dump('hs', hsum[:, :, :], [('hs', c) for c in range(8)])
                dump('cols', COLS[:, :, :, :], ['ml_cols'])
            T.free(k2)
            es3.close()
            og = es.enter_context(sbt("ml_og", [128, 8, 512], BF16))
            gml4 = es.enter_context(sbt("ml_g4", [128, 512], F32))
            ssq = es.enter_context(sbt("ml_ssq", [128, 2, 4], F32))
            ytm = es.enter_context(sbt("ml_ytm", [128, 2, 512], BF16))
            k3 = [('og', c) for c in range(8)] + ['gml4', ('ssq', 0), ('ssq', 1), ('ytm', 0), ('ytm', 1)]
            T.adopt(k3)
            for h in range(4):
                DMA('sp', gml4[:, h * 128:(h + 1) * 128], ml_norm_g[l].partition_broadcast(128), (), ['gml4'], 'di')
            w, wk = wload(('w_in', (l,)), 0, 8, 3072, 512)
            for tc in range(8):
                bk = 4 + tc % 2
                for kc in range(8):
                    MM(bank(bk), hT[:, kc, tc * 128:(tc + 1) * 128], w[:, kc, :], kc == 0, kc == 7, [('h', kc), wk], [ps(bk)])
                tmp, tk = fs_next()
                ACT(tmp[:, 0:512], bank(bk), AF.Sigmoid, [ps(bk)], [tk])
                TT('pool', og[:, tc, :], tmp[:, 0:512], gml4[:, :], ALU.mult, [tk, 'gml4'], [('og', tc)])
            for c in range(8):
                b2 = c % 2
                sq, sqk = fs_next()
                ACT(sq[:, 0:512], hsum[:, c, :], AF.Square, [('hs', c)], [sqk])
                T.op('dve', lambda: nc.vector.tensor_reduce(out=ssq[:, b2, 0:4], in_=sq[:, 0:512].rearrange("p (h d) -> p h d", h=4),
                                                             axis=AX.X, op=ALU.add), [sqk], [('ssq', b2)])
                TS('pool', ssq[:, b2, 0:4], ssq[:, b2, 0:4], 1.0 / 128.0, EPS, ALU.mult, ALU.add, [('ssq', b2)], [('ssq', b2)])
                TT('pool', ssq[:, b2, 0:4], ssq[:, b2, 0:4], mhalf[:, 0:4], ALU.pow, [('ssq', b2), 'mhalf'], [('ssq', b2)])
                for hd in range(4):
                    hsl = slice(hd * 128, (hd + 1) * 128)
                    STT(ytm[:, b2, hsl], hsum[:, c, hsl], ssq[:, b2, hd:hd + 1], og[:, c, hsl], ALU.mult, ALU.mult,
                        [('hs', c), ('ssq', b2), ('og', c)], [('ytm', b2)])
                for hd in range(4):
                    hsl = slice(hd * 128, (hd + 1) * 128)
                    TR(PT[:, hsl], ytm[:, b2, hsl], ident_b[:], [('ytm', b2), 'ident_b'], [ps(7)])
                CP('act', ymlT[:, :, c * 128:(c + 1) * 128], PT[:, 0:512].rearrange("p (h t) -> p h t", h=4), [ps(7)],
                   [('yml', h) for h in range(4)])
            T.free(k1 + k3 + [('hs', c) for c in range(8)] + [('cn', r) for r in range(8)])

    def mixer(l):
        norm_mod(l, 0)
        with ExitStack() as es:
            ydaT = es.enter_context(sbt("ydaT", [128, 4, 1024], BF16))
            ymlT = es.enter_context(sbt("ymlT", [128, 4, 1024], BF16))
            ysgT = es.enter_context(sbt("ysgT", [128, 4, 1024], BF16))
            yk = [(n, h) for n in ('yda', 'yml', 'ysg') for h in range(4)]
            T.adopt(yk)
            if 'ml' in phases:
                ml_phase(l, ymlT)
            else:
                MS('pool', ymlT[:, :, :], 0.0, [('yml', h) for h in range(4)])
            if 'da' in phases:
                da_phase(l, ydaT)
            else:
                MS('pool', ydaT[:, :, :], 0.0, [('yda', h) for h in range(4)])
            if 'sg' in phases:
                sg_phase(l, ysgT)
            else:
                MS('pool', ysgT[:, :, :], 0.0, [('ysg', h) for h in range(4)])
            if l == 0:
                dump('yda', ydaT[:, :, :], [('yda', h) for h in range(4)])
                dump('yml', ymlT[:, :, :], [('yml', h) for h in range(4)])
                dump('ysg', ysgT[:, :, :], [('ysg', h) for h in range(4)])
            merge(l, [ydaT, ymlT, ysgT])
            T.free(yk)

    compute_mod(0)
    for l in range(depth):
        prep_w0n(l)
        if l + 1 < depth and 'ml' not in phases:
            compute_mod(l + 1)
        if any(p in phases for p in ('ml', 'da', 'sg')):
            mixer(l)
        if 'ffn' in phases:
            ffn(l)

    with ExitStack() as es:
        yT = es.enter_context(sbt("yT", [128, 8, 1024], F32))
        T.adopt([('yT', dc) for dc in range(8)])
        rms_stats([xT[:, dc, :] for dc in range(8)], [('x', dc) for dc in range(8)],
                  [hT[:, dc, :] for dc in range(8)], [('h', dc) for dc in range(8)], 1024.0)
        for dc in range(8):
            STT(yT[:, dc, :], xT[:, dc, :], gcol[:, dc:dc + 1], rstd[:], ALU.mult, ALU.mult, [('x', dc), 'gcol', 'rstd'],
                [('yT', dc)])
        for tc in range(8):
            st, sk = fs_next()
            for half in range(2):
                bk = 2 + half
                for j in range(4):
                    dc = half * 4 + j
                    TR(bank(bk)[:, j * 128:(j + 1) * 128], yT[:, dc, tc * 128:(tc + 1) * 128], ident_f[:],
                       [('yT', dc), 'ident_f'], [ps(bk)])
                CP('dve' if half == 0 else 'act', st[:, half * 512:(half + 1) * 512], bank(bk), [ps(bk)], [sk])
            DMA('sp', y_o[tc * 128:(tc + 1) * 128, :], st[:, :], [sk], (), 'do')
        T.free([('yT', dc) for dc in range(8)])
    T.finish()
    return nc, T, dumps, wrec


def _consts():
    ident = np.eye(128, dtype=np.float32)
    prot = np.zeros((128, 128), np.float32)
    for m in range(128):
        prot[m ^ 16, m] = 1.0
    s = np.arange(128)[:, None]
    t = np.arange(128)[None, :]
    maskF = (s <= t).astype(np.float32)
    maskB = (s >= t).astype(np.float32)
    sel = np.zeros((8, 8, 128), np.float32)
    for r in range(8):
        sel[r, r, :] = 1.0
    dirm = np.zeros((8, 2), np.float32)
    dirm[0:4, 0] = 1.0
    dirm[4:8, 1] = 1.0
    return dict(c_ident=ident, c_prot=prot, c_maskF=maskF, c_maskB=maskB, c_sel=sel.reshape(8, 1024), c_dirm=dirm)


def _rope_tables():
    t = np.arange(1024)
    row = (t // 64).astype(np.float32)
    col = (t % 64).astype(np.float32)
    nf = 16
    inv = (10000.0 ** (-np.arange(nf, dtype=np.float32) / nf)).astype(np.float32)
    ang = np.stack([row[:, None] * inv, col[:, None] * inv], axis=1)
    cos = np.cos(ang).astype(np.float32)
    sin = np.sin(ang).astype(np.float32)
    C = np.zeros((128, 1024), np.float32)
    S = np.zeros((128, 1024), np.float32)
    for d in range(128):
        axis = (d >> 5) & 1
        half = (d >> 4) & 1
        f = d & 15
        C[d] = cos[:, axis, f]
        S[d] = sin[:, axis, f] * (-1.0 if half == 0 else 1.0)
    return C, S


_CACHE = {}


def kernel(x_prompt, x_sample, c, cache_k, cache_v, state_C, state_n, state_m, c_ctx,
           w_mod, b_mod, w_in, da_lambda, da_norm_g, ml_conv_w, ml_conv_b, ml_gate_b,
           ml_norm_g, sg_norm_g, sg_w, sg_b, w_branch, w_out, w_up, ffn_conv_w, ffn_conv_b,
           w_down, final_g, _depth=L, _phases=('ml', 'da', 'sg', 'ffn'), _dbg=False):
    f = lambda a: np.ascontiguousarray(np.asarray(a, dtype=np.float32))
    key = (_depth, tuple(_phases), _dbg)
    if key not in _CACHE:
        rec = build(_depth, _phases, _dbg is True)[3]
        _CACHE[key] = build(_depth, _phases, _dbg is True, rec)
    nc, T, dumps, _ = _CACHE[key]
    dp = _depth
    consts = _consts()
    rC, rS = _rope_tables()
    shared = dict(consts)
    shared.update(
        w_mod=f(w_mod[:dp]), b_mod=f(b_mod).reshape(L, 48, 128), w_in=f(w_in[:dp]), da_lambda=f(da_lambda).reshape(1, L * 256),
        da_norm_g=f(da_norm_g).reshape(L, 1, 128), ml_conv_w=f(ml_conv_w).reshape(L, 24, 128),
        ml_conv_b=f(ml_conv_b).reshape(L, 8, 128), ml_gate_b=f(ml_gate_b).reshape(L, 16, 1),
        ml_norm_g=f(ml_norm_g).reshape(L, 1, 128), sg_norm_g=f(sg_norm_g).reshape(L, 1, 512), sg_w=f(sg_w),
        sg_b=f(sg_b).reshape(L, 1, 512), w_branch=f(w_branch[:dp]), w_out=f(w_out[:dp]), w_up=f(w_up[:dp]),
        ffn_conv_w=f(ffn_conv_w).reshape(L, 132, 128), ffn_conv_b=f(ffn_conv_b).reshape(L, 44, 128), w_down=f(w_down[:dp]),
        final_g=f(final_g).reshape(8, 128))
    x_prompt = f(x_prompt)
    x_sample = f(x_sample)
    in_maps = []
    for core in range(8):
        m = dict(shared)
        if core < 4:
            b = core
            m['xin'] = x_sample[b]
            m['cond'] = f(c)[b].reshape(8, 128)
            m['ck'] = f(cache_k)[b]
            m['cv'] = f(cache_v)[b]
            m['sCn'] = np.ascontiguousarray(np.concatenate([f(state_C)[b], f(state_n)[b][..., None]], axis=-1))
            m['sm'] = f(state_m)[b].reshape(L, 8, 1)
            m['c_ropeC'] = rC
            m['c_ropeS'] = rS
            m['c_maskb'] = np.zeros((128, 20), np.float32)
            lk = np.zeros((128, 2), np.float32)
            lk[:, 0] = 1.0
            m['c_link'] = lk
        else:
            j = core - 4
            m['xin'] = x_prompt[4 * j:4 * j + 4].reshape(1024, 1024)
            m['cond'] = f(c_ctx).reshape(8, 128)
            m['ck'] = np.zeros((L, 4, 256, 128), np.float32)
            m['cv'] = np.zeros((L, 4, 256, 128), np.float32)
            m['sCn'] = np.zeros((L, 2, 4, 128, 129), np.float32)
            m['sm'] = np.zeros((L, 8, 1), np.float32)
            m['c_ropeC'] = np.ones((128, 1024), np.float32)
            m['c_ropeS'] = np.zeros((128, 1024), np.float32)
            mb = np.full((5, 4), -30000.0, np.float32)
            for qt in range(4):
                mb[1 + qt, qt] = 0.0
            m['c_maskb'] = np.tile(mb.reshape(1, 20), (128, 1))
            lk = np.zeros((128, 2), np.float32)
            lk[:, 1] = 1.0
            m['c_link'] = lk
        in_maps.append(m)
    if _dbg == 'maps':
        return nc, in_maps
    if _dbg == 'time':
        res = run_bass_kernel_spmd(nc, in_maps, core_ids=list(range(8)), trace=True)
        return res.exec_time_ns
    res = run_bass_kernel_spmd(nc, in_maps, core_ids=list(range(8)))
    R = res.results
    if _dbg:
        kernel.dbg = [{n: R[c][n] for n in dumps} for c in range(8)]
    y_sample = np.stack([R[b]['y_o'] for b in range(4)], axis=0)
    y_prompt = np.concatenate([R[4 + j]['y_o'].reshape(4, 256, 1024) for j in range(4)], axis=0)
    nk = np.concatenate([R[4 + j]['ok_o'].reshape(L, 4, 4, 256, 128).transpose(2, 0, 1, 3, 4) for j in range(4)], axis=0)
    nv = np.concatenate([R[4 + j]['ov_o'].reshape(L, 4, 4, 256, 128).transpose(2, 0, 1, 3, 4) for j in range(4)], axis=0)
    nC, nn, nm = [], [], []
    for j in range(4):
        oCn = R[4 + j]['oCn_o']
        oC = oCn[..., 0:128]
        on = oCn[..., 128]
        om = R[4 + j]['om_o'].reshape(L, 2, 4, 4)
        for sq in range(4):
            nC.append(np.stack([oC[:, 0, sq], oC[:, 1, 3 - sq]], axis=1))
            nn.append(np.stack([on[:, 0, sq], on[:, 1, 3 - sq]], axis=1))
            nm.append(np.stack([om[:, 0, :, sq], om[:, 1, :, 3 - sq]], axis=1))
    return (y_prompt.astype(np.float32), y_sample.astype(np.float32), np.ascontiguousarray(nk), np.ascontiguousarray(nv),
            np.stack(nC, axis=0), np.stack(nn, axis=0), np.stack(nm, axis=0))
```

```python
import math
from contextlib import ExitStack
import numpy as np
import concourse.bass as bass
import concourse.mybir as mybir
from concourse.bass_utils import run_bass_kernel_spmd

F32 = mybir.dt.float32
BF16 = mybir.dt.bfloat16
AF = mybir.ActivationFunctionType
ALU = mybir.AluOpType
AX = mybir.AxisListType

L = 4
NIN = 7696
DFF = 2816
EPS = 1e-6


class Tr:
    EP = 30000
    NSEM = 8

    def __init__(s, nc):
        s.nc = nc
        s.E = dict(pe=nc.tensor, act=nc.scalar, dve=nc.vector, pool=nc.gpsimd, sp=nc.sync)
        s.sems = {}
        s.cnt = {}
        s.known = {e: {} for e in s.E}
        s.res = {}
        s.grave = {}
        s.nwait = 0
        s.nops = 0
        s.dcount = {}
        s.dlast = {}

    def _sem(s, st, c):
        mult = 1 if st in s.E else 16
        ep = s.EP // mult
        e = (c - 1) // ep
        lst = s.sems.setdefault(st, [])
        while len(lst) <= e:
            nm = st if isinstance(st, str) else f"{st[0]}{st[1]}"
            lst.append(s.nc.alloc_semaphore(f"s_{nm}_{len(lst)}"))
        return lst[e], ((c - 1) % ep + 1) * mult

    def op(s, eng, fn, reads=(), writes=(), sig=True, dma=None):
        deps = {}

        def add(ev):
            if ev is None:
                return
            st = ev[0]
            if st == 'pe' and eng == 'pe' and dma is None:
                return
            if st not in deps or deps[st][1] < ev[1]:
                deps[st] = ev

        for k in reads:
            r = s.res.get(k)
            if r is None:
                continue
            add(r[0])
            if k[0] == 'ps':
                for ev in r[1].values():
                    add(ev)
        for k in writes:
            r = s.res.get(k)
            if r is None:
                continue
            add(r[0])
            for ev in r[1].values():
                add(ev)
        if dma is not None:
            i = s.dcount.get(dma, 0)
            s.dcount[dma] = i + 1
            dma = (dma, i % s.NSEM)
            add(s.dlast.get(dma))
        kn = s.known[eng]
        for st in sorted(deps, key=lambda a: -deps[a][1]):
            _, c, clk = deps[st]
            if kn.get(st, 0) >= c:
                continue
            if st == 'pe':
                assert s.cnt.get('pe', 0) >= c, "dependency on unsignalled PE op"
            sem, v = s._sem(st, c)
            s.E[eng].wait_ge(sem, v)
            s.nwait += 1
            kn[st] = c
            for a, b in clk.items():
                if kn.get(a, 0) < b:
                    kn[a] = b
        ins = fn()
        s.nops += 1
        st = dma or eng
        c = s.cnt.get(st, 0) + 1
        if sig:
            s.cnt[st] = c
            sem, v = s._sem(st, c)
            ins.then_inc(sem, 16 if dma else 1)
        clk = dict(kn)
        clk[st] = c
        ev = (st, c, clk)
        if dma is not None:
            s.dlast[dma] = ev
        for k in writes:
            s.res[k] = [ev, {}]
        for k in reads:
            r = s.res.setdefault(k, [None, {}])
            r[1][st] = ev
        return ins

    def free(s, keys):
        for k in keys:
            r = s.res.pop(k, None)
            if r is None:
                continue
            evs = list(r[1].values())
            if r[0] is not None:
                evs.append(r[0])
            for ev in evs:
                st = ev[0]
                if st not in s.grave or s.grave[st][1] < ev[1]:
                    s.grave[st] = ev

    def adopt(s, keys):
        for k in keys:
            s.res[k] = [None, dict(s.grave)]

    def finish(s, eng='sp'):
        for st, c in s.cnt.items():
            if st in s.E:
                continue
            ep = s.EP // 16
            for e in range((c - 1) // ep + 1 if c > 0 else 0):
                last = min(c, (e + 1) * ep)
                sem, v = s._sem(st, last)
                s.E[eng].wait_ge(sem, v)


def build(depth=L, phases=('ml', 'da', 'sg', 'ffn'), dbg=False, wsched_n=None):
    nc = bass.Bass("TRN2", target_bir_lowering=False)
    T = Tr(nc)
    LW = depth
    dumps = []
    wrec = []
    wissued = [0]
    LOOK = 2

    def din(n, sh):
        return nc.dram_tensor(n, list(sh), F32, kind="ExternalInput").ap()

    def dout(n, sh):
        return nc.dram_tensor(n, list(sh), F32, kind="ExternalOutput").ap()

    xin = din("xin", [1024, 1024])
    cond = din("cond", [8, 128])
    ck = din("ck", [L, 4, 256, 128])
    cv = din("cv", [L, 4, 256, 128])
    sCn = din("sCn", [L, 2, 4, 128, 129])
    sm = din("sm", [L, 8, 1])
    c_ident = din("c_ident", [128, 128])
    c_prot = din("c_prot", [128, 128])
    c_maskF = din("c_maskF", [128, 128])
    c_maskB = din("c_maskB", [128, 128])
    c_sel = din("c_sel", [8, 1024])
    c_dirm = din("c_dirm", [8, 2])
    c_ropeC = din("c_ropeC", [128, 1024])
    c_ropeS = din("c_ropeS", [128, 1024])
    c_maskb = din("c_maskb", [128, 20])
    c_link = din("c_link", [128, 2])
    w_mod = din("w_mod", [LW, 1024, 6144])
    b_mod = din("b_mod", [L, 48, 128])
    w_in = din("w_in", [LW, 1024, NIN])
    da_lambda = din("da_lambda", [1, L * 256])
    da_norm_g = din("da_norm_g", [L, 1, 128])
    ml_conv_w = din("ml_conv_w", [L, 24, 128])
    ml_conv_b = din("ml_conv_b", [L, 8, 128])
    ml_gate_b = din("ml_gate_b", [L, 16, 1])
    ml_norm_g = din("ml_norm_g", [L, 1, 128])
    sg_norm_g = din("sg_norm_g", [L, 1, 512])
    sg_w = din("sg_w", [L, 4, 128, 128])
    sg_b = din("sg_b", [L, 1, 512])
    w_branch = din("w_branch", [LW, 3, 512, 1024])
    w_out = din("w_out", [LW, 1024, 1024])
    w_up = din("w_up", [LW, 1024, 2 * DFF])
    ffn_conv_w = din("ffn_conv_w", [L, 132, 128])
    ffn_conv_b = din("ffn_conv_b", [L, 44, 128])
    w_down = din("w_down", [LW, DFF, 1024])
    final_g = din("final_g", [8, 128])

    y_o = dout("y_o", [1024, 1024])
    ok_o = dout("ok_o", [L, 4, 1024, 128])
    ov_o = dout("ov_o", [L, 4, 1024, 128])
    oCn_o = dout("oCn_o", [L, 2, 4, 4, 128, 129])
    om_o = dout("om_o", [L, 8, 4])

    WTinit = dict(w_in=w_in, w_mod=w_mod, w_up=w_up, w_down=w_down, w_out=w_out, w_branch=w_branch)
    def veng(e):
        return nc.vector if e == 'dve' else nc.gpsimd

    def ACT(out, in_, func, r, w, bias=None, scale=None):
        kw = {}
        if bias is not None:
            kw['bias'] = bias
        if scale is not None:
            kw['scale'] = scale
        return T.op('act', lambda: nc.scalar.activation(out=out, in_=in_, func=func, **kw), r, w)

    def TT(e, out, a, b, op, r, w):
        return T.op(e, lambda: veng(e).tensor_tensor(out=out, in0=a, in1=b, op=op), r, w)

    def TS(e, out, a, s1, s2, op0, op1, r, w):
        if op1 is None:
            return T.op(e, lambda: veng(e).tensor_scalar(out=out, in0=a, scalar1=s1, scalar2=None, op0=op0), r, w)
        return T.op(e, lambda: veng(e).tensor_scalar(out=out, in0=a, scalar1=s1, scalar2=s2, op0=op0, op1=op1), r, w)

    def STT(out, a, s, b, op0, op1, r, w):
        return T.op('dve', lambda: nc.vector.scalar_tensor_tensor(out=out, in0=a, scalar=s, in1=b, op0=op0, op1=op1), r, w)

    def CP(e, out, in_, r, w):
        if e == 'act':
            return T.op('act', lambda: nc.scalar.copy(out=out, in_=in_), r, w)
        return T.op(e, lambda: veng(e).tensor_copy(out=out, in_=in_), r, w)

    def MM(out, lhsT, rhs, start, stop, r, w, sig=None):
        if sig is None:
            sig = stop
        return T.op('pe', lambda: nc.tensor.matmul(out, lhsT=lhsT, rhs=rhs, start=start, stop=stop), r, w, sig=sig)

    def TR(out, in_, ident, r, w):
        return T.op('pe', lambda: nc.tensor.transpose(out, in_, ident), r, w)

    def DMA(e, out, in_, r, w, st):
        eng = {'sp': nc.sync, 'pool': nc.gpsimd, 'act': nc.scalar}[e]
        if e == 'pool':
            st = 'dw'
        return T.op(e, lambda: eng.dma_start(out=out, in_=in_), r, w, dma=st)

    def dump(name, ap, keys):
        if not dbg:
            return
        o = dout("dbg_" + name, list(ap.shape))
        dumps.append("dbg_" + name)
        DMA('pool' if ap.dtype != F32 else 'sp', o, ap, keys, (), 'do')

    def MS(e, ap, val, w):
        return T.op(e, lambda: veng(e).memset(ap, val), (), w)

    def sb(n, sh, dt=F32):
        return nc.alloc_sbuf_tensor(n, list(sh), dt)

    uniq = [0]

    def sbt(n, sh, dt=F32):
        uniq[0] += 1
        return nc.sbuf_tensor(f"{n}_u{uniq[0]}", list(sh), dt)

    xT = sb("xT", [128, 8, 1024])
    hT = sb("hT", [128, 8, 1024], BF16)
    NW = 4
    wpool = [sb(f"wp{i}", [128, 4096], BF16) for i in range(NW)]
    wstate = [0]
    ident_f = sb("ident_f", [128, 128])
    ident_b = sb("ident_b", [128, 128], BF16)
    prot_b = sb("prot_b", [128, 128], BF16)
    ones_b = sb("ones_b", [128, 128], BF16)
    maskF = sb("maskF", [128, 128])
    maskB = sb("maskB", [128, 128])
    sel = sb("sel", [8, 1024])
    dirm = sb("dirm", [8, 2])
    ropeC = sb("ropeC", [128, 1024])
    ropeS = sb("ropeS", [128, 1024])
    maskb = sb("maskb", [128, 20])
    link = sb("link", [128, 2])
    eps_t = sb("eps_t", [128, 1])
    one_t = sb("one_t", [128, 1])
    mhalf = sb("mhalf", [128, 4])
    CPm = sb("CPm", [128, L, 3, 128])
    gcol = sb("gcol", [128, 16])
    cond_b = sb("cond_b", [128, 8], BF16)
    modT = sb("modT", [128, 2, 48])
    scp = sb("scp", [128, 2, 16])
    lam = sb("lam", [128, L])
    nlam = sb("nlam", [128, L])
    GB = sb("GB", [8, L, 2])
    rstd = sb("rstd", [128, 1024])
    FS = [sb(f"fs{i}", [128, 1024]) for i in range(4)]
    fstate = [0]
    ones8 = sb("ones8", [8, 128])
    w0n = sb("w0n", [128, 2, 52])

    PA = nc.alloc_psum_tensor("PA", [128, 1024], F32)
    PB = nc.alloc_psum_tensor("PB", [128, 1024], F32)
    PC = nc.alloc_psum_tensor("PC", [128, 1024], F32)
    PD = nc.alloc_psum_tensor("PD", [128, 512], F32)
    PT = nc.alloc_psum_tensor("PT", [128, 1024], BF16)
    P2 = [PA, PB, PC]
    p2state = [0]

    def bank(i):
        if i < 6:
            return P2[i // 2][:, (i % 2) * 512:(i % 2 + 1) * 512]
        return PD[:, :]

    def ps(i):
        return ('ps', i)

    def fs_next():
        i = fstate[0] % 4
        fstate[0] += 1
        return FS[i], ('fs', i)

    def p2_next():
        i = p2state[0] % 3
        p2state[0] += 1
        return P2[i], [ps(2 * i), ps(2 * i + 1)]

    WT = WTinit

    def wmk(desc):
        (name, idx), r0, nk, c0, ncol = desc
        t = WT[name]
        for i in idx:
            t = t[i]
        return t[r0:r0 + nk * 128, c0:c0 + ncol].rearrange("(k p) n -> p k n", p=128)

    def w_issue(j):
        desc = wsched_n[j]
        nk, ncol = desc[2], desc[4]
        i = j % NW
        dst = wpool[i][:, 0:nk * ncol].rearrange("p (k n) -> p k n", k=nk)
        DMA('pool', dst, wmk(desc), (), [('w', i)], 'dw')

    def wload(w2d, r0, nk, c0, ncol):
        desc = (w2d, r0, nk, c0, ncol)
        k = wstate[0]
        wstate[0] += 1
        i = k % NW
        dst = wpool[i][:, 0:nk * ncol].rearrange("p (k n) -> p k n", k=nk)
        if wsched_n is None:
            wrec.append(desc)
            DMA('pool', dst, wmk(desc), (), [('w', i)], 'dw')
        else:
            assert wsched_n[k] == desc
            while wissued[0] <= min(k + LOOK, len(wsched_n) - 1):
                w_issue(wissued[0])
                wissued[0] += 1
        return dst, ('w', i)

    def wview(w2d, r0, nk, c0, ncol):
        return w2d[r0:r0 + nk * 128, c0:c0 + ncol].rearrange("(k p) n -> p k n", p=128)

    DMA('sp', ident_f[:], c_ident, (), ['ident_f'], 'di')
    DMA('pool', ident_b[:], c_ident, (), ['ident_b'], 'dw')
    DMA('pool', prot_b[:], c_prot, (), ['prot_b'], 'dw')
    DMA('sp', maskF[:], c_maskF, (), ['maskF'], 'di')
    DMA('sp', maskB[:], c_maskB, (), ['maskB'], 'di')
    DMA('sp', sel[:], c_sel, (), ['sel'], 'di')
    DMA('sp', dirm[:], c_dirm, (), ['dirm'], 'di')
    DMA('sp', ropeC[:], c_ropeC, (), ['ropeC'], 'di')
    DMA('sp', ropeS[:], c_ropeS, (), ['ropeS'], 'di')
    DMA('sp', maskb[:], c_maskb, (), ['maskb'], 'di')
    DMA('sp', link[:], c_link, (), ['link'], 'di')
    MS('dve', ones_b[:], 1.0, ['ones_b'])
    MS('dve', eps_t[:], EPS, ['eps_t'])
    MS('dve', one_t[:], 1.0, ['one_t'])
    MS('dve', mhalf[:], -0.5, ['mhalf'])
    MS('dve', ones8[:], 1.0, ['ones8'])
    for l in range(L):
        DMA('sp', GB[:, l, 0:1], ml_gate_b[l, 0:8, :], (), ['GB'], 'di')
        DMA('sp', GB[:, l, 1:2], ml_gate_b[l, 8:16, :], (), ['GB'], 'di')

    def colparams(rows_list, dst, dkey):
        st, sk = fs_next()
        r0 = 0
        for ap, R in rows_list:
            DMA('sp', st[r0:r0 + R, 0:128], ap, (), [sk], 'di')
            r0 += R
        TR(bank(6)[:, 0:r0], st[0:r0, 0:128], ident_f[0:r0, 0:r0], [sk, 'ident_f'], [ps(6)])
        CP('dve', dst[:, 0:r0], bank(6)[:, 0:r0], [ps(6)], [dkey])

    for l in range(depth):
        colparams([(b_mod[l], 48), (ml_conv_w[l], 24), (ml_conv_b[l], 8), (ffn_conv_b[l], 44), (da_norm_g[l], 1)],
                  CPm[:, l, 0, :], 'CPm')
        colparams([(ffn_conv_w[l, 0:128, :], 128)], CPm[:, l, 1, :], 'CPm')
        colparams([(ffn_conv_w[l, 128:132, :], 4)], CPm[:, l, 2, :], 'CPm')
    colparams([(final_g, 8), (cond, 8)], gcol[:, :], 'gcol')

    def bmod_c(l):
        return CPm[:, l, 0, 0:48]

    def mlw_c(l, k, c):
        return CPm[:, l, 0, 48 + k * 8 + c:48 + k * 8 + c + 1]

    def mlb_c(l, c):
        return CPm[:, l, 0, 72 + c:73 + c]

    def ffb_c(l, c):
        return CPm[:, l, 0, 80 + c:81 + c]

    def dag_c(l):
        return CPm[:, l, 0, 124:125]

    def ffw_c(l, k, c):
        j = k * 44 + c
        if j < 128:
            return CPm[:, l, 1, j:j + 1]
        return CPm[:, l, 2, j - 128:j - 127]

    ACT(cond_b[:], gcol[:, 8:16], AF.Silu, ['gcol'], ['cond_b'])

    with ExitStack() as es:
        dl = es.enter_context(sbt("dl", [128, L * 256], F32))
        pr = es.enter_context(sbt("pr", [128, L * 128], F32))
        sm2 = es.enter_context(sbt("sm2", [128, L * 2], F32))
        DMA('sp', dl[:], da_lambda.partition_broadcast(128), (), ['dl'], 'di')
        dlv = dl[:].rearrange("p (l a b d) -> p l a b d", l=L, a=2, b=2)
        TT('dve', pr[:].rearrange("p (l a d) -> p l a d", l=L, a=2), dlv[:, :, :, 0, :], dlv[:, :, :, 1, :], ALU.mult,
           ['dl'], ['pr'])
        T.op('dve', lambda: nc.vector.tensor_reduce(out=sm2[:], in_=pr[:].rearrange("p (q d) -> p q d", d=64),
                                                     axis=AX.X, op=ALU.add), ['pr'], ['sm2'])
        ACT(sm2[:], sm2[:], AF.Exp, ['sm2'], ['sm2'])
        s2v = sm2[:].rearrange("p (l a) -> p l a", a=2)
        TT('dve', lam[:], s2v[:, :, 0], s2v[:, :, 1], ALU.subtract, ['sm2'], ['lam'])
        for l in range(L):
            li = 0.8 - 0.6 * math.exp(-0.3 * l)
            TS('dve', lam[:, l:l + 1], lam[:, l:l + 1], li, None, ALU.add, None, ['lam'], ['lam'])
        TS('dve', nlam[:], lam[:], -1.0, None, ALU.mult, None, ['lam'], ['nlam'])
        T.free(['dl', 'pr', 'sm2'])

    for tc in range(8):
        st, sk = fs_next()
        DMA('sp', st[:, :], xin[tc * 128:(tc + 1) * 128, :], (), [sk], 'di')
        for half in range(2):
            bk = half
            for j in range(4):
                dc = half * 4 + j
                TR(bank(bk)[:, j * 128:(j + 1) * 128], st[:, dc * 128:(dc + 1) * 128], ident_f[:], [sk, 'ident_f'], [ps(bk)])
            CP('dve' if half == 0 else 'act', xT[:, half * 4:half * 4 + 4, tc * 128:(tc + 1) * 128],
               bank(bk).rearrange("p (a b) -> p a b", a=4), [ps(bk)], [('x', half * 4 + j) for j in range(4)])

    dump('xT0', xT[:, :, :], [('x', dc) for dc in range(8)])
    def compute_mod_gen(l):
        par = l % 2
        for g in range(12):
            w, wk = wload(('w_mod', (l,)), 0, 8, g * 512, 512)
            for j in range(4):
                col = g * 4 + j
                for kc in range(8):
                    MM(bank(6)[:, col:col + 1], w[:, kc, j * 128:(j + 1) * 128], cond_b[:, kc:kc + 1], kc == 0, kc == 7,
                       [wk, 'cond_b'], [ps(6)])
            yield
        TT('dve', modT[:, par, :], bank(6)[:, 0:48], bmod_c(l), ALU.add, [ps(6), 'CPm'], [('mod', par)])
        TS('dve', scp[:, par, 0:8], modT[:, par, 8:16], 1.0, None, ALU.add, None, [('mod', par)], [('scp', par)])
        TS('dve', scp[:, par, 8:16], modT[:, par, 32:40], 1.0, None, ALU.add, None, [('mod', par)], [('scp', par)])

    def compute_mod(l):
        for _ in compute_mod_gen(l):
            pass


    def rms_stats(src_chunks, rkeys, sq_dst, sqkeys, nfeat):
        n = len(src_chunks)
        for i in range(n):
            ACT(sq_dst[i], src_chunks[i], AF.Square, [rkeys[i]], [sqkeys[i]])
        for th in range(2):
            for i in range(n):
                MM(PA[:, th * 512:(th + 1) * 512], ones_b[:], sq_dst[i][:, th * 512:(th + 1) * 512], i == 0, i == n - 1,
                   [sqkeys[i], 'ones_b'], [ps(th)])
        ACT(rstd[:], PA[:, :], AF.Ln, [ps(0), ps(1), 'eps_t'], ['rstd'], bias=eps_t[:], scale=1.0 / nfeat)
        ACT(rstd[:], rstd[:], AF.Exp, ['rstd'], ['rstd'], scale=-0.5)

    def norm_mod(l, which):
        par = l % 2
        rms_stats([xT[:, dc, :] for dc in range(8)], [('x', dc) for dc in range(8)],
                  [hT[:, dc, :] for dc in range(8)], [('h', dc) for dc in range(8)], 1024.0)
        so = 0 if which == 0 else 24
        for dc in range(8):
            tmp, tk = fs_next()
            STT(tmp[:], xT[:, dc, :], scp[:, par, which * 8 + dc:which * 8 + dc + 1], rstd[:], ALU.mult, ALU.mult,
                [('x', dc), ('scp', par), 'rstd'], [tk])
            ACT(hT[:, dc, :], tmp[:], AF.Identity, [tk, ('mod', par)], [('h', dc)], bias=modT[:, par, so + dc:so + dc + 1])

    def dwconv(l, zps, zkeys, acc, akey, w0, w1, w2, bia, w0nn, w2nn):
        ACT(acc[:, :], zps[:, :], AF.Identity, zkeys + ['CPm'], [akey], bias=bia, scale=w1)
        STT(acc[:, 1:1024], zps[:, 0:1023], w0, acc[:, 1:1024], ALU.mult, ALU.add, zkeys + ['CPm', akey], [akey])
        STT(acc[:, 0:1023], zps[:, 1:1024], w2, acc[:, 0:1023], ALU.mult, ALU.add, zkeys + ['CPm', akey], [akey])
        STT(acc[:, 256:1024:256], zps[:, 255:1023:256], w0nn, acc[:, 256:1024:256], ALU.mult, ALU.add,
            zkeys + [('w0n', l % 2), akey], [akey])
        STT(acc[:, 255:1023:256], zps[:, 256:1024:256], w2nn, acc[:, 255:1023:256], ALU.mult, ALU.add,
            zkeys + [('w0n', l % 2), akey], [akey])

    def prep_w0n(l):
        par = l % 2
        TS('pool', w0n[:, par, 0:44], CPm[:, l, 1, 0:44], link[:, 1:2], -1.0, ALU.mult, ALU.mult, ['CPm', 'link'], [('w0n', par)])
        TS('pool', w2n[:, par, 0:40], CPm[:, l, 1, 88:128], link[:, 1:2], -1.0, ALU.mult, ALU.mult, ['CPm', 'link'], [('w0n', par)])
        TS('pool', w2n[:, par, 40:44], CPm[:, l, 2, 0:4], link[:, 1:2], -1.0, ALU.mult, ALU.mult, ['CPm', 'link'], [('w0n', par)])
        TS('pool', w0n[:, par, 44:52], CPm[:, l, 0, 48:56], link[:, 1:2], -1.0, ALU.mult, ALU.mult, ['CPm', 'link'], [('w0n', par)])
        TS('pool', w2n[:, par, 44:52], CPm[:, l, 0, 64:72], link[:, 1:2], -1.0, ALU.mult, ALU.mult, ['CPm', 'link'], [('w0n', par)])

    w2n = sb("w2n", [128, 2, 52])

    def ffn(l):
        par = l % 2
        with ExitStack() as es:
            act = es.enter_context(sbt("ffn_act", [128, 22, 1024], BF16))
            sa = es.enter_context(sbt("ffn_sa", [128, 4, 1024], F32))
            T.adopt([('act', j) for j in range(22)] + [('sa', j) for j in range(4)])
            norm_mod(l, 1)
            if l == 0:
                dump('mod0', modT[:, 0, :], [('mod', 0)])
                dump('h2', hT[:, :, :], [('h', dc) for dc in range(8)])
                dump('rstd', rstd[:, :], ['rstd'])
            for g in range(6):
                nchunk = 4 if g < 5 else 2
                for ab in range(2):
                    c0 = ab * DFF + g * 512
                    w, wk = wload(('w_up', (l,)), 0, 8, c0, nchunk * 128)
                    for j in range(nchunk):
                        cc = ab * 22 + g * 4 + j
                        zp, zk = p2_next()
                        for th in range(2):
                            for kc in range(8):
                                MM(zp[:, th * 512:(th + 1) * 512], w[:, kc, j * 128:(j + 1) * 128],
                                   hT[:, kc, th * 512:(th + 1) * 512], kc == 0, kc == 7, [wk, ('h', kc)], [zk[th]])
                        acc, ak = fs_next()
                        dwconv(l, zp, zk, acc, ak, ffw_c(l, 0, cc), ffw_c(l, 1, cc), ffw_c(l, 2, cc), ffb_c(l, cc),
                               w0n[:, par, cc:cc + 1],
                               w2n[:, par, cc:cc + 1])
                        if ab == 0:
                            ACT(sa[:, j, :], acc[:, :], AF.Silu, [ak], [('sa', j)])
                        else:
                            TT('pool', act[:, g * 4 + j, :], sa[:, j, :], acc[:, :], ALU.mult, [('sa', j), ak],
                               [('act', g * 4 + j)])
            if l == 0:
                dump('act', act[:, :, :], [('act', j) for j in range(22)])
            for jp in range(4):
                wA, wkA = wload(('w_down', (l,)), 0, 11, jp * 256, 256)
                wB, wkB = wload(('w_down', (l,)), 1408, 11, jp * 256, 256)
                for half, (w, wk) in enumerate(((wA, wkA), (wB, wkB))):
                    for dj in range(2):
                        for th in range(2):
                            bk = dj * 2 + th
                            for kk in range(11):
                                kc = half * 11 + kk
                                MM(bank(bk), w[:, kk, dj * 128:(dj + 1) * 128], act[:, kc, th * 512:(th + 1) * 512],
                                   half == 0 and kk == 0, half == 1 and kk == 10, [wk, ('act', kc)], [ps(bk)])
                for dj in range(2):
                    dc = jp * 2 + dj
                    for th in range(2):
                        bk = dj * 2 + th
                        STT(xT[:, dc, th * 512:(th + 1) * 512], bank(bk), modT[:, par, 40 + dc:41 + dc],
                            xT[:, dc, th * 512:(th + 1) * 512], ALU.mult, ALU.add, [ps(bk), ('mod', par), ('x', dc)],
                            [('x', dc)])
            T.free([('act', j) for j in range(22)] + [('sa', j) for j in range(4)])


    def run_pipeline(jobs, make_gen, nslots, extra=None, extra_every=1):
        active = []
        free_slots = list(range(nslots))
        nxt = 0
        step = 0
        while nxt < len(jobs) or active:
            if nxt < len(jobs) and free_slots:
                sl_ = free_slots.pop(0)
                g = make_gen(jobs[nxt], sl_)
                nxt += 1
                next(g)
                active.append((g, sl_, True))
            still = []
            for (g, sl_, fresh) in active:
                if fresh:
                    still.append((g, sl_, False))
                    continue
                try:
                    next(g)
                    still.append((g, sl_, False))
                except StopIteration:
                    free_slots.append(sl_)
            active = still
            step += 1
            if extra is not None and extra[0] is not None and step % extra_every == 0:
                try:
                    next(extra[0])
                except StopIteration:
                    extra[0] = None
        if extra is not None and extra[0] is not None:
            for _ in extra[0]:
                pass

    def sg_phase(l, ysgT):
        wl = w_in[l]
        with ExitStack() as es:
            uT = es.enter_context(sbt("sg_uT", [128, 4, 1024], BF16))
            sgwT = es.enter_context(sbt("sg_wT", [128, 4, 128], BF16))
            sgwf = es.enter_context(sbt("sg_wf", [128, 4, 128], F32))
            gb = es.enter_context(sbt("sg_gb", [128, 2, 512], F32))
            zz = es.enter_context(sbt("sg_zz", [128, 4, 512], F32))
            svb = es.enter_context(sbt("sg_svb", [128, 4, 512], BF16))
            st6 = es.enter_context(sbt("sg_st", [128, 4, 8], F32))
            keys = [('sg_u', j) for j in range(4)] + ['sg_wT', 'sg_wf', 'sg_gb'] + [(n_, b_) for n_ in ('sg_zz', 'sg_svb', 'sg_st') for b_ in range(4)]
            T.adopt(keys)
            DMA('sp', gb[:, 0, :], sg_norm_g[l].partition_broadcast(128), (), ['sg_gb'], 'di')
            DMA('sp', gb[:, 1, :], sg_b[l].partition_broadcast(128), (), ['sg_gb'], 'di')
            DMA('sp', sgwf[:, :, :], sg_w[l].rearrange("g p q -> p g q"), (), ['sg_wf'], 'di')
            for g in range(4):
                TR(bank(6)[:, g * 128:(g + 1) * 128], sgwf[:, g, :], ident_f[:], ['sg_wf', 'ident_f'], [ps(6)])
            CP('dve', sgwT[:, :, :], bank(6).rearrange("p (g q) -> p g q", g=4), [ps(6)], ['sg_wT'])
            w, wk = wload(('w_in', (l,)), 0, 8, 3600, 512)
            for j in range(4):
                zp, zk = p2_next()
                for th in range(2):
                    for kc in range(8):
                        MM(zp[:, th * 512:(th + 1) * 512], w[:, kc, j * 128:(j + 1) * 128], hT[:, kc, th * 512:(th + 1) * 512],
                           kc == 0, kc == 7, [wk, ('h', kc)], [zk[th]])
                ACT(uT[:, j, :], zp[:, :], AF.Gelu_apprx_tanh, zk, [('sg_u', j)])
            w, wk = wload(('w_in', (l,)), 0, 8, 4112, 512)

            def sg_job(tc, b):
                bk = b
                gbk = 4 + b % 2
                for kc in range(8):
                    MM(bank(bk), hT[:, kc, tc * 128:(tc + 1) * 128], w[:, kc, :], kc == 0, kc == 7, [('h', kc), wk], [ps(bk)])
                yield
                ACT(zz[:, b, :], bank(bk), AF.Gelu_apprx_tanh, [ps(bk)], [('sg_zz', b)])
                yield
                T.op('dve', lambda: nc.vector.bn_stats(out=st6[:, b, 0:6], in_=zz[:, b, :]), [('sg_zz', b)], [('sg_st', b)])
                T.op('dve', lambda: nc.vector.bn_aggr(out=st6[:, b, 6:8], in_=st6[:, b, 0:6]), [('sg_st', b)], [('sg_st', b)])
                yield
                TS('pool', st6[:, b, 7:8], st6[:, b, 7:8], EPS, None, ALU.add, None, [('sg_st', b)], [('sg_st', b)])
                TT('pool', st6[:, b, 7:8], st6[:, b, 7:8], mhalf[:, 0:1], ALU.pow, [('sg_st', b), 'mhalf'], [('sg_st', b)])
                yield
                TS('dve', zz[:, b, :], zz[:, b, :], st6[:, b, 6:7], st6[:, b, 7:8], ALU.subtract, ALU.mult,
                   [('sg_zz', b), ('sg_st', b)], [('sg_zz', b)])
                yield
                TT('pool', svb[:, b, :], zz[:, b, :], gb[:, 0, :], ALU.mult, [('sg_zz', b), 'sg_gb'], [('sg_svb', b)])
                yield
                for g in range(4):
                    MM(bank(gbk)[:, g * 128:(g + 1) * 128], svb[:, b, g * 128:(g + 1) * 128], sgwT[:, g, :], True, True,
                       [('sg_svb', b), 'sg_wT'], [ps(gbk)])
                yield
                tmp, tk = fs_next()
                TT('dve', tmp[:, 0:512], bank(gbk), gb[:, 1, :], ALU.add, [ps(gbk), 'sg_gb'], [tk])
                yield
                TT('pool', ysgT[:, :, tc * 128:(tc + 1) * 128], tmp[:, 0:512].rearrange("p (g t) -> p g t", g=4),
                   uT[:, :, tc * 128:(tc + 1) * 128], ALU.mult, [tk] + [('sg_u', j) for j in range(4)],
                   [('ysg', j) for j in range(4)])

            run_pipeline(list(range(8)), sg_job, 4)
            T.free(keys)

    def merge(l, ys):
        par = l % 2
        ynames = ['yda', 'yml', 'ysg']
        with ExitStack() as es:
            mg = es.enter_context(sbt("mg", [128, 8, 1024], BF16))
            accf = es.enter_context(sbt("mg_acc", [128, 4, 1024], F32))
            keys = [('mg', dc) for dc in range(8)] + [('mg_acc', j) for j in range(4)]
            T.adopt(keys)
            for dcg in range(2):
                for n in range(3):
                    w, wk = wload(('w_in', (l,)), 0, 8, 4624 + n * 1024 + dcg * 512, 512)
                    wb, wbk = wload(('w_branch', (l, n)), 0, 4, 0, 1024)
                    for j in range(4):
                        dc = dcg * 4 + j
                        gp, gk = p2_next()
                        for th in range(2):
                            for kc in range(8):
                                MM(gp[:, th * 512:(th + 1) * 512], w[:, kc, j * 128:(j + 1) * 128],
                                   hT[:, kc, th * 512:(th + 1) * 512], kc == 0, kc == 7, [wk, ('h', kc)], [gk[th]])
                        pp, pk = p2_next()
                        for th in range(2):
                            for kc in range(4):
                                MM(pp[:, th * 512:(th + 1) * 512], wb[:, kc, dc * 128:(dc + 1) * 128],
                                   ys[n][:, kc, th * 512:(th + 1) * 512], kc == 0, kc == 3, [wbk, (ynames[n], kc)], [pk[th]])
                        sgt, sk = fs_next()
                        ACT(sgt[:, :], gp[:, :], AF.Sigmoid, gk, [sk])
                        if n == 0:
                            TT('dve', accf[:, j, :], pp[:, :], sgt[:, :], ALU.mult, pk + [sk], [('mg_acc', j)])
                        else:
                            t2, t2k = fs_next()
                            TT('dve', t2[:, :], pp[:, :], sgt[:, :], ALU.mult, pk + [sk], [t2k])
                            if n == 1:
                                TT('pool', accf[:, j, :], accf[:, j, :], t2[:, :], ALU.add, [('mg_acc', j), t2k], [('mg_acc', j)])
                            else:
                                TT('pool', mg[:, dc, :], accf[:, j, :], t2[:, :], ALU.add, [('mg_acc', j), t2k], [('mg', dc)])
            for og_ in range(2):
                w, wk = wload(('w_out', (l,)), 0, 8, og_ * 512, 512)
                for j in range(4):
                    dc = og_ * 4 + j
                    zp, zk = p2_next()
                    for th in range(2):
                        for kc in range(8):
                            MM(zp[:, th * 512:(th + 1) * 512], w[:, kc, j * 128:(j + 1) * 128], mg[:, kc, th * 512:(th + 1) * 512],
                               kc == 0, kc == 7, [wk, ('mg', kc)], [zk[th]])
                    for th in range(2):
                        STT(xT[:, dc, th * 512:(th + 1) * 512], zp[:, th * 512:(th + 1) * 512], modT[:, par, 16 + dc:17 + dc],
                            xT[:, dc, th * 512:(th + 1) * 512], ALU.mult, ALU.add, [zk[th], ('mod', par), ('x', dc)], [('x', dc)])
            T.free(keys)

    def da_phase(l, ydaT):
        wl = w_in[l]
        with ExitStack() as es:
            KT = es.enter_context(sbt("da_KT", [128, 4, 1280], BF16))
            qT = es.enter_context(sbt("da_qT", [128, 4, 1024], BF16))
            V = es.enter_context(sbt("da_V", [128, 10, 512], BF16))
            oall = es.enter_context(sbt("da_oall", [128, 4, 1024], F32))
            ET = es.enter_context(sbt("da_ET", [128, 3, 512], BF16))
            kbf = es.enter_context(sbt("da_kbf", [128, 2, 1024], BF16))
            ckf = es.enter_context(sbt("da_ckf", [128, 2, 128], F32))
            osb = es.enter_context(sbt("da_osb", [128, 2, 256], F32))
            rs = es.enter_context(sbt("da_rs", [128, 2, 256], F32))
            vst = es.enter_context(sbt("da_vst", [128, 2, 512], F32))
            dagl = es.enter_context(sbt("da_gl", [128, 1], F32))
            keys = ([('KT', h) for h in range(4)] + [('qT', h) for h in range(4)] + [('V', c) for c in range(10)] +
                    [('oall', h) for h in range(4)] + [('ET', i) for i in range(3)] + [('kbf', 0), ('kbf', 1), 'ckf', ('osb', 0), ('osb', 1),
                                                                                        ('rs', 0), ('rs', 1), ('vst', 0), ('vst', 1), 'dagl'])
            T.adopt(keys)
            TS('dve', dagl[:, :], dag_c(l), 1.0 - (0.8 - 0.6 * math.exp(-0.3 * l)), None, ALU.mult, None, ['CPm'], ['dagl'])
            for hd in range(4):
                DMA('pool', V[:, 0:2, hd * 128:(hd + 1) * 128], cv[l, hd].rearrange("(c p) d -> p c d", p=128), (),
                    [('V', 0), ('V', 1)], 'dw')
            vck = vst[:, :, :].rearrange("p a (h d) -> p (a h) d", h=2)
            for hd in range(4):
                DMA('sp', vck[:, hd, :].rearrange("p (c d) -> p c d", c=2), ck[l, hd].rearrange("(c p) d -> p c d", p=128), (),
                    [('vst', hd // 2)], 'di')
            for hp in range(2):
                for hh in range(2):
                    hd = hp * 2 + hh
                    for c in range(2):
                        TR(bank(6)[:, (hh * 2 + c) * 128:(hh * 2 + c + 1) * 128], vck[:, hd, c * 128:(c + 1) * 128], ident_f[:],
                           [('vst', hp), 'ident_f'], [ps(6)])
                CP('dve', KT[:, hp * 2:hp * 2 + 2, 0:256], bank(6).rearrange("p (h k) -> p h k", h=2), [ps(6)],
                   [('KT', hp * 2), ('KT', hp * 2 + 1)])
            wqk = {}

            def proj_job(job, slot):
                which, hd = job
                if hd == 0:
                    c0 = 512 if which == 0 else 0
                    wqk[which] = wload(('w_in', (l,)), 0, 8, c0, 512)
                w, wk = wqk[which]
                zp, zk = P2[slot], [ps(2 * slot), ps(2 * slot + 1)]
                for th in range(2):
                    for kc in range(8):
                        MM(zp[:, th * 512:(th + 1) * 512], w[:, kc, hd * 128:(hd + 1) * 128], hT[:, kc, th * 512:(th + 1) * 512],
                           kc == 0, kc == 7, [wk, ('h', kc)], [zk[th]])
                yield
                CP('act', kbf[:, slot, :], zp[:, :], zk, [('kbf', slot)])
                yield
                sp_, spk = PC, [ps(4), ps(5)]
                for th in range(2):
                    MM(sp_[:, th * 512:(th + 1) * 512], prot_b[:], kbf[:, slot, th * 512:(th + 1) * 512], True, True,
                       ['prot_b', ('kbf', slot)], [spk[th]])
                yield
                t1, t1k = fs_next()
                t2, t2k = fs_next()
                TT('dve', t1[:, :], zp[:, :], ropeC[:, :], ALU.mult, zk + ['ropeC'], [t1k])
                TT('dve', t2[:, :], sp_[:, :], ropeS[:, :], ALU.mult, spk + ['ropeS'], [t2k])
                yield
                if which == 0:
                    TT('pool', t1[:, :], t1[:, :], t2[:, :], ALU.add, [t1k, t2k], [t1k])
                    CP('pool', KT[:, hd, 256:1280], t1[:, :], [t1k], [('KT', hd)])
                    yield
                    for tcg in range(2):
                        for j in range(4):
                            tc = tcg * 4 + j
                            TR(bank(6)[:, j * 128:(j + 1) * 128], t1[:, tc * 128:(tc + 1) * 128], ident_f[:],
                               [t1k, 'ident_f'], [ps(6)])
                        CP('act', vst[:, tcg, :], bank(6), [ps(6)], [('vst', tcg)])
                        DMA('sp', ok_o[l, hd, tcg * 512:(tcg + 1) * 512, :].rearrange("(j p) d -> p j d", p=128),
                            vst[:, tcg, :].rearrange("p (j d) -> p j d", j=4), [('vst', tcg)], (), 'do')
                else:
                    TT('pool', qT[:, hd, :], t1[:, :], t2[:, :], ALU.add, [t1k, t2k], [('qT', hd)])

            run_pipeline([(which, hd) for which in range(2) for hd in range(4)], proj_job, 2)
            w, wk = wload(('w_in', (l,)), 0, 8, 1024, 512)
            for tc in range(8):
                bk = 4 + tc % 2
                for kc in range(8):
                    MM(bank(bk), hT[:, kc, tc * 128:(tc + 1) * 128], w[:, kc, :], kc == 0, kc == 7, [('h', kc), wk], [ps(bk)])
                CP('act', V[:, 2 + tc, :], bank(bk), [ps(bk)], [('V', 2 + tc)])
                CP('dve', vst[:, tc % 2, :], bank(bk), [ps(bk)], [('vst', tc % 2)])
                DMA('sp', ov_o[l, :, tc * 128:(tc + 1) * 128, :].rearrange("h p d -> p h d"),
                    vst[:, tc % 2, :].rearrange("p (h d) -> p h d", h=4), [('vst', tc % 2)], (), 'do')
            iters = [(hd, qt, kc) for hd in range(4) for qt in range(4) for kc in range(10)]
            rsf = rs[:, :, :].rearrange("p a b -> p (a b)")
            osf = osb[:, :, :].rearrange("p a b -> p (a b)")

            PT32 = PT[:, :].bitcast(F32)
            accb = [(bank(4), bank(5), ps(4), ps(5)), (bank(6), PT32, ps(6), ps(7))]
            SR = [(PA, ps(0), ps(1)), (PB, ps(2), ps(3))]

            def emit_qk(i):
                hd, qt, kc = iters[i]
                reg, k0, k1 = SR[i % 2]
                for half in range(2):
                    lo, hi = half * 64, half * 64 + 64
                    MM(reg[:, half * 512:half * 512 + 256], KT[lo:hi, hd, kc * 128:(kc + 1) * 128],
                       qT[lo:hi, hd, qt * 256:(qt + 1) * 256], True, True, [('KT', hd), ('qT', hd)], [k0 if half == 0 else k1])

            def emit_rest(i):
                hd, qt, kc = iters[i]
                reg, k0, k1 = SR[i % 2]
                sbk = i % 3
                ao, as_, ko, ks = accb[(hd * 4 + qt) % 2]
                ACT(ET[:, sbk, :].rearrange("p (a b) -> p a b", a=2), reg[:, :].rearrange("p (a b) -> p a b", a=2)[:, :, 0:256],
                    AF.Exp, [k0, k1, 'maskb'], [('ET', sbk)],
                    bias=maskb[:, (kc // 2) * 4 + qt:(kc // 2) * 4 + qt + 1], scale=0.125)
                MM(ao, V[:, kc, hd * 128:(hd + 1) * 128], ET[:, sbk, :], kc == 0, kc == 9, [('V', kc), ('ET', sbk)], [ko])
                MM(as_, ones_b[:], ET[:, sbk, :], kc == 0, kc == 9, ['ones_b', ('ET', sbk)], [ks])
                if kc == 9:
                    T.op('dve', lambda: nc.vector.reciprocal(out=rsf, in_=as_), [ks], [('rs', 0), ('rs', 1)])
                    TT('dve', osf, ao, rsf, ALU.mult, [ko, ('rs', 0), ('rs', 1)], [('osb', 0), ('osb', 1)])
                    STT(oall[:, hd, qt * 256:(qt + 1) * 256], osb[:, 1, :], nlam[:, l:l + 1], osb[:, 0, :], ALU.mult, ALU.add,
                        [('osb', 0), ('osb', 1), 'nlam'], [('oall', hd)])

            emit_qk(0)
            for i in range(len(iters)):
                if i + 1 < len(iters):
                    emit_qk(i + 1)
                emit_rest(i)
            def danorm_job(hd, slot):
                reg, k0, k1 = SR[slot]
                ACT(ydaT[:, hd, :], oall[:, hd, :], AF.Square, [('oall', hd)], [('yda', hd)])
                yield
                for th in range(2):
                    MM(reg[:, th * 512:(th + 1) * 512], ones_b[:], ydaT[:, hd, th * 512:(th + 1) * 512], True, True,
                       [('yda', hd), 'ones_b'], [k0 if th == 0 else k1])
                yield
                rs_, rsk = fs_next()
                ACT(rs_[:, :], reg[:, :], AF.Ln, [k0, k1, 'eps_t'], [rsk], bias=eps_t[:], scale=1.0 / 128.0)
                ACT(rs_[:, :], rs_[:, :], AF.Exp, [rsk], [rsk], scale=-0.5)
                yield
                STT(ydaT[:, hd, :], oall[:, hd, :], dagl[:, 0:1], rs_[:, :], ALU.mult, ALU.mult, [('oall', hd), 'dagl', rsk],
                    [('yda', hd)])

            run_pipeline(list(range(4)), danorm_job, 2)
            T.free(keys)

    def ml_phase(l, ymlT):
        par = l % 2
        wl = w_in[l]
        LK = (2, 4, 6)
        with ExitStack() as es:
            COLS = es.enter_context(sbt("ml_cols", [128, 8, 4, 8], F32))
            DECB = es.enter_context(sbt("ml_decb", [128, 8, 8], F32))
            MP = es.enter_context(sbt("ml_mp", [8, 16], F32))
            MPL = es.enter_context(sbt("ml_mpl", [8, 16], F32))
            NM = es.enter_context(sbt("ml_nm", [8, 16], F32))
            DT = es.enter_context(sbt("ml_dt", [8, 16], F32))
            DEC = es.enter_context(sbt("ml_dec", [8, 16], F32))
            k1 = ['ml_cols', 'ml_decb', 'ml_mp', 'ml_mpl', 'ml_nm', 'ml_dt', 'ml_dec']
            T.adopt(k1)
            with ExitStack() as es2:
                Rr = [es2.enter_context(sbt(f"ml_r{i}", [8, 1024], F32)) for i in range(9)]
                rk = [('ml_r', i) for i in range(9)]
                T.adopt(rk)
                R0, R1, R2, R3, R4, R5, R6, R7a, R7b = Rr
                w, wk = wload(('w_in', (l,)), 0, 8, 3584, 16)
                for (pp, c0, kk) in ((PA, 0, [ps(0), ps(1)]), (PB, 8, [ps(2), ps(3)])):
                    for th in range(2):
                        for kc in range(8):
                            MM(pp[0:8, th * 512:(th + 1) * 512], w[:, kc, c0:c0 + 8], hT[:, kc, th * 512:(th + 1) * 512],
                               kc == 0, kc == 7, [wk, ('h', kc)], [kk[th]])
                for (pp, kk, dst, dk, bcol) in ((PA, [ps(0), ps(1)], R0, rk[0], 0), (PB, [ps(2), ps(3)], R1, rk[1], 1)):
                    ACT(R4[:, :], pp[0:8, :], AF.Identity, kk + ['GB'], [rk[4]], bias=GB[:, l, bcol:bcol + 1])
                    ACT(R5[:, :], pp[0:8, ::-1], AF.Identity, kk + ['GB'], [rk[5]], bias=GB[:, l, bcol:bcol + 1])
                    TS('dve', dst[:, :], R4[:, :], dirm[:, 0:1], None, ALU.mult, None, [rk[4], 'dirm'], [dk])
                    STT(dst[:, :], R5[:, :], dirm[:, 1:2], dst[:, :], ALU.mult, ALU.add, [rk[5], 'dirm', dk], [dk])
                ACT(R1[:, :], R1[:, :], AF.Exp, [rk[1]], [rk[1]], scale=-1.0)
                ACT(R1[:, :], R1[:, :], AF.Ln, [rk[1], 'one_t'], [rk[1]], bias=one_t[0:8, :], scale=1.0)
                for c in range(8):
                    sl = slice(c * 128, (c + 1) * 128)
                    T.op('dve', lambda: nc.vector.tensor_tensor_scan(out=R2[:, sl], data0=ones8[:, :], data1=R1[:, sl], initial=0.0,
                                                                      op0=ALU.mult, op1=ALU.add), [rk[1], 'ones8'], [rk[2]])
                TT('dve', R0[:, :], R0[:, :], R2[:, :], ALU.add, [rk[0], rk[2]], [rk[0]])
                for c in range(8):
                    sl = slice(c * 128, (c + 1) * 128)
                    T.op('dve', lambda: nc.vector.tensor_tensor_scan(out=R3[:, sl], data0=R0[:, sl], data1=R0[:, sl], initial=-1e30,
                                                                      op0=ALU.max, op1=ALU.max), [rk[0]], [rk[3]])
                DMA('sp', MP[:, 0:1], sm[l], (), ['ml_mp'], 'di')
                for cs in range(8):
                    sl = slice(cs * 128, (cs + 1) * 128)
                    la = cs * 128 + 127
                    mprev = MP[:, cs:cs + 1]
                    mpk = 'ml_mp'
                    if cs in LK:
                        TS('dve', MPL[:, cs:cs + 1], mprev, link[0:8, 0:1], None, ALU.mult, None, ['ml_mp', 'link'], ['ml_mpl'])
                        mprev = MPL[:, cs:cs + 1]
                        mpk = 'ml_mpl'
                    TS('dve', R3[:, sl], R3[:, sl], mprev, None, ALU.max, None, [rk[3], mpk], [rk[3]])
                    TT('dve', MP[:, cs + 1:cs + 2], R3[:, la:la + 1], R2[:, la:la + 1], ALU.subtract, [rk[3], rk[2]], ['ml_mp'])
                    ACT(R5[:, sl], R3[:, sl], AF.Exp, [rk[3], mpk], [rk[5]], bias=mprev, scale=-1.0)
                    if cs in LK:
                        TS('dve', R5[:, sl], R5[:, sl], link[0:8, 0:1], None, ALU.mult, None, [rk[5], 'link'], [rk[5]])
                    ACT(R4[:, sl], R3[:, sl], AF.Exp, [rk[3]], [rk[4]], bias=R3[:, la:la + 1], scale=-1.0)
                    TS('dve', NM[:, cs:cs + 1], R3[:, la:la + 1], -1.0, None, ALU.mult, None, [rk[3]], ['ml_nm'])
                    ACT(R0[:, sl], R0[:, sl], AF.Exp, [rk[0], 'ml_nm'], [rk[0]], bias=NM[:, cs:cs + 1], scale=1.0)
                    TT('dve', DT[:, cs:cs + 1], mprev, R2[:, la:la + 1], ALU.subtract, [mpk, rk[2]], ['ml_dt'])
                    TT('dve', DT[:, cs:cs + 1], DT[:, cs:cs + 1], MP[:, cs + 1:cs + 2], ALU.subtract, ['ml_dt', 'ml_mp'], ['ml_dt'])
                    ACT(DEC[:, cs:cs + 1], DT[:, cs:cs + 1], AF.Exp, ['ml_dt'], ['ml_dec'])
                    if cs in LK:
                        TS('dve', DEC[:, cs:cs + 1], DEC[:, cs:cs + 1], link[0:8, 0:1], None, ALU.mult, None, ['ml_dec', 'link'],
                           ['ml_dec'])
                TT('dve', R2[:, :], R2[:, :], R3[:, :], ALU.subtract, [rk[2], rk[3]], [rk[2]])
                ACT(R2[:, :], R2[:, :], AF.Exp, [rk[2]], [rk[2]])
                CP('dve', MPL[:, 12:16], MP[:, 2:10:2], ['ml_mp', 'ml_mpl'], ['ml_mpl'])
                DMA('sp', om_o[l], MPL[:, 12:16], ['ml_mpl'], (), 'do')
                for qi, (Q, qk) in enumerate(((R0, rk[0]), (R4, rk[4]), (R5, rk[5]), (R2, rk[2]))):
                    Ro, rok = (R7a, rk[7]) if qi % 2 == 0 else (R7b, rk[8])
                    TS('dve', R6[:, :], Q[:, :], dirm[:, 0:1], None, ALU.mult, None, [qk, 'dirm'], [rk[6]])
                    STT(Ro[:, :], Q[:, ::-1], dirm[:, 1:2], R6[:, :], ALU.mult, ALU.add, [rk[6], 'dirm', qk], [rok])
                    for c in range(8):
                        o0 = (c * 4 + qi) * 8
                        TR(bank(6)[:, o0:o0 + 8], Ro[:, c * 128:(c + 1) * 128], ident_f[0:8, 0:8], [rok, 'ident_f'], [ps(6)])
                CP('dve', COLS[:, :, :, :].rearrange("p a b c -> p (a b c)"), bank(6)[:, 0:256], [ps(6)], ['ml_cols'])
                for r in range(8):
                    MM(bank(5)[:, r * 8:(r + 1) * 8], sel[0:8, r * 128:(r + 1) * 128], DEC[0:8, 0:8], True, True, ['sel', 'ml_dec'],
                       [ps(5)])
                CP('dve', DECB[:, :, :].rearrange("p a b -> p (a b)"), bank(5)[:, 0:64], [ps(5)], ['ml_decb'])
                T.free(rk)
            hsum = es.enter_context(sbt("ml_hs", [128, 8, 512], F32))
            Cn32 = es.enter_context(sbt("ml_cn", [128, 8, 130], F32))
            T.adopt([('hs', c) for c in range(8)] + [('cn', r) for r in range(8)])
            es3 = ExitStack()
            mqT = es3.enter_context(sbt("ml_qT", [128, 4, 1024], BF16))
            mkT = es3.enter_context(sbt("ml_kT", [128, 4, 1024], BF16))
            mva = es3.enter_context(sbt("ml_va", [128, 8, 4, 130], BF16))
            Cnb = es3.enter_context(sbt("ml_cnb", [128, 8, 130], BF16))
            PTs = es3.enter_context(sbt("ml_pts", [128, 6, 128], BF16))
            kg = es3.enter_context(sbt("ml_kg", [128, 6, 128], BF16))
            vg = es3.enter_context(sbt("ml_vg", [128, 6, 130], BF16))
            tB = es3.enter_context(sbt("ml_tb", [128, 6, 130], F32))
            nd = es3.enter_context(sbt("ml_nd", [128, 6, 130], F32))
            dn = es3.enter_context(sbt("ml_dn", [128, 6, 2], F32))
            k2 = ([('mqT', h) for h in range(4)] + [('mkT', h) for h in range(4)] + [('mva', c) for c in range(8)] +
                  [('cnb', r) for r in range(8)] +
                  [(nm_, b) for nm_ in ('pts', 'kg', 'vg', 'tb', 'nd', 'dn') for b in range(6)])
            T.adopt(k2)
            for dr in range(2):
                for hd in range(4):
                    r = dr * 4 + hd
                    DMA('sp', Cn32[:, r, 0:129], sCn[l, dr, hd], (), [('cn', r)], 'di')
                    CP('pool', Cnb[:, r, 0:129], Cn32[:, r, 0:129], [('cn', r)], [('cnb', r)])
            for which, c0 in ((0, 1536), (1, 2048)):
                w, wk = wload(('w_in', (l,)), 0, 8, c0, 512)
                for hd in range(4):
                    zp, zk = p2_next()
                    for th in range(2):
                        for kc in range(8):
                            MM(zp[:, th * 512:(th + 1) * 512], w[:, kc, hd * 128:(hd + 1) * 128], hT[:, kc, th * 512:(th + 1) * 512],
                               kc == 0, kc == 7, [wk, ('h', kc)], [zk[th]])
                    ch = which * 4 + hd
                    acc, ak = fs_next()
                    dwconv(l, zp, zk, acc, ak, mlw_c(l, 0, ch), mlw_c(l, 1, ch), mlw_c(l, 2, ch), mlb_c(l, ch),
                           w0n[:, par, 44 + ch:45 + ch], w2n[:, par, 44 + ch:45 + ch])
                    if which == 0:
                        ACT(mqT[:, hd, :], acc[:, :], AF.Silu, [ak], [('mqT', hd)])
                    else:
                        sgt, sk = fs_next()
                        ACT(sgt[:, :], acc[:, :], AF.Sigmoid, [ak], [sk])
                        STT(mkT[:, hd, :], acc[:, :], 128.0 ** -0.5, sgt[:, :], ALU.mult, ALU.mult, [ak, sk], [('mkT', hd)])
            w, wk = wload(('w_in', (l,)), 0, 8, 2560, 512)
            MS('pool', mva[:, :, :, 128:130], 1.0, [('mva', c) for c in range(8)])
            for tc in range(8):
                bk = 4 + tc % 2
                for kc in range(8):
                    MM(bank(bk), hT[:, kc, tc * 128:(tc + 1) * 128], w[:, kc, :], kc == 0, kc == 7, [('h', kc), wk], [ps(bk)])
                CP('act', mva[:, tc, :, 0:128], bank(bk).rearrange("p (h d) -> p h d", h=4), [ps(bk)], [('mva', tc)])
            def core_iter(cs, hd, dr, slot):
                c = cs if dr == 0 else 7 - cs
                r = dr * 4 + hd
                tsl = slice(c * 128, (c + 1) * 128)
                mask = maskF if dr == 0 else maskB
                mkey = 'maskF' if dr == 0 else 'maskB'
                pb = bank(slot)
                pk_ = ps(slot)
                Sps, Aps, Bps, CNps = pb[:, 0:128], pb[:, 130:259], pb[:, 260:389], pb[:, 0:129]
                MM(Sps, mkT[:, hd, tsl], mqT[:, hd, tsl], True, True, [('mkT', hd), ('mqT', hd)], [pk_])
                TR(PT[:, slot * 128:(slot + 1) * 128], mkT[:, hd, tsl], ident_b[:], [('mkT', hd), 'ident_b'], [ps(7)])
                yield
                TT('dve', PTs[:, slot, :], Sps, mask[:, :], ALU.mult, [pk_, mkey], [('pts', slot)])
                ACT(kg[:, slot, :], PT[:, slot * 128:(slot + 1) * 128], AF.Identity, [ps(7), 'ml_cols'], [('kg', slot)],
                    scale=COLS[:, c, 0, r:r + 1])
                ACT(vg[:, slot, :], mva[:, c, hd, :], AF.Identity, [('mva', c), 'ml_cols'], [('vg', slot)],
                    scale=COLS[:, c, 0, r:r + 1])
                yield
                MM(Aps, PTs[:, slot, :], vg[:, slot, 0:129], True, True, [('pts', slot), ('vg', slot)], [pk_])
                MM(Bps, mqT[:, hd, tsl], Cnb[:, r, 0:129], True, True, [('mqT', hd), ('cnb', r)], [pk_])
                yield
                ACT(tB[:, slot, 0:129], Bps, AF.Identity, [pk_, 'ml_cols'], [('tb', slot)], scale=COLS[:, c, 2, r:r + 1])
                STT(nd[:, slot, 0:129], Aps, COLS[:, c, 1, r:r + 1], tB[:, slot, 0:129], ALU.mult, ALU.add,
                    [pk_, 'ml_cols', ('tb', slot)], [('nd', slot)])
                STT(dn[:, slot, 0:1], nd[:, slot, 128:129], -1.0, nd[:, slot, 128:129], ALU.mult, ALU.max, [('nd', slot)],
                    [('dn', slot)])
                TS('dve', dn[:, slot, 0:1], dn[:, slot, 0:1], COLS[:, c, 3, r:r + 1], None, ALU.max, None,
                   [('dn', slot), 'ml_cols'], [('dn', slot)])
                T.op('dve', lambda: nc.vector.reciprocal(out=dn[:, slot, 1:2], in_=dn[:, slot, 0:1]), [('dn', slot)], [('dn', slot)])
                hs = hsum[:, c, hd * 128:(hd + 1) * 128]
                if cs < 4:
                    TS('dve', hs, nd[:, slot, 0:128], dn[:, slot, 1:2], None, ALU.mult, None, [('nd', slot), ('dn', slot)],
                       [('hs', c)])
                else:
                    STT(hs, nd[:, slot, 0:128], dn[:, slot, 1:2], hs, ALU.mult, ALU.add, [('nd', slot), ('dn', slot), ('hs', c)],
                        [('hs', c)])
                yield
                MM(CNps, kg[:, slot, :], mva[:, c, hd, 0:129], True, True, [('kg', slot), ('mva', c)], [pk_])
                yield
                STT(Cn32[:, r, 0:129], Cn32[:, r, 0:129], DECB[:, r, cs:cs + 1], CNps, ALU.mult, ALU.add,
                    [('cn', r), 'ml_decb', pk_], [('cn', r)])
                CP('pool', Cnb[:, r, 0:129], Cn32[:, r, 0:129], [('cn', r)], [('cnb', r)])
                if cs % 2 == 1:
                    DMA('sp', oCn_o[l, dr, cs // 2, hd], Cn32[:, r, 0:129], [('cn', r)], (), 'do')

            todo = [(cs, hd, dr) for cs in range(8) for hd in range(4) for dr in range(2)]
            modgen = [compute_mod_gen(l + 1) if l + 1 < depth else None]
            run_pipeline(todo, lambda job, sl_: core_iter(job[0], job[1], job[2], sl_), 6, modgen, 5)
            if l == 0:
                dump('hs', hsum[:, :, :], [('hs', c) for c in range(8)])
                dump('cols', COLS[:, :, :, :], ['ml_cols'])
            T.free(k2)
            es3.close()
            og = es.enter_context(sbt("ml_og", [128, 8, 512], BF16))
            gml4 = es.enter_context(sbt("ml_g4", [128, 512], F32))
            ssq = es.enter_context(sbt("ml_ssq", [128, 4, 4], F32))
            ytm = es.enter_context(sbt("ml_ytm", [128, 4, 512], BF16))
            k3 = [('og', c) for c in range(8)] + ['gml4'] + [(n_, b_) for n_ in ('ssq', 'ytm') for b_ in range(4)]
            T.adopt(k3)
            for h in range(4):
                DMA('sp', gml4[:, h * 128:(h + 1) * 128], ml_norm_g[l].partition_broadcast(128), (), ['gml4'], 'di')
            w, wk = wload(('w_in', (l,)), 0, 8, 3072, 512)

            def mlout_job(c, slot):
                bk = slot
                for kc in range(8):
                    MM(bank(bk), hT[:, kc, c * 128:(c + 1) * 128], w[:, kc, :], kc == 0, kc == 7, [('h', kc), wk], [ps(bk)])
                yield
                tmp, tk = fs_next()
                ACT(tmp[:, 0:512], bank(bk), AF.Sigmoid, [ps(bk)], [tk])
                ACT(tmp[:, 512:1024], hsum[:, c, :], AF.Square, [('hs', c)], [tk])
                yield
                TT('pool', og[:, c, :], tmp[:, 0:512], gml4[:, :], ALU.mult, [tk, 'gml4'], [('og', c)])
                T.op('dve', lambda: nc.vector.tensor_reduce(out=ssq[:, slot, 0:4],
                                                             in_=tmp[:, 512:1024].rearrange("p (h d) -> p h d", h=4),
                                                             axis=AX.X, op=ALU.add), [tk], [('ssq', slot)])
                yield
                TS('pool', ssq[:, slot, 0:4], ssq[:, slot, 0:4], 1.0 / 128.0, EPS, ALU.mult, ALU.add, [('ssq', slot)], [('ssq', slot)])
                TT('pool', ssq[:, slot, 0:4], ssq[:, slot, 0:4], mhalf[:, 0:4], ALU.pow, [('ssq', slot), 'mhalf'], [('ssq', slot)])
                yield
                for hd in range(4):
                    hsl = slice(hd * 128, (hd + 1) * 128)
                    STT(ytm[:, slot, hsl], hsum[:, c, hsl], ssq[:, slot, hd:hd + 1], og[:, c, hsl], ALU.mult, ALU.mult,
                        [('hs', c), ('ssq', slot), ('og', c)], [('ytm', slot)])
                yield
                po = (slot % 2) * 512
                for hd in range(4):
                    hsl = slice(hd * 128, (hd + 1) * 128)
                    TR(PT[:, po + hd * 128:po + (hd + 1) * 128], ytm[:, slot, hsl], ident_b[:], [('ytm', slot), 'ident_b'], [ps(7)])
                yield
                CP('act', ymlT[:, :, c * 128:(c + 1) * 128], PT[:, po:po + 512].rearrange("p (h t) -> p h t", h=4), [ps(7)],
                   [('yml', h) for h in range(4)])

            run_pipeline(list(range(8)), mlout_job, 4)
            T.free(k1 + k3 + [('hs', c) for c in range(8)] + [('cn', r) for r in range(8)])

    def mixer(l):
        norm_mod(l, 0)
        with ExitStack() as es:
            ydaT = es.enter_context(sbt("ydaT", [128, 4, 1024], BF16))
            ymlT = es.enter_context(sbt("ymlT", [128, 4, 1024], BF16))
            ysgT = es.enter_context(sbt("ysgT", [128, 4, 1024], BF16))
            yk = [(n, h) for n in ('yda', 'yml', 'ysg') for h in range(4)]
            T.adopt(yk)
            if 'ml' in phases:
                ml_phase(l, ymlT)
            else:
                MS('pool', ymlT[:, :, :], 0.0, [('yml', h) for h in range(4)])
            if 'da' in phases:
                da_phase(l, ydaT)
            else:
                MS('pool', ydaT[:, :, :], 0.0, [('yda', h) for h in range(4)])
            if 'sg' in phases:
                sg_phase(l, ysgT)
            else:
                MS('pool', ysgT[:, :, :], 0.0, [('ysg', h) for h in range(4)])
            if l == 0:
                dump('yda', ydaT[:, :, :], [('yda', h) for h in range(4)])
                dump('yml', ymlT[:, :, :], [('yml', h) for h in range(4)])
                dump('ysg', ysgT[:, :, :], [('ysg', h) for h in range(4)])
            merge(l, [ydaT, ymlT, ysgT])
            T.free(yk)

    compute_mod(0)
    for l in range(depth):
        prep_w0n(l)
        if l + 1 < depth and 'ml' not in phases:
            compute_mod(l + 1)
        if any(p in phases for p in ('ml', 'da', 'sg')):
            mixer(l)
        if 'ffn' in phases:
            ffn(l)

    with ExitStack() as es:
        yT = es.enter_context(sbt("yT", [128, 8, 1024], F32))
        T.adopt([('yT', dc) for dc in range(8)])
        rms_stats([xT[:, dc, :] for dc in range(8)], [('x', dc) for dc in range(8)],
                  [hT[:, dc, :] for dc in range(8)], [('h', dc) for dc in range(8)], 1024.0)
        for dc in range(8):
            STT(yT[:, dc, :], xT[:, dc, :], gcol[:, dc:dc + 1], rstd[:], ALU.mult, ALU.mult, [('x', dc), 'gcol', 'rstd'],
                [('yT', dc)])
        for tc in range(8):
            st, sk = fs_next()
            for half in range(2):
                bk = 2 + half
                for j in range(4):
                    dc = half * 4 + j
                    TR(bank(bk)[:, j * 128:(j + 1) * 128], yT[:, dc, tc * 128:(tc + 1) * 128], ident_f[:],
                       [('yT', dc), 'ident_f'], [ps(bk)])
                CP('dve' if half == 0 else 'act', st[:, half * 512:(half + 1) * 512], bank(bk), [ps(bk)], [sk])
            DMA('sp', y_o[tc * 128:(tc + 1) * 128, :], st[:, :], [sk], (), 'do')
        T.free([('yT', dc) for dc in range(8)])
    T.finish()
    return nc, T, dumps, wrec


def _consts():
    ident = np.eye(128, dtype=np.float32)
    prot = np.zeros((128, 128), np.float32)
    for m in range(128):
        prot[m ^ 16, m] = 1.0
    s = np.arange(128)[:, None]
    t = np.arange(128)[None, :]
    maskF = (s <= t).astype(np.float32)
    maskB = (s >= t).astype(np.float32)
    sel = np.zeros((8, 8, 128), np.float32)
    for r in range(8):
        sel[r, r, :] = 1.0
    dirm = np.zeros((8, 2), np.float32)
    dirm[0:4, 0] = 1.0
    dirm[4:8, 1] = 1.0
    return dict(c_ident=ident, c_prot=prot, c_maskF=maskF, c_maskB=maskB, c_sel=sel.reshape(8, 1024), c_dirm=dirm)


def _rope_tables():
    t = np.arange(1024)
    row = (t // 64).astype(np.float32)
    col = (t % 64).astype(np.float32)
    nf = 16
    inv = (10000.0 ** (-np.arange(nf, dtype=np.float32) / nf)).astype(np.float32)
    ang = np.stack([row[:, None] * inv, col[:, None] * inv], axis=1)
    cos = np.cos(ang).astype(np.float32)
    sin = np.sin(ang).astype(np.float32)
    C = np.zeros((128, 1024), np.float32)
    S = np.zeros((128, 1024), np.float32)
    for d in range(128):
        axis = (d >> 5) & 1
        half = (d >> 4) & 1
        f = d & 15
        C[d] = cos[:, axis, f]
        S[d] = sin[:, axis, f] * (-1.0 if half == 0 else 1.0)
    return C, S


_CACHE = {}


def kernel(x_prompt, x_sample, c, cache_k, cache_v, state_C, state_n, state_m, c_ctx,
           w_mod, b_mod, w_in, da_lambda, da_norm_g, ml_conv_w, ml_conv_b, ml_gate_b,
           ml_norm_g, sg_norm_g, sg_w, sg_b, w_branch, w_out, w_up, ffn_conv_w, ffn_conv_b,
           w_down, final_g, _depth=L, _phases=('ml', 'da', 'sg', 'ffn'), _dbg=False):
    f = lambda a: np.ascontiguousarray(np.asarray(a, dtype=np.float32))
    key = (_depth, tuple(_phases), _dbg)
    if key not in _CACHE:
        rec = build(_depth, _phases, _dbg is True)[3]
        _CACHE[key] = build(_depth, _phases, _dbg is True, rec)
    nc, T, dumps, _ = _CACHE[key]
    dp = _depth
    consts = _consts()
    rC, rS = _rope_tables()
    shared = dict(consts)
    shared.update(
        w_mod=f(w_mod[:dp]), b_mod=f(b_mod).reshape(L, 48, 128), w_in=f(w_in[:dp]), da_lambda=f(da_lambda).reshape(1, L * 256),
        da_norm_g=f(da_norm_g).reshape(L, 1, 128), ml_conv_w=f(ml_conv_w).reshape(L, 24, 128),
        ml_conv_b=f(ml_conv_b).reshape(L, 8, 128), ml_gate_b=f(ml_gate_b).reshape(L, 16, 1),
        ml_norm_g=f(ml_norm_g).reshape(L, 1, 128), sg_norm_g=f(sg_norm_g).reshape(L, 1, 512), sg_w=f(sg_w),
        sg_b=f(sg_b).reshape(L, 1, 512), w_branch=f(w_branch[:dp]), w_out=f(w_out[:dp]), w_up=f(w_up[:dp]),
        ffn_conv_w=f(ffn_conv_w).reshape(L, 132, 128), ffn_conv_b=f(ffn_conv_b).reshape(L, 44, 128), w_down=f(w_down[:dp]),
        final_g=f(final_g).reshape(8, 128))
    x_prompt = f(x_prompt)
    x_sample = f(x_sample)
    in_maps = []
    for core in range(8):
        m = dict(shared)
        if core < 4:
            b = core
            m['xin'] = x_sample[b]
            m['cond'] = f(c)[b].reshape(8, 128)
            m['ck'] = f(cache_k)[b]
            m['cv'] = f(cache_v)[b]
            m['sCn'] = np.ascontiguousarray(np.concatenate([f(state_C)[b], f(state_n)[b][..., None]], axis=-1))
            m['sm'] = f(state_m)[b].reshape(L, 8, 1)
            m['c_ropeC'] = rC
            m['c_ropeS'] = rS
            m['c_maskb'] = np.zeros((128, 20), np.float32)
            lk = np.zeros((128, 2), np.float32)
            lk[:, 0] = 1.0
            m['c_link'] = lk
        else:
            j = core - 4
            m['xin'] = x_prompt[4 * j:4 * j + 4].reshape(1024, 1024)
            m['cond'] = f(c_ctx).reshape(8, 128)
            m['ck'] = np.zeros((L, 4, 256, 128), np.float32)
            m['cv'] = np.zeros((L, 4, 256, 128), np.float32)
            m['sCn'] = np.zeros((L, 2, 4, 128, 129), np.float32)
            m['sm'] = np.zeros((L, 8, 1), np.float32)
            m['c_ropeC'] = np.ones((128, 1024), np.float32)
            m['c_ropeS'] = np.zeros((128, 1024), np.float32)
            mb = np.full((5, 4), -30000.0, np.float32)
            for qt in range(4):
                mb[1 + qt, qt] = 0.0
            m['c_maskb'] = np.tile(mb.reshape(1, 20), (128, 1))
            lk = np.zeros((128, 2), np.float32)
            lk[:, 1] = 1.0
            m['c_link'] = lk
        in_maps.append(m)
    if _dbg == 'maps':
        return nc, in_maps
    if _dbg == 'time':
        res = run_bass_kernel_spmd(nc, in_maps, core_ids=list(range(8)), trace=True)
        return res.exec_time_ns
    res = run_bass_kernel_spmd(nc, in_maps, core_ids=list(range(8)))
    R = res.results
    if _dbg:
        kernel.dbg = [{n: R[c][n] for n in dumps} for c in range(8)]
    y_sample = np.stack([R[b]['y_o'] for b in range(4)], axis=0)
    y_prompt = np.concatenate([R[4 + j]['y_o'].reshape(4, 256, 1024) for j in range(4)], axis=0)
    nk = np.concatenate([R[4 + j]['ok_o'].reshape(L, 4, 4, 256, 128).transpose(2, 0, 1, 3, 4) for j in range(4)], axis=0)
    nv = np.concatenate([R[4 + j]['ov_o'].reshape(L, 4, 4, 256, 128).transpose(2, 0, 1, 3, 4) for j in range(4)], axis=0)
    nC, nn, nm = [], [], []
    for j in range(4):
        oCn = R[4 + j]['oCn_o']
        oC = oCn[..., 0:128]
        on = oCn[..., 128]
        om = R[4 + j]['om_o'].reshape(L, 2, 4, 4)
        for sq in range(4):
            nC.append(np.stack([oC[:, 0, sq], oC[:, 1, 3 - sq]], axis=1))
            nn.append(np.stack([on[:, 0, sq], on[:, 1, 3 - sq]], axis=1))
            nm.append(np.stack([om[:, 0, :, sq], om[:, 1, :, 3 - sq]], axis=1))
    return (y_prompt.astype(np.float32), y_sample.astype(np.float32), np.ascontiguousarray(nk), np.ascontiguousarray(nv),
            np.stack(nC, axis=0), np.stack(nn, axis=0), np.stack(nm, axis=0))
```

```python
import math
from contextlib import ExitStack
import numpy as np
import concourse.bass as bass
import concourse.mybir as mybir
from concourse.bass_utils import run_bass_kernel_spmd

F32 = mybir.dt.float32
BF16 = mybir.dt.bfloat16
AF = mybir.ActivationFunctionType
ALU = mybir.AluOpType
AX = mybir.AxisListType

L = 4
NIN = 7696
DFF = 2816
EPS = 1e-6


class Tr:
    EP = 30000
    NSEM = 8

    def __init__(s, nc):
        s.nc = nc
        s.E = dict(pe=nc.tensor, act=nc.scalar, dve=nc.vector, pool=nc.gpsimd, sp=nc.sync)
        s.sems = {}
        s.cnt = {}
        s.known = {e: {} for e in s.E}
        s.res = {}
        s.grave = {}
        s.nwait = 0
        s.nops = 0
        s.dcount = {}
        s.dlast = {}

    def _sem(s, st, c):
        mult = 1 if st in s.E else 16
        ep = s.EP // mult
        e = (c - 1) // ep
        lst = s.sems.setdefault(st, [])
        while len(lst) <= e:
            nm = st if isinstance(st, str) else f"{st[0]}{st[1]}"
            lst.append(s.nc.alloc_semaphore(f"s_{nm}_{len(lst)}"))
        return lst[e], ((c - 1) % ep + 1) * mult

    def op(s, eng, fn, reads=(), writes=(), sig=True, dma=None):
        deps = {}

        def add(ev):
            if ev is None:
                return
            st = ev[0]
            if st == 'pe' and eng == 'pe' and dma is None:
                return
            if st not in deps or deps[st][1] < ev[1]:
                deps[st] = ev

        for k in reads:
            r = s.res.get(k)
            if r is None:
                continue
            add(r[0])
            if k[0] == 'ps':
                for ev in r[1].values():
                    add(ev)
        for k in writes:
            r = s.res.get(k)
            if r is None:
                continue
            add(r[0])
            for ev in r[1].values():
                add(ev)
        if dma is not None:
            i = s.dcount.get(dma, 0)
            s.dcount[dma] = i + 1
            dma = (dma, i % s.NSEM)
            add(s.dlast.get(dma))
        kn = s.known[eng]
        for st in sorted(deps, key=lambda a: -deps[a][1]):
            _, c, clk = deps[st]
            if kn.get(st, 0) >= c:
                continue
            if st == 'pe':
                assert s.cnt.get('pe', 0) >= c, "dependency on unsignalled PE op"
            sem, v = s._sem(st, c)
            s.E[eng].wait_ge(sem, v)
            s.nwait += 1
            kn[st] = c
            for a, b in clk.items():
                if kn.get(a, 0) < b:
                    kn[a] = b
        ins = fn()
        s.nops += 1
        st = dma or eng
        c = s.cnt.get(st, 0) + 1
        if sig:
            s.cnt[st] = c
            sem, v = s._sem(st, c)
            ins.then_inc(sem, 16 if dma else 1)
        clk = dict(kn)
        clk[st] = c
        ev = (st, c, clk)
        if dma is not None:
            s.dlast[dma] = ev
        for k in writes:
            s.res[k] = [ev, {}]
        for k in reads:
            r = s.res.setdefault(k, [None, {}])
            r[1][st] = ev
        return ins

    def free(s, keys):
        for k in keys:
            r = s.res.pop(k, None)
            if r is None:
                continue
            evs = list(r[1].values())
            if r[0] is not None:
                evs.append(r[0])
            for ev in evs:
                st = ev[0]
                if st not in s.grave or s.grave[st][1] < ev[1]:
                    s.grave[st] = ev

    def adopt(s, keys):
        for k in keys:
            s.res[k] = [None, dict(s.grave)]

    def finish(s, eng='sp'):
        for st, c in s.cnt.items():
            if st in s.E:
                continue
            ep = s.EP // 16
            for e in range((c - 1) // ep + 1 if c > 0 else 0):
                last = min(c, (e + 1) * ep)
                sem, v = s._sem(st, last)
                s.E[eng].wait_ge(sem, v)


def build(depth=L, phases=('ml', 'da', 'sg', 'ffn'), dbg=False, wsched_n=None):
    nc = bass.Bass("TRN2", target_bir_lowering=False)
    T = Tr(nc)
    LW = depth
    dumps = []
    wrec = []
    wissued = [0]
    LOOK = 2

    def din(n, sh):
        return nc.dram_tensor(n, list(sh), F32, kind="ExternalInput").ap()

    def dout(n, sh):
        return nc.dram_tensor(n, list(sh), F32, kind="ExternalOutput").ap()

    xin = din("xin", [1024, 1024])
    cond = din("cond", [8, 128])
    ck = din("ck", [L, 4, 256, 128])
    cv = din("cv", [L, 4, 256, 128])
    sCn = din("sCn", [L, 2, 4, 128, 129])
    sm = din("sm", [L, 8, 1])
    c_ident = din("c_ident", [128, 128])
    c_prot = din("c_prot", [128, 128])
    c_maskF = din("c_maskF", [128, 128])
    c_maskB = din("c_maskB", [128, 128])
    c_sel = din("c_sel", [8, 1024])
    c_dirm = din("c_dirm", [8, 2])
    c_ropeC = din("c_ropeC", [128, 1024])
    c_ropeS = din("c_ropeS", [128, 1024])
    c_maskb = din("c_maskb", [128, 20])
    c_link = din("c_link", [128, 2])
    w_mod = din("w_mod", [LW, 1024, 6144])
    b_mod = din("b_mod", [L, 48, 128])
    w_in = din("w_in", [LW, 1024, NIN])
    da_lambda = din("da_lambda", [1, L * 256])
    da_norm_g = din("da_norm_g", [L, 1, 128])
    ml_conv_w = din("ml_conv_w", [L, 24, 128])
    ml_conv_b = din("ml_conv_b", [L, 8, 128])
    ml_gate_b = din("ml_gate_b", [L, 16, 1])
    ml_norm_g = din("ml_norm_g", [L, 1, 128])
    sg_norm_g = din("sg_norm_g", [L, 1, 512])
    sg_w = din("sg_w", [L, 4, 128, 128])
    sg_b = din("sg_b", [L, 1, 512])
    w_branch = din("w_branch", [LW, 3, 512, 1024])
    w_out = din("w_out", [LW, 1024, 1024])
    w_up = din("w_up", [LW, 1024, 2 * DFF])
    ffn_conv_w = din("ffn_conv_w", [L, 132, 128])
    ffn_conv_b = din("ffn_conv_b", [L, 44, 128])
    w_down = din("w_down", [LW, DFF, 1024])
    final_g = din("final_g", [8, 128])

    y_o = dout("y_o", [1024, 1024])
    ok_o = dout("ok_o", [L, 4, 1024, 128])
    ov_o = dout("ov_o", [L, 4, 1024, 128])
    oCn_o = dout("oCn_o", [L, 2, 4, 4, 128, 129])
    om_o = dout("om_o", [L, 8, 4])

    WTinit = dict(w_in=w_in, w_mod=w_mod, w_up=w_up, w_down=w_down, w_out=w_out, w_branch=w_branch)
    def veng(e):
        return nc.vector if e == 'dve' else nc.gpsimd

    def ACT(out, in_, func, r, w, bias=None, scale=None):
        kw = {}
        if bias is not None:
            kw['bias'] = bias
        if scale is not None:
            kw['scale'] = scale
        return T.op('act', lambda: nc.scalar.activation(out=out, in_=in_, func=func, **kw), r, w)

    def TT(e, out, a, b, op, r, w):
        return T.op(e, lambda: veng(e).tensor_tensor(out=out, in0=a, in1=b, op=op), r, w)

    def TS(e, out, a, s1, s2, op0, op1, r, w):
        if op1 is None:
            return T.op(e, lambda: veng(e).tensor_scalar(out=out, in0=a, scalar1=s1, scalar2=None, op0=op0), r, w)
        return T.op(e, lambda: veng(e).tensor_scalar(out=out, in0=a, scalar1=s1, scalar2=s2, op0=op0, op1=op1), r, w)

    def STT(out, a, s, b, op0, op1, r, w):
        return T.op('dve', lambda: nc.vector.scalar_tensor_tensor(out=out, in0=a, scalar=s, in1=b, op0=op0, op1=op1), r, w)

    def CP(e, out, in_, r, w):
        if e == 'act':
            return T.op('act', lambda: nc.scalar.copy(out=out, in_=in_), r, w)
        return T.op(e, lambda: veng(e).tensor_copy(out=out, in_=in_), r, w)

    def MM(out, lhsT, rhs, start, stop, r, w, sig=None):
        if sig is None:
            sig = stop
        return T.op('pe', lambda: nc.tensor.matmul(out, lhsT=lhsT, rhs=rhs, start=start, stop=stop), r, w, sig=sig)

    def TR(out, in_, ident, r, w):
        return T.op('pe', lambda: nc.tensor.transpose(out, in_, ident), r, w)

    def DMA(e, out, in_, r, w, st):
        eng = {'sp': nc.sync, 'pool': nc.gpsimd, 'act': nc.scalar}[e]
        if e == 'pool':
            st = 'dw'
        return T.op(e, lambda: eng.dma_start(out=out, in_=in_), r, w, dma=st)

    def dump(name, ap, keys):
        if not dbg:
            return
        o = dout("dbg_" + name, list(ap.shape))
        dumps.append("dbg_" + name)
        DMA('pool' if ap.dtype != F32 else 'sp', o, ap, keys, (), 'do')

    def MS(e, ap, val, w):
        return T.op(e, lambda: veng(e).memset(ap, val), (), w)

    def sb(n, sh, dt=F32):
        return nc.alloc_sbuf_tensor(n, list(sh), dt)

    uniq = [0]

    def sbt(n, sh, dt=F32):
        uniq[0] += 1
        return nc.sbuf_tensor(f"{n}_u{uniq[0]}", list(sh), dt)

    xT = sb("xT", [128, 8, 1024])
    hT = sb("hT", [128, 8, 1024], BF16)
    NW = 4
    wpool = [sb(f"wp{i}", [128, 4096], BF16) for i in range(NW)]
    wstate = [0]
    ident_f = sb("ident_f", [128, 128])
    ident_b = sb("ident_b", [128, 128], BF16)
    prot_b = sb("prot_b", [128, 128], BF16)
    ones_b = sb("ones_b", [128, 128], BF16)
    maskF = sb("maskF", [128, 128])
    maskB = sb("maskB", [128, 128])
    sel = sb("sel", [8, 1024])
    dirm = sb("dirm", [8, 2])
    ropeC = sb("ropeC", [128, 1024])
    ropeS = sb("ropeS", [128, 1024])
    maskb = sb("maskb", [128, 20])
    link = sb("link", [128, 2])
    eps_t = sb("eps_t", [128, 1])
    one_t = sb("one_t", [128, 1])
    mhalf = sb("mhalf", [128, 4])
    CPm = sb("CPm", [128, L, 3, 128])
    gcol = sb("gcol", [128, 16])
    cond_b = sb("cond_b", [128, 8], BF16)
    modT = sb("modT", [128, 2, 48])
    scp = sb("scp", [128, 2, 16])
    lam = sb("lam", [128, L])
    nlam = sb("nlam", [128, L])
    GB = sb("GB", [8, L, 2])
    rstd = sb("rstd", [128, 1024])
    FS = [sb(f"fs{i}", [128, 1024]) for i in range(4)]
    fstate = [0]
    ones8 = sb("ones8", [8, 128])
    w0n = sb("w0n", [128, 2, 52])

    PA = nc.alloc_psum_tensor("PA", [128, 1024], F32)
    PB = nc.alloc_psum_tensor("PB", [128, 1024], F32)
    PC = nc.alloc_psum_tensor("PC", [128, 1024], F32)
    PD = nc.alloc_psum_tensor("PD", [128, 512], F32)
    PT = nc.alloc_psum_tensor("PT", [128, 1024], BF16)
    P2 = [PA, PB, PC]
    p2state = [0]

    def bank(i):
        if i < 6:
            return P2[i // 2][:, (i % 2) * 512:(i % 2 + 1) * 512]
        return PD[:, :]

    def ps(i):
        return ('ps', i)

    def fs_next():
        i = fstate[0] % 4
        fstate[0] += 1
        return FS[i], ('fs', i)

    def p2_next():
        i = p2state[0] % 3
        p2state[0] += 1
        return P2[i], [ps(2 * i), ps(2 * i + 1)]

    WT = WTinit

    def wmk(desc):
        (name, idx), r0, nk, c0, ncol = desc
        t = WT[name]
        for i in idx:
            t = t[i]
        return t[r0:r0 + nk * 128, c0:c0 + ncol].rearrange("(k p) n -> p k n", p=128)

    def w_issue(j):
        desc = wsched_n[j]
        nk, ncol = desc[2], desc[4]
        i = j % NW
        dst = wpool[i][:, 0:nk * ncol].rearrange("p (k n) -> p k n", k=nk)
        DMA('pool', dst, wmk(desc), (), [('w', i)], 'dw')

    def wload(w2d, r0, nk, c0, ncol):
        desc = (w2d, r0, nk, c0, ncol)
        k = wstate[0]
        wstate[0] += 1
        i = k % NW
        dst = wpool[i][:, 0:nk * ncol].rearrange("p (k n) -> p k n", k=nk)
        if wsched_n is None:
            wrec.append(desc)
            DMA('pool', dst, wmk(desc), (), [('w', i)], 'dw')
        else:
            assert wsched_n[k] == desc
            while wissued[0] <= min(k + LOOK, len(wsched_n) - 1):
                w_issue(wissued[0])
                wissued[0] += 1
        return dst, ('w', i)

    def wview(w2d, r0, nk, c0, ncol):
        return w2d[r0:r0 + nk * 128, c0:c0 + ncol].rearrange("(k p) n -> p k n", p=128)

    DMA('sp', ident_f[:], c_ident, (), ['ident_f'], 'di')
    DMA('pool', ident_b[:], c_ident, (), ['ident_b'], 'dw')
    DMA('pool', prot_b[:], c_prot, (), ['prot_b'], 'dw')
    DMA('sp', maskF[:], c_maskF, (), ['maskF'], 'di')
    DMA('sp', maskB[:], c_maskB, (), ['maskB'], 'di')
    DMA('sp', sel[:], c_sel, (), ['sel'], 'di')
    DMA('sp', dirm[:], c_dirm, (), ['dirm'], 'di')
    DMA('sp', ropeC[:], c_ropeC, (), ['ropeC'], 'di')
    DMA('sp', ropeS[:], c_ropeS, (), ['ropeS'], 'di')
    DMA('sp', maskb[:], c_maskb, (), ['maskb'], 'di')
    DMA('sp', link[:], c_link, (), ['link'], 'di')
    MS('dve', ones_b[:], 1.0, ['ones_b'])
    MS('dve', eps_t[:], EPS, ['eps_t'])
    MS('dve', one_t[:], 1.0, ['one_t'])
    MS('dve', mhalf[:], -0.5, ['mhalf'])
    MS('dve', ones8[:], 1.0, ['ones8'])
    for l in range(L):
        DMA('sp', GB[:, l, 0:1], ml_gate_b[l, 0:8, :], (), ['GB'], 'di')
        DMA('sp', GB[:, l, 1:2], ml_gate_b[l, 8:16, :], (), ['GB'], 'di')

    def colparams(rows_list, dst, dkey):
        st, sk = fs_next()
        r0 = 0
        for ap, R in rows_list:
            DMA('sp', st[r0:r0 + R, 0:128], ap, (), [sk], 'di')
            r0 += R
        TR(bank(6)[:, 0:r0], st[0:r0, 0:128], ident_f[0:r0, 0:r0], [sk, 'ident_f'], [ps(6)])
        CP('dve', dst[:, 0:r0], bank(6)[:, 0:r0], [ps(6)], [dkey])

    colparams([(final_g, 8), (cond, 8)], gcol[:, :], 'gcol')
    ACT(cond_b[:], gcol[:, 8:16], AF.Silu, ['gcol'], ['cond_b'])

    for tc in range(8):
        st, sk = fs_next()
        DMA('sp', st[:, :], xin[tc * 128:(tc + 1) * 128, :], (), [sk], 'di')
        for half in range(2):
            bk = half
            for j in range(4):
                dc = half * 4 + j
                TR(bank(bk)[:, j * 128:(j + 1) * 128], st[:, dc * 128:(dc + 1) * 128], ident_f[:], [sk, 'ident_f'], [ps(bk)])
            CP('dve' if half == 0 else 'act', xT[:, half * 4:half * 4 + 4, tc * 128:(tc + 1) * 128],
               bank(bk).rearrange("p (a b) -> p a b", a=4), [ps(bk)], [('x', half * 4 + j) for j in range(4)])

    for l in range(depth):
        colparams([(b_mod[l], 48), (ml_conv_w[l], 24), (ml_conv_b[l], 8), (ffn_conv_b[l], 44), (da_norm_g[l], 1)],
                  CPm[:, l, 0, :], 'CPm')
        colparams([(ffn_conv_w[l, 0:128, :], 128)], CPm[:, l, 1, :], 'CPm')
        colparams([(ffn_conv_w[l, 128:132, :], 4)], CPm[:, l, 2, :], 'CPm')

    def bmod_c(l):
        return CPm[:, l, 0, 0:48]

    def mlw_c(l, k, c):
        return CPm[:, l, 0, 48 + k * 8 + c:48 + k * 8 + c + 1]

    def mlb_c(l, c):
        return CPm[:, l, 0, 72 + c:73 + c]

    def ffb_c(l, c):
        return CPm[:, l, 0, 80 + c:81 + c]

    def dag_c(l):
        return CPm[:, l, 0, 124:125]

    def ffw_c(l, k, c):
        j = k * 44 + c
        if j < 128:
            return CPm[:, l, 1, j:j + 1]
        return CPm[:, l, 2, j - 128:j - 127]

    with ExitStack() as es:
        dl = es.enter_context(sbt("dl", [128, L * 256], F32))
        pr = es.enter_context(sbt("pr", [128, L * 128], F32))
        sm2 = es.enter_context(sbt("sm2", [128, L * 2], F32))
        DMA('sp', dl[:], da_lambda.partition_broadcast(128), (), ['dl'], 'di')
        dlv = dl[:].rearrange("p (l a b d) -> p l a b d", l=L, a=2, b=2)
        TT('dve', pr[:].rearrange("p (l a d) -> p l a d", l=L, a=2), dlv[:, :, :, 0, :], dlv[:, :, :, 1, :], ALU.mult,
           ['dl'], ['pr'])
        T.op('dve', lambda: nc.vector.tensor_reduce(out=sm2[:], in_=pr[:].rearrange("p (q d) -> p q d", d=64),
                                                     axis=AX.X, op=ALU.add), ['pr'], ['sm2'])
        ACT(sm2[:], sm2[:], AF.Exp, ['sm2'], ['sm2'])
        s2v = sm2[:].rearrange("p (l a) -> p l a", a=2)
        TT('dve', lam[:], s2v[:, :, 0], s2v[:, :, 1], ALU.subtract, ['sm2'], ['lam'])
        for l in range(L):
            li = 0.8 - 0.6 * math.exp(-0.3 * l)
            TS('dve', lam[:, l:l + 1], lam[:, l:l + 1], li, None, ALU.add, None, ['lam'], ['lam'])
        TS('dve', nlam[:], lam[:], -1.0, None, ALU.mult, None, ['lam'], ['nlam'])
        T.free(['dl', 'pr', 'sm2'])

    dump('xT0', xT[:, :, :], [('x', dc) for dc in range(8)])
    def compute_mod_gen(l):
        par = l % 2
        for g in range(12):
            w, wk = wload(('w_mod', (l,)), 0, 8, g * 512, 512)
            for j in range(4):
                col = g * 4 + j
                for kc in range(8):
                    MM(bank(6)[:, col:col + 1], w[:, kc, j * 128:(j + 1) * 128], cond_b[:, kc:kc + 1], kc == 0, kc == 7,
                       [wk, 'cond_b'], [ps(6)])
            yield
        TT('dve', modT[:, par, :], bank(6)[:, 0:48], bmod_c(l), ALU.add, [ps(6), 'CPm'], [('mod', par)])
        TS('dve', scp[:, par, 0:8], modT[:, par, 8:16], 1.0, None, ALU.add, None, [('mod', par)], [('scp', par)])
        TS('dve', scp[:, par, 8:16], modT[:, par, 32:40], 1.0, None, ALU.add, None, [('mod', par)], [('scp', par)])

    def compute_mod(l):
        for _ in compute_mod_gen(l):
            pass


    def rms_stats(src_chunks, rkeys, sq_dst, sqkeys, nfeat):
        n = len(src_chunks)
        for i in range(n):
            ACT(sq_dst[i], src_chunks[i], AF.Square, [rkeys[i]], [sqkeys[i]])
        for th in range(2):
            for i in range(n):
                MM(PA[:, th * 512:(th + 1) * 512], ones_b[:], sq_dst[i][:, th * 512:(th + 1) * 512], i == 0, i == n - 1,
                   [sqkeys[i], 'ones_b'], [ps(th)])
        ACT(rstd[:], PA[:, :], AF.Ln, [ps(0), ps(1), 'eps_t'], ['rstd'], bias=eps_t[:], scale=1.0 / nfeat)
        ACT(rstd[:], rstd[:], AF.Exp, ['rstd'], ['rstd'], scale=-0.5)

    def norm_mod(l, which):
        par = l % 2
        rms_stats([xT[:, dc, :] for dc in range(8)], [('x', dc) for dc in range(8)],
                  [hT[:, dc, :] for dc in range(8)], [('h', dc) for dc in range(8)], 1024.0)
        so = 0 if which == 0 else 24
        for dc in range(8):
            tmp, tk = fs_next()
            STT(tmp[:], xT[:, dc, :], scp[:, par, which * 8 + dc:which * 8 + dc + 1], rstd[:], ALU.mult, ALU.mult,
                [('x', dc), ('scp', par), 'rstd'], [tk])
            ACT(hT[:, dc, :], tmp[:], AF.Identity, [tk, ('mod', par)], [('h', dc)], bias=modT[:, par, so + dc:so + dc + 1])

    def dwconv(l, zps, zkeys, acc, akey, w0, w1, w2, bia, w0nn, w2nn):
        ACT(acc[:, :], zps[:, :], AF.Identity, zkeys + ['CPm'], [akey], bias=bia, scale=w1)
        STT(acc[:, 1:1024], zps[:, 0:1023], w0, acc[:, 1:1024], ALU.mult, ALU.add, zkeys + ['CPm', akey], [akey])
        STT(acc[:, 0:1023], zps[:, 1:1024], w2, acc[:, 0:1023], ALU.mult, ALU.add, zkeys + ['CPm', akey], [akey])
        STT(acc[:, 256:1024:256], zps[:, 255:1023:256], w0nn, acc[:, 256:1024:256], ALU.mult, ALU.add,
            zkeys + [('w0n', l % 2), akey], [akey])
        STT(acc[:, 255:1023:256], zps[:, 256:1024:256], w2nn, acc[:, 255:1023:256], ALU.mult, ALU.add,
            zkeys + [('w0n', l % 2), akey], [akey])

    def prep_w0n(l):
        par = l % 2
        TS('pool', w0n[:, par, 0:44], CPm[:, l, 1, 0:44], link[:, 1:2], -1.0, ALU.mult, ALU.mult, ['CPm', 'link'], [('w0n', par)])
        TS('pool', w2n[:, par, 0:40], CPm[:, l, 1, 88:128], link[:, 1:2], -1.0, ALU.mult, ALU.mult, ['CPm', 'link'], [('w0n', par)])
        TS('pool', w2n[:, par, 40:44], CPm[:, l, 2, 0:4], link[:, 1:2], -1.0, ALU.mult, ALU.mult, ['CPm', 'link'], [('w0n', par)])
        TS('pool', w0n[:, par, 44:52], CPm[:, l, 0, 48:56], link[:, 1:2], -1.0, ALU.mult, ALU.mult, ['CPm', 'link'], [('w0n', par)])
        TS('pool', w2n[:, par, 44:52], CPm[:, l, 0, 64:72], link[:, 1:2], -1.0, ALU.mult, ALU.mult, ['CPm', 'link'], [('w0n', par)])

    w2n = sb("w2n", [128, 2, 52])

    def ffn(l):
        par = l % 2
        with ExitStack() as es:
            act = es.enter_context(sbt("ffn_act", [128, 22, 1024], BF16))
            sa = es.enter_context(sbt("ffn_sa", [128, 4, 1024], F32))
            T.adopt([('act', j) for j in range(22)] + [('sa', j) for j in range(4)])
            norm_mod(l, 1)
            if l == 0:
                dump('mod0', modT[:, 0, :], [('mod', 0)])
                dump('h2', hT[:, :, :], [('h', dc) for dc in range(8)])
                dump('rstd', rstd[:, :], ['rstd'])
            for g in range(6):
                nchunk = 4 if g < 5 else 2
                for ab in range(2):
                    c0 = ab * DFF + g * 512
                    w, wk = wload(('w_up', (l,)), 0, 8, c0, nchunk * 128)
                    for j in range(nchunk):
                        cc = ab * 22 + g * 4 + j
                        zp, zk = p2_next()
                        for th in range(2):
                            for kc in range(8):
                                MM(zp[:, th * 512:(th + 1) * 512], w[:, kc, j * 128:(j + 1) * 128],
                                   hT[:, kc, th * 512:(th + 1) * 512], kc == 0, kc == 7, [wk, ('h', kc)], [zk[th]])
                        acc, ak = fs_next()
                        dwconv(l, zp, zk, acc, ak, ffw_c(l, 0, cc), ffw_c(l, 1, cc), ffw_c(l, 2, cc), ffb_c(l, cc),
                               w0n[:, par, cc:cc + 1],
                               w2n[:, par, cc:cc + 1])
                        if ab == 0:
                            ACT(sa[:, j, :], acc[:, :], AF.Silu, [ak], [('sa', j)])
                        else:
                            TT('pool', act[:, g * 4 + j, :], sa[:, j, :], acc[:, :], ALU.mult, [('sa', j), ak],
                               [('act', g * 4 + j)])
            if l == 0:
                dump('act', act[:, :, :], [('act', j) for j in range(22)])
            for jp in range(4):
                wA, wkA = wload(('w_down', (l,)), 0, 11, jp * 256, 256)
                wB, wkB = wload(('w_down', (l,)), 1408, 11, jp * 256, 256)
                for half, (w, wk) in enumerate(((wA, wkA), (wB, wkB))):
                    for dj in range(2):
                        for th in range(2):
                            bk = dj * 2 + th
                            for kk in range(11):
                                kc = half * 11 + kk
                                MM(bank(bk), w[:, kk, dj * 128:(dj + 1) * 128], act[:, kc, th * 512:(th + 1) * 512],
                                   half == 0 and kk == 0, half == 1 and kk == 10, [wk, ('act', kc)], [ps(bk)])
                for dj in range(2):
                    dc = jp * 2 + dj
                    for th in range(2):
                        bk = dj * 2 + th
                        STT(xT[:, dc, th * 512:(th + 1) * 512], bank(bk), modT[:, par, 40 + dc:41 + dc],
                            xT[:, dc, th * 512:(th + 1) * 512], ALU.mult, ALU.add, [ps(bk), ('mod', par), ('x', dc)],
                            [('x', dc)])
            T.free([('act', j) for j in range(22)] + [('sa', j) for j in range(4)])


    def run_pipeline(jobs, make_gen, nslots, extra=None, extra_every=1):
        active = []
        free_slots = list(range(nslots))
        nxt = 0
        step = 0
        while nxt < len(jobs) or active:
            if nxt < len(jobs) and free_slots:
                sl_ = free_slots.pop(0)
                g = make_gen(jobs[nxt], sl_)
                nxt += 1
                next(g)
                active.append((g, sl_, True))
            still = []
            for (g, sl_, fresh) in active:
                if fresh:
                    still.append((g, sl_, False))
                    continue
                try:
                    next(g)
                    still.append((g, sl_, False))
                except StopIteration:
                    free_slots.append(sl_)
            active = still
            step += 1
            if extra is not None and extra[0] is not None and step % extra_every == 0:
                try:
                    next(extra[0])
                except StopIteration:
                    extra[0] = None
        if extra is not None and extra[0] is not None:
            for _ in extra[0]:
                pass

    def sg_phase(l, ysgT):
        wl = w_in[l]
        with ExitStack() as es:
            uT = es.enter_context(sbt("sg_uT", [128, 4, 1024], BF16))
            sgwT = es.enter_context(sbt("sg_wT", [128, 4, 128], BF16))
            sgwf = es.enter_context(sbt("sg_wf", [128, 4, 128], F32))
            gb = es.enter_context(sbt("sg_gb", [128, 2, 512], F32))
            zz = es.enter_context(sbt("sg_zz", [128, 4, 512], F32))
            svb = es.enter_context(sbt("sg_svb", [128, 4, 512], BF16))
            st6 = es.enter_context(sbt("sg_st", [128, 4, 8], F32))
            keys = [('sg_u', j) for j in range(4)] + ['sg_wT', 'sg_wf', 'sg_gb'] + [(n_, b_) for n_ in ('sg_zz', 'sg_svb', 'sg_st') for b_ in range(4)]
            T.adopt(keys)
            DMA('sp', gb[:, 0, :], sg_norm_g[l].partition_broadcast(128), (), ['sg_gb'], 'di')
            DMA('sp', gb[:, 1, :], sg_b[l].partition_broadcast(128), (), ['sg_gb'], 'di')
            DMA('sp', sgwf[:, :, :], sg_w[l].rearrange("g p q -> p g q"), (), ['sg_wf'], 'di')
            for g in range(4):
                TR(bank(6)[:, g * 128:(g + 1) * 128], sgwf[:, g, :], ident_f[:], ['sg_wf', 'ident_f'], [ps(6)])
            CP('dve', sgwT[:, :, :], bank(6).rearrange("p (g q) -> p g q", g=4), [ps(6)], ['sg_wT'])
            w, wk = wload(('w_in', (l,)), 0, 8, 3600, 512)
            for j in range(4):
                zp, zk = p2_next()
                for th in range(2):
                    for kc in range(8):
                        MM(zp[:, th * 512:(th + 1) * 512], w[:, kc, j * 128:(j + 1) * 128], hT[:, kc, th * 512:(th + 1) * 512],
                           kc == 0, kc == 7, [wk, ('h', kc)], [zk[th]])
                ACT(uT[:, j, :], zp[:, :], AF.Gelu_apprx_tanh, zk, [('sg_u', j)])
            w, wk = wload(('w_in', (l,)), 0, 8, 4112, 512)

            def sg_job(tc, b):
                bk = b
                gbk = 4 + b % 2
                for kc in range(8):
                    MM(bank(bk), hT[:, kc, tc * 128:(tc + 1) * 128], w[:, kc, :], kc == 0, kc == 7, [('h', kc), wk], [ps(bk)])
                yield
                ACT(zz[:, b, :], bank(bk), AF.Gelu_apprx_tanh, [ps(bk)], [('sg_zz', b)])
                yield
                T.op('dve', lambda: nc.vector.bn_stats(out=st6[:, b, 0:6], in_=zz[:, b, :]), [('sg_zz', b)], [('sg_st', b)])
                T.op('dve', lambda: nc.vector.bn_aggr(out=st6[:, b, 6:8], in_=st6[:, b, 0:6]), [('sg_st', b)], [('sg_st', b)])
                yield
                TS('pool', st6[:, b, 7:8], st6[:, b, 7:8], EPS, None, ALU.add, None, [('sg_st', b)], [('sg_st', b)])
                TT('pool', st6[:, b, 7:8], st6[:, b, 7:8], mhalf[:, 0:1], ALU.pow, [('sg_st', b), 'mhalf'], [('sg_st', b)])
                yield
                TS('dve', zz[:, b, :], zz[:, b, :], st6[:, b, 6:7], st6[:, b, 7:8], ALU.subtract, ALU.mult,
                   [('sg_zz', b), ('sg_st', b)], [('sg_zz', b)])
                yield
                TT('pool', svb[:, b, :], zz[:, b, :], gb[:, 0, :], ALU.mult, [('sg_zz', b), 'sg_gb'], [('sg_svb', b)])
                yield
                for g in range(4):
                    MM(bank(gbk)[:, g * 128:(g + 1) * 128], svb[:, b, g * 128:(g + 1) * 128], sgwT[:, g, :], True, True,
                       [('sg_svb', b), 'sg_wT'], [ps(gbk)])
                yield
                tmp, tk = fs_next()
                TT('dve', tmp[:, 0:512], bank(gbk), gb[:, 1, :], ALU.add, [ps(gbk), 'sg_gb'], [tk])
                yield
                TT('pool', ysgT[:, :, tc * 128:(tc + 1) * 128], tmp[:, 0:512].rearrange("p (g t) -> p g t", g=4),
                   uT[:, :, tc * 128:(tc + 1) * 128], ALU.mult, [tk] + [('sg_u', j) for j in range(4)],
                   [('ysg', j) for j in range(4)])

            run_pipeline(list(range(8)), sg_job, 4)
            T.free(keys)

    def merge(l, ys):
        par = l % 2
        ynames = ['yda', 'yml', 'ysg']
        with ExitStack() as es:
            mg = es.enter_context(sbt("mg", [128, 8, 1024], BF16))
            accf = es.enter_context(sbt("mg_acc", [128, 4, 1024], F32))
            keys = [('mg', dc) for dc in range(8)] + [('mg_acc', j) for j in range(4)]
            T.adopt(keys)
            for dcg in range(2):
                for n in range(3):
                    w, wk = wload(('w_in', (l,)), 0, 8, 4624 + n * 1024 + dcg * 512, 512)
                    wb, wbk = wload(('w_branch', (l, n)), 0, 4, 0, 1024)
                    for j in range(4):
                        dc = dcg * 4 + j
                        gp, gk = p2_next()
                        for th in range(2):
                            for kc in range(8):
                                MM(gp[:, th * 512:(th + 1) * 512], w[:, kc, j * 128:(j + 1) * 128],
                                   hT[:, kc, th * 512:(th + 1) * 512], kc == 0, kc == 7, [wk, ('h', kc)], [gk[th]])
                        pp, pk = p2_next()
                        for th in range(2):
                            for kc in range(4):
                                MM(pp[:, th * 512:(th + 1) * 512], wb[:, kc, dc * 128:(dc + 1) * 128],
                                   ys[n][:, kc, th * 512:(th + 1) * 512], kc == 0, kc == 3, [wbk, (ynames[n], kc)], [pk[th]])
                        sgt, sk = fs_next()
                        ACT(sgt[:, :], gp[:, :], AF.Sigmoid, gk, [sk])
                        if n == 0:
                            TT('dve', accf[:, j, :], pp[:, :], sgt[:, :], ALU.mult, pk + [sk], [('mg_acc', j)])
                        else:
                            t2, t2k = fs_next()
                            TT('dve', t2[:, :], pp[:, :], sgt[:, :], ALU.mult, pk + [sk], [t2k])
                            if n == 1:
                                TT('pool', accf[:, j, :], accf[:, j, :], t2[:, :], ALU.add, [('mg_acc', j), t2k], [('mg_acc', j)])
                            else:
                                TT('pool', mg[:, dc, :], accf[:, j, :], t2[:, :], ALU.add, [('mg_acc', j), t2k], [('mg', dc)])
            for og_ in range(2):
                w, wk = wload(('w_out', (l,)), 0, 8, og_ * 512, 512)
                for j in range(4):
                    dc = og_ * 4 + j
                    zp, zk = p2_next()
                    for th in range(2):
                        for kc in range(8):
                            MM(zp[:, th * 512:(th + 1) * 512], w[:, kc, j * 128:(j + 1) * 128], mg[:, kc, th * 512:(th + 1) * 512],
                               kc == 0, kc == 7, [wk, ('mg', kc)], [zk[th]])
                    for th in range(2):
                        STT(xT[:, dc, th * 512:(th + 1) * 512], zp[:, th * 512:(th + 1) * 512], modT[:, par, 16 + dc:17 + dc],
                            xT[:, dc, th * 512:(th + 1) * 512], ALU.mult, ALU.add, [zk[th], ('mod', par), ('x', dc)], [('x', dc)])
            T.free(keys)

    def da_phase(l, ydaT):
        wl = w_in[l]
        with ExitStack() as es:
            KT = es.enter_context(sbt("da_KT", [128, 4, 1280], BF16))
            qT = es.enter_context(sbt("da_qT", [128, 4, 1024], BF16))
            V = es.enter_context(sbt("da_V", [128, 10, 512], BF16))
            oall = es.enter_context(sbt("da_oall", [128, 4, 1024], F32))
            ET = es.enter_context(sbt("da_ET", [128, 3, 512], BF16))
            kbf = es.enter_context(sbt("da_kbf", [128, 2, 1024], BF16))
            ckf = es.enter_context(sbt("da_ckf", [128, 2, 128], F32))
            osb = es.enter_context(sbt("da_osb", [128, 2, 256], F32))
            rs = es.enter_context(sbt("da_rs", [128, 2, 256], F32))
            vst = es.enter_context(sbt("da_vst", [128, 2, 512], F32))
            dagl = es.enter_context(sbt("da_gl", [128, 1], F32))
            keys = ([('KT', h) for h in range(4)] + [('qT', h) for h in range(4)] + [('V', c) for c in range(10)] +
                    [('oall', h) for h in range(4)] + [('ET', i) for i in range(3)] + [('kbf', 0), ('kbf', 1), 'ckf', ('osb', 0), ('osb', 1),
                                                                                        ('rs', 0), ('rs', 1), ('vst', 0), ('vst', 1), 'dagl'])
            T.adopt(keys)
            TS('dve', dagl[:, :], dag_c(l), 1.0 - (0.8 - 0.6 * math.exp(-0.3 * l)), None, ALU.mult, None, ['CPm'], ['dagl'])
            for hd in range(4):
                DMA('pool', V[:, 0:2, hd * 128:(hd + 1) * 128], cv[l, hd].rearrange("(c p) d -> p c d", p=128), (),
                    [('V', 0), ('V', 1)], 'dw')
            vck = vst[:, :, :].rearrange("p a (h d) -> p (a h) d", h=2)
            for hd in range(4):
                DMA('sp', vck[:, hd, :].rearrange("p (c d) -> p c d", c=2), ck[l, hd].rearrange("(c p) d -> p c d", p=128), (),
                    [('vst', hd // 2)], 'di')
            for hp in range(2):
                for hh in range(2):
                    hd = hp * 2 + hh
                    for c in range(2):
                        TR(bank(6)[:, (hh * 2 + c) * 128:(hh * 2 + c + 1) * 128], vck[:, hd, c * 128:(c + 1) * 128], ident_f[:],
                           [('vst', hp), 'ident_f'], [ps(6)])
                CP('dve', KT[:, hp * 2:hp * 2 + 2, 0:256], bank(6).rearrange("p (h k) -> p h k", h=2), [ps(6)],
                   [('KT', hp * 2), ('KT', hp * 2 + 1)])
            wqk = {}

            def proj_job(job, slot):
                which, hd = job
                if hd == 0:
                    c0 = 512 if which == 0 else 0
                    wqk[which] = wload(('w_in', (l,)), 0, 8, c0, 512)
                w, wk = wqk[which]
                zp, zk = P2[slot], [ps(2 * slot), ps(2 * slot + 1)]
                for th in range(2):
                    for kc in range(8):
                        MM(zp[:, th * 512:(th + 1) * 512], w[:, kc, hd * 128:(hd + 1) * 128], hT[:, kc, th * 512:(th + 1) * 512],
                           kc == 0, kc == 7, [wk, ('h', kc)], [zk[th]])
                yield
                CP('act', kbf[:, slot, :], zp[:, :], zk, [('kbf', slot)])
                yield
                sp_, spk = PC, [ps(4), ps(5)]
                for th in range(2):
                    MM(sp_[:, th * 512:(th + 1) * 512], prot_b[:], kbf[:, slot, th * 512:(th + 1) * 512], True, True,
                       ['prot_b', ('kbf', slot)], [spk[th]])
                yield
                t1, t1k = fs_next()
                t2, t2k = fs_next()
                TT('dve', t1[:, :], zp[:, :], ropeC[:, :], ALU.mult, zk + ['ropeC'], [t1k])
                TT('dve', t2[:, :], sp_[:, :], ropeS[:, :], ALU.mult, spk + ['ropeS'], [t2k])
                yield
                if which == 0:
                    TT('pool', t1[:, :], t1[:, :], t2[:, :], ALU.add, [t1k, t2k], [t1k])
                    CP('pool', KT[:, hd, 256:1280], t1[:, :], [t1k], [('KT', hd)])
                    yield
                    for tcg in range(2):
                        for j in range(4):
                            tc = tcg * 4 + j
                            TR(bank(6)[:, j * 128:(j + 1) * 128], t1[:, tc * 128:(tc + 1) * 128], ident_f[:],
                               [t1k, 'ident_f'], [ps(6)])
                        CP('act', vst[:, tcg, :], bank(6), [ps(6)], [('vst', tcg)])
                        DMA('sp', ok_o[l, hd, tcg * 512:(tcg + 1) * 512, :].rearrange("(j p) d -> p j d", p=128),
                            vst[:, tcg, :].rearrange("p (j d) -> p j d", j=4), [('vst', tcg)], (), 'do')
                else:
                    TT('pool', qT[:, hd, :], t1[:, :], t2[:, :], ALU.add, [t1k, t2k], [('qT', hd)])

            run_pipeline([(which, hd) for which in range(2) for hd in range(4)], proj_job, 2)
            w, wk = wload(('w_in', (l,)), 0, 8, 1024, 512)
            for tc in range(8):
                bk = 4 + tc % 2
                for kc in range(8):
                    MM(bank(bk), hT[:, kc, tc * 128:(tc + 1) * 128], w[:, kc, :], kc == 0, kc == 7, [('h', kc), wk], [ps(bk)])
                CP('act', V[:, 2 + tc, :], bank(bk), [ps(bk)], [('V', 2 + tc)])
                CP('dve', vst[:, tc % 2, :], bank(bk), [ps(bk)], [('vst', tc % 2)])
                DMA('sp', ov_o[l, :, tc * 128:(tc + 1) * 128, :].rearrange("h p d -> p h d"),
                    vst[:, tc % 2, :].rearrange("p (h d) -> p h d", h=4), [('vst', tc % 2)], (), 'do')
            iters = [(hd, qt, kc) for hd in range(4) for qt in range(4) for kc in range(10)]
            rsf = rs[:, :, :].rearrange("p a b -> p (a b)")
            osf = osb[:, :, :].rearrange("p a b -> p (a b)")

            PT32 = PT[:, :].bitcast(F32)
            accb = [(bank(4), bank(5), ps(4), ps(5)), (bank(6), PT32, ps(6), ps(7))]
            SR = [(PA, ps(0), ps(1)), (PB, ps(2), ps(3))]

            def emit_qk(i):
                hd, qt, kc = iters[i]
                reg, k0, k1 = SR[i % 2]
                for half in range(2):
                    lo, hi = half * 64, half * 64 + 64
                    MM(reg[:, half * 512:half * 512 + 256], KT[lo:hi, hd, kc * 128:(kc + 1) * 128],
                       qT[lo:hi, hd, qt * 256:(qt + 1) * 256], True, True, [('KT', hd), ('qT', hd)], [k0 if half == 0 else k1])

            def emit_rest(i):
                hd, qt, kc = iters[i]
                reg, k0, k1 = SR[i % 2]
                sbk = i % 3
                ao, as_, ko, ks = accb[(hd * 4 + qt) % 2]
                ACT(ET[:, sbk, :].rearrange("p (a b) -> p a b", a=2), reg[:, :].rearrange("p (a b) -> p a b", a=2)[:, :, 0:256],
                    AF.Exp, [k0, k1, 'maskb'], [('ET', sbk)],
                    bias=maskb[:, (kc // 2) * 4 + qt:(kc // 2) * 4 + qt + 1], scale=0.125)
                MM(ao, V[:, kc, hd * 128:(hd + 1) * 128], ET[:, sbk, :], kc == 0, kc == 9, [('V', kc), ('ET', sbk)], [ko])
                MM(as_, ones_b[:], ET[:, sbk, :], kc == 0, kc == 9, ['ones_b', ('ET', sbk)], [ks])
                if kc == 9:
                    T.op('dve', lambda: nc.vector.reciprocal(out=rsf, in_=as_), [ks], [('rs', 0), ('rs', 1)])
                    TT('dve', osf, ao, rsf, ALU.mult, [ko, ('rs', 0), ('rs', 1)], [('osb', 0), ('osb', 1)])
                    STT(oall[:, hd, qt * 256:(qt + 1) * 256], osb[:, 1, :], nlam[:, l:l + 1], osb[:, 0, :], ALU.mult, ALU.add,
                        [('osb', 0), ('osb', 1), 'nlam'], [('oall', hd)])

            emit_qk(0)
            for i in range(len(iters)):
                if i + 1 < len(iters):
                    emit_qk(i + 1)
                emit_rest(i)
            def danorm_job(hd, slot):
                reg, k0, k1 = SR[slot]
                ACT(ydaT[:, hd, :], oall[:, hd, :], AF.Square, [('oall', hd)], [('yda', hd)])
                yield
                for th in range(2):
                    MM(reg[:, th * 512:(th + 1) * 512], ones_b[:], ydaT[:, hd, th * 512:(th + 1) * 512], True, True,
                       [('yda', hd), 'ones_b'], [k0 if th == 0 else k1])
                yield
                rs_, rsk = fs_next()
                ACT(rs_[:, :], reg[:, :], AF.Ln, [k0, k1, 'eps_t'], [rsk], bias=eps_t[:], scale=1.0 / 128.0)
                ACT(rs_[:, :], rs_[:, :], AF.Exp, [rsk], [rsk], scale=-0.5)
                yield
                STT(ydaT[:, hd, :], oall[:, hd, :], dagl[:, 0:1], rs_[:, :], ALU.mult, ALU.mult, [('oall', hd), 'dagl', rsk],
                    [('yda', hd)])

            run_pipeline(list(range(4)), danorm_job, 2)
            T.free(keys)

    def ml_phase(l, ymlT):
        par = l % 2
        wl = w_in[l]
        LK = (2, 4, 6)
        with ExitStack() as es:
            COLS = es.enter_context(sbt("ml_cols", [128, 8, 4, 8], F32))
            DECB = es.enter_context(sbt("ml_decb", [128, 8, 8], F32))
            MP = es.enter_context(sbt("ml_mp", [8, 16], F32))
            MPL = es.enter_context(sbt("ml_mpl", [8, 16], F32))
            NM = es.enter_context(sbt("ml_nm", [8, 16], F32))
            DT = es.enter_context(sbt("ml_dt", [8, 16], F32))
            DEC = es.enter_context(sbt("ml_dec", [8, 16], F32))
            k1 = ['ml_cols', 'ml_decb', 'ml_mp', 'ml_mpl', 'ml_nm', 'ml_dt', 'ml_dec']
            T.adopt(k1)
            with ExitStack() as es2:
                Rr = [es2.enter_context(sbt(f"ml_r{i}", [8, 1024], F32)) for i in range(9)]
                rk = [('ml_r', i) for i in range(9)]
                T.adopt(rk)
                R0, R1, R2, R3, R4, R5, R6, R7a, R7b = Rr
                w, wk = wload(('w_in', (l,)), 0, 8, 3584, 16)
                for (pp, c0, kk) in ((PA, 0, [ps(0), ps(1)]), (PB, 8, [ps(2), ps(3)])):
                    for th in range(2):
                        for kc in range(8):
                            MM(pp[0:8, th * 512:(th + 1) * 512], w[:, kc, c0:c0 + 8], hT[:, kc, th * 512:(th + 1) * 512],
                               kc == 0, kc == 7, [wk, ('h', kc)], [kk[th]])
                for (pp, kk, dst, dk, bcol) in ((PA, [ps(0), ps(1)], R0, rk[0], 0), (PB, [ps(2), ps(3)], R1, rk[1], 1)):
                    ACT(R4[:, :], pp[0:8, :], AF.Identity, kk + ['GB'], [rk[4]], bias=GB[:, l, bcol:bcol + 1])
                    ACT(R5[:, :], pp[0:8, ::-1], AF.Identity, kk + ['GB'], [rk[5]], bias=GB[:, l, bcol:bcol + 1])
                    TS('dve', dst[:, :], R4[:, :], dirm[:, 0:1], None, ALU.mult, None, [rk[4], 'dirm'], [dk])
                    STT(dst[:, :], R5[:, :], dirm[:, 1:2], dst[:, :], ALU.mult, ALU.add, [rk[5], 'dirm', dk], [dk])
                ACT(R1[:, :], R1[:, :], AF.Exp, [rk[1]], [rk[1]], scale=-1.0)
                ACT(R1[:, :], R1[:, :], AF.Ln, [rk[1], 'one_t'], [rk[1]], bias=one_t[0:8, :], scale=1.0)
                for c in range(8):
                    sl = slice(c * 128, (c + 1) * 128)
                    T.op('dve', lambda: nc.vector.tensor_tensor_scan(out=R2[:, sl], data0=ones8[:, :], data1=R1[:, sl], initial=0.0,
                                                                      op0=ALU.mult, op1=ALU.add), [rk[1], 'ones8'], [rk[2]])
                TT('dve', R0[:, :], R0[:, :], R2[:, :], ALU.add, [rk[0], rk[2]], [rk[0]])
                for c in range(8):
                    sl = slice(c * 128, (c + 1) * 128)
                    T.op('dve', lambda: nc.vector.tensor_tensor_scan(out=R3[:, sl], data0=R0[:, sl], data1=R0[:, sl], initial=-1e30,
                                                                      op0=ALU.max, op1=ALU.max), [rk[0]], [rk[3]])
                DMA('sp', MP[:, 0:1], sm[l], (), ['ml_mp'], 'di')
                for cs in range(8):
                    sl = slice(cs * 128, (cs + 1) * 128)
                    la = cs * 128 + 127
                    mprev = MP[:, cs:cs + 1]
                    mpk = 'ml_mp'
                    if cs in LK:
                        TS('dve', MPL[:, cs:cs + 1], mprev, link[0:8, 0:1], None, ALU.mult, None, ['ml_mp', 'link'], ['ml_mpl'])
                        mprev = MPL[:, cs:cs + 1]
                        mpk = 'ml_mpl'
                    TS('dve', R3[:, sl], R3[:, sl], mprev, None, ALU.max, None, [rk[3], mpk], [rk[3]])
                    TT('dve', MP[:, cs + 1:cs + 2], R3[:, la:la + 1], R2[:, la:la + 1], ALU.subtract, [rk[3], rk[2]], ['ml_mp'])
                    ACT(R5[:, sl], R3[:, sl], AF.Exp, [rk[3], mpk], [rk[5]], bias=mprev, scale=-1.0)
                    if cs in LK:
                        TS('dve', R5[:, sl], R5[:, sl], link[0:8, 0:1], None, ALU.mult, None, [rk[5], 'link'], [rk[5]])
                    ACT(R4[:, sl], R3[:, sl], AF.Exp, [rk[3]], [rk[4]], bias=R3[:, la:la + 1], scale=-1.0)
                    TS('dve', NM[:, cs:cs + 1], R3[:, la:la + 1], -1.0, None, ALU.mult, None, [rk[3]], ['ml_nm'])
                    ACT(R0[:, sl], R0[:, sl], AF.Exp, [rk[0], 'ml_nm'], [rk[0]], bias=NM[:, cs:cs + 1], scale=1.0)
                    TT('dve', DT[:, cs:cs + 1], mprev, R2[:, la:la + 1], ALU.subtract, [mpk, rk[2]], ['ml_dt'])
                    TT('dve', DT[:, cs:cs + 1], DT[:, cs:cs + 1], MP[:, cs + 1:cs + 2], ALU.subtract, ['ml_dt', 'ml_mp'], ['ml_dt'])
                    ACT(DEC[:, cs:cs + 1], DT[:, cs:cs + 1], AF.Exp, ['ml_dt'], ['ml_dec'])
                    if cs in LK:
                        TS('dve', DEC[:, cs:cs + 1], DEC[:, cs:cs + 1], link[0:8, 0:1], None, ALU.mult, None, ['ml_dec', 'link'],
                           ['ml_dec'])
                TT('dve', R2[:, :], R2[:, :], R3[:, :], ALU.subtract, [rk[2], rk[3]], [rk[2]])
                ACT(R2[:, :], R2[:, :], AF.Exp, [rk[2]], [rk[2]])
                CP('dve', MPL[:, 12:16], MP[:, 2:10:2], ['ml_mp', 'ml_mpl'], ['ml_mpl'])
                DMA('sp', om_o[l], MPL[:, 12:16], ['ml_mpl'], (), 'do')
                for qi, (Q, qk) in enumerate(((R0, rk[0]), (R4, rk[4]), (R5, rk[5]), (R2, rk[2]))):
                    Ro, rok = (R7a, rk[7]) if qi % 2 == 0 else (R7b, rk[8])
                    TS('dve', R6[:, :], Q[:, :], dirm[:, 0:1], None, ALU.mult, None, [qk, 'dirm'], [rk[6]])
                    STT(Ro[:, :], Q[:, ::-1], dirm[:, 1:2], R6[:, :], ALU.mult, ALU.add, [rk[6], 'dirm', qk], [rok])
                    for c in range(8):
                        o0 = (c * 4 + qi) * 8
                        TR(bank(6)[:, o0:o0 + 8], Ro[:, c * 128:(c + 1) * 128], ident_f[0:8, 0:8], [rok, 'ident_f'], [ps(6)])
                CP('dve', COLS[:, :, :, :].rearrange("p a b c -> p (a b c)"), bank(6)[:, 0:256], [ps(6)], ['ml_cols'])
                for r in range(8):
                    MM(bank(5)[:, r * 8:(r + 1) * 8], sel[0:8, r * 128:(r + 1) * 128], DEC[0:8, 0:8], True, True, ['sel', 'ml_dec'],
                       [ps(5)])
                CP('dve', DECB[:, :, :].rearrange("p a b -> p (a b)"), bank(5)[:, 0:64], [ps(5)], ['ml_decb'])
                T.free(rk)
            hsum = es.enter_context(sbt("ml_hs", [128, 8, 512], F32))
            Cn32 = es.enter_context(sbt("ml_cn", [128, 8, 130], F32))
            T.adopt([('hs', c) for c in range(8)] + [('cn', r) for r in range(8)])
            es3 = ExitStack()
            mqT = es3.enter_context(sbt("ml_qT", [128, 4, 1024], BF16))
            mkT = es3.enter_context(sbt("ml_kT", [128, 4, 1024], BF16))
            mva = es3.enter_context(sbt("ml_va", [128, 8, 4, 130], BF16))
            Cnb = es3.enter_context(sbt("ml_cnb", [128, 8, 130], BF16))
            PTs = es3.enter_context(sbt("ml_pts", [128, 6, 128], BF16))
            kg = es3.enter_context(sbt("ml_kg", [128, 6, 128], BF16))
            vg = es3.enter_context(sbt("ml_vg", [128, 6, 130], BF16))
            tB = es3.enter_context(sbt("ml_tb", [128, 6, 130], F32))
            nd = es3.enter_context(sbt("ml_nd", [128, 6, 130], F32))
            dn = es3.enter_context(sbt("ml_dn", [128, 6, 2], F32))
            k2 = ([('mqT', h) for h in range(4)] + [('mkT', h) for h in range(4)] + [('mva', c) for c in range(8)] +
                  [('cnb', r) for r in range(8)] +
                  [(nm_, b) for nm_ in ('pts', 'kg', 'vg', 'tb', 'nd', 'dn') for b in range(6)])
            T.adopt(k2)
            for dr in range(2):
                for hd in range(4):
                    r = dr * 4 + hd
                    DMA('sp', Cn32[:, r, 0:129], sCn[l, dr, hd], (), [('cn', r)], 'di')
                    CP('pool', Cnb[:, r, 0:129], Cn32[:, r, 0:129], [('cn', r)], [('cnb', r)])
            for which, c0 in ((0, 1536), (1, 2048)):
                w, wk = wload(('w_in', (l,)), 0, 8, c0, 512)
                for hd in range(4):
                    zp, zk = p2_next()
                    for th in range(2):
                        for kc in range(8):
                            MM(zp[:, th * 512:(th + 1) * 512], w[:, kc, hd * 128:(hd + 1) * 128], hT[:, kc, th * 512:(th + 1) * 512],
                               kc == 0, kc == 7, [wk, ('h', kc)], [zk[th]])
                    ch = which * 4 + hd
                    acc, ak = fs_next()
                    dwconv(l, zp, zk, acc, ak, mlw_c(l, 0, ch), mlw_c(l, 1, ch), mlw_c(l, 2, ch), mlb_c(l, ch),
                           w0n[:, par, 44 + ch:45 + ch], w2n[:, par, 44 + ch:45 + ch])
                    if which == 0:
                        ACT(mqT[:, hd, :], acc[:, :], AF.Silu, [ak], [('mqT', hd)])
                    else:
                        sgt, sk = fs_next()
                        ACT(sgt[:, :], acc[:, :], AF.Sigmoid, [ak], [sk])
                        STT(mkT[:, hd, :], acc[:, :], 128.0 ** -0.5, sgt[:, :], ALU.mult, ALU.mult, [ak, sk], [('mkT', hd)])
            w, wk = wload(('w_in', (l,)), 0, 8, 2560, 512)
            MS('pool', mva[:, :, :, 128:130], 1.0, [('mva', c) for c in range(8)])
            for tc in range(8):
                bk = 4 + tc % 2
                for kc in range(8):
                    MM(bank(bk), hT[:, kc, tc * 128:(tc + 1) * 128], w[:, kc, :], kc == 0, kc == 7, [('h', kc), wk], [ps(bk)])
                CP('act', mva[:, tc, :, 0:128], bank(bk).rearrange("p (h d) -> p h d", h=4), [ps(bk)], [('mva', tc)])
            def core_iter(cs, hd, dr, slot):
                c = cs if dr == 0 else 7 - cs
                r = dr * 4 + hd
                tsl = slice(c * 128, (c + 1) * 128)
                mask = maskF if dr == 0 else maskB
                mkey = 'maskF' if dr == 0 else 'maskB'
                pb = bank(slot)
                pk_ = ps(slot)
                Sps, Aps, Bps, CNps = pb[:, 0:128], pb[:, 130:259], pb[:, 260:389], pb[:, 0:129]
                MM(Sps, mkT[:, hd, tsl], mqT[:, hd, tsl], True, True, [('mkT', hd), ('mqT', hd)], [pk_])
                TR(PT[:, slot * 128:(slot + 1) * 128], mkT[:, hd, tsl], ident_b[:], [('mkT', hd), 'ident_b'], [ps(7)])
                yield
                TT('dve', PTs[:, slot, :], Sps, mask[:, :], ALU.mult, [pk_, mkey], [('pts', slot)])
                ACT(kg[:, slot, :], PT[:, slot * 128:(slot + 1) * 128], AF.Identity, [ps(7), 'ml_cols'], [('kg', slot)],
                    scale=COLS[:, c, 0, r:r + 1])
                ACT(vg[:, slot, :], mva[:, c, hd, :], AF.Identity, [('mva', c), 'ml_cols'], [('vg', slot)],
                    scale=COLS[:, c, 0, r:r + 1])
                yield
                MM(Aps, PTs[:, slot, :], vg[:, slot, 0:129], True, True, [('pts', slot), ('vg', slot)], [pk_])
                MM(Bps, mqT[:, hd, tsl], Cnb[:, r, 0:129], True, True, [('mqT', hd), ('cnb', r)], [pk_])
                yield
                ACT(tB[:, slot, 0:129], Bps, AF.Identity, [pk_, 'ml_cols'], [('tb', slot)], scale=COLS[:, c, 2, r:r + 1])
                STT(nd[:, slot, 0:129], Aps, COLS[:, c, 1, r:r + 1], tB[:, slot, 0:129], ALU.mult, ALU.add,
                    [pk_, 'ml_cols', ('tb', slot)], [('nd', slot)])
                STT(dn[:, slot, 0:1], nd[:, slot, 128:129], -1.0, nd[:, slot, 128:129], ALU.mult, ALU.max, [('nd', slot)],
                    [('dn', slot)])
                TS('dve', dn[:, slot, 0:1], dn[:, slot, 0:1], COLS[:, c, 3, r:r + 1], None, ALU.max, None,
                   [('dn', slot), 'ml_cols'], [('dn', slot)])
                T.op('dve', lambda: nc.vector.reciprocal(out=dn[:, slot, 1:2], in_=dn[:, slot, 0:1]), [('dn', slot)], [('dn', slot)])
                hs = hsum[:, c, hd * 128:(hd + 1) * 128]
                if cs < 4:
                    TS('dve', hs, nd[:, slot, 0:128], dn[:, slot, 1:2], None, ALU.mult, None, [('nd', slot), ('dn', slot)],
                       [('hs', c)])
                else:
                    STT(hs, nd[:, slot, 0:128], dn[:, slot, 1:2], hs, ALU.mult, ALU.add, [('nd', slot), ('dn', slot), ('hs', c)],
                        [('hs', c)])
                yield
                MM(CNps, kg[:, slot, :], mva[:, c, hd, 0:129], True, True, [('kg', slot), ('mva', c)], [pk_])
                yield
                STT(Cn32[:, r, 0:129], Cn32[:, r, 0:129], DECB[:, r, cs:cs + 1], CNps, ALU.mult, ALU.add,
                    [('cn', r), 'ml_decb', pk_], [('cn', r)])
                CP('pool', Cnb[:, r, 0:129], Cn32[:, r, 0:129], [('cn', r)], [('cnb', r)])
                if cs % 2 == 1:
                    DMA('sp', oCn_o[l, dr, cs // 2, hd], Cn32[:, r, 0:129], [('cn', r)], (), 'do')

            todo = [(cs, hd, dr) for cs in range(8) for hd in range(4) for dr in range(2)]
            modgen = [compute_mod_gen(l + 1) if l + 1 < depth else None]
            run_pipeline(todo, lambda job, sl_: core_iter(job[0], job[1], job[2], sl_), 6, modgen, 5)
            if l == 0:
                dump('hs', hsum[:, :, :], [('hs', c) for c in range(8)])
                dump('cols', COLS[:, :, :, :], ['ml_cols'])
            T.free(k2)
            es3.close()
            og = es.enter_context(sbt("ml_og", [128, 8, 512], BF16))
            gml4 = es.enter_context(sbt("ml_g4", [128, 512], F32))
            ssq = es.enter_context(sbt("ml_ssq", [128, 4, 4], F32))
            ytm = es.enter_context(sbt("ml_ytm", [128, 4, 512], BF16))
            k3 = [('og', c) for c in range(8)] + ['gml4'] + [(n_, b_) for n_ in ('ssq', 'ytm') for b_ in range(4)]
            T.adopt(k3)
            for h in range(4):
                DMA('sp', gml4[:, h * 128:(h + 1) * 128], ml_norm_g[l].partition_broadcast(128), (), ['gml4'], 'di')
            w, wk = wload(('w_in', (l,)), 0, 8, 3072, 512)

            def mlout_job(c, slot):
                bk = slot
                for kc in range(8):
                    MM(bank(bk), hT[:, kc, c * 128:(c + 1) * 128], w[:, kc, :], kc == 0, kc == 7, [('h', kc), wk], [ps(bk)])
                yield
                tmp, tk = fs_next()
                ACT(tmp[:, 0:512], bank(bk), AF.Sigmoid, [ps(bk)], [tk])
                ACT(tmp[:, 512:1024], hsum[:, c, :], AF.Square, [('hs', c)], [tk])
                yield
                TT('pool', og[:, c, :], tmp[:, 0:512], gml4[:, :], ALU.mult, [tk, 'gml4'], [('og', c)])
                T.op('dve', lambda: nc.vector.tensor_reduce(out=ssq[:, slot, 0:4],
                                                             in_=tmp[:, 512:1024].rearrange("p (h d) -> p h d", h=4),
                                                             axis=AX.X, op=ALU.add), [tk], [('ssq', slot)])
                yield
                TS('pool', ssq[:, slot, 0:4], ssq[:, slot, 0:4], 1.0 / 128.0, EPS, ALU.mult, ALU.add, [('ssq', slot)], [('ssq', slot)])
                TT('pool', ssq[:, slot, 0:4], ssq[:, slot, 0:4], mhalf[:, 0:4], ALU.pow, [('ssq', slot), 'mhalf'], [('ssq', slot)])
                yield
                for hd in range(4):
                    hsl = slice(hd * 128, (hd + 1) * 128)
                    STT(ytm[:, slot, hsl], hsum[:, c, hsl], ssq[:, slot, hd:hd + 1], og[:, c, hsl], ALU.mult, ALU.mult,
                        [('hs', c), ('ssq', slot), ('og', c)], [('ytm', slot)])
                yield
                po = (slot % 2) * 512
                for hd in range(4):
                    hsl = slice(hd * 128, (hd + 1) * 128)
                    TR(PT[:, po + hd * 128:po + (hd + 1) * 128], ytm[:, slot, hsl], ident_b[:], [('ytm', slot), 'ident_b'], [ps(7)])
                yield
                CP('act', ymlT[:, :, c * 128:(c + 1) * 128], PT[:, po:po + 512].rearrange("p (h t) -> p h t", h=4), [ps(7)],
                   [('yml', h) for h in range(4)])

            run_pipeline(list(range(8)), mlout_job, 4)
            T.free(k1 + k3 + [('hs', c) for c in range(8)] + [('cn', r) for r in range(8)])

    def mixer(l):
        norm_mod(l, 0)
        with ExitStack() as es:
            ydaT = es.enter_context(sbt("ydaT", [128, 4, 1024], BF16))
            ymlT = es.enter_context(sbt("ymlT", [128, 4, 1024], BF16))
            ysgT = es.enter_context(sbt("ysgT", [128, 4, 1024], BF16))
            yk = [(n, h) for n in ('yda', 'yml', 'ysg') for h in range(4)]
            T.adopt(yk)
            if 'ml' in phases:
                ml_phase(l, ymlT)
            else:
                MS('pool', ymlT[:, :, :], 0.0, [('yml', h) for h in range(4)])
            if 'da' in phases:
                da_phase(l, ydaT)
            else:
                MS('pool', ydaT[:, :, :], 0.0, [('yda', h) for h in range(4)])
            if 'sg' in phases:
                sg_phase(l, ysgT)
            else:
                MS('pool', ysgT[:, :, :], 0.0, [('ysg', h) for h in range(4)])
            if l == 0:
                dump('yda', ydaT[:, :, :], [('yda', h) for h in range(4)])
                dump('yml', ymlT[:, :, :], [('yml', h) for h in range(4)])
                dump('ysg', ysgT[:, :, :], [('ysg', h) for h in range(4)])
            merge(l, [ydaT, ymlT, ysgT])
            T.free(yk)

    compute_mod(0)
    for l in range(depth):
        prep_w0n(l)
        if l + 1 < depth and 'ml' not in phases:
            compute_mod(l + 1)
        if any(p in phases for p in ('ml', 'da', 'sg')):
            mixer(l)
        if 'ffn' in phases:
            ffn(l)

    with ExitStack() as es:
        yT = es.enter_context(sbt("yT", [128, 8, 1024], F32))
        T.adopt([('yT', dc) for dc in range(8)])
        rms_stats([xT[:, dc, :] for dc in range(8)], [('x', dc) for dc in range(8)],
                  [hT[:, dc, :] for dc in range(8)], [('h', dc) for dc in range(8)], 1024.0)
        for dc in range(8):
            STT(yT[:, dc, :], xT[:, dc, :], gcol[:, dc:dc + 1], rstd[:], ALU.mult, ALU.mult, [('x', dc), 'gcol', 'rstd'],
                [('yT', dc)])
        for tc in range(8):
            st, sk = fs_next()
            for half in range(2):
                bk = 2 + half
                for j in range(4):
                    dc = half * 4 + j
                    TR(bank(bk)[:, j * 128:(j + 1) * 128], yT[:, dc, tc * 128:(tc + 1) * 128], ident_f[:],
                       [('yT', dc), 'ident_f'], [ps(bk)])
                CP('dve' if half == 0 else 'act', st[:, half * 512:(half + 1) * 512], bank(bk), [ps(bk)], [sk])
            DMA('sp', y_o[tc * 128:(tc + 1) * 128, :], st[:, :], [sk], (), 'do')
        T.free([('yT', dc) for dc in range(8)])
    T.finish()
    return nc, T, dumps, wrec


def _consts():
    ident = np.eye(128, dtype=np.float32)
    prot = np.zeros((128, 128), np.float32)
    for m in range(128):
        prot[m ^ 16, m] = 1.0
    s = np.arange(128)[:, None]
    t = np.arange(128)[None, :]
    maskF = (s <= t).astype(np.float32)
    maskB = (s >= t).astype(np.float32)
    sel = np.zeros((8, 8, 128), np.float32)
    for r in range(8):
        sel[r, r, :] = 1.0
    dirm = np.zeros((8, 2), np.float32)
    dirm[0:4, 0] = 1.0
    dirm[4:8, 1] = 1.0
    return dict(c_ident=ident, c_prot=prot, c_maskF=maskF, c_maskB=maskB, c_sel=sel.reshape(8, 1024), c_dirm=dirm)


def _rope_tables():
    t = np.arange(1024)
    row = (t // 64).astype(np.float32)
    col = (t % 64).astype(np.float32)
    nf = 16
    inv = (10000.0 ** (-np.arange(nf, dtype=np.float32) / nf)).astype(np.float32)
    ang = np.stack([row[:, None] * inv, col[:, None] * inv], axis=1)
    cos = np.cos(ang).astype(np.float32)
    sin = np.sin(ang).astype(np.float32)
    C = np.zeros((128, 1024), np.float32)
    S = np.zeros((128, 1024), np.float32)
    for d in range(128):
        axis = (d >> 5) & 1
        half = (d >> 4) & 1
        f = d & 15
        C[d] = cos[:, axis, f]
        S[d] = sin[:, axis, f] * (-1.0 if half == 0 else 1.0)
    return C, S


_CACHE = {}


def kernel(x_prompt, x_sample, c, cache_k, cache_v, state_C, state_n, state_m, c_ctx,
           w_mod, b_mod, w_in, da_lambda, da_norm_g, ml_conv_w, ml_conv_b, ml_gate_b,
           ml_norm_g, sg_norm_g, sg_w, sg_b, w_branch, w_out, w_up, ffn_conv_w, ffn_conv_b,
           w_down, final_g, _depth=L, _phases=('ml', 'da', 'sg', 'ffn'), _dbg=False):
    f = lambda a: np.ascontiguousarray(np.asarray(a, dtype=np.float32))
    key = (_depth, tuple(_phases), _dbg)
    if key not in _CACHE:
        rec = build(_depth, _phases, _dbg is True)[3]
        _CACHE[key] = build(_depth, _phases, _dbg is True, rec)
    nc, T, dumps, _ = _CACHE[key]
    dp = _depth
    consts = _consts()
    rC, rS = _rope_tables()
    shared = dict(consts)
    shared.update(
        w_mod=f(w_mod[:dp]), b_mod=f(b_mod).reshape(L, 48, 128), w_in=f(w_in[:dp]), da_lambda=f(da_lambda).reshape(1, L * 256),
        da_norm_g=f(da_norm_g).reshape(L, 1, 128), ml_conv_w=f(ml_conv_w).reshape(L, 24, 128),
        ml_conv_b=f(ml_conv_b).reshape(L, 8, 128), ml_gate_b=f(ml_gate_b).reshape(L, 16, 1),
        ml_norm_g=f(ml_norm_g).reshape(L, 1, 128), sg_norm_g=f(sg_norm_g).reshape(L, 1, 512), sg_w=f(sg_w),
        sg_b=f(sg_b).reshape(L, 1, 512), w_branch=f(w_branch[:dp]), w_out=f(w_out[:dp]), w_up=f(w_up[:dp]),
        ffn_conv_w=f(ffn_conv_w).reshape(L, 132, 128), ffn_conv_b=f(ffn_conv_b).reshape(L, 44, 128), w_down=f(w_down[:dp]),
        final_g=f(final_g).reshape(8, 128))
    x_prompt = f(x_prompt)
    x_sample = f(x_sample)
    in_maps = []
    for core in range(8):
        m = dict(shared)
        if core < 4:
            b = core
            m['xin'] = x_sample[b]
            m['cond'] = f(c)[b].reshape(8, 128)
            m['ck'] = f(cache_k)[b]
            m['cv'] = f(cache_v)[b]
            m['sCn'] = np.ascontiguousarray(np.concatenate([f(state_C)[b], f(state_n)[b][..., None]], axis=-1))
            m['sm'] = f(state_m)[b].reshape(L, 8, 1)
            m['c_ropeC'] = rC
            m['c_ropeS'] = rS
            m['c_maskb'] = np.zeros((128, 20), np.float32)
            lk = np.zeros((128, 2), np.float32)
            lk[:, 0] = 1.0
            m['c_link'] = lk
        else:
            j = core - 4
            m['xin'] = x_prompt[4 * j:4 * j + 4].reshape(1024, 1024)
            m['cond'] = f(c_ctx).reshape(8, 128)
            m['ck'] = np.zeros((L, 4, 256, 128), np.float32)
            m['cv'] = np.zeros((L, 4, 256, 128), np.float32)
            m['sCn'] = np.zeros((L, 2, 4, 128, 129), np.float32)
            m['sm'] = np.zeros((L, 8, 1), np.float32)
            m['c_ropeC'] = np.ones((128, 1024), np.float32)
            m['c_ropeS'] = np.zeros((128, 1024), np.float32)
            mb = np.full((5, 4), -30000.0, np.float32)
            for qt in range(4):
                mb[1 + qt, qt] = 0.0
            m['c_maskb'] = np.tile(mb.reshape(1, 20), (128, 1))
            lk = np.zeros((128, 2), np.float32)
            lk[:, 1] = 1.0
            m['c_link'] = lk
        in_maps.append(m)
    if _dbg == 'maps':
        return nc, in_maps
    if _dbg == 'time':
        res = run_bass_kernel_spmd(nc, in_maps, core_ids=list(range(8)), trace=True)
        return res.exec_time_ns
    res = run_bass_kernel_spmd(nc, in_maps, core_ids=list(range(8)))
    R = res.results
    if _dbg:
        kernel.dbg = [{n: R[c][n] for n in dumps} for c in range(8)]
    y_sample = np.stack([R[b]['y_o'] for b in range(4)], axis=0)
    y_prompt = np.concatenate([R[4 + j]['y_o'].reshape(4, 256, 1024) for j in range(4)], axis=0)
    nk = np.concatenate([R[4 + j]['ok_o'].reshape(L, 4, 4, 256, 128).transpose(2, 0, 1, 3, 4) for j in range(4)], axis=0)
    nv = np.concatenate([R[4 + j]['ov_o'].reshape(L, 4, 4, 256, 128).transpose(2, 0, 1, 3, 4) for j in range(4)], axis=0)
    nC, nn, nm = [], [], []
    for j in range(4):
        oCn = R[4 + j]['oCn_o']
        oC = oCn[..., 0:128]
        on = oCn[..., 128]
        om = R[4 + j]['om_o'].reshape(L, 2, 4, 4)
        for sq in range(4):
            nC.append(np.stack([oC[:, 0, sq], oC[:, 1, 3 - sq]], axis=1))
            nn.append(np.stack([on[:, 0, sq], on[:, 1, 3 - sq]], axis=1))
            nm.append(np.stack([om[:, 0, :, sq], om[:, 1, :, 3 - sq]], axis=1))
    return (y_prompt.astype(np.float32), y_sample.astype(np.float32), np.ascontiguousarray(nk), np.ascontiguousarray(nv),
            np.stack(nC, axis=0), np.stack(nn, axis=0), np.stack(nm, axis=0))
```

```python
import math
from contextlib import ExitStack
import numpy as np
import concourse.bass as bass
import concourse.mybir as mybir
from concourse.bass_utils import run_bass_kernel_spmd

F32 = mybir.dt.float32
BF16 = mybir.dt.bfloat16
AF = mybir.ActivationFunctionType
ALU = mybir.AluOpType
AX = mybir.AxisListType

L = 4
NIN = 7696
DFF = 2816
EPS = 1e-6


class Tr:
    EP = 30000
    NSEM = 8

    def __init__(s, nc):
        s.nc = nc
        s.E = dict(pe=nc.tensor, act=nc.scalar, dve=nc.vector, pool=nc.gpsimd, sp=nc.sync)
        s.sems = {}
        s.cnt = {}
        s.known = {e: {} for e in s.E}
        s.res = {}
        s.grave = {}
        s.nwait = 0
        s.nops = 0
        s.dcount = {}
        s.dlast = {}

    def _sem(s, st, c):
        mult = 1 if st in s.E else 16
        ep = s.EP // mult
        e = (c - 1) // ep
        lst = s.sems.setdefault(st, [])
        while len(lst) <= e:
            nm = st if isinstance(st, str) else f"{st[0]}{st[1]}"
            lst.append(s.nc.alloc_semaphore(f"s_{nm}_{len(lst)}"))
        return lst[e], ((c - 1) % ep + 1) * mult

    def op(s, eng, fn, reads=(), writes=(), sig=True, dma=None):
        deps = {}

        def add(ev):
            if ev is None:
                return
            st = ev[0]
            if st == 'pe' and eng == 'pe' and dma is None:
                return
            if st not in deps or deps[st][1] < ev[1]:
                deps[st] = ev

        for k in reads:
            r = s.res.get(k)
            if r is None:
                continue
            add(r[0])
            if k[0] == 'ps':
                for ev in r[1].values():
                    add(ev)
        for k in writes:
            r = s.res.get(k)
            if r is None:
                continue
            add(r[0])
            for ev in r[1].values():
                add(ev)
        if dma is not None:
            i = s.dcount.get(dma, 0)
            s.dcount[dma] = i + 1
            dma = (dma, i % s.NSEM)
            add(s.dlast.get(dma))
        kn = s.known[eng]
        for st in sorted(deps, key=lambda a: -deps[a][1]):
            _, c, clk = deps[st]
            if kn.get(st, 0) >= c:
                continue
            if st == 'pe':
                assert s.cnt.get('pe', 0) >= c, "dependency on unsignalled PE op"
            sem, v = s._sem(st, c)
            s.E[eng].wait_ge(sem, v)
            s.nwait += 1
            kn[st] = c
            for a, b in clk.items():
                if kn.get(a, 0) < b:
                    kn[a] = b
        ins = fn()
        s.nops += 1
        st = dma or eng
        c = s.cnt.get(st, 0) + 1
        if sig:
            s.cnt[st] = c
            sem, v = s._sem(st, c)
            ins.then_inc(sem, 16 if dma else 1)
        clk = dict(kn)
        clk[st] = c
        ev = (st, c, clk)
        if dma is not None:
            s.dlast[dma] = ev
        for k in writes:
            s.res[k] = [ev, {}]
        for k in reads:
            r = s.res.setdefault(k, [None, {}])
            r[1][st] = ev
        return ins

    def free(s, keys):
        for k in keys:
            r = s.res.pop(k, None)
            if r is None:
                continue
            evs = list(r[1].values())
            if r[0] is not None:
                evs.append(r[0])
            for ev in evs:
                st = ev[0]
                if st not in s.grave or s.grave[st][1] < ev[1]:
                    s.grave[st] = ev

    def adopt(s, keys):
        for k in keys:
            s.res[k] = [None, dict(s.grave)]

    def finish(s, eng='sp'):
        for st, c in s.cnt.items():
            if st in s.E:
                continue
            ep = s.EP // 16
            for e in range((c - 1) // ep + 1 if c > 0 else 0):
                last = min(c, (e + 1) * ep)
                sem, v = s._sem(st, last)
                s.E[eng].wait_ge(sem, v)


def build(depth=L, phases=('ml', 'da', 'sg', 'ffn'), dbg=False, wsched_n=None):
    nc = bass.Bass("TRN2", target_bir_lowering=False)
    T = Tr(nc)
    LW = depth
    dumps = []
    wrec = []
    wissued = [0]
    LOOK = 2

    def din(n, sh):
        return nc.dram_tensor(n, list(sh), F32, kind="ExternalInput").ap()

    def dout(n, sh):
        return nc.dram_tensor(n, list(sh), F32, kind="ExternalOutput").ap()

    xin = din("xin", [1024, 1024])
    cond = din("cond", [8, 128])
    ck = din("ck", [L, 4, 256, 128])
    cv = din("cv", [L, 4, 256, 128])
    sCn = din("sCn", [L, 2, 4, 128, 129])
    sm = din("sm", [L, 8, 1])
    c_ident = din("c_ident", [128, 128])
    c_prot = din("c_prot", [128, 128])
    c_maskF = din("c_maskF", [128, 128])
    c_maskB = din("c_maskB", [128, 128])
    c_sel = din("c_sel", [8, 1024])
    c_dirm = din("c_dirm", [8, 2])
    c_ropeC = din("c_ropeC", [128, 1024])
    c_ropeS = din("c_ropeS", [128, 1024])
    c_maskb = din("c_maskb", [128, 20])
    c_link = din("c_link", [128, 2])
    w_mod = din("w_mod", [LW, 1024, 6144])
    b_mod = din("b_mod", [L, 48, 128])
    w_in = din("w_in", [LW, 1024, NIN])
    da_lambda = din("da_lambda", [1, L * 256])
    da_norm_g = din("da_norm_g", [L, 1, 128])
    ml_conv_w = din("ml_conv_w", [L, 24, 128])
    ml_conv_b = din("ml_conv_b", [L, 8, 128])
    ml_gate_b = din("ml_gate_b", [L, 16, 1])
    ml_norm_g = din("ml_norm_g", [L, 1, 128])
    sg_norm_g = din("sg_norm_g", [L, 1, 512])
    sg_w = din("sg_w", [L, 4, 128, 128])
    sg_b = din("sg_b", [L, 1, 512])
    w_branch = din("w_branch", [LW, 3, 512, 1024])
    w_out = din("w_out", [LW, 1024, 1024])
    w_up = din("w_up", [LW, 1024, 2 * DFF])
    ffn_conv_w = din("ffn_conv_w", [L, 132, 128])
    ffn_conv_b = din("ffn_conv_b", [L, 44, 128])
    w_down = din("w_down", [LW, DFF, 1024])
    final_g = din("final_g", [8, 128])

    y_o = dout("y_o", [1024, 1024])
    ok_o = dout("ok_o", [L, 4, 1024, 128])
    ov_o = dout("ov_o", [L, 4, 1024, 128])
    oCn_o = dout("oCn_o", [L, 2, 4, 4, 128, 129])
    om_o = dout("om_o", [L, 8, 4])

    WTinit = dict(w_in=w_in, w_mod=w_mod, w_up=w_up, w_down=w_down, w_out=w_out, w_branch=w_branch)
    def veng(e):
        return nc.vector if e == 'dve' else nc.gpsimd

    def ACT(out, in_, func, r, w, bias=None, scale=None):
        kw = {}
        if bias is not None:
            kw['bias'] = bias
        if scale is not None:
            kw['scale'] = scale
        return T.op('act', lambda: nc.scalar.activation(out=out, in_=in_, func=func, **kw), r, w)

    def TT(e, out, a, b, op, r, w):
        return T.op(e, lambda: veng(e).tensor_tensor(out=out, in0=a, in1=b, op=op), r, w)

    def TS(e, out, a, s1, s2, op0, op1, r, w):
        if op1 is None:
            return T.op(e, lambda: veng(e).tensor_scalar(out=out, in0=a, scalar1=s1, scalar2=None, op0=op0), r, w)
        return T.op(e, lambda: veng(e).tensor_scalar(out=out, in0=a, scalar1=s1, scalar2=s2, op0=op0, op1=op1), r, w)

    def STT(out, a, s, b, op0, op1, r, w):
        return T.op('dve', lambda: nc.vector.scalar_tensor_tensor(out=out, in0=a, scalar=s, in1=b, op0=op0, op1=op1), r, w)

    def CP(e, out, in_, r, w):
        if e == 'act':
            return T.op('act', lambda: nc.scalar.copy(out=out, in_=in_), r, w)
        return T.op(e, lambda: veng(e).tensor_copy(out=out, in_=in_), r, w)

    def MM(out, lhsT, rhs, start, stop, r, w, sig=None):
        if sig is None:
            sig = stop
        return T.op('pe', lambda: nc.tensor.matmul(out, lhsT=lhsT, rhs=rhs, start=start, stop=stop), r, w, sig=sig)

    def TR(out, in_, ident, r, w):
        return T.op('pe', lambda: nc.tensor.transpose(out, in_, ident), r, w)

    def DMA(e, out, in_, r, w, st):
        eng = {'sp': nc.sync, 'pool': nc.gpsimd, 'act': nc.scalar}[e]
        if e == 'pool':
            st = 'dw'
        return T.op(e, lambda: eng.dma_start(out=out, in_=in_), r, w, dma=st)

    def dump(name, ap, keys):
        if not dbg:
            return
        o = dout("dbg_" + name, list(ap.shape))
        dumps.append("dbg_" + name)
        DMA('pool' if ap.dtype != F32 else 'sp', o, ap, keys, (), 'do')

    def MS(e, ap, val, w):
        return T.op(e, lambda: veng(e).memset(ap, val), (), w)

    def sb(n, sh, dt=F32):
        return nc.alloc_sbuf_tensor(n, list(sh), dt)

    uniq = [0]

    def sbt(n, sh, dt=F32):
        uniq[0] += 1
        return nc.sbuf_tensor(f"{n}_u{uniq[0]}", list(sh), dt)

    xT = sb("xT", [128, 8, 1024])
    hT = sb("hT", [128, 8, 1024], BF16)
    NW = 4
    wpool = [sb(f"wp{i}", [128, 4096], BF16) for i in range(NW)]
    wstate = [0]
    ident_f = sb("ident_f", [128, 128])
    ident_b = sb("ident_b", [128, 128], BF16)
    prot_b = sb("prot_b", [128, 128], BF16)
    ones_b = sb("ones_b", [128, 128], BF16)
    maskF = sb("maskF", [128, 128])
    maskB = sb("maskB", [128, 128])
    sel = sb("sel", [8, 1024])
    dirm = sb("dirm", [8, 2])
    ropeC = sb("ropeC", [128, 1024])
    ropeS = sb("ropeS", [128, 1024])
    maskb = sb("maskb", [128, 20])
    link = sb("link", [128, 2])
    eps_t = sb("eps_t", [128, 1])
    one_t = sb("one_t", [128, 1])
    mhalf = sb("mhalf", [128, 4])
    CPm = sb("CPm", [128, L, 3, 128])
    gcol = sb("gcol", [128, 16])
    cond_b = sb("cond_b", [128, 8], BF16)
    modT = sb("modT", [128, 2, 48])
    scp = sb("scp", [128, 2, 16])
    lam = sb("lam", [128, L])
    nlam = sb("nlam", [128, L])
    GB = sb("GB", [8, L, 2])
    rstd = sb("rstd", [128, 1024])
    FS = [sb(f"fs{i}", [128, 1024]) for i in range(4)]
    fstate = [0]
    ones8 = sb("ones8", [8, 128])
    w0n = sb("w0n", [128, 2, 52])

    PA = nc.alloc_psum_tensor("PA", [128, 1024], F32)
    PB = nc.alloc_psum_tensor("PB", [128, 1024], F32)
    PC = nc.alloc_psum_tensor("PC", [128, 1024], F32)
    PDT = nc.alloc_psum_tensor("PDT", [128, 1024], F32)
    PD = PDT[:, 0:512]
    PT = PDT[:, 512:1024].bitcast(BF16)
    P2 = [PA, PB, PC, PDT]
    p2state = [0]

    def bank(i):
        if i < 6:
            return P2[i // 2][:, (i % 2) * 512:(i % 2 + 1) * 512]
        return PD[:, :]

    def ps(i):
        return ('ps', i)

    def fs_next():
        i = fstate[0] % 4
        fstate[0] += 1
        return FS[i], ('fs', i)

    def p2_next():
        i = p2state[0] % 4
        p2state[0] += 1
        return P2[i], [ps(2 * i), ps(2 * i + 1)]

    WT = WTinit

    def wmk(desc):
        (name, idx), r0, nk, c0, ncol = desc
        t = WT[name]
        for i in idx:
            t = t[i]
        return t[r0:r0 + nk * 128, c0:c0 + ncol].rearrange("(k p) n -> p k n", p=128)

    def w_issue(j):
        desc = wsched_n[j]
        nk, ncol = desc[2], desc[4]
        i = j % NW
        dst = wpool[i][:, 0:nk * ncol].rearrange("p (k n) -> p k n", k=nk)
        DMA('pool', dst, wmk(desc), (), [('w', i)], 'dw')

    def wload(w2d, r0, nk, c0, ncol):
        desc = (w2d, r0, nk, c0, ncol)
        k = wstate[0]
        wstate[0] += 1
        i = k % NW
        dst = wpool[i][:, 0:nk * ncol].rearrange("p (k n) -> p k n", k=nk)
        if wsched_n is None:
            wrec.append(desc)
            DMA('pool', dst, wmk(desc), (), [('w', i)], 'dw')
        else:
            assert wsched_n[k] == desc
            while wissued[0] <= min(k + LOOK, len(wsched_n) - 1):
                w_issue(wissued[0])
                wissued[0] += 1
        return dst, ('w', i)

    def wview(w2d, r0, nk, c0, ncol):
        return w2d[r0:r0 + nk * 128, c0:c0 + ncol].rearrange("(k p) n -> p k n", p=128)

    DMA('sp', ident_f[:], c_ident, (), ['ident_f'], 'di')
    DMA('pool', ident_b[:], c_ident, (), ['ident_b'], 'dw')
    DMA('pool', prot_b[:], c_prot, (), ['prot_b'], 'dw')
    DMA('sp', maskF[:], c_maskF, (), ['maskF'], 'di')
    DMA('sp', maskB[:], c_maskB, (), ['maskB'], 'di')
    DMA('sp', sel[:], c_sel, (), ['sel'], 'di')
    DMA('sp', dirm[:], c_dirm, (), ['dirm'], 'di')
    DMA('sp', ropeC[:], c_ropeC, (), ['ropeC'], 'di')
    DMA('sp', ropeS[:], c_ropeS, (), ['ropeS'], 'di')
    DMA('sp', maskb[:], c_maskb, (), ['maskb'], 'di')
    DMA('sp', link[:], c_link, (), ['link'], 'di')
    MS('dve', ones_b[:], 1.0, ['ones_b'])
    MS('dve', eps_t[:], EPS, ['eps_t'])
    MS('dve', one_t[:], 1.0, ['one_t'])
    MS('dve', mhalf[:], -0.5, ['mhalf'])
    MS('dve', ones8[:], 1.0, ['ones8'])
    for l in range(L):
        DMA('sp', GB[:, l, 0:1], ml_gate_b[l, 0:8, :], (), ['GB'], 'di')
        DMA('sp', GB[:, l, 1:2], ml_gate_b[l, 8:16, :], (), ['GB'], 'di')

    def colparams(rows_list, dst, dkey):
        st, sk = fs_next()
        r0 = 0
        for ap, R in rows_list:
            DMA('sp', st[r0:r0 + R, 0:128], ap, (), [sk], 'di')
            r0 += R
        TR(bank(6)[:, 0:r0], st[0:r0, 0:128], ident_f[0:r0, 0:r0], [sk, 'ident_f'], [ps(6)])
        CP('dve', dst[:, 0:r0], bank(6)[:, 0:r0], [ps(6)], [dkey])

    colparams([(final_g, 8), (cond, 8)], gcol[:, :], 'gcol')
    ACT(cond_b[:], gcol[:, 8:16], AF.Silu, ['gcol'], ['cond_b'])

    for tc in range(8):
        st, sk = fs_next()
        DMA('sp', st[:, :], xin[tc * 128:(tc + 1) * 128, :], (), [sk], 'di')
        for half in range(2):
            bk = half
            for j in range(4):
                dc = half * 4 + j
                TR(bank(bk)[:, j * 128:(j + 1) * 128], st[:, dc * 128:(dc + 1) * 128], ident_f[:], [sk, 'ident_f'], [ps(bk)])
            CP('dve' if half == 0 else 'act', xT[:, half * 4:half * 4 + 4, tc * 128:(tc + 1) * 128],
               bank(bk).rearrange("p (a b) -> p a b", a=4), [ps(bk)], [('x', half * 4 + j) for j in range(4)])

    for l in range(depth):
        colparams([(b_mod[l], 48), (ml_conv_w[l], 24), (ml_conv_b[l], 8), (ffn_conv_b[l], 44), (da_norm_g[l], 1)],
                  CPm[:, l, 0, :], 'CPm')
        colparams([(ffn_conv_w[l, 0:128, :], 128)], CPm[:, l, 1, :], 'CPm')
        colparams([(ffn_conv_w[l, 128:132, :], 4)], CPm[:, l, 2, :], 'CPm')

    def bmod_c(l):
        return CPm[:, l, 0, 0:48]

    def mlw_c(l, k, c):
        return CPm[:, l, 0, 48 + k * 8 + c:48 + k * 8 + c + 1]

    def mlb_c(l, c):
        return CPm[:, l, 0, 72 + c:73 + c]

    def ffb_c(l, c):
        return CPm[:, l, 0, 80 + c:81 + c]

    def dag_c(l):
        return CPm[:, l, 0, 124:125]

    def ffw_c(l, k, c):
        j = k * 44 + c
        if j < 128:
            return CPm[:, l, 1, j:j + 1]
        return CPm[:, l, 2, j - 128:j - 127]

    with ExitStack() as es:
        dl = es.enter_context(sbt("dl", [128, L * 256], F32))
        pr = es.enter_context(sbt("pr", [128, L * 128], F32))
        sm2 = es.enter_context(sbt("sm2", [128, L * 2], F32))
        DMA('sp', dl[:], da_lambda.partition_broadcast(128), (), ['dl'], 'di')
        dlv = dl[:].rearrange("p (l a b d) -> p l a b d", l=L, a=2, b=2)
        TT('dve', pr[:].rearrange("p (l a d) -> p l a d", l=L, a=2), dlv[:, :, :, 0, :], dlv[:, :, :, 1, :], ALU.mult,
           ['dl'], ['pr'])
        T.op('dve', lambda: nc.vector.tensor_reduce(out=sm2[:], in_=pr[:].rearrange("p (q d) -> p q d", d=64),
                                                     axis=AX.X, op=ALU.add), ['pr'], ['sm2'])
        ACT(sm2[:], sm2[:], AF.Exp, ['sm2'], ['sm2'])
        s2v = sm2[:].rearrange("p (l a) -> p l a", a=2)
        TT('dve', lam[:], s2v[:, :, 0], s2v[:, :, 1], ALU.subtract, ['sm2'], ['lam'])
        for l in range(L):
            li = 0.8 - 0.6 * math.exp(-0.3 * l)
            TS('dve', lam[:, l:l + 1], lam[:, l:l + 1], li, None, ALU.add, None, ['lam'], ['lam'])
        TS('dve', nlam[:], lam[:], -1.0, None, ALU.mult, None, ['lam'], ['nlam'])
        T.free(['dl', 'pr', 'sm2'])

    dump('xT0', xT[:, :, :], [('x', dc) for dc in range(8)])
    def compute_mod_gen(l):
        par = l % 2
        for g in range(12):
            w, wk = wload(('w_mod', (l,)), 0, 8, g * 512, 512)
            for j in range(4):
                col = g * 4 + j
                for kc in range(8):
                    MM(bank(6)[:, col:col + 1], w[:, kc, j * 128:(j + 1) * 128], cond_b[:, kc:kc + 1], kc == 0, kc == 7,
                       [wk, 'cond_b'], [ps(6)])
            yield
        TT('dve', modT[:, par, :], bank(6)[:, 0:48], bmod_c(l), ALU.add, [ps(6), 'CPm'], [('mod', par)])
        TS('dve', scp[:, par, 0:8], modT[:, par, 8:16], 1.0, None, ALU.add, None, [('mod', par)], [('scp', par)])
        TS('dve', scp[:, par, 8:16], modT[:, par, 32:40], 1.0, None, ALU.add, None, [('mod', par)], [('scp', par)])

    def compute_mod(l):
        for _ in compute_mod_gen(l):
            pass


    def rms_stats(src_chunks, rkeys, sq_dst, sqkeys, nfeat):
        n = len(src_chunks)
        for i in range(n):
            ACT(sq_dst[i], src_chunks[i], AF.Square, [rkeys[i]], [sqkeys[i]])
        for th in range(2):
            for i in range(n):
                MM(PA[:, th * 512:(th + 1) * 512], ones_b[:], sq_dst[i][:, th * 512:(th + 1) * 512], i == 0, i == n - 1,
                   [sqkeys[i], 'ones_b'], [ps(th)])
        ACT(rstd[:], PA[:, :], AF.Ln, [ps(0), ps(1), 'eps_t'], ['rstd'], bias=eps_t[:], scale=1.0 / nfeat)
        ACT(rstd[:], rstd[:], AF.Exp, ['rstd'], ['rstd'], scale=-0.5)

    def norm_mod(l, which):
        par = l % 2
        rms_stats([xT[:, dc, :] for dc in range(8)], [('x', dc) for dc in range(8)],
                  [hT[:, dc, :] for dc in range(8)], [('h', dc) for dc in range(8)], 1024.0)
        so = 0 if which == 0 else 24
        for dc in range(8):
            tmp, tk = fs_next()
            STT(tmp[:], xT[:, dc, :], scp[:, par, which * 8 + dc:which * 8 + dc + 1], rstd[:], ALU.mult, ALU.mult,
                [('x', dc), ('scp', par), 'rstd'], [tk])
            ACT(hT[:, dc, :], tmp[:], AF.Identity, [tk, ('mod', par)], [('h', dc)], bias=modT[:, par, so + dc:so + dc + 1])

    def dwconv(l, zps, zkeys, acc, akey, w0, w1, w2, bia, w0nn, w2nn):
        ACT(acc[:, :], zps[:, :], AF.Identity, zkeys + ['CPm'], [akey], bias=bia, scale=w1)
        STT(acc[:, 1:1024], zps[:, 0:1023], w0, acc[:, 1:1024], ALU.mult, ALU.add, zkeys + ['CPm', akey], [akey])
        STT(acc[:, 0:1023], zps[:, 1:1024], w2, acc[:, 0:1023], ALU.mult, ALU.add, zkeys + ['CPm', akey], [akey])
        STT(acc[:, 256:1024:256], zps[:, 255:1023:256], w0nn, acc[:, 256:1024:256], ALU.mult, ALU.add,
            zkeys + [('w0n', l % 2), akey], [akey])
        STT(acc[:, 255:1023:256], zps[:, 256:1024:256], w2nn, acc[:, 255:1023:256], ALU.mult, ALU.add,
            zkeys + [('w0n', l % 2), akey], [akey])

    def prep_w0n(l):
        par = l % 2
        TS('pool', w0n[:, par, 0:44], CPm[:, l, 1, 0:44], link[:, 1:2], -1.0, ALU.mult, ALU.mult, ['CPm', 'link'], [('w0n', par)])
        TS('pool', w2n[:, par, 0:40], CPm[:, l, 1, 88:128], link[:, 1:2], -1.0, ALU.mult, ALU.mult, ['CPm', 'link'], [('w0n', par)])
        TS('pool', w2n[:, par, 40:44], CPm[:, l, 2, 0:4], link[:, 1:2], -1.0, ALU.mult, ALU.mult, ['CPm', 'link'], [('w0n', par)])
        TS('pool', w0n[:, par, 44:52], CPm[:, l, 0, 48:56], link[:, 1:2], -1.0, ALU.mult, ALU.mult, ['CPm', 'link'], [('w0n', par)])
        TS('pool', w2n[:, par, 44:52], CPm[:, l, 0, 64:72], link[:, 1:2], -1.0, ALU.mult, ALU.mult, ['CPm', 'link'], [('w0n', par)])

    w2n = sb("w2n", [128, 2, 52])

    def ffn(l):
        par = l % 2
        with ExitStack() as es:
            act = es.enter_context(sbt("ffn_act", [128, 22, 1024], BF16))
            sa = es.enter_context(sbt("ffn_sa", [128, 4, 1024], F32))
            T.adopt([('act', j) for j in range(22)] + [('sa', j) for j in range(4)])
            norm_mod(l, 1)
            if l == 0:
                dump('mod0', modT[:, 0, :], [('mod', 0)])
                dump('h2', hT[:, :, :], [('h', dc) for dc in range(8)])
                dump('rstd', rstd[:, :], ['rstd'])
            for g in range(6):
                nchunk = 4 if g < 5 else 2
                for ab in range(2):
                    c0 = ab * DFF + g * 512
                    w, wk = wload(('w_up', (l,)), 0, 8, c0, nchunk * 128)
                    for j in range(nchunk):
                        cc = ab * 22 + g * 4 + j
                        zp, zk = p2_next()
                        for th in range(2):
                            for kc in range(8):
                                MM(zp[:, th * 512:(th + 1) * 512], w[:, kc, j * 128:(j + 1) * 128],
                                   hT[:, kc, th * 512:(th + 1) * 512], kc == 0, kc == 7, [wk, ('h', kc)], [zk[th]])
                        acc, ak = fs_next()
                        dwconv(l, zp, zk, acc, ak, ffw_c(l, 0, cc), ffw_c(l, 1, cc), ffw_c(l, 2, cc), ffb_c(l, cc),
                               w0n[:, par, cc:cc + 1],
                               w2n[:, par, cc:cc + 1])
                        if ab == 0:
                            ACT(sa[:, j, :], acc[:, :], AF.Silu, [ak], [('sa', j)])
                        else:
                            TT('pool', act[:, g * 4 + j, :], sa[:, j, :], acc[:, :], ALU.mult, [('sa', j), ak],
                               [('act', g * 4 + j)])
            if l == 0:
                dump('act', act[:, :, :], [('act', j) for j in range(22)])
            for jp in range(4):
                wA, wkA = wload(('w_down', (l,)), 0, 11, jp * 256, 256)
                wB, wkB = wload(('w_down', (l,)), 1408, 11, jp * 256, 256)
                for half, (w, wk) in enumerate(((wA, wkA), (wB, wkB))):
                    for dj in range(2):
                        for th in range(2):
                            bk = dj * 2 + th
                            for kk in range(11):
                                kc = half * 11 + kk
                                MM(bank(bk), w[:, kk, dj * 128:(dj + 1) * 128], act[:, kc, th * 512:(th + 1) * 512],
                                   half == 0 and kk == 0, half == 1 and kk == 10, [wk, ('act', kc)], [ps(bk)])
                for dj in range(2):
                    dc = jp * 2 + dj
                    for th in range(2):
                        bk = dj * 2 + th
                        STT(xT[:, dc, th * 512:(th + 1) * 512], bank(bk), modT[:, par, 40 + dc:41 + dc],
                            xT[:, dc, th * 512:(th + 1) * 512], ALU.mult, ALU.add, [ps(bk), ('mod', par), ('x', dc)],
                            [('x', dc)])
            T.free([('act', j) for j in range(22)] + [('sa', j) for j in range(4)])


    def run_pipeline(jobs, make_gen, nslots, extra=None, extra_every=1):
        active = []
        free_slots = list(range(nslots))
        nxt = 0
        step = 0
        while nxt < len(jobs) or active:
            if nxt < len(jobs) and free_slots:
                sl_ = free_slots.pop(0)
                g = make_gen(jobs[nxt], sl_)
                nxt += 1
                next(g)
                active.append((g, sl_, True))
            still = []
            for (g, sl_, fresh) in active:
                if fresh:
                    still.append((g, sl_, False))
                    continue
                try:
                    next(g)
                    still.append((g, sl_, False))
                except StopIteration:
                    free_slots.append(sl_)
            active = still
            step += 1
            if extra is not None and extra[0] is not None and step % extra_every == 0:
                try:
                    next(extra[0])
                except StopIteration:
                    extra[0] = None
        if extra is not None and extra[0] is not None:
            for _ in extra[0]:
                pass

    def sg_phase(l, ysgT):
        wl = w_in[l]
        with ExitStack() as es:
            uT = es.enter_context(sbt("sg_uT", [128, 4, 1024], BF16))
            sgwT = es.enter_context(sbt("sg_wT", [128, 4, 128], BF16))
            sgwf = es.enter_context(sbt("sg_wf", [128, 4, 128], F32))
            gb = es.enter_context(sbt("sg_gb", [128, 2, 512], F32))
            zz = es.enter_context(sbt("sg_zz", [128, 4, 512], F32))
            svb = es.enter_context(sbt("sg_svb", [128, 4, 512], BF16))
            st6 = es.enter_context(sbt("sg_st", [128, 4, 8], F32))
            keys = [('sg_u', j) for j in range(4)] + ['sg_wT', 'sg_wf', 'sg_gb'] + [(n_, b_) for n_ in ('sg_zz', 'sg_svb', 'sg_st') for b_ in range(4)]
            T.adopt(keys)
            DMA('sp', gb[:, 0, :], sg_norm_g[l].partition_broadcast(128), (), ['sg_gb'], 'di')
            DMA('sp', gb[:, 1, :], sg_b[l].partition_broadcast(128), (), ['sg_gb'], 'di')
            DMA('sp', sgwf[:, :, :], sg_w[l].rearrange("g p q -> p g q"), (), ['sg_wf'], 'di')
            for g in range(4):
                TR(bank(6)[:, g * 128:(g + 1) * 128], sgwf[:, g, :], ident_f[:], ['sg_wf', 'ident_f'], [ps(6)])
            CP('dve', sgwT[:, :, :], bank(6).rearrange("p (g q) -> p g q", g=4), [ps(6)], ['sg_wT'])
            w, wk = wload(('w_in', (l,)), 0, 8, 3600, 512)
            for j in range(4):
                zp, zk = p2_next()
                for th in range(2):
                    for kc in range(8):
                        MM(zp[:, th * 512:(th + 1) * 512], w[:, kc, j * 128:(j + 1) * 128], hT[:, kc, th * 512:(th + 1) * 512],
                           kc == 0, kc == 7, [wk, ('h', kc)], [zk[th]])
                ACT(uT[:, j, :], zp[:, :], AF.Gelu_apprx_tanh, zk, [('sg_u', j)])
            w, wk = wload(('w_in', (l,)), 0, 8, 4112, 512)

            def sg_job(tc, b):
                bk = b
                gbk = 4 + b % 2
                for kc in range(8):
                    MM(bank(bk), hT[:, kc, tc * 128:(tc + 1) * 128], w[:, kc, :], kc == 0, kc == 7, [('h', kc), wk], [ps(bk)])
                yield
                ACT(zz[:, b, :], bank(bk), AF.Gelu_apprx_tanh, [ps(bk)], [('sg_zz', b)])
                yield
                T.op('dve', lambda: nc.vector.bn_stats(out=st6[:, b, 0:6], in_=zz[:, b, :]), [('sg_zz', b)], [('sg_st', b)])
                T.op('dve', lambda: nc.vector.bn_aggr(out=st6[:, b, 6:8], in_=st6[:, b, 0:6]), [('sg_st', b)], [('sg_st', b)])
                yield
                TS('pool', st6[:, b, 7:8], st6[:, b, 7:8], EPS, None, ALU.add, None, [('sg_st', b)], [('sg_st', b)])
                TT('pool', st6[:, b, 7:8], st6[:, b, 7:8], mhalf[:, 0:1], ALU.pow, [('sg_st', b), 'mhalf'], [('sg_st', b)])
                yield
                TS('dve', zz[:, b, :], zz[:, b, :], st6[:, b, 6:7], st6[:, b, 7:8], ALU.subtract, ALU.mult,
                   [('sg_zz', b), ('sg_st', b)], [('sg_zz', b)])
                yield
                TT('pool', svb[:, b, :], zz[:, b, :], gb[:, 0, :], ALU.mult, [('sg_zz', b), 'sg_gb'], [('sg_svb', b)])
                yield
                for g in range(4):
                    MM(bank(gbk)[:, g * 128:(g + 1) * 128], svb[:, b, g * 128:(g + 1) * 128], sgwT[:, g, :], True, True,
                       [('sg_svb', b), 'sg_wT'], [ps(gbk)])
                yield
                tmp, tk = fs_next()
                TT('dve', tmp[:, 0:512], bank(gbk), gb[:, 1, :], ALU.add, [ps(gbk), 'sg_gb'], [tk])
                yield
                TT('pool', ysgT[:, :, tc * 128:(tc + 1) * 128], tmp[:, 0:512].rearrange("p (g t) -> p g t", g=4),
                   uT[:, :, tc * 128:(tc + 1) * 128], ALU.mult, [tk] + [('sg_u', j) for j in range(4)],
                   [('ysg', j) for j in range(4)])

            run_pipeline(list(range(8)), sg_job, 4)
            T.free(keys)

    def merge(l, ys):
        par = l % 2
        ynames = ['yda', 'yml', 'ysg']
        with ExitStack() as es:
            mg = es.enter_context(sbt("mg", [128, 8, 1024], BF16))
            accf = es.enter_context(sbt("mg_acc", [128, 4, 1024], F32))
            keys = [('mg', dc) for dc in range(8)] + [('mg_acc', j) for j in range(4)]
            T.adopt(keys)
            for dcg in range(2):
                for n in range(3):
                    w, wk = wload(('w_in', (l,)), 0, 8, 4624 + n * 1024 + dcg * 512, 512)
                    wb, wbk = wload(('w_branch', (l, n)), 0, 4, 0, 1024)
                    for j in range(4):
                        dc = dcg * 4 + j
                        gp, gk = p2_next()
                        for th in range(2):
                            for kc in range(8):
                                MM(gp[:, th * 512:(th + 1) * 512], w[:, kc, j * 128:(j + 1) * 128],
                                   hT[:, kc, th * 512:(th + 1) * 512], kc == 0, kc == 7, [wk, ('h', kc)], [gk[th]])
                        pp, pk = p2_next()
                        for th in range(2):
                            for kc in range(4):
                                MM(pp[:, th * 512:(th + 1) * 512], wb[:, kc, dc * 128:(dc + 1) * 128],
                                   ys[n][:, kc, th * 512:(th + 1) * 512], kc == 0, kc == 3, [wbk, (ynames[n], kc)], [pk[th]])
                        sgt, sk = fs_next()
                        ACT(sgt[:, :], gp[:, :], AF.Sigmoid, gk, [sk])
                        if n == 0:
                            TT('dve', accf[:, j, :], pp[:, :], sgt[:, :], ALU.mult, pk + [sk], [('mg_acc', j)])
                        else:
                            t2, t2k = fs_next()
                            TT('dve', t2[:, :], pp[:, :], sgt[:, :], ALU.mult, pk + [sk], [t2k])
                            if n == 1:
                                TT('pool', accf[:, j, :], accf[:, j, :], t2[:, :], ALU.add, [('mg_acc', j), t2k], [('mg_acc', j)])
                            else:
                                TT('pool', mg[:, dc, :], accf[:, j, :], t2[:, :], ALU.add, [('mg_acc', j), t2k], [('mg', dc)])
            for og_ in range(2):
                w, wk = wload(('w_out', (l,)), 0, 8, og_ * 512, 512)
                for j in range(4):
                    dc = og_ * 4 + j
                    zp, zk = p2_next()
                    for th in range(2):
                        for kc in range(8):
                            MM(zp[:, th * 512:(th + 1) * 512], w[:, kc, j * 128:(j + 1) * 128], mg[:, kc, th * 512:(th + 1) * 512],
                               kc == 0, kc == 7, [wk, ('mg', kc)], [zk[th]])
                    for th in range(2):
                        STT(xT[:, dc, th * 512:(th + 1) * 512], zp[:, th * 512:(th + 1) * 512], modT[:, par, 16 + dc:17 + dc],
                            xT[:, dc, th * 512:(th + 1) * 512], ALU.mult, ALU.add, [zk[th], ('mod', par), ('x', dc)], [('x', dc)])
            T.free(keys)

    def da_phase(l, ydaT):
        wl = w_in[l]
        with ExitStack() as es:
            KT = es.enter_context(sbt("da_KT", [128, 4, 1280], BF16))
            qT = es.enter_context(sbt("da_qT", [128, 4, 1024], BF16))
            V = es.enter_context(sbt("da_V", [128, 10, 512], BF16))
            oall = es.enter_context(sbt("da_oall", [128, 4, 1024], F32))
            ET = es.enter_context(sbt("da_ET", [128, 3, 512], BF16))
            kbf = es.enter_context(sbt("da_kbf", [128, 2, 1024], BF16))
            ckf = es.enter_context(sbt("da_ckf", [128, 2, 128], F32))
            osb = es.enter_context(sbt("da_osb", [128, 2, 256], F32))
            rs = es.enter_context(sbt("da_rs", [128, 2, 256], F32))
            vst = es.enter_context(sbt("da_vst", [128, 2, 512], F32))
            dagl = es.enter_context(sbt("da_gl", [128, 1], F32))
            keys = ([('KT', h) for h in range(4)] + [('qT', h) for h in range(4)] + [('V', c) for c in range(10)] +
                    [('oall', h) for h in range(4)] + [('ET', i) for i in range(3)] + [('kbf', 0), ('kbf', 1), 'ckf', ('osb', 0), ('osb', 1),
                                                                                        ('rs', 0), ('rs', 1), ('vst', 0), ('vst', 1), 'dagl'])
            T.adopt(keys)
            TS('dve', dagl[:, :], dag_c(l), 1.0 - (0.8 - 0.6 * math.exp(-0.3 * l)), None, ALU.mult, None, ['CPm'], ['dagl'])
            for hd in range(4):
                DMA('pool', V[:, 0:2, hd * 128:(hd + 1) * 128], cv[l, hd].rearrange("(c p) d -> p c d", p=128), (),
                    [('V', 0), ('V', 1)], 'dw')
            vck = vst[:, :, :].rearrange("p a (h d) -> p (a h) d", h=2)
            for hd in range(4):
                DMA('sp', vck[:, hd, :].rearrange("p (c d) -> p c d", c=2), ck[l, hd].rearrange("(c p) d -> p c d", p=128), (),
                    [('vst', hd // 2)], 'di')
            for hp in range(2):
                for hh in range(2):
                    hd = hp * 2 + hh
                    for c in range(2):
                        TR(bank(6)[:, (hh * 2 + c) * 128:(hh * 2 + c + 1) * 128], vck[:, hd, c * 128:(c + 1) * 128], ident_f[:],
                           [('vst', hp), 'ident_f'], [ps(6)])
                CP('dve', KT[:, hp * 2:hp * 2 + 2, 0:256], bank(6).rearrange("p (h k) -> p h k", h=2), [ps(6)],
                   [('KT', hp * 2), ('KT', hp * 2 + 1)])
            wqk = {}

            def proj_job(job, slot):
                which, hd = job
                if hd == 0:
                    c0 = 512 if which == 0 else 0
                    wqk[which] = wload(('w_in', (l,)), 0, 8, c0, 512)
                w, wk = wqk[which]
                zp, zk = P2[slot], [ps(2 * slot), ps(2 * slot + 1)]
                for th in range(2):
                    for kc in range(8):
                        MM(zp[:, th * 512:(th + 1) * 512], w[:, kc, hd * 128:(hd + 1) * 128], hT[:, kc, th * 512:(th + 1) * 512],
                           kc == 0, kc == 7, [wk, ('h', kc)], [zk[th]])
                yield
                CP('act', kbf[:, slot, :], zp[:, :], zk, [('kbf', slot)])
                yield
                sp_, spk = PC, [ps(4), ps(5)]
                for th in range(2):
                    MM(sp_[:, th * 512:(th + 1) * 512], prot_b[:], kbf[:, slot, th * 512:(th + 1) * 512], True, True,
                       ['prot_b', ('kbf', slot)], [spk[th]])
                yield
                t1, t1k = fs_next()
                t2, t2k = fs_next()
                TT('dve', t1[:, :], zp[:, :], ropeC[:, :], ALU.mult, zk + ['ropeC'], [t1k])
                TT('dve', t2[:, :], sp_[:, :], ropeS[:, :], ALU.mult, spk + ['ropeS'], [t2k])
                yield
                if which == 0:
                    TT('pool', t1[:, :], t1[:, :], t2[:, :], ALU.add, [t1k, t2k], [t1k])
                    CP('pool', KT[:, hd, 256:1280], t1[:, :], [t1k], [('KT', hd)])
                    yield
                    for tcg in range(2):
                        for j in range(4):
                            tc = tcg * 4 + j
                            TR(bank(6)[:, j * 128:(j + 1) * 128], t1[:, tc * 128:(tc + 1) * 128], ident_f[:],
                               [t1k, 'ident_f'], [ps(6)])
                        CP('act', vst[:, tcg, :], bank(6), [ps(6)], [('vst', tcg)])
                        DMA('sp', ok_o[l, hd, tcg * 512:(tcg + 1) * 512, :].rearrange("(j p) d -> p j d", p=128),
                            vst[:, tcg, :].rearrange("p (j d) -> p j d", j=4), [('vst', tcg)], (), 'do')
                else:
                    TT('pool', qT[:, hd, :], t1[:, :], t2[:, :], ALU.add, [t1k, t2k], [('qT', hd)])

            run_pipeline([(which, hd) for which in range(2) for hd in range(4)], proj_job, 2)
            w, wk = wload(('w_in', (l,)), 0, 8, 1024, 512)
            for tc in range(8):
                bk = 4 + tc % 2
                for kc in range(8):
                    MM(bank(bk), hT[:, kc, tc * 128:(tc + 1) * 128], w[:, kc, :], kc == 0, kc == 7, [('h', kc), wk], [ps(bk)])
                CP('act', V[:, 2 + tc, :], bank(bk), [ps(bk)], [('V', 2 + tc)])
                CP('dve', vst[:, tc % 2, :], bank(bk), [ps(bk)], [('vst', tc % 2)])
                DMA('sp', ov_o[l, :, tc * 128:(tc + 1) * 128, :].rearrange("h p d -> p h d"),
                    vst[:, tc % 2, :].rearrange("p (h d) -> p h d", h=4), [('vst', tc % 2)], (), 'do')
            iters = [(hd, qt, kc) for hd in range(4) for qt in range(4) for kc in range(10)]
            rsf = rs[:, :, :].rearrange("p a b -> p (a b)")
            osf = osb[:, :, :].rearrange("p a b -> p (a b)")

            PT32 = PT[:, :].bitcast(F32)
            accb = [(bank(4), bank(5), ps(4), ps(5)), (bank(6), PT32, ps(6), ps(7))]
            SR = [(PA, ps(0), ps(1)), (PB, ps(2), ps(3))]

            def emit_qk(i):
                hd, qt, kc = iters[i]
                reg, k0, k1 = SR[i % 2]
                for half in range(2):
                    lo, hi = half * 64, half * 64 + 64
                    MM(reg[:, half * 512:half * 512 + 256], KT[lo:hi, hd, kc * 128:(kc + 1) * 128],
                       qT[lo:hi, hd, qt * 256:(qt + 1) * 256], True, True, [('KT', hd), ('qT', hd)], [k0 if half == 0 else k1])

            def emit_rest(i):
                hd, qt, kc = iters[i]
                reg, k0, k1 = SR[i % 2]
                sbk = i % 3
                ao, as_, ko, ks = accb[(hd * 4 + qt) % 2]
                ACT(ET[:, sbk, :].rearrange("p (a b) -> p a b", a=2), reg[:, :].rearrange("p (a b) -> p a b", a=2)[:, :, 0:256],
                    AF.Exp, [k0, k1, 'maskb'], [('ET', sbk)],
                    bias=maskb[:, (kc // 2) * 4 + qt:(kc // 2) * 4 + qt + 1], scale=0.125)
                MM(ao, V[:, kc, hd * 128:(hd + 1) * 128], ET[:, sbk, :], kc == 0, kc == 9, [('V', kc), ('ET', sbk)], [ko])
                MM(as_, ones_b[:], ET[:, sbk, :], kc == 0, kc == 9, ['ones_b', ('ET', sbk)], [ks])
                if kc == 9:
                    T.op('dve', lambda: nc.vector.reciprocal(out=rsf, in_=as_), [ks], [('rs', 0), ('rs', 1)])
                    TT('dve', osf, ao, rsf, ALU.mult, [ko, ('rs', 0), ('rs', 1)], [('osb', 0), ('osb', 1)])
                    STT(oall[:, hd, qt * 256:(qt + 1) * 256], osb[:, 1, :], nlam[:, l:l + 1], osb[:, 0, :], ALU.mult, ALU.add,
                        [('osb', 0), ('osb', 1), 'nlam'], [('oall', hd)])

            emit_qk(0)
            for i in range(len(iters)):
                if i + 1 < len(iters):
                    emit_qk(i + 1)
                emit_rest(i)
            def danorm_job(hd, slot):
                reg, k0, k1 = SR[slot]
                ACT(ydaT[:, hd, :], oall[:, hd, :], AF.Square, [('oall', hd)], [('yda', hd)])
                yield
                for th in range(2):
                    MM(reg[:, th * 512:(th + 1) * 512], ones_b[:], ydaT[:, hd, th * 512:(th + 1) * 512], True, True,
                       [('yda', hd), 'ones_b'], [k0 if th == 0 else k1])
                yield
                rs_, rsk = fs_next()
                ACT(rs_[:, :], reg[:, :], AF.Ln, [k0, k1, 'eps_t'], [rsk], bias=eps_t[:], scale=1.0 / 128.0)
                ACT(rs_[:, :], rs_[:, :], AF.Exp, [rsk], [rsk], scale=-0.5)
                yield
                STT(ydaT[:, hd, :], oall[:, hd, :], dagl[:, 0:1], rs_[:, :], ALU.mult, ALU.mult, [('oall', hd), 'dagl', rsk],
                    [('yda', hd)])

            run_pipeline(list(range(4)), danorm_job, 2)
            T.free(keys)

    def ml_phase(l, ymlT):
        par = l % 2
        wl = w_in[l]
        LK = (2, 4, 6)
        with ExitStack() as es:
            COLS = es.enter_context(sbt("ml_cols", [128, 8, 4, 8], F32))
            DECB = es.enter_context(sbt("ml_decb", [128, 8, 8], F32))
            MP = es.enter_context(sbt("ml_mp", [8, 16], F32))
            MPL = es.enter_context(sbt("ml_mpl", [8, 16], F32))
            NM = es.enter_context(sbt("ml_nm", [8, 16], F32))
            DT = es.enter_context(sbt("ml_dt", [8, 16], F32))
            DEC = es.enter_context(sbt("ml_dec", [8, 16], F32))
            k1 = ['ml_cols', 'ml_decb', 'ml_mp', 'ml_mpl', 'ml_nm', 'ml_dt', 'ml_dec']
            T.adopt(k1)
            with ExitStack() as es2:
                Rr = [es2.enter_context(sbt(f"ml_r{i}", [8, 1024], F32)) for i in range(9)]
                rk = [('ml_r', i) for i in range(9)]
                T.adopt(rk)
                R0, R1, R2, R3, R4, R5, R6, R7a, R7b = Rr
                w, wk = wload(('w_in', (l,)), 0, 8, 3584, 16)
                for (pp, c0, kk) in ((PA, 0, [ps(0), ps(1)]), (PB, 8, [ps(2), ps(3)])):
                    for th in range(2):
                        for kc in range(8):
                            MM(pp[0:8, th * 512:(th + 1) * 512], w[:, kc, c0:c0 + 8], hT[:, kc, th * 512:(th + 1) * 512],
                               kc == 0, kc == 7, [wk, ('h', kc)], [kk[th]])
                for (pp, kk, dst, dk, bcol) in ((PA, [ps(0), ps(1)], R0, rk[0], 0), (PB, [ps(2), ps(3)], R1, rk[1], 1)):
                    ACT(R4[:, :], pp[0:8, :], AF.Identity, kk + ['GB'], [rk[4]], bias=GB[:, l, bcol:bcol + 1])
                    ACT(R5[:, :], pp[0:8, ::-1], AF.Identity, kk + ['GB'], [rk[5]], bias=GB[:, l, bcol:bcol + 1])
                    TS('dve', dst[:, :], R4[:, :], dirm[:, 0:1], None, ALU.mult, None, [rk[4], 'dirm'], [dk])
                    STT(dst[:, :], R5[:, :], dirm[:, 1:2], dst[:, :], ALU.mult, ALU.add, [rk[5], 'dirm', dk], [dk])
                ACT(R1[:, :], R1[:, :], AF.Exp, [rk[1]], [rk[1]], scale=-1.0)
                ACT(R1[:, :], R1[:, :], AF.Ln, [rk[1], 'one_t'], [rk[1]], bias=one_t[0:8, :], scale=1.0)
                for c in range(8):
                    sl = slice(c * 128, (c + 1) * 128)
                    T.op('dve', lambda: nc.vector.tensor_tensor_scan(out=R2[:, sl], data0=ones8[:, :], data1=R1[:, sl], initial=0.0,
                                                                      op0=ALU.mult, op1=ALU.add), [rk[1], 'ones8'], [rk[2]])
                TT('dve', R0[:, :], R0[:, :], R2[:, :], ALU.add, [rk[0], rk[2]], [rk[0]])
                for c in range(8):
                    sl = slice(c * 128, (c + 1) * 128)
                    T.op('dve', lambda: nc.vector.tensor_tensor_scan(out=R3[:, sl], data0=R0[:, sl], data1=R0[:, sl], initial=-1e30,
                                                                      op0=ALU.max, op1=ALU.max), [rk[0]], [rk[3]])
                DMA('sp', MP[:, 0:1], sm[l], (), ['ml_mp'], 'di')
                for cs in range(8):
                    sl = slice(cs * 128, (cs + 1) * 128)
                    la = cs * 128 + 127
                    mprev = MP[:, cs:cs + 1]
                    mpk = 'ml_mp'
                    if cs in LK:
                        TS('dve', MPL[:, cs:cs + 1], mprev, link[0:8, 0:1], None, ALU.mult, None, ['ml_mp', 'link'], ['ml_mpl'])
                        mprev = MPL[:, cs:cs + 1]
                        mpk = 'ml_mpl'
                    TS('dve', R3[:, sl], R3[:, sl], mprev, None, ALU.max, None, [rk[3], mpk], [rk[3]])
                    TT('dve', MP[:, cs + 1:cs + 2], R3[:, la:la + 1], R2[:, la:la + 1], ALU.subtract, [rk[3], rk[2]], ['ml_mp'])
                    ACT(R5[:, sl], R3[:, sl], AF.Exp, [rk[3], mpk], [rk[5]], bias=mprev, scale=-1.0)
                    if cs in LK:
                        TS('dve', R5[:, sl], R5[:, sl], link[0:8, 0:1], None, ALU.mult, None, [rk[5], 'link'], [rk[5]])
                    ACT(R4[:, sl], R3[:, sl], AF.Exp, [rk[3]], [rk[4]], bias=R3[:, la:la + 1], scale=-1.0)
                    TS('dve', NM[:, cs:cs + 1], R3[:, la:la + 1], -1.0, None, ALU.mult, None, [rk[3]], ['ml_nm'])
                    ACT(R0[:, sl], R0[:, sl], AF.Exp, [rk[0], 'ml_nm'], [rk[0]], bias=NM[:, cs:cs + 1], scale=1.0)
                    TT('dve', DT[:, cs:cs + 1], mprev, R2[:, la:la + 1], ALU.subtract, [mpk, rk[2]], ['ml_dt'])
                    TT('dve', DT[:, cs:cs + 1], DT[:, cs:cs + 1], MP[:, cs + 1:cs + 2], ALU.subtract, ['ml_dt', 'ml_mp'], ['ml_dt'])
                    ACT(DEC[:, cs:cs + 1], DT[:, cs:cs + 1], AF.Exp, ['ml_dt'], ['ml_dec'])
                    if cs in LK:
                        TS('dve', DEC[:, cs:cs + 1], DEC[:, cs:cs + 1], link[0:8, 0:1], None, ALU.mult, None, ['ml_dec', 'link'],
                           ['ml_dec'])
                TT('dve', R2[:, :], R2[:, :], R3[:, :], ALU.subtract, [rk[2], rk[3]], [rk[2]])
                ACT(R2[:, :], R2[:, :], AF.Exp, [rk[2]], [rk[2]])
                CP('dve', MPL[:, 12:16], MP[:, 2:10:2], ['ml_mp', 'ml_mpl'], ['ml_mpl'])
                DMA('sp', om_o[l], MPL[:, 12:16], ['ml_mpl'], (), 'do')
                for qi, (Q, qk) in enumerate(((R0, rk[0]), (R4, rk[4]), (R5, rk[5]), (R2, rk[2]))):
                    Ro, rok = (R7a, rk[7]) if qi % 2 == 0 else (R7b, rk[8])
                    TS('dve', R6[:, :], Q[:, :], dirm[:, 0:1], None, ALU.mult, None, [qk, 'dirm'], [rk[6]])
                    STT(Ro[:, :], Q[:, ::-1], dirm[:, 1:2], R6[:, :], ALU.mult, ALU.add, [rk[6], 'dirm', qk], [rok])
                    for c in range(8):
                        o0 = (c * 4 + qi) * 8
                        TR(bank(6)[:, o0:o0 + 8], Ro[:, c * 128:(c + 1) * 128], ident_f[0:8, 0:8], [rok, 'ident_f'], [ps(6)])
                CP('dve', COLS[:, :, :, :].rearrange("p a b c -> p (a b c)"), bank(6)[:, 0:256], [ps(6)], ['ml_cols'])
                for r in range(8):
                    MM(bank(5)[:, r * 8:(r + 1) * 8], sel[0:8, r * 128:(r + 1) * 128], DEC[0:8, 0:8], True, True, ['sel', 'ml_dec'],
                       [ps(5)])
                CP('dve', DECB[:, :, :].rearrange("p a b -> p (a b)"), bank(5)[:, 0:64], [ps(5)], ['ml_decb'])
                T.free(rk)
            hsum = es.enter_context(sbt("ml_hs", [128, 8, 512], F32))
            Cn32 = es.enter_context(sbt("ml_cn", [128, 8, 130], F32))
            T.adopt([('hs', c) for c in range(8)] + [('cn', r) for r in range(8)])
            es3 = ExitStack()
            mqT = es3.enter_context(sbt("ml_qT", [128, 4, 1024], BF16))
            mkT = es3.enter_context(sbt("ml_kT", [128, 4, 1024], BF16))
            mva = es3.enter_context(sbt("ml_va", [128, 8, 4, 130], BF16))
            Cnb = es3.enter_context(sbt("ml_cnb", [128, 8, 130], BF16))
            PTs = es3.enter_context(sbt("ml_pts", [128, 6, 128], BF16))
            kg = es3.enter_context(sbt("ml_kg", [128, 6, 128], BF16))
            vg = es3.enter_context(sbt("ml_vg", [128, 6, 130], BF16))
            tB = es3.enter_context(sbt("ml_tb", [128, 6, 130], F32))
            nd = es3.enter_context(sbt("ml_nd", [128, 6, 130], F32))
            dn = es3.enter_context(sbt("ml_dn", [128, 6, 2], F32))
            k2 = ([('mqT', h) for h in range(4)] + [('mkT', h) for h in range(4)] + [('mva', c) for c in range(8)] +
                  [('cnb', r) for r in range(8)] +
                  [(nm_, b) for nm_ in ('pts', 'kg', 'vg', 'tb', 'nd', 'dn') for b in range(6)])
            T.adopt(k2)
            for dr in range(2):
                for hd in range(4):
                    r = dr * 4 + hd
                    DMA('sp', Cn32[:, r, 0:129], sCn[l, dr, hd], (), [('cn', r)], 'di')
                    CP('pool', Cnb[:, r, 0:129], Cn32[:, r, 0:129], [('cn', r)], [('cnb', r)])
            for which, c0 in ((0, 1536), (1, 2048)):
                w, wk = wload(('w_in', (l,)), 0, 8, c0, 512)
                for hd in range(4):
                    zp, zk = p2_next()
                    for th in range(2):
                        for kc in range(8):
                            MM(zp[:, th * 512:(th + 1) * 512], w[:, kc, hd * 128:(hd + 1) * 128], hT[:, kc, th * 512:(th + 1) * 512],
                               kc == 0, kc == 7, [wk, ('h', kc)], [zk[th]])
                    ch = which * 4 + hd
                    acc, ak = fs_next()
                    dwconv(l, zp, zk, acc, ak, mlw_c(l, 0, ch), mlw_c(l, 1, ch), mlw_c(l, 2, ch), mlb_c(l, ch),
                           w0n[:, par, 44 + ch:45 + ch], w2n[:, par, 44 + ch:45 + ch])
                    if which == 0:
                        ACT(mqT[:, hd, :], acc[:, :], AF.Silu, [ak], [('mqT', hd)])
                    else:
                        sgt, sk = fs_next()
                        ACT(sgt[:, :], acc[:, :], AF.Sigmoid, [ak], [sk])
                        STT(mkT[:, hd, :], acc[:, :], 128.0 ** -0.5, sgt[:, :], ALU.mult, ALU.mult, [ak, sk], [('mkT', hd)])
            w, wk = wload(('w_in', (l,)), 0, 8, 2560, 512)
            MS('pool', mva[:, :, :, 128:130], 1.0, [('mva', c) for c in range(8)])
            for tc in range(8):
                bk = 4 + tc % 2
                for kc in range(8):
                    MM(bank(bk), hT[:, kc, tc * 128:(tc + 1) * 128], w[:, kc, :], kc == 0, kc == 7, [('h', kc), wk], [ps(bk)])
                CP('act', mva[:, tc, :, 0:128], bank(bk).rearrange("p (h d) -> p h d", h=4), [ps(bk)], [('mva', tc)])
            def core_iter(cs, hd, dr, slot):
                c = cs if dr == 0 else 7 - cs
                r = dr * 4 + hd
                tsl = slice(c * 128, (c + 1) * 128)
                mask = maskF if dr == 0 else maskB
                mkey = 'maskF' if dr == 0 else 'maskB'
                pb = bank(slot)
                pk_ = ps(slot)
                Sps, Aps, Bps, CNps = pb[:, 0:128], pb[:, 130:259], pb[:, 260:389], pb[:, 0:129]
                MM(Sps, mkT[:, hd, tsl], mqT[:, hd, tsl], True, True, [('mkT', hd), ('mqT', hd)], [pk_])
                TR(PT[:, slot * 128:(slot + 1) * 128], mkT[:, hd, tsl], ident_b[:], [('mkT', hd), 'ident_b'], [ps(7)])
                yield
                TT('dve', PTs[:, slot, :], Sps, mask[:, :], ALU.mult, [pk_, mkey], [('pts', slot)])
                ACT(kg[:, slot, :], PT[:, slot * 128:(slot + 1) * 128], AF.Identity, [ps(7), 'ml_cols'], [('kg', slot)],
                    scale=COLS[:, c, 0, r:r + 1])
                ACT(vg[:, slot, :], mva[:, c, hd, :], AF.Identity, [('mva', c), 'ml_cols'], [('vg', slot)],
                    scale=COLS[:, c, 0, r:r + 1])
                yield
                MM(Aps, PTs[:, slot, :], vg[:, slot, 0:129], True, True, [('pts', slot), ('vg', slot)], [pk_])
                MM(Bps, mqT[:, hd, tsl], Cnb[:, r, 0:129], True, True, [('mqT', hd), ('cnb', r)], [pk_])
                yield
                ACT(tB[:, slot, 0:129], Bps, AF.Identity, [pk_, 'ml_cols'], [('tb', slot)], scale=COLS[:, c, 2, r:r + 1])
                STT(nd[:, slot, 0:129], Aps, COLS[:, c, 1, r:r + 1], tB[:, slot, 0:129], ALU.mult, ALU.add,
                    [pk_, 'ml_cols', ('tb', slot)], [('nd', slot)])
                STT(dn[:, slot, 0:1], nd[:, slot, 128:129], -1.0, nd[:, slot, 128:129], ALU.mult, ALU.max, [('nd', slot)],
                    [('dn', slot)])
                TS('dve', dn[:, slot, 0:1], dn[:, slot, 0:1], COLS[:, c, 3, r:r + 1], None, ALU.max, None,
                   [('dn', slot), 'ml_cols'], [('dn', slot)])
                T.op('dve', lambda: nc.vector.reciprocal(out=dn[:, slot, 1:2], in_=dn[:, slot, 0:1]), [('dn', slot)], [('dn', slot)])
                hs = hsum[:, c, hd * 128:(hd + 1) * 128]
                if cs < 4:
                    TS('dve', hs, nd[:, slot, 0:128], dn[:, slot, 1:2], None, ALU.mult, None, [('nd', slot), ('dn', slot)],
                       [('hs', c)])
                else:
                    STT(hs, nd[:, slot, 0:128], dn[:, slot, 1:2], hs, ALU.mult, ALU.add, [('nd', slot), ('dn', slot), ('hs', c)],
                        [('hs', c)])
                yield
                MM(CNps, kg[:, slot, :], mva[:, c, hd, 0:129], True, True, [('kg', slot), ('mva', c)], [pk_])
                yield
                STT(Cn32[:, r, 0:129], Cn32[:, r, 0:129], DECB[:, r, cs:cs + 1], CNps, ALU.mult, ALU.add,
                    [('cn', r), 'ml_decb', pk_], [('cn', r)])
                CP('pool', Cnb[:, r, 0:129], Cn32[:, r, 0:129], [('cn', r)], [('cnb', r)])
                if cs % 2 == 1:
                    DMA('sp', oCn_o[l, dr, cs // 2, hd], Cn32[:, r, 0:129], [('cn', r)], (), 'do')

            todo = [(cs, hd, dr) for cs in range(8) for hd in range(4) for dr in range(2)]
            modgen = [compute_mod_gen(l + 1) if l + 1 < depth else None]
            run_pipeline(todo, lambda job, sl_: core_iter(job[0], job[1], job[2], sl_), 6, modgen, 5)
            if l == 0:
                dump('hs', hsum[:, :, :], [('hs', c) for c in range(8)])
                dump('cols', COLS[:, :, :, :], ['ml_cols'])
            T.free(k2)
            es3.close()
            og = es.enter_context(sbt("ml_og", [128, 8, 512], BF16))
            gml4 = es.enter_context(sbt("ml_g4", [128, 512], F32))
            ssq = es.enter_context(sbt("ml_ssq", [128, 4, 4], F32))
            ytm = es.enter_context(sbt("ml_ytm", [128, 4, 512], BF16))
            k3 = [('og', c) for c in range(8)] + ['gml4'] + [(n_, b_) for n_ in ('ssq', 'ytm') for b_ in range(4)]
            T.adopt(k3)
            for h in range(4):
                DMA('sp', gml4[:, h * 128:(h + 1) * 128], ml_norm_g[l].partition_broadcast(128), (), ['gml4'], 'di')
            w, wk = wload(('w_in', (l,)), 0, 8, 3072, 512)

            def mlout_job(c, slot):
                bk = slot
                for kc in range(8):
                    MM(bank(bk), hT[:, kc, c * 128:(c + 1) * 128], w[:, kc, :], kc == 0, kc == 7, [('h', kc), wk], [ps(bk)])
                yield
                tmp, tk = fs_next()
                ACT(tmp[:, 0:512], bank(bk), AF.Sigmoid, [ps(bk)], [tk])
                ACT(tmp[:, 512:1024], hsum[:, c, :], AF.Square, [('hs', c)], [tk])
                yield
                TT('pool', og[:, c, :], tmp[:, 0:512], gml4[:, :], ALU.mult, [tk, 'gml4'], [('og', c)])
                T.op('dve', lambda: nc.vector.tensor_reduce(out=ssq[:, slot, 0:4],
                                                             in_=tmp[:, 512:1024].rearrange("p (h d) -> p h d", h=4),
                                                             axis=AX.X, op=ALU.add), [tk], [('ssq', slot)])
                yield
                TS('pool', ssq[:, slot, 0:4], ssq[:, slot, 0:4], 1.0 / 128.0, EPS, ALU.mult, ALU.add, [('ssq', slot)], [('ssq', slot)])
                TT('pool', ssq[:, slot, 0:4], ssq[:, slot, 0:4], mhalf[:, 0:4], ALU.pow, [('ssq', slot), 'mhalf'], [('ssq', slot)])
                yield
                for hd in range(4):
                    hsl = slice(hd * 128, (hd + 1) * 128)
                    STT(ytm[:, slot, hsl], hsum[:, c, hsl], ssq[:, slot, hd:hd + 1], og[:, c, hsl], ALU.mult, ALU.mult,
                        [('hs', c), ('ssq', slot), ('og', c)], [('ytm', slot)])
                yield
                po = (slot % 2) * 512
                for hd in range(4):
                    hsl = slice(hd * 128, (hd + 1) * 128)
                    TR(PT[:, po + hd * 128:po + (hd + 1) * 128], ytm[:, slot, hsl], ident_b[:], [('ytm', slot), 'ident_b'], [ps(7)])
                yield
                CP('act', ymlT[:, :, c * 128:(c + 1) * 128], PT[:, po:po + 512].rearrange("p (h t) -> p h t", h=4), [ps(7)],
                   [('yml', h) for h in range(4)])

            run_pipeline(list(range(8)), mlout_job, 4)
            T.free(k1 + k3 + [('hs', c) for c in range(8)] + [('cn', r) for r in range(8)])

    def mixer(l):
        norm_mod(l, 0)
        with ExitStack() as es:
            ydaT = es.enter_context(sbt("ydaT", [128, 4, 1024], BF16))
            ymlT = es.enter_context(sbt("ymlT", [128, 4, 1024], BF16))
            ysgT = es.enter_context(sbt("ysgT", [128, 4, 1024], BF16))
            yk = [(n, h) for n in ('yda', 'yml', 'ysg') for h in range(4)]
            T.adopt(yk)
            if 'ml' in phases:
                ml_phase(l, ymlT)
            else:
                MS('pool', ymlT[:, :, :], 0.0, [('yml', h) for h in range(4)])
            if 'da' in phases:
                da_phase(l, ydaT)
            else:
                MS('pool', ydaT[:, :, :], 0.0, [('yda', h) for h in range(4)])
            if 'sg' in phases:
                sg_phase(l, ysgT)
            else:
                MS('pool', ysgT[:, :, :], 0.0, [('ysg', h) for h in range(4)])
            if l == 0:
                dump('yda', ydaT[:, :, :], [('yda', h) for h in range(4)])
                dump('yml', ymlT[:, :, :], [('yml', h) for h in range(4)])
                dump('ysg', ysgT[:, :, :], [('ysg', h) for h in range(4)])
            merge(l, [ydaT, ymlT, ysgT])
            T.free(yk)

    compute_mod(0)
    for l in range(depth):
        prep_w0n(l)
        if l + 1 < depth and 'ml' not in phases:
            compute_mod(l + 1)
        if any(p in phases for p in ('ml', 'da', 'sg')):
            mixer(l)
        if 'ffn' in phases:
            ffn(l)

    with ExitStack() as es:
        yT = es.enter_context(sbt("yT", [128, 8, 1024], F32))
        T.adopt([('yT', dc) for dc in range(8)])
        rms_stats([xT[:, dc, :] for dc in range(8)], [('x', dc) for dc in range(8)],
                  [hT[:, dc, :] for dc in range(8)], [('h', dc) for dc in range(8)], 1024.0)
        for dc in range(8):
            STT(yT[:, dc, :], xT[:, dc, :], gcol[:, dc:dc + 1], rstd[:], ALU.mult, ALU.mult, [('x', dc), 'gcol', 'rstd'],
                [('yT', dc)])
        for tc in range(8):
            st, sk = fs_next()
            for half in range(2):
                bk = 2 + half
                for j in range(4):
                    dc = half * 4 + j
                    TR(bank(bk)[:, j * 128:(j + 1) * 128], yT[:, dc, tc * 128:(tc + 1) * 128], ident_f[:],
                       [('yT', dc), 'ident_f'], [ps(bk)])
                CP('dve' if half == 0 else 'act', st[:, half * 512:(half + 1) * 512], bank(bk), [ps(bk)], [sk])
            DMA('sp', y_o[tc * 128:(tc + 1) * 128, :], st[:, :], [sk], (), 'do')
        T.free([('yT', dc) for dc in range(8)])
    T.finish()
    return nc, T, dumps, wrec


def _consts():
    ident = np.eye(128, dtype=np.float32)
    prot = np.zeros((128, 128), np.float32)
    for m in range(128):
        prot[m ^ 16, m] = 1.0
    s = np.arange(128)[:, None]
    t = np.arange(128)[None, :]
    maskF = (s <= t).astype(np.float32)
    maskB = (s >= t).astype(np.float32)
    sel = np.zeros((8, 8, 128), np.float32)
    for r in range(8):
        sel[r, r, :] = 1.0
    dirm = np.zeros((8, 2), np.float32)
    dirm[0:4, 0] = 1.0
    dirm[4:8, 1] = 1.0
    return dict(c_ident=ident, c_prot=prot, c_maskF=maskF, c_maskB=maskB, c_sel=sel.reshape(8, 1024), c_dirm=dirm)


def _rope_tables():
    t = np.arange(1024)
    row = (t // 64).astype(np.float32)
    col = (t % 64).astype(np.float32)
    nf = 16
    inv = (10000.0 ** (-np.arange(nf, dtype=np.float32) / nf)).astype(np.float32)
    ang = np.stack([row[:, None] * inv, col[:, None] * inv], axis=1)
    cos = np.cos(ang).astype(np.float32)
    sin = np.sin(ang).astype(np.float32)
    C = np.zeros((128, 1024), np.float32)
    S = np.zeros((128, 1024), np.float32)
    for d in range(128):
        axis = (d >> 5) & 1
        half = (d >> 4) & 1
        f = d & 15
        C[d] = cos[:, axis, f]
        S[d] = sin[:, axis, f] * (-1.0 if half == 0 else 1.0)
    return C, S


_CACHE = {}


def kernel(x_prompt, x_sample, c, cache_k, cache_v, state_C, state_n, state_m, c_ctx,
           w_mod, b_mod, w_in, da_lambda, da_norm_g, ml_conv_w, ml_conv_b, ml_gate_b,
           ml_norm_g, sg_norm_g, sg_w, sg_b, w_branch, w_out, w_up, ffn_conv_w, ffn_conv_b,
           w_down, final_g, _depth=L, _phases=('ml', 'da', 'sg', 'ffn'), _dbg=False):
    f = lambda a: np.ascontiguousarray(np.asarray(a, dtype=np.float32))
    key = (_depth, tuple(_phases), _dbg)
    if key not in _CACHE:
        rec = build(_depth, _phases, _dbg is True)[3]
        _CACHE[key] = build(_depth, _phases, _dbg is True, rec)
    nc, T, dumps, _ = _CACHE[key]
    dp = _depth
    consts = _consts()
    rC, rS = _rope_tables()
    shared = dict(consts)
    shared.update(
        w_mod=f(w_mod[:dp]), b_mod=f(b_mod).reshape(L, 48, 128), w_in=f(w_in[:dp]), da_lambda=f(da_lambda).reshape(1, L * 256),
        da_norm_g=f(da_norm_g).reshape(L, 1, 128), ml_conv_w=f(ml_conv_w).reshape(L, 24, 128),
        ml_conv_b=f(ml_conv_b).reshape(L, 8, 128), ml_gate_b=f(ml_gate_b).reshape(L, 16, 1),
        ml_norm_g=f(ml_norm_g).reshape(L, 1, 128), sg_norm_g=f(sg_norm_g).reshape(L, 1, 512), sg_w=f(sg_w),
        sg_b=f(sg_b).reshape(L, 1, 512), w_branch=f(w_branch[:dp]), w_out=f(w_out[:dp]), w_up=f(w_up[:dp]),
        ffn_conv_w=f(ffn_conv_w).reshape(L, 132, 128), ffn_conv_b=f(ffn_conv_b).reshape(L, 44, 128), w_down=f(w_down[:dp]),
        final_g=f(final_g).reshape(8, 128))
    x_prompt = f(x_prompt)
    x_sample = f(x_sample)
    in_maps = []
    for core in range(8):
        m = dict(shared)
        if core < 4:
            b = core
            m['xin'] = x_sample[b]
            m['cond'] = f(c)[b].reshape(8, 128)
            m['ck'] = f(cache_k)[b]
            m['cv'] = f(cache_v)[b]
            m['sCn'] = np.ascontiguousarray(np.concatenate([f(state_C)[b], f(state_n)[b][..., None]], axis=-1))
            m['sm'] = f(state_m)[b].reshape(L, 8, 1)
            m['c_ropeC'] = rC
            m['c_ropeS'] = rS
            m['c_maskb'] = np.zeros((128, 20), np.float32)
            lk = np.zeros((128, 2), np.float32)
            lk[:, 0] = 1.0
            m['c_link'] = lk
        else:
            j = core - 4
            m['xin'] = x_prompt[4 * j:4 * j + 4].reshape(1024, 1024)
            m['cond'] = f(c_ctx).reshape(8, 128)
            m['ck'] = np.zeros((L, 4, 256, 128), np.float32)
            m['cv'] = np.zeros((L, 4, 256, 128), np.float32)
            m['sCn'] = np.zeros((L, 2, 4, 128, 129), np.float32)
            m['sm'] = np.zeros((L, 8, 1), np.float32)
            m['c_ropeC'] = np.ones((128, 1024), np.float32)
            m['c_ropeS'] = np.zeros((128, 1024), np.float32)
            mb = np.full((5, 4), -30000.0, np.float32)
            for qt in range(4):
                mb[1 + qt, qt] = 0.0
            m['c_maskb'] = np.tile(mb.reshape(1, 20), (128, 1))
            lk = np.zeros((128, 2), np.float32)
            lk[:, 1] = 1.0
            m['c_link'] = lk
        in_maps.append(m)
    if _dbg == 'maps':
        return nc, in_maps
    if _dbg == 'time':
        res = run_bass_kernel_spmd(nc, in_maps, core_ids=list(range(8)), trace=True)
        return res.exec_time_ns
    res = run_bass_kernel_spmd(nc, in_maps, core_ids=list(range(8)))
    R = res.results
    if _dbg:
        kernel.dbg = [{n: R[c][n] for n in dumps} for c in range(8)]
    y_sample = np.stack([R[b]['y_o'] for b in range(4)], axis=0)
    y_prompt = np.concatenate([R[4 + j]['y_o'].reshape(4, 256, 1024) for j in range(4)], axis=0)
    nk = np.concatenate([R[4 + j]['ok_o'].reshape(L, 4, 4, 256, 128).transpose(2, 0, 1, 3, 4) for j in range(4)], axis=0)
    nv = np.concatenate([R[4 + j]['ov_o'].reshape(L, 4, 4, 256, 128).transpose(2, 0, 1, 3, 4) for j in range(4)], axis=0)
    nC, nn, nm = [], [], []
    for j in range(4):
        oCn = R[4 + j]['oCn_o']
        oC = oCn[..., 0:128]
        on = oCn[..., 128]
        om = R[4 + j]['om_o'].reshape(L, 2, 4, 4)
        for sq in range(4):
            nC.append(np.stack([oC[:, 0, sq], oC[:, 1, 3 - sq]], axis=1))
            nn.append(np.stack([on[:, 0, sq], on[:, 1, 3 - sq]], axis=1))
            nm.append(np.stack([om[:, 0, :, sq], om[:, 1, :, 3 - sq]], axis=1))
    return (y_prompt.astype(np.float32), y_sample.astype(np.float32), np.ascontiguousarray(nk), np.ascontiguousarray(nv),
            np.stack(nC, axis=0), np.stack(nn, axis=0), np.stack(nm, axis=0))
```

```python
import math
from contextlib import ExitStack
import numpy as np
import concourse.bass as bass
import concourse.mybir as mybir
from concourse.bass_utils import run_bass_kernel_spmd

F32 = mybir.dt.float32
BF16 = mybir.dt.bfloat16
AF = mybir.ActivationFunctionType
ALU = mybir.AluOpType
AX = mybir.AxisListType

L = 4
NIN = 7696
DFF = 2816
EPS = 1e-6


class Tr:
    EP = 30000
    NSEM = 8

    def __init__(s, nc):
        s.nc = nc
        s.E = dict(pe=nc.tensor, act=nc.scalar, dve=nc.vector, pool=nc.gpsimd, sp=nc.sync)
        s.sems = {}
        s.cnt = {}
        s.known = {e: {} for e in s.E}
        s.res = {}
        s.grave = {}
        s.nwait = 0
        s.nops = 0
        s.dcount = {}
        s.dlast = {}

    def _sem(s, st, c):
        mult = 1 if st in s.E else 16
        ep = s.EP // mult
        e = (c - 1) // ep
        lst = s.sems.setdefault(st, [])
        while len(lst) <= e:
            nm = st if isinstance(st, str) else f"{st[0]}{st[1]}"
            lst.append(s.nc.alloc_semaphore(f"s_{nm}_{len(lst)}"))
        return lst[e], ((c - 1) % ep + 1) * mult

    def op(s, eng, fn, reads=(), writes=(), sig=True, dma=None):
        deps = {}

        def add(ev):
            if ev is None:
                return
            st = ev[0]
            if st == 'pe' and eng == 'pe' and dma is None:
                return
            if st not in deps or deps[st][1] < ev[1]:
                deps[st] = ev

        for k in reads:
            r = s.res.get(k)
            if r is None:
                continue
            add(r[0])
            if k[0] == 'ps':
                for ev in r[1].values():
                    add(ev)
        for k in writes:
            r = s.res.get(k)
            if r is None:
                continue
            add(r[0])
            for ev in r[1].values():
                add(ev)
        if dma is not None:
            i = s.dcount.get(dma, 0)
            s.dcount[dma] = i + 1
            dma = (dma, i % s.NSEM)
            add(s.dlast.get(dma))
        kn = s.known[eng]
        for st in sorted(deps, key=lambda a: -deps[a][1]):
            _, c, clk = deps[st]
            if kn.get(st, 0) >= c:
                continue
            if st == 'pe':
                assert s.cnt.get('pe', 0) >= c, "dependency on unsignalled PE op"
            sem, v = s._sem(st, c)
            s.E[eng].wait_ge(sem, v)
            s.nwait += 1
            kn[st] = c
            for a, b in clk.items():
                if kn.get(a, 0) < b:
                    kn[a] = b
        ins = fn()
        s.nops += 1
        st = dma or eng
        c = s.cnt.get(st, 0) + 1
        if sig:
            s.cnt[st] = c
            sem, v = s._sem(st, c)
            ins.then_inc(sem, 16 if dma else 1)
        clk = dict(kn)
        clk[st] = c
        ev = (st, c, clk)
        if dma is not None:
            s.dlast[dma] = ev
        for k in writes:
            s.res[k] = [ev, {}]
        for k in reads:
            r = s.res.setdefault(k, [None, {}])
            r[1][st] = ev
        return ins

    def free(s, keys):
        for k in keys:
            r = s.res.pop(k, None)
            if r is None:
                continue
            evs = list(r[1].values())
            if r[0] is not None:
                evs.append(r[0])
            for ev in evs:
                st = ev[0]
                if st not in s.grave or s.grave[st][1] < ev[1]:
                    s.grave[st] = ev

    def adopt(s, keys):
        for k in keys:
            s.res[k] = [None, dict(s.grave)]

    def finish(s, eng='sp'):
        for st, c in s.cnt.items():
            if st in s.E:
                continue
            ep = s.EP // 16
            for e in range((c - 1) // ep + 1 if c > 0 else 0):
                last = min(c, (e + 1) * ep)
                sem, v = s._sem(st, last)
                s.E[eng].wait_ge(sem, v)


def build(depth=L, phases=('ml', 'da', 'sg', 'ffn'), dbg=False, wsched_n=None):
    nc = bass.Bass("TRN2", target_bir_lowering=False)
    T = Tr(nc)
    LW = depth
    dumps = []
    wrec = []
    wissued = [0]
    LOOK = 2

    def din(n, sh):
        return nc.dram_tensor(n, list(sh), F32, kind="ExternalInput").ap()

    def dout(n, sh):
        return nc.dram_tensor(n, list(sh), F32, kind="ExternalOutput").ap()

    xin = din("xin", [1024, 1024])
    cond = din("cond", [8, 128])
    ck = din("ck", [L, 4, 256, 128])
    cv = din("cv", [L, 4, 256, 128])
    sCn = din("sCn", [L, 2, 4, 128, 129])
    sm = din("sm", [L, 8, 1])
    c_ident = din("c_ident", [128, 128])
    c_prot = din("c_prot", [128, 128])
    c_maskF = din("c_maskF", [128, 128])
    c_maskB = din("c_maskB", [128, 128])
    c_sel = din("c_sel", [8, 1024])
    c_dirm = din("c_dirm", [8, 2])
    c_ropeC = din("c_ropeC", [128, 1024])
    c_ropeS = din("c_ropeS", [128, 1024])
    c_maskb = din("c_maskb", [128, 20])
    c_link = din("c_link", [128, 2])
    w_mod = din("w_mod", [LW, 1024, 6144])
    b_mod = din("b_mod", [L, 48, 128])
    w_in = din("w_in", [LW, 1024, NIN])
    da_lambda = din("da_lambda", [1, L * 256])
    da_norm_g = din("da_norm_g", [L, 1, 128])
    ml_conv_w = din("ml_conv_w", [L, 24, 128])
    ml_conv_b = din("ml_conv_b", [L, 8, 128])
    ml_gate_b = din("ml_gate_b", [L, 16, 1])
    ml_norm_g = din("ml_norm_g", [L, 1, 128])
    sg_norm_g = din("sg_norm_g", [L, 1, 512])
    sg_w = din("sg_w", [L, 4, 128, 128])
    sg_b = din("sg_b", [L, 1, 512])
    w_branch = din("w_branch", [LW, 3, 512, 1024])
    w_out = din("w_out", [LW, 1024, 1024])
    w_up = din("w_up", [LW, 1024, 2 * DFF])
    ffn_conv_w = din("ffn_conv_w", [L, 132, 128])
    ffn_conv_b = din("ffn_conv_b", [L, 44, 128])
    w_down = din("w_down", [LW, DFF, 1024])
    final_g = din("final_g", [8, 128])

    y_o = dout("y_o", [1024, 1024])
    ok_o = dout("ok_o", [L, 4, 1024, 128])
    ov_o = dout("ov_o", [L, 4, 1024, 128])
    oCn_o = dout("oCn_o", [L, 2, 4, 4, 128, 129])
    om_o = dout("om_o", [L, 8, 4])

    WTinit = dict(w_in=w_in, w_mod=w_mod, w_up=w_up, w_down=w_down, w_out=w_out, w_branch=w_branch)
    def veng(e):
        return nc.vector if e == 'dve' else nc.gpsimd

    def ACT(out, in_, func, r, w, bias=None, scale=None):
        kw = {}
        if bias is not None:
            kw['bias'] = bias
        if scale is not None:
            kw['scale'] = scale
        return T.op('act', lambda: nc.scalar.activation(out=out, in_=in_, func=func, **kw), r, w)

    def TT(e, out, a, b, op, r, w):
        return T.op(e, lambda: veng(e).tensor_tensor(out=out, in0=a, in1=b, op=op), r, w)

    def TS(e, out, a, s1, s2, op0, op1, r, w):
        if op1 is None:
            return T.op(e, lambda: veng(e).tensor_scalar(out=out, in0=a, scalar1=s1, scalar2=None, op0=op0), r, w)
        return T.op(e, lambda: veng(e).tensor_scalar(out=out, in0=a, scalar1=s1, scalar2=s2, op0=op0, op1=op1), r, w)

    def STT(out, a, s, b, op0, op1, r, w):
        return T.op('dve', lambda: nc.vector.scalar_tensor_tensor(out=out, in0=a, scalar=s, in1=b, op0=op0, op1=op1), r, w)

    def CP(e, out, in_, r, w):
        if e == 'act':
            return T.op('act', lambda: nc.scalar.copy(out=out, in_=in_), r, w)
        return T.op(e, lambda: veng(e).tensor_copy(out=out, in_=in_), r, w)

    def MM(out, lhsT, rhs, start, stop, r, w, sig=None):
        if sig is None:
            sig = stop
        return T.op('pe', lambda: nc.tensor.matmul(out, lhsT=lhsT, rhs=rhs, start=start, stop=stop), r, w, sig=sig)

    def TR(out, in_, ident, r, w):
        return T.op('pe', lambda: nc.tensor.transpose(out, in_, ident), r, w)

    def DMA(e, out, in_, r, w, st):
        eng = {'sp': nc.sync, 'pool': nc.gpsimd, 'act': nc.scalar}[e]
        if e == 'pool':
            st = 'dw'
        return T.op(e, lambda: eng.dma_start(out=out, in_=in_), r, w, dma=st)

    def dump(name, ap, keys):
        if not dbg:
            return
        o = dout("dbg_" + name, list(ap.shape))
        dumps.append("dbg_" + name)
        DMA('pool' if ap.dtype != F32 else 'sp', o, ap, keys, (), 'do')

    def MS(e, ap, val, w):
        return T.op(e, lambda: veng(e).memset(ap, val), (), w)

    def sb(n, sh, dt=F32):
        return nc.alloc_sbuf_tensor(n, list(sh), dt)

    uniq = [0]

    def sbt(n, sh, dt=F32):
        uniq[0] += 1
        return nc.sbuf_tensor(f"{n}_u{uniq[0]}", list(sh), dt)

    xT = sb("xT", [128, 8, 1024])
    hT = sb("hT", [128, 8, 1024], BF16)
    NW = 4
    wpool = [sb(f"wp{i}", [128, 4096], BF16) for i in range(NW)]
    wstate = [0]
    ident_f = sb("ident_f", [128, 128])
    ident_b = sb("ident_b", [128, 128], BF16)
    prot_b = sb("prot_b", [128, 128], BF16)
    ones_b = sb("ones_b", [128, 128], BF16)
    maskF = sb("maskF", [128, 128])
    maskB = sb("maskB", [128, 128])
    sel = sb("sel", [8, 1024])
    dirm = sb("dirm", [8, 2])
    ropeC = sb("ropeC", [128, 1024])
    ropeS = sb("ropeS", [128, 1024])
    maskb = sb("maskb", [128, 20])
    link = sb("link", [128, 2])
    eps_t = sb("eps_t", [128, 1])
    one_t = sb("one_t", [128, 1])
    mhalf = sb("mhalf", [128, 4])
    CPm = sb("CPm", [128, L, 3, 128])
    gcol = sb("gcol", [128, 16])
    cond_b = sb("cond_b", [128, 8], BF16)
    modT = sb("modT", [128, 2, 48])
    scp = sb("scp", [128, 2, 16])
    lam = sb("lam", [128, L])
    nlam = sb("nlam", [128, L])
    GB = sb("GB", [8, L, 2])
    rstd = sb("rstd", [128, 1024])
    FS = [sb(f"fs{i}", [128, 1024]) for i in range(4)]
    fstate = [0]
    ones8 = sb("ones8", [8, 128])
    w0n = sb("w0n", [128, 2, 52])

    PA = nc.alloc_psum_tensor("PA", [128, 1024], F32)
    PB = nc.alloc_psum_tensor("PB", [128, 1024], F32)
    PC = nc.alloc_psum_tensor("PC", [128, 1024], F32)
    PDT = nc.alloc_psum_tensor("PDT", [128, 1024], F32)
    PD = PDT[:, 0:512]
    PT = PDT[:, 512:1024].bitcast(BF16)
    P2 = [PA, PB, PC, PDT]
    p2state = [0]

    def bank(i):
        if i < 6:
            return P2[i // 2][:, (i % 2) * 512:(i % 2 + 1) * 512]
        return PD[:, :]

    def ps(i):
        return ('ps', i)

    def fs_next():
        i = fstate[0] % 4
        fstate[0] += 1
        return FS[i], ('fs', i)

    def p2_next():
        i = p2state[0] % 4
        p2state[0] += 1
        return P2[i], [ps(2 * i), ps(2 * i + 1)]

    WT = WTinit

    def wmk(desc):
        (name, idx), r0, nk, c0, ncol = desc
        t = WT[name]
        for i in idx:
            t = t[i]
        return t[r0:r0 + nk * 128, c0:c0 + ncol].rearrange("(k p) n -> p k n", p=128)

    def w_issue(j):
        desc = wsched_n[j]
        nk, ncol = desc[2], desc[4]
        i = j % NW
        dst = wpool[i][:, 0:nk * ncol].rearrange("p (k n) -> p k n", k=nk)
        DMA('pool', dst, wmk(desc), (), [('w', i)], 'dw')

    def wload(w2d, r0, nk, c0, ncol):
        desc = (w2d, r0, nk, c0, ncol)
        k = wstate[0]
        wstate[0] += 1
        i = k % NW
        dst = wpool[i][:, 0:nk * ncol].rearrange("p (k n) -> p k n", k=nk)
        if wsched_n is None:
            wrec.append(desc)
            DMA('pool', dst, wmk(desc), (), [('w', i)], 'dw')
        else:
            assert wsched_n[k] == desc
            while wissued[0] <= min(k + LOOK, len(wsched_n) - 1):
                w_issue(wissued[0])
                wissued[0] += 1
        return dst, ('w', i)

    def wview(w2d, r0, nk, c0, ncol):
        return w2d[r0:r0 + nk * 128, c0:c0 + ncol].rearrange("(k p) n -> p k n", p=128)

    DMA('sp', ident_f[:], c_ident, (), ['ident_f'], 'di')
    DMA('pool', ident_b[:], c_ident, (), ['ident_b'], 'dw')
    DMA('pool', prot_b[:], c_prot, (), ['prot_b'], 'dw')
    DMA('sp', maskF[:], c_maskF, (), ['maskF'], 'di')
    DMA('sp', maskB[:], c_maskB, (), ['maskB'], 'di')
    DMA('sp', sel[:], c_sel, (), ['sel'], 'di')
    DMA('sp', dirm[:], c_dirm, (), ['dirm'], 'di')
    DMA('sp', ropeC[:], c_ropeC, (), ['ropeC'], 'di')
    DMA('sp', ropeS[:], c_ropeS, (), ['ropeS'], 'di')
    DMA('sp', maskb[:], c_maskb, (), ['maskb'], 'di')
    DMA('sp', link[:], c_link, (), ['link'], 'di')
    MS('dve', ones_b[:], 1.0, ['ones_b'])
    MS('dve', eps_t[:], EPS, ['eps_t'])
    MS('dve', one_t[:], 1.0, ['one_t'])
    MS('dve', mhalf[:], -0.5, ['mhalf'])
    MS('dve', ones8[:], 1.0, ['ones8'])
    for l in range(L):
        DMA('sp', GB[:, l, 0:1], ml_gate_b[l, 0:8, :], (), ['GB'], 'di')
        DMA('sp', GB[:, l, 1:2], ml_gate_b[l, 8:16, :], (), ['GB'], 'di')

    def colparams(rows_list, dst, dkey):
        st, sk = fs_next()
        r0 = 0
        for ap, R in rows_list:
            DMA('sp', st[r0:r0 + R, 0:128], ap, (), [sk], 'di')
            r0 += R
        TR(bank(6)[:, 0:r0], st[0:r0, 0:128], ident_f[0:r0, 0:r0], [sk, 'ident_f'], [ps(6)])
        CP('dve', dst[:, 0:r0], bank(6)[:, 0:r0], [ps(6)], [dkey])

    colparams([(final_g, 8), (cond, 8)], gcol[:, :], 'gcol')
    ACT(cond_b[:], gcol[:, 8:16], AF.Silu, ['gcol'], ['cond_b'])

    for tc in range(8):
        st, sk = fs_next()
        DMA('sp', st[:, :], xin[tc * 128:(tc + 1) * 128, :], (), [sk], 'di')
        for half in range(2):
            bk = half
            for j in range(4):
                dc = half * 4 + j
                TR(bank(bk)[:, j * 128:(j + 1) * 128], st[:, dc * 128:(dc + 1) * 128], ident_f[:], [sk, 'ident_f'], [ps(bk)])
            CP('dve' if half == 0 else 'act', xT[:, half * 4:half * 4 + 4, tc * 128:(tc + 1) * 128],
               bank(bk).rearrange("p (a b) -> p a b", a=4), [ps(bk)], [('x', half * 4 + j) for j in range(4)])

    for l in range(depth):
        colparams([(b_mod[l], 48), (ml_conv_w[l], 24), (ml_conv_b[l], 8), (ffn_conv_b[l], 44), (da_norm_g[l], 1)],
                  CPm[:, l, 0, :], 'CPm')
        colparams([(ffn_conv_w[l, 0:128, :], 128)], CPm[:, l, 1, :], 'CPm')
        colparams([(ffn_conv_w[l, 128:132, :], 4)], CPm[:, l, 2, :], 'CPm')

    def bmod_c(l):
        return CPm[:, l, 0, 0:48]

    def mlw_c(l, k, c):
        return CPm[:, l, 0, 48 + k * 8 + c:48 + k * 8 + c + 1]

    def mlb_c(l, c):
        return CPm[:, l, 0, 72 + c:73 + c]

    def ffb_c(l, c):
        return CPm[:, l, 0, 80 + c:81 + c]

    def dag_c(l):
        return CPm[:, l, 0, 124:125]

    def ffw_c(l, k, c):
        j = k * 44 + c
        if j < 128:
            return CPm[:, l, 1, j:j + 1]
        return CPm[:, l, 2, j - 128:j - 127]

    with ExitStack() as es:
        dl = es.enter_context(sbt("dl", [128, L * 256], F32))
        pr = es.enter_context(sbt("pr", [128, L * 128], F32))
        sm2 = es.enter_context(sbt("sm2", [128, L * 2], F32))
        DMA('sp', dl[:], da_lambda.partition_broadcast(128), (), ['dl'], 'di')
        dlv = dl[:].rearrange("p (l a b d) -> p l a b d", l=L, a=2, b=2)
        TT('dve', pr[:].rearrange("p (l a d) -> p l a d", l=L, a=2), dlv[:, :, :, 0, :], dlv[:, :, :, 1, :], ALU.mult,
           ['dl'], ['pr'])
        T.op('dve', lambda: nc.vector.tensor_reduce(out=sm2[:], in_=pr[:].rearrange("p (q d) -> p q d", d=64),
                                                     axis=AX.X, op=ALU.add), ['pr'], ['sm2'])
        ACT(sm2[:], sm2[:], AF.Exp, ['sm2'], ['sm2'])
        s2v = sm2[:].rearrange("p (l a) -> p l a", a=2)
        TT('dve', lam[:], s2v[:, :, 0], s2v[:, :, 1], ALU.subtract, ['sm2'], ['lam'])
        for l in range(L):
            li = 0.8 - 0.6 * math.exp(-0.3 * l)
            TS('dve', lam[:, l:l + 1], lam[:, l:l + 1], li, None, ALU.add, None, ['lam'], ['lam'])
        TS('dve', nlam[:], lam[:], -1.0, None, ALU.mult, None, ['lam'], ['nlam'])
        T.free(['dl', 'pr', 'sm2'])

    dump('xT0', xT[:, :, :], [('x', dc) for dc in range(8)])
    def compute_mod_gen(l):
        par = l % 2
        for g in range(12):
            w, wk = wload(('w_mod', (l,)), 0, 8, g * 512, 512)
            for j in range(4):
                col = g * 4 + j
                for kc in range(8):
                    MM(bank(6)[:, col:col + 1], w[:, kc, j * 128:(j + 1) * 128], cond_b[:, kc:kc + 1], kc == 0, kc == 7,
                       [wk, 'cond_b'], [ps(6)])
            yield
        TT('dve', modT[:, par, :], bank(6)[:, 0:48], bmod_c(l), ALU.add, [ps(6), 'CPm'], [('mod', par)])
        TS('dve', scp[:, par, 0:8], modT[:, par, 8:16], 1.0, None, ALU.add, None, [('mod', par)], [('scp', par)])
        TS('dve', scp[:, par, 8:16], modT[:, par, 32:40], 1.0, None, ALU.add, None, [('mod', par)], [('scp', par)])

    def compute_mod(l):
        for _ in compute_mod_gen(l):
            pass


    def rms_stats(src_chunks, rkeys, sq_dst, sqkeys, nfeat):
        n = len(src_chunks)
        for i in range(n):
            ACT(sq_dst[i], src_chunks[i], AF.Square, [rkeys[i]], [sqkeys[i]])
        for th in range(2):
            for i in range(n):
                MM(PA[:, th * 512:(th + 1) * 512], ones_b[:], sq_dst[i][:, th * 512:(th + 1) * 512], i == 0, i == n - 1,
                   [sqkeys[i], 'ones_b'], [ps(th)])
        ACT(rstd[:], PA[:, :], AF.Ln, [ps(0), ps(1), 'eps_t'], ['rstd'], bias=eps_t[:], scale=1.0 / nfeat)
        ACT(rstd[:], rstd[:], AF.Exp, ['rstd'], ['rstd'], scale=-0.5)

    def norm_mod(l, which):
        par = l % 2
        rms_stats([xT[:, dc, :] for dc in range(8)], [('x', dc) for dc in range(8)],
                  [hT[:, dc, :] for dc in range(8)], [('h', dc) for dc in range(8)], 1024.0)
        so = 0 if which == 0 else 24
        for dc in range(8):
            tmp, tk = fs_next()
            STT(tmp[:], xT[:, dc, :], scp[:, par, which * 8 + dc:which * 8 + dc + 1], rstd[:], ALU.mult, ALU.mult,
                [('x', dc), ('scp', par), 'rstd'], [tk])
            ACT(hT[:, dc, :], tmp[:], AF.Identity, [tk, ('mod', par)], [('h', dc)], bias=modT[:, par, so + dc:so + dc + 1])

    def dwconv(l, zps, zkeys, acc, akey, w0, w1, w2, bia, w0nn, w2nn):
        ACT(acc[:, :], zps[:, :], AF.Identity, zkeys + ['CPm'], [akey], bias=bia, scale=w1)
        STT(acc[:, 1:1024], zps[:, 0:1023], w0, acc[:, 1:1024], ALU.mult, ALU.add, zkeys + ['CPm', akey], [akey])
        STT(acc[:, 0:1023], zps[:, 1:1024], w2, acc[:, 0:1023], ALU.mult, ALU.add, zkeys + ['CPm', akey], [akey])
        STT(acc[:, 256:1024:256], zps[:, 255:1023:256], w0nn, acc[:, 256:1024:256], ALU.mult, ALU.add,
            zkeys + [('w0n', l % 2), akey], [akey])
        STT(acc[:, 255:1023:256], zps[:, 256:1024:256], w2nn, acc[:, 255:1023:256], ALU.mult, ALU.add,
            zkeys + [('w0n', l % 2), akey], [akey])

    def prep_w0n(l):
        par = l % 2
        TS('pool', w0n[:, par, 0:44], CPm[:, l, 1, 0:44], link[:, 1:2], -1.0, ALU.mult, ALU.mult, ['CPm', 'link'], [('w0n', par)])
        TS('pool', w2n[:, par, 0:40], CPm[:, l, 1, 88:128], link[:, 1:2], -1.0, ALU.mult, ALU.mult, ['CPm', 'link'], [('w0n', par)])
        TS('pool', w2n[:, par, 40:44], CPm[:, l, 2, 0:4], link[:, 1:2], -1.0, ALU.mult, ALU.mult, ['CPm', 'link'], [('w0n', par)])
        TS('pool', w0n[:, par, 44:52], CPm[:, l, 0, 48:56], link[:, 1:2], -1.0, ALU.mult, ALU.mult, ['CPm', 'link'], [('w0n', par)])
        TS('pool', w2n[:, par, 44:52], CPm[:, l, 0, 64:72], link[:, 1:2], -1.0, ALU.mult, ALU.mult, ['CPm', 'link'], [('w0n', par)])

    w2n = sb("w2n", [128, 2, 52])

    def ffn(l):
        par = l % 2
        with ExitStack() as es:
            act = es.enter_context(sbt("ffn_act", [128, 22, 1024], BF16))
            sa = es.enter_context(sbt("ffn_sa", [128, 4, 1024], F32))
            T.adopt([('act', j) for j in range(22)] + [('sa', j) for j in range(4)])
            norm_mod(l, 1)
            pend = []
            if l == 0:
                dump('mod0', modT[:, 0, :], [('mod', 0)])
                dump('h2', hT[:, :, :], [('h', dc) for dc in range(8)])
                dump('rstd', rstd[:, :], ['rstd'])
            for g in range(6):
                nchunk = 4 if g < 5 else 2
                for ab in range(2):
                    c0 = ab * DFF + g * 512
                    w, wk = wload(('w_up', (l,)), 0, 8, c0, nchunk * 128)
                    for j in range(nchunk):
                        cc = ab * 22 + g * 4 + j
                        zp, zk = p2_next()
                        for th in range(2):
                            for kc in range(8):
                                MM(zp[:, th * 512:(th + 1) * 512], w[:, kc, j * 128:(j + 1) * 128],
                                   hT[:, kc, th * 512:(th + 1) * 512], kc == 0, kc == 7, [wk, ('h', kc)], [zk[th]])
                        acc, ak = fs_next()
                        dwconv(l, zp, zk, acc, ak, ffw_c(l, 0, cc), ffw_c(l, 1, cc), ffw_c(l, 2, cc), ffb_c(l, cc),
                               w0n[:, par, cc:cc + 1],
                               w2n[:, par, cc:cc + 1])
                        for (pj, pacc, pak) in pend:
                            ACT(sa[:, pj, :], pacc[:, :], AF.Silu, [pak], [('sa', pj)])
                        del pend[:]
                        if ab == 0:
                            pend.append((j, acc, ak))
                        else:
                            TT('pool', act[:, g * 4 + j, :], sa[:, j, :], acc[:, :], ALU.mult, [('sa', j), ak],
                               [('act', g * 4 + j)])
            if l == 0:
                dump('act', act[:, :, :], [('act', j) for j in range(22)])
            for jp in range(4):
                wA, wkA = wload(('w_down', (l,)), 0, 11, jp * 256, 256)
                wB, wkB = wload(('w_down', (l,)), 1408, 11, jp * 256, 256)
                for half, (w, wk) in enumerate(((wA, wkA), (wB, wkB))):
                    for dj in range(2):
                        for th in range(2):
                            bk = dj * 2 + th
                            for kk in range(11):
                                kc = half * 11 + kk
                                MM(bank(bk), w[:, kk, dj * 128:(dj + 1) * 128], act[:, kc, th * 512:(th + 1) * 512],
                                   half == 0 and kk == 0, half == 1 and kk == 10, [wk, ('act', kc)], [ps(bk)])
                for dj in range(2):
                    dc = jp * 2 + dj
                    for th in range(2):
                        bk = dj * 2 + th
                        STT(xT[:, dc, th * 512:(th + 1) * 512], bank(bk), modT[:, par, 40 + dc:41 + dc],
                            xT[:, dc, th * 512:(th + 1) * 512], ALU.mult, ALU.add, [ps(bk), ('mod', par), ('x', dc)],
                            [('x', dc)])
            T.free([('act', j) for j in range(22)] + [('sa', j) for j in range(4)])


    def run_pipeline(jobs, make_gen, nslots, extra=None, extra_every=1):
        active = []
        free_slots = list(range(nslots))
        nxt = 0
        step = 0
        while nxt < len(jobs) or active:
            if nxt < len(jobs) and free_slots:
                sl_ = free_slots.pop(0)
                g = make_gen(jobs[nxt], sl_)
                nxt += 1
                next(g)
                active.append((g, sl_, True))
            still = []
            for (g, sl_, fresh) in active:
                if fresh:
                    still.append((g, sl_, False))
                    continue
                try:
                    next(g)
                    still.append((g, sl_, False))
                except StopIteration:
                    free_slots.append(sl_)
            active = still
            step += 1
            if extra is not None and extra[0] is not None and step % extra_every == 0:
                try:
                    next(extra[0])
                except StopIteration:
                    extra[0] = None
        if extra is not None and extra[0] is not None:
            for _ in extra[0]:
                pass

    def sg_phase(l, ysgT):
        wl = w_in[l]
        with ExitStack() as es:
            uT = es.enter_context(sbt("sg_uT", [128, 4, 1024], BF16))
            sgwT = es.enter_context(sbt("sg_wT", [128, 4, 128], BF16))
            sgwf = es.enter_context(sbt("sg_wf", [128, 4, 128], F32))
            gb = es.enter_context(sbt("sg_gb", [128, 2, 512], F32))
            zz = es.enter_context(sbt("sg_zz", [128, 4, 512], F32))
            svb = es.enter_context(sbt("sg_svb", [128, 4, 512], BF16))
            st6 = es.enter_context(sbt("sg_st", [128, 4, 8], F32))
            keys = [('sg_u', j) for j in range(4)] + ['sg_wT', 'sg_wf', 'sg_gb'] + [(n_, b_) for n_ in ('sg_zz', 'sg_svb', 'sg_st') for b_ in range(4)]
            T.adopt(keys)
            DMA('sp', gb[:, 0, :], sg_norm_g[l].partition_broadcast(128), (), ['sg_gb'], 'di')
            DMA('sp', gb[:, 1, :], sg_b[l].partition_broadcast(128), (), ['sg_gb'], 'di')
            DMA('sp', sgwf[:, :, :], sg_w[l].rearrange("g p q -> p g q"), (), ['sg_wf'], 'di')
            for g in range(4):
                TR(bank(6)[:, g * 128:(g + 1) * 128], sgwf[:, g, :], ident_f[:], ['sg_wf', 'ident_f'], [ps(6)])
            CP('dve', sgwT[:, :, :], bank(6).rearrange("p (g q) -> p g q", g=4), [ps(6)], ['sg_wT'])
            w, wk = wload(('w_in', (l,)), 0, 8, 3600, 512)
            for j in range(4):
                zp, zk = p2_next()
                for th in range(2):
                    for kc in range(8):
                        MM(zp[:, th * 512:(th + 1) * 512], w[:, kc, j * 128:(j + 1) * 128], hT[:, kc, th * 512:(th + 1) * 512],
                           kc == 0, kc == 7, [wk, ('h', kc)], [zk[th]])
                ACT(uT[:, j, :], zp[:, :], AF.Gelu_apprx_tanh, zk, [('sg_u', j)])
            w, wk = wload(('w_in', (l,)), 0, 8, 4112, 512)

            def sg_job(tc, b):
                bk = b
                gbk = 4 + b % 2
                for kc in range(8):
                    MM(bank(bk), hT[:, kc, tc * 128:(tc + 1) * 128], w[:, kc, :], kc == 0, kc == 7, [('h', kc), wk], [ps(bk)])
                yield
                ACT(zz[:, b, :], bank(bk), AF.Gelu_apprx_tanh, [ps(bk)], [('sg_zz', b)])
                yield
                T.op('dve', lambda: nc.vector.bn_stats(out=st6[:, b, 0:6], in_=zz[:, b, :]), [('sg_zz', b)], [('sg_st', b)])
                T.op('dve', lambda: nc.vector.bn_aggr(out=st6[:, b, 6:8], in_=st6[:, b, 0:6]), [('sg_st', b)], [('sg_st', b)])
                yield
                TS('pool', st6[:, b, 7:8], st6[:, b, 7:8], EPS, None, ALU.add, None, [('sg_st', b)], [('sg_st', b)])
                TT('pool', st6[:, b, 7:8], st6[:, b, 7:8], mhalf[:, 0:1], ALU.pow, [('sg_st', b), 'mhalf'], [('sg_st', b)])
                yield
                TS('dve', zz[:, b, :], zz[:, b, :], st6[:, b, 6:7], st6[:, b, 7:8], ALU.subtract, ALU.mult,
                   [('sg_zz', b), ('sg_st', b)], [('sg_zz', b)])
                yield
                TT('pool', svb[:, b, :], zz[:, b, :], gb[:, 0, :], ALU.mult, [('sg_zz', b), 'sg_gb'], [('sg_svb', b)])
                yield
                for g in range(4):
                    MM(bank(gbk)[:, g * 128:(g + 1) * 128], svb[:, b, g * 128:(g + 1) * 128], sgwT[:, g, :], True, True,
                       [('sg_svb', b), 'sg_wT'], [ps(gbk)])
                yield
                tmp, tk = fs_next()
                TT('dve', tmp[:, 0:512], bank(gbk), gb[:, 1, :], ALU.add, [ps(gbk), 'sg_gb'], [tk])
                yield
                TT('pool', ysgT[:, :, tc * 128:(tc + 1) * 128], tmp[:, 0:512].rearrange("p (g t) -> p g t", g=4),
                   uT[:, :, tc * 128:(tc + 1) * 128], ALU.mult, [tk] + [('sg_u', j) for j in range(4)],
                   [('ysg', j) for j in range(4)])

            run_pipeline(list(range(8)), sg_job, 4)
            T.free(keys)

    def merge(l, ys):
        par = l % 2
        ynames = ['yda', 'yml', 'ysg']
        with ExitStack() as es:
            mg = es.enter_context(sbt("mg", [128, 8, 1024], BF16))
            accf = es.enter_context(sbt("mg_acc", [128, 4, 1024], F32))
            keys = [('mg', dc) for dc in range(8)] + [('mg_acc', j) for j in range(4)]
            T.adopt(keys)
            for dcg in range(2):
                for n in range(3):
                    w, wk = wload(('w_in', (l,)), 0, 8, 4624 + n * 1024 + dcg * 512, 512)
                    wb, wbk = wload(('w_branch', (l, n)), 0, 4, 0, 1024)
                    for j in range(4):
                        dc = dcg * 4 + j
                        gp, gk = p2_next()
                        for th in range(2):
                            for kc in range(8):
                                MM(gp[:, th * 512:(th + 1) * 512], w[:, kc, j * 128:(j + 1) * 128],
                                   hT[:, kc, th * 512:(th + 1) * 512], kc == 0, kc == 7, [wk, ('h', kc)], [gk[th]])
                        pp, pk = p2_next()
                        for th in range(2):
                            for kc in range(4):
                                MM(pp[:, th * 512:(th + 1) * 512], wb[:, kc, dc * 128:(dc + 1) * 128],
                                   ys[n][:, kc, th * 512:(th + 1) * 512], kc == 0, kc == 3, [wbk, (ynames[n], kc)], [pk[th]])
                        sgt, sk = fs_next()
                        ACT(sgt[:, :], gp[:, :], AF.Sigmoid, gk, [sk])
                        if n == 0:
                            TT('dve', accf[:, j, :], pp[:, :], sgt[:, :], ALU.mult, pk + [sk], [('mg_acc', j)])
                        else:
                            t2, t2k = fs_next()
                            TT('dve', t2[:, :], pp[:, :], sgt[:, :], ALU.mult, pk + [sk], [t2k])
                            if n == 1:
                                TT('pool', accf[:, j, :], accf[:, j, :], t2[:, :], ALU.add, [('mg_acc', j), t2k], [('mg_acc', j)])
                            else:
                                TT('pool', mg[:, dc, :], accf[:, j, :], t2[:, :], ALU.add, [('mg_acc', j), t2k], [('mg', dc)])
            for og_ in range(2):
                w, wk = wload(('w_out', (l,)), 0, 8, og_ * 512, 512)
                for j in range(4):
                    dc = og_ * 4 + j
                    zp, zk = p2_next()
                    for th in range(2):
                        for kc in range(8):
                            MM(zp[:, th * 512:(th + 1) * 512], w[:, kc, j * 128:(j + 1) * 128], mg[:, kc, th * 512:(th + 1) * 512],
                               kc == 0, kc == 7, [wk, ('mg', kc)], [zk[th]])
                    for th in range(2):
                        STT(xT[:, dc, th * 512:(th + 1) * 512], zp[:, th * 512:(th + 1) * 512], modT[:, par, 16 + dc:17 + dc],
                            xT[:, dc, th * 512:(th + 1) * 512], ALU.mult, ALU.add, [zk[th], ('mod', par), ('x', dc)], [('x', dc)])
            T.free(keys)

    def da_phase(l, ydaT):
        wl = w_in[l]
        with ExitStack() as es:
            KT = es.enter_context(sbt("da_KT", [128, 4, 1280], BF16))
            qT = es.enter_context(sbt("da_qT", [128, 4, 1024], BF16))
            V = es.enter_context(sbt("da_V", [128, 10, 512], BF16))
            oall = es.enter_context(sbt("da_oall", [128, 4, 1024], F32))
            ET = es.enter_context(sbt("da_ET", [128, 3, 512], BF16))
            kbf = es.enter_context(sbt("da_kbf", [128, 2, 1024], BF16))
            ckf = es.enter_context(sbt("da_ckf", [128, 2, 128], F32))
            osb = es.enter_context(sbt("da_osb", [128, 2, 256], F32))
            rs = es.enter_context(sbt("da_rs", [128, 2, 256], F32))
            vst = es.enter_context(sbt("da_vst", [128, 2, 512], F32))
            dagl = es.enter_context(sbt("da_gl", [128, 1], F32))
            keys = ([('KT', h) for h in range(4)] + [('qT', h) for h in range(4)] + [('V', c) for c in range(10)] +
                    [('oall', h) for h in range(4)] + [('ET', i) for i in range(3)] + [('kbf', 0), ('kbf', 1), 'ckf', ('osb', 0), ('osb', 1),
                                                                                        ('rs', 0), ('rs', 1), ('vst', 0), ('vst', 1), 'dagl'])
            T.adopt(keys)
            TS('dve', dagl[:, :], dag_c(l), 1.0 - (0.8 - 0.6 * math.exp(-0.3 * l)), None, ALU.mult, None, ['CPm'], ['dagl'])
            for hd in range(4):
                DMA('pool', V[:, 0:2, hd * 128:(hd + 1) * 128], cv[l, hd].rearrange("(c p) d -> p c d", p=128), (),
                    [('V', 0), ('V', 1)], 'dw')
            vck = vst[:, :, :].rearrange("p a (h d) -> p (a h) d", h=2)
            for hd in range(4):
                DMA('sp', vck[:, hd, :].rearrange("p (c d) -> p c d", c=2), ck[l, hd].rearrange("(c p) d -> p c d", p=128), (),
                    [('vst', hd // 2)], 'di')
            for hp in range(2):
                for hh in range(2):
                    hd = hp * 2 + hh
                    for c in range(2):
                        TR(bank(6)[:, (hh * 2 + c) * 128:(hh * 2 + c + 1) * 128], vck[:, hd, c * 128:(c + 1) * 128], ident_f[:],
                           [('vst', hp), 'ident_f'], [ps(6)])
                CP('dve', KT[:, hp * 2:hp * 2 + 2, 0:256], bank(6).rearrange("p (h k) -> p h k", h=2), [ps(6)],
                   [('KT', hp * 2), ('KT', hp * 2 + 1)])
            wqk = {}

            def proj_job(job, slot):
                which, hd = job
                if hd == 0:
                    c0 = 512 if which == 0 else 0
                    wqk[which] = wload(('w_in', (l,)), 0, 8, c0, 512)
                w, wk = wqk[which]
                zp, zk = P2[slot], [ps(2 * slot), ps(2 * slot + 1)]
                for th in range(2):
                    for kc in range(8):
                        MM(zp[:, th * 512:(th + 1) * 512], w[:, kc, hd * 128:(hd + 1) * 128], hT[:, kc, th * 512:(th + 1) * 512],
                           kc == 0, kc == 7, [wk, ('h', kc)], [zk[th]])
                yield
                CP('act', kbf[:, slot, :], zp[:, :], zk, [('kbf', slot)])
                yield
                sp_, spk = PC, [ps(4), ps(5)]
                for th in range(2):
                    MM(sp_[:, th * 512:(th + 1) * 512], prot_b[:], kbf[:, slot, th * 512:(th + 1) * 512], True, True,
                       ['prot_b', ('kbf', slot)], [spk[th]])
                yield
                t1, t1k = fs_next()
                t2, t2k = fs_next()
                TT('dve', t1[:, :], zp[:, :], ropeC[:, :], ALU.mult, zk + ['ropeC'], [t1k])
                TT('dve', t2[:, :], sp_[:, :], ropeS[:, :], ALU.mult, spk + ['ropeS'], [t2k])
                yield
                if which == 0:
                    TT('pool', t1[:, :], t1[:, :], t2[:, :], ALU.add, [t1k, t2k], [t1k])
                    CP('pool', KT[:, hd, 256:1280], t1[:, :], [t1k], [('KT', hd)])
                    yield
                    for tcg in range(2):
                        for j in range(4):
                            tc = tcg * 4 + j
                            TR(bank(6)[:, j * 128:(j + 1) * 128], t1[:, tc * 128:(tc + 1) * 128], ident_f[:],
                               [t1k, 'ident_f'], [ps(6)])
                        CP('act', vst[:, tcg, :], bank(6), [ps(6)], [('vst', tcg)])
                        DMA('sp', ok_o[l, hd, tcg * 512:(tcg + 1) * 512, :].rearrange("(j p) d -> p j d", p=128),
                            vst[:, tcg, :].rearrange("p (j d) -> p j d", j=4), [('vst', tcg)], (), 'do')
                else:
                    TT('pool', qT[:, hd, :], t1[:, :], t2[:, :], ALU.add, [t1k, t2k], [('qT', hd)])

            run_pipeline([(which, hd) for which in range(2) for hd in range(4)], proj_job, 2)
            w, wk = wload(('w_in', (l,)), 0, 8, 1024, 512)
            for tc in range(8):
                bk = 4 + tc % 2
                for kc in range(8):
                    MM(bank(bk), hT[:, kc, tc * 128:(tc + 1) * 128], w[:, kc, :], kc == 0, kc == 7, [('h', kc), wk], [ps(bk)])
                CP('act', V[:, 2 + tc, :], bank(bk), [ps(bk)], [('V', 2 + tc)])
                CP('dve', vst[:, tc % 2, :], bank(bk), [ps(bk)], [('vst', tc % 2)])
                DMA('sp', ov_o[l, :, tc * 128:(tc + 1) * 128, :].rearrange("h p d -> p h d"),
                    vst[:, tc % 2, :].rearrange("p (h d) -> p h d", h=4), [('vst', tc % 2)], (), 'do')
            iters = [(hd, qt, kc) for hd in range(4) for qt in range(4) for kc in range(10)]
            rsf = rs[:, :, :].rearrange("p a b -> p (a b)")
            osf = osb[:, :, :].rearrange("p a b -> p (a b)")

            PT32 = PT[:, :].bitcast(F32)
            accb = [(bank(4), bank(5), ps(4), ps(5)), (bank(6), PT32, ps(6), ps(7))]
            SR = [(PA, ps(0), ps(1)), (PB, ps(2), ps(3))]

            def emit_qk(i):
                hd, qt, kc = iters[i]
                reg, k0, k1 = SR[i % 2]
                for half in range(2):
                    lo, hi = half * 64, half * 64 + 64
                    MM(reg[:, half * 512:half * 512 + 256], KT[lo:hi, hd, kc * 128:(kc + 1) * 128],
                       qT[lo:hi, hd, qt * 256:(qt + 1) * 256], True, True, [('KT', hd), ('qT', hd)], [k0 if half == 0 else k1])

            def emit_rest(i):
                hd, qt, kc = iters[i]
                reg, k0, k1 = SR[i % 2]
                sbk = i % 3
                ao, as_, ko, ks = accb[(hd * 4 + qt) % 2]
                ACT(ET[:, sbk, :].rearrange("p (a b) -> p a b", a=2), reg[:, :].rearrange("p (a b) -> p a b", a=2)[:, :, 0:256],
                    AF.Exp, [k0, k1, 'maskb'], [('ET', sbk)],
                    bias=maskb[:, (kc // 2) * 4 + qt:(kc // 2) * 4 + qt + 1], scale=0.125)
                MM(ao, V[:, kc, hd * 128:(hd + 1) * 128], ET[:, sbk, :], kc == 0, kc == 9, [('V', kc), ('ET', sbk)], [ko])
                MM(as_, ones_b[:], ET[:, sbk, :], kc == 0, kc == 9, ['ones_b', ('ET', sbk)], [ks])
                if kc == 9:
                    T.op('dve', lambda: nc.vector.reciprocal(out=rsf, in_=as_), [ks], [('rs', 0), ('rs', 1)])
                    TT('dve', osf, ao, rsf, ALU.mult, [ko, ('rs', 0), ('rs', 1)], [('osb', 0), ('osb', 1)])
                    STT(oall[:, hd, qt * 256:(qt + 1) * 256], osb[:, 1, :], nlam[:, l:l + 1], osb[:, 0, :], ALU.mult, ALU.add,
                        [('osb', 0), ('osb', 1), 'nlam'], [('oall', hd)])

            emit_qk(0)
            for i in range(len(iters)):
                if i + 1 < len(iters):
                    emit_qk(i + 1)
                emit_rest(i)
            def danorm_job(hd, slot):
                reg, k0, k1 = SR[slot]
                ACT(ydaT[:, hd, :], oall[:, hd, :], AF.Square, [('oall', hd)], [('yda', hd)])
                yield
                for th in range(2):
                    MM(reg[:, th * 512:(th + 1) * 512], ones_b[:], ydaT[:, hd, th * 512:(th + 1) * 512], True, True,
                       [('yda', hd), 'ones_b'], [k0 if th == 0 else k1])
                yield
                rs_, rsk = fs_next()
                ACT(rs_[:, :], reg[:, :], AF.Ln, [k0, k1, 'eps_t'], [rsk], bias=eps_t[:], scale=1.0 / 128.0)
                ACT(rs_[:, :], rs_[:, :], AF.Exp, [rsk], [rsk], scale=-0.5)
                yield
                STT(ydaT[:, hd, :], oall[:, hd, :], dagl[:, 0:1], rs_[:, :], ALU.mult, ALU.mult, [('oall', hd), 'dagl', rsk],
                    [('yda', hd)])

            run_pipeline(list(range(4)), danorm_job, 2)
            T.free(keys)

    def ml_phase(l, ymlT):
        par = l % 2
        wl = w_in[l]
        LK = (2, 4, 6)
        with ExitStack() as es:
            COLS = es.enter_context(sbt("ml_cols", [128, 8, 4, 8], F32))
            DECB = es.enter_context(sbt("ml_decb", [128, 8, 8], F32))
            MP = es.enter_context(sbt("ml_mp", [8, 16], F32))
            MPL = es.enter_context(sbt("ml_mpl", [8, 16], F32))
            NM = es.enter_context(sbt("ml_nm", [8, 16], F32))
            DT = es.enter_context(sbt("ml_dt", [8, 16], F32))
            DEC = es.enter_context(sbt("ml_dec", [8, 16], F32))
            k1 = ['ml_cols', 'ml_decb', 'ml_mp', 'ml_mpl', 'ml_nm', 'ml_dt', 'ml_dec']
            T.adopt(k1)
            with ExitStack() as es2:
                Rr = [es2.enter_context(sbt(f"ml_r{i}", [8, 1024], F32)) for i in range(9)]
                rk = [('ml_r', i) for i in range(9)]
                T.adopt(rk)
                R0, R1, R2, R3, R4, R5, R6, R7a, R7b = Rr
                w, wk = wload(('w_in', (l,)), 0, 8, 3584, 16)
                for (pp, c0, kk) in ((PA, 0, [ps(0), ps(1)]), (PB, 8, [ps(2), ps(3)])):
                    for th in range(2):
                        for kc in range(8):
                            MM(pp[0:8, th * 512:(th + 1) * 512], w[:, kc, c0:c0 + 8], hT[:, kc, th * 512:(th + 1) * 512],
                               kc == 0, kc == 7, [wk, ('h', kc)], [kk[th]])
                for (pp, kk, dst, dk, bcol) in ((PA, [ps(0), ps(1)], R0, rk[0], 0), (PB, [ps(2), ps(3)], R1, rk[1], 1)):
                    ACT(R4[:, :], pp[0:8, :], AF.Identity, kk + ['GB'], [rk[4]], bias=GB[:, l, bcol:bcol + 1])
                    ACT(R5[:, :], pp[0:8, ::-1], AF.Identity, kk + ['GB'], [rk[5]], bias=GB[:, l, bcol:bcol + 1])
                    TS('dve', dst[:, :], R4[:, :], dirm[:, 0:1], None, ALU.mult, None, [rk[4], 'dirm'], [dk])
                    STT(dst[:, :], R5[:, :], dirm[:, 1:2], dst[:, :], ALU.mult, ALU.add, [rk[5], 'dirm', dk], [dk])
                ACT(R1[:, :], R1[:, :], AF.Exp, [rk[1]], [rk[1]], scale=-1.0)
                ACT(R1[:, :], R1[:, :], AF.Ln, [rk[1], 'one_t'], [rk[1]], bias=one_t[0:8, :], scale=1.0)
                for c in range(8):
                    sl = slice(c * 128, (c + 1) * 128)
                    T.op('dve', lambda: nc.vector.tensor_tensor_scan(out=R2[:, sl], data0=ones8[:, :], data1=R1[:, sl], initial=0.0,
                                                                      op0=ALU.mult, op1=ALU.add), [rk[1], 'ones8'], [rk[2]])
                TT('dve', R0[:, :], R0[:, :], R2[:, :], ALU.add, [rk[0], rk[2]], [rk[0]])
                for c in range(8):
                    sl = slice(c * 128, (c + 1) * 128)
                    T.op('dve', lambda: nc.vector.tensor_tensor_scan(out=R3[:, sl], data0=R0[:, sl], data1=R0[:, sl], initial=-1e30,
                                                                      op0=ALU.max, op1=ALU.max), [rk[0]], [rk[3]])
                DMA('sp', MP[:, 0:1], sm[l], (), ['ml_mp'], 'di')
                for cs in range(8):
                    sl = slice(cs * 128, (cs + 1) * 128)
                    la = cs * 128 + 127
                    mprev = MP[:, cs:cs + 1]
                    mpk = 'ml_mp'
                    if cs in LK:
                        TS('dve', MPL[:, cs:cs + 1], mprev, link[0:8, 0:1], None, ALU.mult, None, ['ml_mp', 'link'], ['ml_mpl'])
                        mprev = MPL[:, cs:cs + 1]
                        mpk = 'ml_mpl'
                    TS('dve', R3[:, sl], R3[:, sl], mprev, None, ALU.max, None, [rk[3], mpk], [rk[3]])
                    TT('dve', MP[:, cs + 1:cs + 2], R3[:, la:la + 1], R2[:, la:la + 1], ALU.subtract, [rk[3], rk[2]], ['ml_mp'])
                    ACT(R5[:, sl], R3[:, sl], AF.Exp, [rk[3], mpk], [rk[5]], bias=mprev, scale=-1.0)
                    if cs in LK:
                        TS('dve', R5[:, sl], R5[:, sl], link[0:8, 0:1], None, ALU.mult, None, [rk[5], 'link'], [rk[5]])
                    ACT(R4[:, sl], R3[:, sl], AF.Exp, [rk[3]], [rk[4]], bias=R3[:, la:la + 1], scale=-1.0)
                    TS('dve', NM[:, cs:cs + 1], R3[:, la:la + 1], -1.0, None, ALU.mult, None, [rk[3]], ['ml_nm'])
                    ACT(R0[:, sl], R0[:, sl], AF.Exp, [rk[0], 'ml_nm'], [rk[0]], bias=NM[:, cs:cs + 1], scale=1.0)
                    TT('dve', DT[:, cs:cs + 1], mprev, R2[:, la:la + 1], ALU.subtract, [mpk, rk[2]], ['ml_dt'])
                    TT('dve', DT[:, cs:cs + 1], DT[:, cs:cs + 1], MP[:, cs + 1:cs + 2], ALU.subtract, ['ml_dt', 'ml_mp'], ['ml_dt'])
                    ACT(DEC[:, cs:cs + 1], DT[:, cs:cs + 1], AF.Exp, ['ml_dt'], ['ml_dec'])
                    if cs in LK:
                        TS('dve', DEC[:, cs:cs + 1], DEC[:, cs:cs + 1], link[0:8, 0:1], None, ALU.mult, None, ['ml_dec', 'link'],
                           ['ml_dec'])
                TT('dve', R2[:, :], R2[:, :], R3[:, :], ALU.subtract, [rk[2], rk[3]], [rk[2]])
                ACT(R2[:, :], R2[:, :], AF.Exp, [rk[2]], [rk[2]])
                CP('dve', MPL[:, 12:16], MP[:, 2:10:2], ['ml_mp', 'ml_mpl'], ['ml_mpl'])
                DMA('sp', om_o[l], MPL[:, 12:16], ['ml_mpl'], (), 'do')
                for qi, (Q, qk) in enumerate(((R0, rk[0]), (R4, rk[4]), (R5, rk[5]), (R2, rk[2]))):
                    Ro, rok = (R7a, rk[7]) if qi % 2 == 0 else (R7b, rk[8])
                    TS('dve', R6[:, :], Q[:, :], dirm[:, 0:1], None, ALU.mult, None, [qk, 'dirm'], [rk[6]])
                    STT(Ro[:, :], Q[:, ::-1], dirm[:, 1:2], R6[:, :], ALU.mult, ALU.add, [rk[6], 'dirm', qk], [rok])
                    for c in range(8):
                        o0 = (c * 4 + qi) * 8
                        TR(bank(6)[:, o0:o0 + 8], Ro[:, c * 128:(c + 1) * 128], ident_f[0:8, 0:8], [rok, 'ident_f'], [ps(6)])
                CP('dve', COLS[:, :, :, :].rearrange("p a b c -> p (a b c)"), bank(6)[:, 0:256], [ps(6)], ['ml_cols'])
                for r in range(8):
                    MM(bank(5)[:, r * 8:(r + 1) * 8], sel[0:8, r * 128:(r + 1) * 128], DEC[0:8, 0:8], True, True, ['sel', 'ml_dec'],
                       [ps(5)])
                CP('dve', DECB[:, :, :].rearrange("p a b -> p (a b)"), bank(5)[:, 0:64], [ps(5)], ['ml_decb'])
                T.free(rk)
            hsum = es.enter_context(sbt("ml_hs", [128, 8, 512], F32))
            Cn32 = es.enter_context(sbt("ml_cn", [128, 8, 130], F32))
            T.adopt([('hs', c) for c in range(8)] + [('cn', r) for r in range(8)])
            es3 = ExitStack()
            mqT = es3.enter_context(sbt("ml_qT", [128, 4, 1024], BF16))
            mkT = es3.enter_context(sbt("ml_kT", [128, 4, 1024], BF16))
            mva = es3.enter_context(sbt("ml_va", [128, 8, 4, 130], BF16))
            Cnb = es3.enter_context(sbt("ml_cnb", [128, 8, 130], BF16))
            PTs = es3.enter_context(sbt("ml_pts", [128, 6, 128], BF16))
            kg = es3.enter_context(sbt("ml_kg", [128, 6, 128], BF16))
            vg = es3.enter_context(sbt("ml_vg", [128, 6, 130], BF16))
            tB = es3.enter_context(sbt("ml_tb", [128, 6, 130], F32))
            nd = es3.enter_context(sbt("ml_nd", [128, 6, 130], F32))
            dn = es3.enter_context(sbt("ml_dn", [128, 6, 2], F32))
            k2 = ([('mqT', h) for h in range(4)] + [('mkT', h) for h in range(4)] + [('mva', c) for c in range(8)] +
                  [('cnb', r) for r in range(8)] +
                  [(nm_, b) for nm_ in ('pts', 'kg', 'vg', 'tb', 'nd', 'dn') for b in range(6)])
            T.adopt(k2)
            for dr in range(2):
                for hd in range(4):
                    r = dr * 4 + hd
                    DMA('sp', Cn32[:, r, 0:129], sCn[l, dr, hd], (), [('cn', r)], 'di')
                    CP('pool', Cnb[:, r, 0:129], Cn32[:, r, 0:129], [('cn', r)], [('cnb', r)])
            for which, c0 in ((0, 1536), (1, 2048)):
                w, wk = wload(('w_in', (l,)), 0, 8, c0, 512)
                for hd in range(4):
                    zp, zk = p2_next()
                    for th in range(2):
                        for kc in range(8):
                            MM(zp[:, th * 512:(th + 1) * 512], w[:, kc, hd * 128:(hd + 1) * 128], hT[:, kc, th * 512:(th + 1) * 512],
                               kc == 0, kc == 7, [wk, ('h', kc)], [zk[th]])
                    ch = which * 4 + hd
                    acc, ak = fs_next()
                    dwconv(l, zp, zk, acc, ak, mlw_c(l, 0, ch), mlw_c(l, 1, ch), mlw_c(l, 2, ch), mlb_c(l, ch),
                           w0n[:, par, 44 + ch:45 + ch], w2n[:, par, 44 + ch:45 + ch])
                    if which == 0:
                        ACT(mqT[:, hd, :], acc[:, :], AF.Silu, [ak], [('mqT', hd)])
                    else:
                        sgt, sk = fs_next()
                        ACT(sgt[:, :], acc[:, :], AF.Sigmoid, [ak], [sk])
                        STT(mkT[:, hd, :], acc[:, :], 128.0 ** -0.5, sgt[:, :], ALU.mult, ALU.mult, [ak, sk], [('mkT', hd)])
            w, wk = wload(('w_in', (l,)), 0, 8, 2560, 512)
            MS('pool', mva[:, :, :, 128:130], 1.0, [('mva', c) for c in range(8)])
            for tc in range(8):
                bk = 4 + tc % 2
                for kc in range(8):
                    MM(bank(bk), hT[:, kc, tc * 128:(tc + 1) * 128], w[:, kc, :], kc == 0, kc == 7, [('h', kc), wk], [ps(bk)])
                CP('act', mva[:, tc, :, 0:128], bank(bk).rearrange("p (h d) -> p h d", h=4), [ps(bk)], [('mva', tc)])
            def core_iter(cs, hd, dr, slot):
                c = cs if dr == 0 else 7 - cs
                r = dr * 4 + hd
                tsl = slice(c * 128, (c + 1) * 128)
                mask = maskF if dr == 0 else maskB
                mkey = 'maskF' if dr == 0 else 'maskB'
                pb = bank(slot)
                pk_ = ps(slot)
                Sps, Aps, Bps, CNps = pb[:, 0:128], pb[:, 130:259], pb[:, 260:389], pb[:, 0:129]
                MM(Sps, mkT[:, hd, tsl], mqT[:, hd, tsl], True, True, [('mkT', hd), ('mqT', hd)], [pk_])
                TR(PT[:, slot * 128:(slot + 1) * 128], mkT[:, hd, tsl], ident_b[:], [('mkT', hd), 'ident_b'], [ps(7)])
                yield
                TT('dve', PTs[:, slot, :], Sps, mask[:, :], ALU.mult, [pk_, mkey], [('pts', slot)])
                ACT(kg[:, slot, :], PT[:, slot * 128:(slot + 1) * 128], AF.Identity, [ps(7), 'ml_cols'], [('kg', slot)],
                    scale=COLS[:, c, 0, r:r + 1])
                ACT(vg[:, slot, :], mva[:, c, hd, :], AF.Identity, [('mva', c), 'ml_cols'], [('vg', slot)],
                    scale=COLS[:, c, 0, r:r + 1])
                yield
                MM(Aps, PTs[:, slot, :], vg[:, slot, 0:129], True, True, [('pts', slot), ('vg', slot)], [pk_])
                MM(Bps, mqT[:, hd, tsl], Cnb[:, r, 0:129], True, True, [('mqT', hd), ('cnb', r)], [pk_])
                yield
                ACT(tB[:, slot, 0:129], Bps, AF.Identity, [pk_, 'ml_cols'], [('tb', slot)], scale=COLS[:, c, 2, r:r + 1])
                STT(nd[:, slot, 0:129], Aps, COLS[:, c, 1, r:r + 1], tB[:, slot, 0:129], ALU.mult, ALU.add,
                    [pk_, 'ml_cols', ('tb', slot)], [('nd', slot)])
                STT(dn[:, slot, 0:1], nd[:, slot, 128:129], -1.0, nd[:, slot, 128:129], ALU.mult, ALU.max, [('nd', slot)],
                    [('dn', slot)])
                TS('dve', dn[:, slot, 0:1], dn[:, slot, 0:1], COLS[:, c, 3, r:r + 1], None, ALU.max, None,
                   [('dn', slot), 'ml_cols'], [('dn', slot)])
                T.op('dve', lambda: nc.vector.reciprocal(out=dn[:, slot, 1:2], in_=dn[:, slot, 0:1]), [('dn', slot)], [('dn', slot)])
                hs = hsum[:, c, hd * 128:(hd + 1) * 128]
                if cs < 4:
                    TS('dve', hs, nd[:, slot, 0:128], dn[:, slot, 1:2], None, ALU.mult, None, [('nd', slot), ('dn', slot)],
                       [('hs', c)])
                else:
                    STT(hs, nd[:, slot, 0:128], dn[:, slot, 1:2], hs, ALU.mult, ALU.add, [('nd', slot), ('dn', slot), ('hs', c)],
                        [('hs', c)])
                yield
                MM(CNps, kg[:, slot, :], mva[:, c, hd, 0:129], True, True, [('kg', slot), ('mva', c)], [pk_])
                yield
                STT(Cn32[:, r, 0:129], Cn32[:, r, 0:129], DECB[:, r, cs:cs + 1], CNps, ALU.mult, ALU.add,
                    [('cn', r), 'ml_decb', pk_], [('cn', r)])
                CP('pool', Cnb[:, r, 0:129], Cn32[:, r, 0:129], [('cn', r)], [('cnb', r)])
                if cs % 2 == 1:
                    DMA('sp', oCn_o[l, dr, cs // 2, hd], Cn32[:, r, 0:129], [('cn', r)], (), 'do')

            todo = [(cs, hd, dr) for cs in range(8) for hd in range(4) for dr in range(2)]
            modgen = [compute_mod_gen(l + 1) if l + 1 < depth else None]
            run_pipeline(todo, lambda job, sl_: core_iter(job[0], job[1], job[2], sl_), 6, modgen, 5)
            if l == 0:
                dump('hs', hsum[:, :, :], [('hs', c) for c in range(8)])
                dump('cols', COLS[:, :, :, :], ['ml_cols'])
            T.free(k2)
            es3.close()
            og = es.enter_context(sbt("ml_og", [128, 8, 512], BF16))
            gml4 = es.enter_context(sbt("ml_g4", [128, 512], F32))
            ssq = es.enter_context(sbt("ml_ssq", [128, 4, 4], F32))
            ytm = es.enter_context(sbt("ml_ytm", [128, 4, 512], BF16))
            k3 = [('og', c) for c in range(8)] + ['gml4'] + [(n_, b_) for n_ in ('ssq', 'ytm') for b_ in range(4)]
            T.adopt(k3)
            for h in range(4):
                DMA('sp', gml4[:, h * 128:(h + 1) * 128], ml_norm_g[l].partition_broadcast(128), (), ['gml4'], 'di')
            w, wk = wload(('w_in', (l,)), 0, 8, 3072, 512)

            def mlout_job(c, slot):
                bk = slot
                for kc in range(8):
                    MM(bank(bk), hT[:, kc, c * 128:(c + 1) * 128], w[:, kc, :], kc == 0, kc == 7, [('h', kc), wk], [ps(bk)])
                yield
                tmp, tk = fs_next()
                ACT(tmp[:, 0:512], bank(bk), AF.Sigmoid, [ps(bk)], [tk])
                ACT(tmp[:, 512:1024], hsum[:, c, :], AF.Square, [('hs', c)], [tk])
                yield
                TT('pool', og[:, c, :], tmp[:, 0:512], gml4[:, :], ALU.mult, [tk, 'gml4'], [('og', c)])
                T.op('dve', lambda: nc.vector.tensor_reduce(out=ssq[:, slot, 0:4],
                                                             in_=tmp[:, 512:1024].rearrange("p (h d) -> p h d", h=4),
                                                             axis=AX.X, op=ALU.add), [tk], [('ssq', slot)])
                yield
                TS('pool', ssq[:, slot, 0:4], ssq[:, slot, 0:4], 1.0 / 128.0, EPS, ALU.mult, ALU.add, [('ssq', slot)], [('ssq', slot)])
                TT('pool', ssq[:, slot, 0:4], ssq[:, slot, 0:4], mhalf[:, 0:4], ALU.pow, [('ssq', slot), 'mhalf'], [('ssq', slot)])
                yield
                for hd in range(4):
                    hsl = slice(hd * 128, (hd + 1) * 128)
                    STT(ytm[:, slot, hsl], hsum[:, c, hsl], ssq[:, slot, hd:hd + 1], og[:, c, hsl], ALU.mult, ALU.mult,
                        [('hs', c), ('ssq', slot), ('og', c)], [('ytm', slot)])
                yield
                po = (slot % 2) * 512
                for hd in range(4):
                    hsl = slice(hd * 128, (hd + 1) * 128)
                    TR(PT[:, po + hd * 128:po + (hd + 1) * 128], ytm[:, slot, hsl], ident_b[:], [('ytm', slot), 'ident_b'], [ps(7)])
                yield
                CP('act', ymlT[:, :, c * 128:(c + 1) * 128], PT[:, po:po + 512].rearrange("p (h t) -> p h t", h=4), [ps(7)],
                   [('yml', h) for h in range(4)])

            run_pipeline(list(range(8)), mlout_job, 4)
            T.free(k1 + k3 + [('hs', c) for c in range(8)] + [('cn', r) for r in range(8)])

    def mixer(l):
        norm_mod(l, 0)
        with ExitStack() as es:
            ydaT = es.enter_context(sbt("ydaT", [128, 4, 1024], BF16))
            ymlT = es.enter_context(sbt("ymlT", [128, 4, 1024], BF16))
            ysgT = es.enter_context(sbt("ysgT", [128, 4, 1024], BF16))
            yk = [(n, h) for n in ('yda', 'yml', 'ysg') for h in range(4)]
            T.adopt(yk)
            if 'ml' in phases:
                ml_phase(l, ymlT)
            else:
                MS('pool', ymlT[:, :, :], 0.0, [('yml', h) for h in range(4)])
            if 'da' in phases:
                da_phase(l, ydaT)
            else:
                MS('pool', ydaT[:, :, :], 0.0, [('yda', h) for h in range(4)])
            if 'sg' in phases:
                sg_phase(l, ysgT)
            else:
                MS('pool', ysgT[:, :, :], 0.0, [('ysg', h) for h in range(4)])
            if l == 0:
                dump('yda', ydaT[:, :, :], [('yda', h) for h in range(4)])
                dump('yml', ymlT[:, :, :], [('yml', h) for h in range(4)])
                dump('ysg', ysgT[:, :, :], [('ysg', h) for h in range(4)])
            merge(l, [ydaT, ymlT, ysgT])
            T.free(yk)

    compute_mod(0)
    for l in range(depth):
        prep_w0n(l)
        if l + 1 < depth and 'ml' not in phases:
            compute_mod(l + 1)
        if any(p in phases for p in ('ml', 'da', 'sg')):
            mixer(l)
        if 'ffn' in phases:
            ffn(l)

    with ExitStack() as es:
        yT = es.enter_context(sbt("yT", [128, 8, 1024], F32))
        T.adopt([('yT', dc) for dc in range(8)])
        rms_stats([xT[:, dc, :] for dc in range(8)], [('x', dc) for dc in range(8)],
                  [hT[:, dc, :] for dc in range(8)], [('h', dc) for dc in range(8)], 1024.0)
        for dc in range(8):
            STT(yT[:, dc, :], xT[:, dc, :], gcol[:, dc:dc + 1], rstd[:], ALU.mult, ALU.mult, [('x', dc), 'gcol', 'rstd'],
                [('yT', dc)])
        for tc in range(8):
            st, sk = fs_next()
            for half in range(2):
                bk = 2 + half
                for j in range(4):
                    dc = half * 4 + j
                    TR(bank(bk)[:, j * 128:(j + 1) * 128], yT[:, dc, tc * 128:(tc + 1) * 128], ident_f[:],
                       [('yT', dc), 'ident_f'], [ps(bk)])
                CP('dve' if half == 0 else 'act', st[:, half * 512:(half + 1) * 512], bank(bk), [ps(bk)], [sk])
            DMA('sp', y_o[tc * 128:(tc + 1) * 128, :], st[:, :], [sk], (), 'do')
        T.free([('yT', dc) for dc in range(8)])
    T.finish()
    return nc, T, dumps, wrec


def _consts():
    ident = np.eye(128, dtype=np.float32)
    prot = np.zeros((128, 128), np.float32)
    for m in range(128):
        prot[m ^ 16, m] = 1.0
    s = np.arange(128)[:, None]
    t = np.arange(128)[None, :]
    maskF = (s <= t).astype(np.float32)
    maskB = (s >= t).astype(np.float32)
    sel = np.zeros((8, 8, 128), np.float32)
    for r in range(8):
        sel[r, r, :] = 1.0
    dirm = np.zeros((8, 2), np.float32)
    dirm[0:4, 0] = 1.0
    dirm[4:8, 1] = 1.0
    return dict(c_ident=ident, c_prot=prot, c_maskF=maskF, c_maskB=maskB, c_sel=sel.reshape(8, 1024), c_dirm=dirm)


def _rope_tables():
    t = np.arange(1024)
    row = (t // 64).astype(np.float32)
    col = (t % 64).astype(np.float32)
    nf = 16
    inv = (10000.0 ** (-np.arange(nf, dtype=np.float32) / nf)).astype(np.float32)
    ang = np.stack([row[:, None] * inv, col[:, None] * inv], axis=1)
    cos = np.cos(ang).astype(np.float32)
    sin = np.sin(ang).astype(np.float32)
    C = np.zeros((128, 1024), np.float32)
    S = np.zeros((128, 1024), np.float32)
    for d in range(128):
        axis = (d >> 5) & 1
        half = (d >> 4) & 1
        f = d & 15
        C[d] = cos[:, axis, f]
        S[d] = sin[:, axis, f] * (-1.0 if half == 0 else 1.0)
    return C, S


_CACHE = {}


def kernel(x_prompt, x_sample, c, cache_k, cache_v, state_C, state_n, state_m, c_ctx,
           w_mod, b_mod, w_in, da_lambda, da_norm_g, ml_conv_w, ml_conv_b, ml_gate_b,
           ml_norm_g, sg_norm_g, sg_w, sg_b, w_branch, w_out, w_up, ffn_conv_w, ffn_conv_b,
           w_down, final_g, _depth=L, _phases=('ml', 'da', 'sg', 'ffn'), _dbg=False):
    f = lambda a: np.ascontiguousarray(np.asarray(a, dtype=np.float32))
    key = (_depth, tuple(_phases), _dbg)
    if key not in _CACHE:
        rec = build(_depth, _phases, _dbg is True)[3]
        _CACHE[key] = build(_depth, _phases, _dbg is True, rec)
    nc, T, dumps, _ = _CACHE[key]
    dp = _depth
    consts = _consts()
    rC, rS = _rope_tables()
    shared = dict(consts)
    shared.update(
        w_mod=f(w_mod[:dp]), b_mod=f(b_mod).reshape(L, 48, 128), w_in=f(w_in[:dp]), da_lambda=f(da_lambda).reshape(1, L * 256),
        da_norm_g=f(da_norm_g).reshape(L, 1, 128), ml_conv_w=f(ml_conv_w).reshape(L, 24, 128),
        ml_conv_b=f(ml_conv_b).reshape(L, 8, 128), ml_gate_b=f(ml_gate_b).reshape(L, 16, 1),
        ml_norm_g=f(ml_norm_g).reshape(L, 1, 128), sg_norm_g=f(sg_norm_g).reshape(L, 1, 512), sg_w=f(sg_w),
        sg_b=f(sg_b).reshape(L, 1, 512), w_branch=f(w_branch[:dp]), w_out=f(w_out[:dp]), w_up=f(w_up[:dp]),
        ffn_conv_w=f(ffn_conv_w).reshape(L, 132, 128), ffn_conv_b=f(ffn_conv_b).reshape(L, 44, 128), w_down=f(w_down[:dp]),
        final_g=f(final_g).reshape(8, 128))
    x_prompt = f(x_prompt)
    x_sample = f(x_sample)
    in_maps = []
    for core in range(8):
        m = dict(shared)
        if core < 4:
            b = core
            m['xin'] = x_sample[b]
            m['cond'] = f(c)[b].reshape(8, 128)
            m['ck'] = f(cache_k)[b]
            m['cv'] = f(cache_v)[b]
            m['sCn'] = np.ascontiguousarray(np.concatenate([f(state_C)[b], f(state_n)[b][..., None]], axis=-1))
            m['sm'] = f(state_m)[b].reshape(L, 8, 1)
            m['c_ropeC'] = rC
            m['c_ropeS'] = rS
            m['c_maskb'] = np.zeros((128, 20), np.float32)
            lk = np.zeros((128, 2), np.float32)
            lk[:, 0] = 1.0
            m['c_link'] = lk
        else:
            j = core - 4
            m['xin'] = x_prompt[4 * j:4 * j + 4].reshape(1024, 1024)
            m['cond'] = f(c_ctx).reshape(8, 128)
            m['ck'] = np.zeros((L, 4, 256, 128), np.float32)
            m['cv'] = np.zeros((L, 4, 256, 128), np.float32)
            m['sCn'] = np.zeros((L, 2, 4, 128, 129), np.float32)
            m['sm'] = np.zeros((L, 8, 1), np.float32)
            m['c_ropeC'] = np.ones((128, 1024), np.float32)
            m['c_ropeS'] = np.zeros((128, 1024), np.float32)
            mb = np.full((5, 4), -30000.0, np.float32)
            for qt in range(4):
                mb[1 + qt, qt] = 0.0
            m['c_maskb'] = np.tile(mb.reshape(1, 20), (128, 1))
            lk = np.zeros((128, 2), np.float32)
            lk[:, 1] = 1.0
            m['c_link'] = lk
        in_maps.append(m)
    if _dbg == 'maps':
        return nc, in_maps
    if _dbg == 'time':
        res = run_bass_kernel_spmd(nc, in_maps, core_ids=list(range(8)), trace=True)
        return res.exec_time_ns
    res = run_bass_kernel_spmd(nc, in_maps, core_ids=list(range(8)))
    R = res.results
    if _dbg:
        kernel.dbg = [{n: R[c][n] for n in dumps} for c in range(8)]
    y_sample = np.stack([R[b]['y_o'] for b in range(4)], axis=0)
    y_prompt = np.concatenate([R[4 + j]['y_o'].reshape(4, 256, 1024) for j in range(4)], axis=0)
    nk = np.concatenate([R[4 + j]['ok_o'].reshape(L, 4, 4, 256, 128).transpose(2, 0, 1, 3, 4) for j in range(4)], axis=0)
    nv = np.concatenate([R[4 + j]['ov_o'].reshape(L, 4, 4, 256, 128).transpose(2, 0, 1, 3, 4) for j in range(4)], axis=0)
    nC, nn, nm = [], [], []
    for j in range(4):
        oCn = R[4 + j]['oCn_o']
        oC = oCn[..., 0:128]
        on = oCn[..., 128]
        om = R[4 + j]['om_o'].reshape(L, 2, 4, 4)
        for sq in range(4):
            nC.append(np.stack([oC[:, 0, sq], oC[:, 1, 3 - sq]], axis=1))
            nn.append(np.stack([on[:, 0, sq], on[:, 1, 3 - sq]], axis=1))
            nm.append(np.stack([om[:, 0, :, sq], om[:, 1, :, 3 - sq]], axis=1))
    return (y_prompt.astype(np.float32), y_sample.astype(np.float32), np.ascontiguousarray(nk), np.ascontiguousarray(nv),
            np.stack(nC, axis=0), np.stack(nn, axis=0), np.stack(nm, axis=0))
```
